# Optimizing a Trainium2 kernel written in Bass

```python
import math
import jax, jax.numpy as jnp
from jax import lax
import numpy as np


D_MODEL = 1024
BATCH = 8
SEQ = 4096
DEPTH = 2
DEC_BATCH = 2
DEC_SEQ = 16384
PAST_LEN = 128

D_MIX = D_MODEL
HY_W = D_MIX // 4
HG_W = D_MIX // 2
RG_W = D_MIX - HY_W - HG_W
HY_SHORT = 3
HY_EMB = 33
HY_BANDS = (HY_EMB - 1) // 2
HY_FFN = 64
HY_FAST_DECAY = 0.3
HY_SLOW_DECAY = 1.5
HY_TARGET = 1e-2
HG_HEADS = 4
HG_DK = HG_W // HG_HEADS
HG_DV = HG_W // HG_HEADS
HG_CHUNK = 64
RG_HEADS = 4
RG_HD = RG_W // RG_HEADS
RG_CONV = 4
RG_C = 8.0
D_FF = -(-8 * D_MODEL // (3 * 256)) * 256
HY_IN = 3 * HY_W
HG_IN = 5 * HG_W
RG_IN = 2 * RG_W
D_IN = HY_IN + HG_IN + RG_IN
EPS = 1e-6

kernel_name = 'hymba_style_hyena_hgrn2_rglru_encoder'


def rmsnorm(x, w):
    xf = x.astype(jnp.float32)
    y = xf * lax.rsqrt(jnp.mean(xf * xf, axis=-1, keepdims=True) + EPS)
    return (y * w.astype(jnp.float32)).astype(x.dtype)


def dwconv(x, w, b, left):
    K = w.shape[0]
    L = x.shape[1]
    xp = jnp.pad(x, ((0, 0), (left, K - 1 - left), (0, 0)))
    y = b
    for k in range(K):
        y = y + xp[:, k:k + L] * w[k]
    return y


def hyena_filter(L, w1, b1, w2, b2, w3, b3, w4, freq):
    f32 = jnp.float32
    t = jnp.linspace(0.0, 1.0, L, dtype=f32)[:, None]
    n = jnp.arange(L, dtype=f32)[:, None]
    fb = jnp.linspace(1e-4, HY_BANDS - 1, HY_BANDS, dtype=f32)[None, :]
    ang = fb * n * (2.0 * math.pi / L)
    z = jnp.concatenate([t, jnp.cos(ang), -jnp.sin(ang)], axis=-1)
    fr = freq.astype(f32)
    h = jnp.sin(fr * (z @ w1.astype(f32) + b1.astype(f32)))
    h = jnp.sin(fr * (h @ w2.astype(f32) + b2.astype(f32)))
    h = jnp.sin(fr * (h @ w3.astype(f32) + b3.astype(f32)))
    h = h @ w4.astype(f32)
    deltas = jnp.abs(jnp.linspace(math.log(HY_TARGET) / HY_SLOW_DECAY,
                                  math.log(HY_TARGET) / HY_FAST_DECAY, HY_W, dtype=f32))
    decay = jnp.exp(-t * deltas)
    return h * jnp.concatenate([decay, decay], axis=-1)


def hyena_mixer(u, conv_w, conv_b, w1, b1, w2, b2, w3, b3, w4, freq, bias):
    B, L, _ = u.shape
    uc = dwconv(u, conv_w, conv_b, (HY_SHORT - 1) // 2)
    x0, x1, v = jnp.split(uc, 3, axis=-1)
    v = (v * x1).astype(jnp.float32)
    k = hyena_filter(L, w1, b1, w2, b2, w3, b3, w4, freq)
    kf, kb = k[:, :HY_W], k[:, HY_W:]
    kc = jnp.concatenate([kf, jnp.zeros((1, HY_W), jnp.float32), kb[:0:-1]], axis=0)
    spec = jnp.fft.rfft(v, n=2 * L, axis=1) * jnp.fft.rfft(kc, n=2 * L, axis=0)[None]
    y = jnp.fft.irfft(spec, n=2 * L, axis=1)[:, :L]
    y = y + v * bias.astype(jnp.float32)
    return y * x0.astype(jnp.float32)


def hgrn2_scan(q, k, v, logf):
    B, L, H, Dk = q.shape
    Dv = v.shape[-1]
    C = HG_CHUNK
    n = L // C

    def chunks(a):
        return a.reshape(B, n, C, H, a.shape[-1]).transpose(1, 0, 3, 2, 4)

    causal = jnp.tril(jnp.ones((C, C), bool))[:, :, None]

    def step(S, inp):
        qc, kc, vc, gc = inp
        b = jnp.cumsum(gc, axis=2)
        inter = jnp.einsum('bhtk,bhkv->bhtv', qc * jnp.exp(b), S)
        diff = b[:, :, :, None, :] - b[:, :, None, :, :]
        w = jnp.where(causal, jnp.exp(jnp.where(causal, diff, 0.0)), 0.0)
        A = jnp.einsum('bhtk,bhtsk,bhsk->bhts', qc, w, kc)
        o = inter + jnp.einsum('bhts,bhsv->bhtv', A, vc)
        bl = b[:, :, -1:, :]
        S = jnp.exp(bl[:, :, 0, :])[..., None] * S + jnp.einsum('bhsk,bhsv->bhkv', kc * jnp.exp(bl - b), vc)
        return S, o

    S0 = jnp.zeros((B, H, Dk, Dv), jnp.float32)
    _, o = lax.scan(step, S0, (chunks(q), chunks(k), chunks(v), chunks(logf)))
    return o.transpose(1, 0, 3, 2, 4).reshape(B, L, H, Dv)


def hgrn2_mixer(u, lb_f, lb_b):
    B, L, _ = u.shape
    q, i, zf, zb, g = jnp.split(u, 5, axis=-1)

    def heads(a):
        return a.astype(jnp.float32).reshape(B, L, HG_HEADS, -1)

    q, v = heads(q), heads(i)

    def direction(z, lb, qd, vd):
        z = heads(z)
        lb = lb.astype(jnp.float32).reshape(HG_HEADS, HG_DK)
        logf = jax.nn.log_sigmoid(z) + jnp.log1p(lb * jnp.exp(-z))
        kk = (1.0 - lb) * jax.nn.sigmoid(-z)
        return hgrn2_scan(qd, kk, vd, logf)

    o_f = direction(zf, lb_f, q, v)
    o_b = jnp.flip(direction(jnp.flip(zb, 1), lb_b, jnp.flip(q, 1), jnp.flip(v, 1)), 1)
    return o_f + o_b, g


def lru_combine(e1, e2):
    a1, b1 = e1
    a2, b2 = e2
    return (a1 * a2, a2 * b1 + b2)


def rglru_mixer(u, conv_w, conv_b, wa, ba, wx, bx, lam):
    B, L, _ = u.shape
    xr_in, gate = jnp.split(u, 2, axis=-1)
    xr = dwconv(xr_in, conv_w, conv_b, RG_CONV // 2).astype(jnp.float32)

    def blockdiag(a, w):
        return jnp.einsum('blhi,hij->blhj', a.reshape(B, L, RG_HEADS, RG_HD),
                          w.astype(jnp.float32)).reshape(B, L, RG_W)

    def direction(xd, d):
        r = jax.nn.sigmoid(blockdiag(xd, wa[d]) + ba[d].astype(jnp.float32))
        ig = jax.nn.sigmoid(blockdiag(xd, wx[d]) + bx[d].astype(jnp.float32))
        log_a = -RG_C * r * jax.nn.softplus(-lam[d].astype(jnp.float32))
        a = jnp.exp(log_a)
        bterm = jnp.sqrt(-jnp.expm1(2.0 * log_a)) * (ig * xd)
        _, h = lax.associative_scan(lru_combine, (a, bterm), axis=1)
        return h

    h = direction(xr, 0) + jnp.flip(direction(jnp.flip(xr, 1), 1), 1)
    return h * jax.nn.gelu(gate.astype(jnp.float32))


def token_mixer(h, l, lb_f, lb_b, p):
    B, L, _ = h.shape
    u = h @ p['w_in'][l]
    u_hy = u[..., :HY_IN]
    u_hg = u[..., HY_IN:HY_IN + HG_IN]
    u_rg = u[..., HY_IN + HG_IN:]
    y_hy = hyena_mixer(u_hy, p['hy_conv_w'][l], p['hy_conv_b'][l], p['hy_w1'][l], p['hy_b1'][l],
                       p['hy_w2'][l], p['hy_b2'][l], p['hy_w3'][l], p['hy_b3'][l], p['hy_w4'][l],
                       p['hy_freq'][l], p['hy_bias'][l])
    o_hg, g_hg = hgrn2_mixer(u_hg, lb_f, lb_b)
    y_rg = rglru_mixer(u_rg, p['rg_conv_w'][l], p['rg_conv_b'][l], p['rg_wa'][l], p['rg_ba'][l],
                       p['rg_wx'][l], p['rg_bx'][l], p['rg_lambda'][l])
    gain = p['out_norm_w'][l]
    y_hy = rmsnorm(y_hy, gain[:HY_W])
    y_hg = rmsnorm(o_hg, gain[HY_W:HY_W + HG_W].reshape(HG_HEADS, HG_DV)).reshape(B, L, HG_W)
    y_hg = y_hg * jax.nn.silu(g_hg.astype(jnp.float32))
    y_rg = rmsnorm(y_rg, gain[HY_W + HG_W:])
    y = jnp.concatenate([y_hy, y_hg, y_rg], axis=-1).astype(h.dtype)
    return y @ p['w_out'][l]


def swiglu(h, wg, wu, wd):
    return (jax.nn.silu(h @ wg) * (h @ wu)) @ wd


def trunk(x, c, p):
    lb_all = jax.nn.softmax(p['hg_lb_logits'].astype(jnp.float32), axis=1)
    lb_all = jnp.cumsum(lb_all, axis=1) - lb_all[:, :1]
    for l in range(DEPTH):
        mod = jax.nn.silu(c) @ p['ada_w'][l] + p['ada_b'][l]
        sh1, sc1, g1, sh2, sc2, g2 = jnp.split(mod[:, None, :], 6, axis=-1)
        h = rmsnorm(x, p['norm1_w'][l]) * (1 + sc1) + sh1
        x = x + g1 * token_mixer(h, l, lb_all[0, l], lb_all[1, l], p)
        h = rmsnorm(x, p['norm2_w'][l]) * (1 + sc2) + sh2
        x = x + g2 * swiglu(h, p['ffn_wg'][l], p['ffn_wu'][l], p['ffn_wd'][l])
    return rmsnorm(x, p['final_norm_w'])


def setup_inputs(seed: int = 0) -> dict:
    key = jax.random.key(seed)
    ks = jax.random.split(key, 34)
    f32 = jnp.float32

    def nrm(i, shape, scale):
        return scale * jax.random.normal(ks[i], shape, f32)

    def gain(i, shape):
        return 1.0 + 0.05 * jax.random.normal(ks[i], shape, f32)

    u = jax.random.uniform(ks[30], (DEPTH, 2, RG_W), f32, minval=0.9, maxval=0.999)
    s = u ** (1.0 / RG_C)
    rg_lambda = jnp.log(s) - jnp.log1p(-s)
    return {
        'x_prompt': nrm(0, (BATCH, SEQ, D_MODEL), 1.0),
        'x_sample': nrm(1, (DEC_BATCH, DEC_SEQ, D_MODEL), 1.0),
        'c_prompt': nrm(2, (BATCH, D_MODEL), 1.0),
        'c_sample': nrm(3, (DEC_BATCH, D_MODEL), 1.0),
        'w_in': nrm(4, (DEPTH, D_MODEL, D_IN), D_MODEL ** -0.5),
        'w_out': nrm(5, (DEPTH, D_MIX, D_MODEL), D_MIX ** -0.5),
        'out_norm_w': gain(6, (DEPTH, D_MIX)),
        'ada_w': nrm(7, (DEPTH, D_MODEL, 6 * D_MODEL), 0.5 * D_MODEL ** -0.5),
        'ada_b': nrm(8, (DEPTH, 6 * D_MODEL), 0.02),
        'norm1_w': gain(9, (DEPTH, D_MODEL)),
        'norm2_w': gain(10, (DEPTH, D_MODEL)),
        'final_norm_w': gain(11, (D_MODEL,)),
        'hy_conv_w': nrm(12, (DEPTH, HY_SHORT, HY_IN), HY_SHORT ** -0.5),
        'hy_conv_b': nrm(13, (DEPTH, HY_IN), 0.02),
        'hy_w1': nrm(14, (DEPTH, HY_EMB, HY_FFN), HY_EMB ** -0.5),
        'hy_b1': nrm(15, (DEPTH, HY_FFN), 0.1),
        'hy_w2': nrm(16, (DEPTH, HY_FFN, HY_FFN), HY_FFN ** -0.5),
        'hy_b2': nrm(17, (DEPTH, HY_FFN), 0.1),
        'hy_w3': nrm(18, (DEPTH, HY_FFN, HY_FFN), HY_FFN ** -0.5),
        'hy_b3': nrm(19, (DEPTH, HY_FFN), 0.1),
        'hy_w4': nrm(20, (DEPTH, HY_FFN, 2 * HY_W), HY_FFN ** -0.5),
        'hy_freq': gain(21, (DEPTH, HY_FFN)),
        'hy_bias': nrm(22, (DEPTH, HY_W), 0.5),
        'hg_lb_logits': nrm(23, (2, DEPTH, HG_W), 0.1),
        'rg_conv_w': nrm(24, (DEPTH, RG_CONV, RG_W), RG_CONV ** -0.5),
        'rg_conv_b': nrm(25, (DEPTH, RG_W), 0.02),
        'rg_wa': nrm(26, (DEPTH, 2, RG_HEADS, RG_HD, RG_HD), RG_HD ** -0.5),
        'rg_ba': nrm(27, (DEPTH, 2, RG_W), 0.1),
        'rg_wx': nrm(28, (DEPTH, 2, RG_HEADS, RG_HD, RG_HD), RG_HD ** -0.5),
        'rg_bx': nrm(29, (DEPTH, 2, RG_W), 0.1),
        'rg_lambda': rg_lambda,
        'ffn_wg': nrm(31, (DEPTH, D_MODEL, D_FF), D_MODEL ** -0.5),
        'ffn_wu': nrm(32, (DEPTH, D_MODEL, D_FF), D_MODEL ** -0.5),
        'ffn_wd': nrm(33, (DEPTH, D_FF, D_MODEL), D_FF ** -0.5),
    }


def reference(x_prompt, x_sample, c_prompt, c_sample, w_in, w_out, out_norm_w, ada_w, ada_b,
              norm1_w, norm2_w, final_norm_w, hy_conv_w, hy_conv_b, hy_w1, hy_b1, hy_w2, hy_b2,
              hy_w3, hy_b3, hy_w4, hy_freq, hy_bias, hg_lb_logits, rg_conv_w, rg_conv_b, rg_wa,
              rg_ba, rg_wx, rg_bx, rg_lambda, ffn_wg, ffn_wu, ffn_wd):
    p = dict(w_in=w_in, w_out=w_out, out_norm_w=out_norm_w, ada_w=ada_w, ada_b=ada_b,
             norm1_w=norm1_w, norm2_w=norm2_w, final_norm_w=final_norm_w,
             hy_conv_w=hy_conv_w, hy_conv_b=hy_conv_b, hy_w1=hy_w1, hy_b1=hy_b1, hy_w2=hy_w2,
             hy_b2=hy_b2, hy_w3=hy_w3, hy_b3=hy_b3, hy_w4=hy_w4, hy_freq=hy_freq, hy_bias=hy_bias,
             hg_lb_logits=hg_lb_logits, rg_conv_w=rg_conv_w, rg_conv_b=rg_conv_b, rg_wa=rg_wa,
             rg_ba=rg_ba, rg_wx=rg_wx, rg_bx=rg_bx, rg_lambda=rg_lambda,
             ffn_wg=ffn_wg, ffn_wu=ffn_wu, ffn_wd=ffn_wd)
    y_prompt = trunk(x_prompt, c_prompt, p)
    y_sample = trunk(x_sample, c_sample, p)
    return (y_prompt, y_sample)
```

```python
import contextlib
import math
import numpy as np
import ml_dtypes
import concourse.bass as bass
import concourse.mybir as mybir
from concourse.bass_utils import run_bass_kernel_spmd

F32 = mybir.dt.float32
BF16 = mybir.dt.bfloat16
AF = mybir.ActivationFunctionType
ALU = mybir.AluOpType
PI = math.pi

D = 1024
DIN = 3840
DFF = 2816
DEPTH = 2
NSEG = 4
EPS = 1e-6
HY_EMB = 33
O_N1, O_N2, O_ON, O_ADAB, O_HCW, O_HCB, O_HBIAS, O_RCW, O_RCB, O_RBA, O_RBX, O_RLAM, O_LB, O_HYB, O_FN, O_ND = (
    0, 8, 16, 24, 72, 90, 96, 98, 106, 108, 112, 116, 120, 136, 140, 148)
NSP = 150


class Buf:
    def __init__(self, name, multi=False, excl=False):
        self.name = name
        self.multi = multi
        self.excl = excl
        self.w = {}
        self.r = {}
        self.slot = None


class Eng:
    def __init__(self, prog, e, name):
        self.e = e
        self.name = name
        self.sem = prog.sem("e_" + name)
        self.cnt = 0
        self.waited = {}


class Prog:
    def __init__(self, nc, es):
        self.nc = nc
        self.es = es
        self.nsem = 0
        self.pe = Eng(self, nc.tensor, "pe")
        self.act = Eng(self, nc.scalar, "act")
        self.dve = Eng(self, nc.vector, "dve")
        self.pool = Eng(self, nc.gpsimd, "pool")
        self.sp = Eng(self, nc.sync, "sp")
        self.slots = []
        self.slot_pool = []
        self.phase_bufs = []

    def sem(self, name):
        h = self.es.enter_context(self.nc.semaphore(f"{name}_{self.nsem}"))
        self.nsem += 1
        return h

    def _wait(self, eng, R, W, nowaw=False):
        deps = {}
        war = {}
        for b in R:
            for k, v in b.w.items():
                if deps.get(k, 0) < v:
                    deps[k] = v
        for b in W:
            if not (b.multi or nowaw):
                for k, v in b.w.items():
                    if deps.get(k, 0) < v:
                        deps[k] = v
            for k, v in b.r.items():
                if war.get(k, 0) < v:
                    war[k] = v
        for sem, val in war.items():
            if sem is eng.sem:
                continue
            if deps.get(sem, 0) < val:
                deps[sem] = val
        for sem, val in deps.items():
            if sem is eng.sem and eng is self.pe:
                continue
            if eng.waited.get(sem, 0) >= val:
                continue
            eng.e.wait_ge(sem, val)
            eng.waited[sem] = val

    def _mark(self, R, W, sem, val, nowaw=False):
        for b in W:
            if b.multi or nowaw:
                b.w[sem] = val
            else:
                b.w = {sem: val}
                b.r = {}
        for b in R:
            if b.r.get(sem, 0) < val:
                b.r[sem] = val

    def op(self, eng, fn, R=(), W=()):
        ex = [b for b in R if b.excl]
        if ex:
            R = [b for b in R if not b.excl]
            W = list(W) + ex
        self._wait(eng, R, W)
        inst = fn(eng.e)
        eng.cnt += 1
        inst.then_inc(eng.sem, 1)
        self._mark(R, W, eng.sem, eng.cnt)

    def dma(self, out, in_, R=(), W=(), via=None, nowaw=False, **kw):
        self._wait(self.sp, R, W, nowaw)
        if via.slot is None:
            if self.slot_pool:
                via.slot = self.slot_pool.pop(0)
            else:
                via.slot = [self.sem("dma"), 0]
                self.slots.append(via.slot)
            self.phase_bufs.append(via)
        sl = via.slot
        inst = self.nc.sync.dma_start(out=out, in_=in_, **kw)
        sl[1] += 16
        inst.then_inc(sl[0], 16)
        self._mark(R, W, sl[0], sl[1], nowaw)
        if _SYNC_DMA:
            self.nc.sync.wait_ge(sl[0], sl[1])
            self.sp.waited[sl[0]] = sl[1]

    def barrier(self):
        engs = [self.pe, self.act, self.dve, self.pool, self.sp]
        for E in engs:
            for F in engs:
                if F is E or F.cnt == 0:
                    continue
                if E.waited.get(F.sem, 0) < F.cnt:
                    E.e.wait_ge(F.sem, F.cnt)
                    E.waited[F.sem] = F.cnt
            for sl in self.slots:
                if sl[1] > 0 and E.waited.get(sl[0], 0) < sl[1]:
                    E.e.wait_ge(sl[0], sl[1])
                    E.waited[sl[0]] = sl[1]

    def phase_end(self):
        self.barrier()
        for b in self.phase_bufs:
            self.slot_pool.append(b.slot)
            b.slot = None
        self.phase_bufs = []

    def finish(self):
        for sl in self.slots:
            if self.sp.waited.get(sl[0], 0) < sl[1]:
                self.nc.sync.wait_ge(sl[0], sl[1])


_DBG_STOP = None
_SYNC_DMA = False
_DBG_SUB = 99
_DBG_X = 99
_STAGES = ["c0", "c1", "p0", "p1", "hy", "hg", "rg", "p3"]


def build_program(T):
    SEG = T // NSEG
    stop = _DBG_STOP
    lvl = _STAGES.index(stop) if stop else 99
    NB = T // 128
    nc = bass.Bass("TRN2", target_bir_lowering=False)
    dt_in = lambda name, shape, dt=F32: nc.dram_tensor(name, shape, dt, kind="ExternalInput").ap()
    x_in = dt_in("x", [T, D])
    cT_in = dt_in("cT", [D, NSEG])
    segflag_in = dt_in("segflag", [128, 1])
    zfeat_in = dt_in("zfeat", [HY_EMB, T])
    trow_in = dt_in("trow", [1, T])
    w_in_d = dt_in("w_in", [DEPTH, D, DIN])
    w_out_d = dt_in("w_out", [DEPTH, D, D])
    ada_w_d = dt_in("ada_w", [DEPTH, D, 6 * D])
    wg_d = dt_in("ffn_wg", [DEPTH, D, DFF])
    wu_d = dt_in("ffn_wu", [DEPTH, D, DFF])
    wd_d = dt_in("ffn_wd", [DEPTH, DFF, D])
    sp_d = dt_in("sp", [DEPTH, 128, NSP])
    hw1_d = dt_in("hy_w1", [DEPTH, HY_EMB, 64])
    hw2_d = dt_in("hy_w2", [DEPTH, 64, 64])
    hw3_d = dt_in("hy_w3", [DEPTH, 64, 64])
    hw4_d = dt_in("hy_w4", [DEPTH, 64, 512])
    rwa_d = dt_in("rg_wa", [DEPTH, 2, 4, 64, 64])
    rwx_d = dt_in("rg_wx", [DEPTH, 2, 4, 64, 64])
    y_out = nc.dram_tensor("y", [T, D], F32, kind="ExternalOutput").ap()
    scr = lambda name, shape, dt=F32: nc.dram_tensor(name, shape, dt, kind=("ExternalOutput" if stop else "Internal")).ap()
    XT = scr("XT", [D, T])
    U = scr("U", [DIN, T])
    YM = scr("YM", [D, T], BF16)
    HF = scr("HF", [256, T])
    OF = scr("OF", [512, T])
    KF = scr("KF", [256, 2 * T], BF16)
    DBGY = scr("DBGY", [128, NB * 128]) if stop else None
    DBGV = scr("DBGV", [128, NB * 256], BF16) if stop else None
    bDBG = Buf("DBG", multi=True)
    bXT, bU, bYM, bHF, bOF, bKF, bY = (Buf(n, multi=True) for n in ("XT", "U", "YM", "HF", "OF", "KF", "Y"))
    bIN = Buf("inputs")

    with contextlib.ExitStack() as es:
        P = Prog(nc, es)
        pe, act, dve, pool = P.pe, P.act, P.dve, P.pool

        uniq = [0]

        def sb(name, shape, dt=F32, stack=es):
            uniq[0] += 1
            return stack.enter_context(nc.sbuf_tensor(f"{name}_u{uniq[0]}", shape, dt))

        dbg_n = [0]

        def dbg(name, ap, buf, shape, dt=F32):
            if not stop:
                return
            dbg_n[0] += 1
            dd = nc.dram_tensor(f"DBG_{name}", shape, dt, kind="ExternalOutput").ap()
            P.dma(dd, ap, R=[buf], W=[bDBG], via=buf)

        psf = [es.enter_context(nc.psum_tensor(f"psf{i}", [128, 512], F32)) for i in range(6)]
        psb = [es.enter_context(nc.psum_tensor(f"psb{i}", [128, 1024], BF16)) for i in range(2)]
        bpsf = [Buf(f"psf{i}", excl=True) for i in range(6)]
        bpsb = [Buf(f"psb{i}", excl=True) for i in range(2)]

        ones_bf = sb("ones_bf", [128, 128], BF16)
        ident_bf = sb("ident_bf", [128, 128], BF16)
        ident_f = sb("ident_f", [128, 128], F32)
        mask_lo = sb("mask_lo", [64, 8, 64], F32)
        mask_hi = sb("mask_hi", [64, 8, 64], F32)
        cm_f = sb("cm_f", [128, 512], F32)
        cm_b = sb("cm_b", [128, 512], F32)
        epsb = sb("epsb", [128, 1], F32)
        segflag = sb("segflag_t", [128, 1], F32)
        zero_c = sb("zero_c", [128, 1], F32)
        dummy = sb("dummy", [128, 8], F32)
        bconst = Buf("const")
        P.op(dve, lambda e: e.memset(ones_bf[:], 1.0), W=[bconst])
        P.op(dve, lambda e: e.memset(ident_bf[:], 0.0), W=[bconst])
        P.op(dve, lambda e: e.memset(ident_f[:], 0.0), W=[bconst])
        P.op(dve, lambda e: e.memset(mask_lo[:], 1.0), W=[bconst])
        P.op(dve, lambda e: e.memset(mask_hi[:], 1.0), W=[bconst])
        P.op(dve, lambda e: e.memset(cm_f[:], 1.0), W=[bconst])
        P.op(dve, lambda e: e.memset(cm_b[:], 1.0), W=[bconst])
        P.op(dve, lambda e: e.memset(epsb[:], EPS), W=[bconst])
        P.op(dve, lambda e: e.memset(zero_c[:], 0.0), W=[bconst])
        P.op(pool, lambda e: e.affine_select(out=ident_bf[:], in_=ident_bf[:], pattern=[[-1, 128]],
                                             compare_op=ALU.not_equal, fill=1.0, base=0, channel_multiplier=1),
             R=[bconst], W=[bconst])
        P.op(pool, lambda e: e.affine_select(out=ident_f[:], in_=ident_f[:], pattern=[[-1, 128]],
                                             compare_op=ALU.not_equal, fill=1.0, base=0, channel_multiplier=1),
             R=[bconst], W=[bconst])
        P.op(pool, lambda e: e.affine_select(out=mask_lo[:], in_=mask_lo[:], pattern=[[0, 8], [1, 64]],
                                             compare_op=ALU.is_ge, fill=0.0, base=0, channel_multiplier=-1),
             R=[bconst], W=[bconst])
        P.op(pool, lambda e: e.affine_select(out=mask_hi[:], in_=mask_hi[:], pattern=[[0, 8], [-1, 64]],
                                             compare_op=ALU.is_ge, fill=0.0, base=0, channel_multiplier=1),
             R=[bconst], W=[bconst])
        cmf3 = cm_f[:].rearrange("p (c t) -> p c t", t=64)
        cmb3 = cm_b[:].rearrange("p (c t) -> p c t", t=64)
        P.op(dve, lambda e: e.memset(cmf3[:, :, 0:1], 0.0), R=[bconst], W=[bconst])
        P.op(dve, lambda e: e.memset(cmb3[:, :, 63:64], 0.0), R=[bconst], W=[bconst])
        P.dma(segflag[:], segflag_in[:, :], R=[bIN], W=[bconst], via=bconst)

        spt = sb("spt", [128, NSP], F32)
        bsp = Buf("spt")
        modA = [sb(f"modA{i}", [128, 8, NSEG], F32) for i in range(2)]
        modB = [sb(f"modB{i}", [128, 8, NSEG], F32) for i in range(2)]
        modG = [sb(f"modG{i}", [128, 8, NSEG], F32) for i in range(2)]
        bmod = Buf("mod")
        lbt = sb("lbt", [128, 2, 4], F32)
        omlt = sb("omlt", [128, 2, 4], F32)
        clam = sb("clam", [128, 4], F32)
        clam2 = sb("clam2", [128, 4], F32)
        blay = Buf("layer_small")

        def load_cast(stack, dst, src3, K, N, piece, bdst, tag):
            stg = [sb(f"wst_{tag}{i}", [128, K, piece], F32, stack) for i in range(2)]
            bst = [Buf(f"wst_{tag}{i}") for i in range(2)]
            n0 = 0
            i = 0
            while n0 < N:
                w = min(piece, N - n0)
                P.dma(stg[i % 2][:, :, 0:w], src3[:, :, n0:n0 + w], R=[bIN], W=[bst[i % 2]], via=bst[i % 2])
                if i % 2 == 0:
                    P.op(dve, lambda e, i=i, n0=n0, w=w: e.tensor_copy(out=dst[:, :, n0:n0 + w], in_=stg[i % 2][:, :, 0:w]),
                         R=[bst[i % 2]], W=[bdst])
                else:
                    P.op(act, lambda e, i=i, n0=n0, w=w: e.activation(out=dst[:, :, n0:n0 + w], in_=stg[i % 2][:, :, 0:w], func=AF.Copy),
                         R=[bst[i % 2]], W=[bdst])
                n0 += w
                i += 1

        def rms_rstd(xt, K, NT, sq, rstd, bx, bsq, brstd, ps, bps, inv_n):
            P.op(act, lambda e: e.activation(out=sq[:, 0:K, 0:NT], in_=xt, func=AF.Square), R=[bx], W=[bsq])

            def mm(e):
                for kc in range(K):
                    i = e.matmul(ps[:, 0:NT], lhsT=ones_bf[:], rhs=sq[:, kc, 0:NT], start=(kc == 0), stop=(kc == K - 1))
                return i
            P.op(pe, mm, R=[bsq, bconst], W=[bps])
            P.op(act, lambda e: e.activation(out=rstd[:, 0:NT], in_=ps[:, 0:NT], func=AF.Sqrt, scale=inv_n, bias=epsb[:, 0:1]),
                 R=[bps, bconst], W=[brstd])
            P.op(dve, lambda e: e.reciprocal(out=rstd[:, 0:NT], in_=rstd[:, 0:NT]), R=[brstd], W=[brstd])

        with contextlib.ExitStack() as ph:
            P.phase_end()
            xin = [sb(f"p0x{i}", [128, D], F32, ph) for i in range(2)]
            xo = [sb(f"p0o{i}", [128, 8, 128], F32, ph) for i in range(2)]
            bxin = [Buf(f"p0x{i}") for i in range(2)]
            bxo = [Buf(f"p0o{i}") for i in range(2)]
            XTv = XT.rearrange("(kc p) t -> p kc t", p=128)
            for b in range(NB if lvl >= 1 else 0):
                j = b % 2
                P.dma(xin[j][:], x_in[b * 128:(b + 1) * 128, :], R=[bIN], W=[bxin[j]], via=bxin[j])
                for half in range(2):
                    pb = (2 * b + half) % 4

                    def tr(e, half=half, pb=pb):
                        for q in range(4):
                            kc = half * 4 + q
                            i = e.transpose(out=psf[pb][:, q * 128:(q + 1) * 128], in_=xin[j][:, kc * 128:(kc + 1) * 128],
                                            identity=ident_f[:])
                        return i
                    P.op(pe, tr, R=[bxin[j], bconst], W=[bpsf[pb]])
                    eng = act if half == 0 else dve
                    if half == 0:
                        P.op(act, lambda e, pb=pb: e.activation(out=xo[j][:, 0:4, :], in_=psf[pb][:].rearrange("p (q t) -> p q t", t=128),
                                                                func=AF.Copy), R=[bpsf[pb]], W=[bxo[j]])
                    else:
                        P.op(dve, lambda e, pb=pb: e.tensor_copy(out=xo[j][:, 4:8, :], in_=psf[pb][:].rearrange("p (q t) -> p q t", t=128)),
                             R=[bpsf[pb]], W=[bxo[j]])
                P.dma(XTv[:, :, b * 128:(b + 1) * 128], xo[j][:], R=[bxo[j]], W=[bXT], via=bxo[j])

        for l in range(DEPTH):
            if lvl < 3 or (l > 0 and lvl < 99):
                break
            P.dma(spt[:], sp_d[l], R=[bIN], W=[bsp], via=bsp)
            with contextlib.ExitStack() as ph:
                P.phase_end()
                ct = sb("ada_c", [128, 8, NSEG], F32, ph)
                bct = Buf("ada_c")
                modT = sb("ada_mod", [128, 48, NSEG], F32, ph)
                bmodT = Buf("ada_mod")
                P.dma(ct[:], cT_in.rearrange("(kc p) s -> p kc s", p=128), R=[bIN], W=[bct], via=bct)
                P.op(act, lambda e: e.activation(out=ct[:], in_=ct[:], func=AF.Silu), R=[bct], W=[bct])
                aw = [sb(f"ada_w{i}", [128, 8, 512], F32, ph) for i in range(2)]
                baw = [Buf(f"ada_w{i}") for i in range(2)]
                awv = ada_w_d[l].rearrange("(kc p) n -> p kc n", p=128)
                for pc in range(12):
                    j = pc % 2
                    P.dma(aw[j][:], awv[:, :, pc * 512:(pc + 1) * 512], R=[bIN], W=[baw[j]], via=baw[j])
                    pb = pc % 2

                    def mm(e, j=j, pb=pb):
                        for m in range(4):
                            for kc in range(8):
                                i = e.matmul(psf[pb][:, m * NSEG:(m + 1) * NSEG], lhsT=aw[j][:, kc, m * 128:(m + 1) * 128],
                                             rhs=ct[:, kc, :], start=(kc == 0), stop=(kc == 7))
                        return i
                    P.op(pe, mm, R=[baw[j], bct], W=[bpsf[pb]])
                    P.op(dve, lambda e, pc=pc, pb=pb: e.tensor_tensor(
                        out=modT[:, pc * 4:(pc + 1) * 4, :], in0=psf[pb][:, 0:4 * NSEG].rearrange("p (m s) -> p m s", s=NSEG),
                        in1=spt[:, O_ADAB + pc * 4:O_ADAB + (pc + 1) * 4].unsqueeze(2).broadcast_to([128, 4, NSEG]), op=ALU.add),
                        R=[bpsf[pb], bsp], W=[bmodT])
                for i2, (osh, osc, og, onw) in enumerate(((0, 8, 16, O_N1), (24, 32, 40, O_N2))):
                    P.op(dve, lambda e, osc=osc, onw=onw, i2=i2: e.scalar_tensor_tensor(
                        out=modA[i2][:], in0=modT[:, osc:osc + 8, :], scalar=1.0,
                        in1=spt[:, onw:onw + 8].unsqueeze(2).broadcast_to([128, 8, NSEG]), op0=ALU.add, op1=ALU.mult),
                        R=[bmodT, bsp], W=[bmod])
                    P.op(dve, lambda e, osh=osh, i2=i2: e.tensor_copy(out=modB[i2][:], in_=modT[:, osh:osh + 8, :]), R=[bmodT], W=[bmod])
                    P.op(dve, lambda e, og=og, i2=i2: e.tensor_copy(out=modG[i2][:], in_=modT[:, og:og + 8, :]), R=[bmodT], W=[bmod])
                lbv = spt[:, O_LB:O_LB + 16].rearrange("p (d l h) -> p d l h", d=2, l=2)
                if l == 0:
                    P.op(dve, lambda e: e.memset(lbt[:], 0.0), W=[blay])
                else:
                    P.op(dve, lambda e: e.tensor_tensor(out=lbt[:], in0=lbv[:, :, 1, :], in1=lbv[:, :, 0, :], op=ALU.subtract),
                         R=[bsp], W=[blay])
                    P.op(act, lambda e: e.activation(out=lbt[:], in_=lbt[:], func=AF.Sigmoid), R=[blay], W=[blay])
                P.op(dve, lambda e: e.tensor_scalar(out=omlt[:], in0=lbt[:], scalar1=-1.0, scalar2=1.0, op0=ALU.mult, op1=ALU.add),
                     R=[blay], W=[blay])
                dbg(f"omlt_a{l}", omlt[:].rearrange("p a b -> p (a b)"), blay, [128, 8])
                P.op(act, lambda e: e.activation(out=clam[:], in_=spt[:, O_RLAM:O_RLAM + 4], func=AF.Exp, scale=-1.0), R=[bsp], W=[blay])
                P.op(dve, lambda e: e.tensor_scalar(out=clam[:], in0=clam[:], scalar1=1.0, scalar2=None, op0=ALU.add), R=[blay], W=[blay])
                P.op(act, lambda e: e.activation(out=clam[:], in_=clam[:], func=AF.Ln), R=[blay], W=[blay])
                P.op(dve, lambda e: e.tensor_scalar(out=clam2[:], in0=clam[:], scalar1=-16.0, scalar2=None, op0=ALU.mult), R=[blay], W=[blay])
                P.op(dve, lambda e: e.tensor_scalar(out=clam[:], in0=clam[:], scalar1=-8.0, scalar2=None, op0=ALU.mult), R=[blay], W=[blay])

            with contextlib.ExitStack() as ph:
                P.phase_end()
                NT = 512
                wbf = sb("p1w", [128, 8, DIN], BF16, ph)
                bw = Buf("p1w", multi=True)
                xt = [sb(f"p1x{i}", [128, 8, NT], F32, ph) for i in range(2)]
                bxt = [Buf(f"p1x{i}") for i in range(2)]
                hb = [sb(f"p1h{i}", [128, 8, NT], BF16, ph) for i in range(2)]
                bhb = [Buf(f"p1h{i}") for i in range(2)]
                sq = sb("p1sq", [128, 8, NT], BF16, ph)
                bsq = Buf("p1sq")
                rstd = sb("p1rstd", [128, NT], F32, ph)
                brstd = Buf("p1rstd")
                stg = [sb(f"p1st{i}", [128, 3, NT], F32, ph) for i in range(2)]
                bstg = [Buf(f"p1st{i}") for i in range(2)]
                with contextlib.ExitStack() as wl:
                    P.phase_end()
                    load_cast(wl, wbf, w_in_d[l].rearrange("(kc p) n -> p kc n", p=128), 8, DIN, 240, bw, "p1")
                XTv = XT.rearrange("(kc p) t -> p kc t", p=128)
                Uv = U.rearrange("(c p) t -> p c t", p=128)
                for i in range(T // NT):
                    j = i % 2
                    t0 = i * NT
                    seg = t0 // SEG
                    P.dma(xt[j][:], XTv[:, :, t0:t0 + NT], R=[bXT], W=[bxt[j]], via=bxt[j])
                    rms_rstd(xt[j][:], 8, NT, sq, rstd, bxt[j], bsq, brstd, psf[0], bpsf[0], 1.0 / D)
                    P.op(dve, lambda e: e.tensor_tensor(out=xt[j][:], in0=xt[j][:], in1=rstd[:].unsqueeze(1).broadcast_to([128, 8, NT]),
                                                        op=ALU.mult), R=[bxt[j], brstd], W=[bxt[j]])
                    P.op(dve, lambda e: e.tensor_tensor(out=xt[j][:], in0=xt[j][:],
                                                        in1=modA[0][:, :, seg:seg + 1].broadcast_to([128, 8, NT]), op=ALU.mult),
                         R=[bxt[j], bmod], W=[bxt[j]])
                    P.op(dve, lambda e: e.tensor_tensor(out=hb[j][:], in0=xt[j][:],
                                                        in1=modB[0][:, :, seg:seg + 1].broadcast_to([128, 8, NT]), op=ALU.add),
                         R=[bxt[j], bmod], W=[bhb[j]])
                    for mc in range(DIN // 128):
                        pb = 1 + mc % 4

                        def mm(e, mc=mc, pb=pb):
                            for kc in range(8):
                                i_ = e.matmul(psf[pb][:], lhsT=wbf[:, kc, mc * 128:(mc + 1) * 128], rhs=hb[j][:, kc, :],
                                              start=(kc == 0), stop=(kc == 7))
                            return i_
                        P.op(pe, mm, R=[bw, bhb[j]], W=[bpsf[pb]])
                        s = (mc // 3) % 2
                        if mc % 2 == 0:
                            P.op(act, lambda e, pb=pb, s=s, mc=mc: e.activation(out=stg[s][:, mc % 3, :], in_=psf[pb][:], func=AF.Copy),
                                 R=[bpsf[pb]], W=[bstg[s]])
                        else:
                            P.op(dve, lambda e, pb=pb, s=s, mc=mc: e.tensor_copy(out=stg[s][:, mc % 3, :], in_=psf[pb][:]),
                                 R=[bpsf[pb]], W=[bstg[s]])
                        if mc % 3 == 2:
                            P.dma(Uv[:, mc - 2:mc + 1, t0:t0 + NT], stg[s][:], R=[bstg[s]], W=[bU], via=bstg[s])

            dbg(f"omlt_b{l}", omlt[:].rearrange("p a b -> p (a b)"), blay, [128, 8])
            if lvl < 4:
                break
            with contextlib.ExitStack() as ph:
                P.phase_end()
                vT = sb("hy_vT", [128, NB, 256], BF16, ph)
                bvT = Buf("hy_vT", multi=True)
                with contextlib.ExitStack() as fg:
                    P.phase_end()
                    NF = 512
                    w1 = sb("hy_w1", [HY_EMB, 64], F32, fg)
                    w2 = sb("hy_w2", [64, 64], F32, fg)
                    w3 = sb("hy_w3", [64, 64], F32, fg)
                    w4 = sb("hy_w4", [64, 512], F32, fg)
                    bwf = Buf("hy_wf")
                    for tl, src in ((w1, hw1_d), (w2, hw2_d), (w3, hw3_d), (w4, hw4_d)):
                        P.dma(tl[:], src[l], R=[bIN], W=[bwf], via=bwf, nowaw=True)
                    zf = [sb(f"hy_zf{i}", [HY_EMB, NF], F32, fg) for i in range(2)]
                    bzf = [Buf(f"hy_zf{i}") for i in range(2)]
                    tr_ = [sb(f"hy_tr{i}", [128, NF], F32, fg) for i in range(2)]
                    btr = [Buf(f"hy_tr{i}") for i in range(2)]
                    hh = sb("hy_h", [64, NF], F32, fg)
                    bhh = Buf("hy_h")
                    tmpa = sb("hy_tmpa", [64, NF], F32, fg)
                    tmpb = sb("hy_tmpb", [64, NF], F32, fg)
                    btmp = Buf("hy_tmp")
                    dec = sb("hy_dec", [128, 2, NF], F32, fg)
                    bdec = Buf("hy_dec")
                    kt = sb("hy_kt", [128, NF], F32, fg)
                    bkt = Buf("hy_kt")
                    kst = [sb(f"hy_kst{i}", [128, NF], BF16, fg) for i in range(4)]
                    bkst = [Buf(f"hy_kst{i}") for i in range(4)]
                    kcount = 0
                    for n0 in range(0, T if _DBG_SUB >= 1 else 0, NF):
                        j = (n0 // NF) % 2
                        P.dma(zf[j][:], zfeat_in[:, n0:n0 + NF], R=[bIN], W=[bzf[j]], via=bzf[j])
                        P.dma(tr_[j][:], trow_in[0:1, n0:n0 + NF].broadcast_to([128, NF]), R=[bIN], W=[btr[j]], via=btr[j])
                        src_t, bsrc, Kd = zf[j], bzf[j], HY_EMB
                        for li, wl_ in enumerate((w1, w2, w3)):
                            P.op(pe, lambda e, wl_=wl_, src_t=src_t, Kd=Kd: e.matmul(psf[0][0:64, :], lhsT=wl_[0:Kd, :], rhs=src_t[0:Kd, :],
                                                                                   start=True, stop=True),
                                 R=[bwf, bsrc], W=[bpsf[0]])
                            P.op(dve, lambda e, li=li: e.tensor_scalar(out=tmpa[:], in0=psf[0][0:64, :], scalar1=spt[0:64, O_HYB + li:O_HYB + li + 1],
                                                                       scalar2=spt[0:64, O_HYB + 3:O_HYB + 4], op0=ALU.add, op1=ALU.mult),
                                 R=[bpsf[0], bsp], W=[btmp])
                            P.op(dve, lambda e: e.tensor_scalar(out=tmpb[:], in0=tmpa[:], scalar1=PI, scalar2=-2 * PI, op0=ALU.is_gt, op1=ALU.mult),
                                 R=[btmp], W=[btmp])
                            P.op(dve, lambda e: e.tensor_tensor(out=tmpb[:], in0=tmpb[:], in1=tmpa[:], op=ALU.add), R=[btmp], W=[btmp])
                            P.op(dve, lambda e: e.tensor_scalar(out=tmpa[:], in0=tmpa[:], scalar1=-PI, scalar2=2 * PI, op0=ALU.is_lt, op1=ALU.mult),
                                 R=[btmp], W=[btmp])
                            P.op(dve, lambda e: e.tensor_tensor(out=tmpb[:], in0=tmpb[:], in1=tmpa[:], op=ALU.add), R=[btmp], W=[btmp])
                            P.op(act, lambda e: e.activation(out=hh[:], in_=tmpb[:], func=AF.Sin), R=[btmp], W=[bhh])
                            src_t, bsrc, Kd = hh, bhh, 64
                        for cc in range(2):
                            P.op(act, lambda e, cc=cc: e.activation(out=dec[:, cc, :], in_=tr_[j][:], func=AF.Exp,
                                                                    scale=spt[:, O_ND + cc:O_ND + cc + 1]), R=[btr[j], bsp], W=[bdec])
                        for q in range(4):
                            cc = q % 2
                            pb = 1 + q % 2
                            P.op(pe, lambda e, q=q, pb=pb: e.matmul(psf[pb][:], lhsT=w4[:, q * 128:(q + 1) * 128], rhs=hh[:], start=True, stop=True),
                                 R=[bwf, bhh], W=[bpsf[pb]])
                            P.op(dve, lambda e, pb=pb, cc=cc: e.tensor_tensor(out=kt[:], in0=psf[pb][:], in1=dec[:, cc, :], op=ALU.mult),
                                 R=[bpsf[pb], bdec], W=[bkt])
                            ks = kcount % 4
                            kcount += 1
                            if q < 2:
                                if n0 == 0:
                                    P.op(dve, lambda e, cc=cc: e.tensor_tensor(out=kt[:, 0:1], in0=kt[:, 0:1],
                                                                               in1=spt[:, O_HBIAS + cc:O_HBIAS + cc + 1], op=ALU.add),
                                         R=[bkt, bsp], W=[bkt])
                                P.op(act, lambda e, ks=ks: e.activation(out=kst[ks][:], in_=kt[:], func=AF.Copy), R=[bkt], W=[bkst[ks]])
                                P.dma(KF[cc * 128:(cc + 1) * 128, T - 1 + n0:T - 1 + n0 + NF], kst[ks][:], R=[bkst[ks]], W=[bKF], via=bkst[ks])
                            else:
                                P.op(dve, lambda e, ks=ks: e.tensor_copy(out=kst[ks][:, ::-1], in_=kt[:]), R=[bkt], W=[bkst[ks]])
                                if n0 == 0:
                                    P.dma(KF[cc * 128:(cc + 1) * 128, T - NF:T - 1], kst[ks][:, 0:NF - 1], R=[bkst[ks]], W=[bKF], via=bkst[ks])
                                else:
                                    P.dma(KF[cc * 128:(cc + 1) * 128, T - n0 - NF:T - n0], kst[ks][:], R=[bkst[ks]], W=[bKF], via=bkst[ks])
                with contextlib.ExitStack() as p1:
                    P.phase_end()
                    NT = 512
                    ux = [sb(f"hy_ux{i}", [128, 4, NT + 2], F32, p1) for i in range(2)]
                    bux = [Buf(f"hy_ux{i}") for i in range(2)]
                    cv = sb("hy_cv", [128, 4, NT], F32, p1)
                    bcv = Buf("hy_cv")
                    vrev = [sb(f"hy_vrev{i}", [128, 2, NT], BF16, p1) for i in range(2)]
                    bvrev = [Buf(f"hy_vrev{i}") for i in range(2)]
                    Uq = U[256:768, :].rearrange("(q p) t -> p q t", p=128)
                    for i in range(T // NT if _DBG_SUB >= 2 else 0):
                        j = i % 2
                        t0 = i * NT
                        lo = max(t0 - 1, 0)
                        hi = min(t0 + NT + 1, T)
                        P.dma(ux[j][:, :, lo - (t0 - 1):hi - (t0 - 1)], Uq[:, :, lo:hi], R=[bU], W=[bux[j]], via=bux[j])
                        if t0 == 0:
                            P.op(dve, lambda e: e.memset(ux[j][:, :, 0:1], 0.0), W=[bux[j]])
                        elif t0 % SEG == 0:
                            P.op(dve, lambda e: e.tensor_scalar(out=ux[j][:, :, 0:1], in0=ux[j][:, :, 0:1], scalar1=segflag[:, 0:1],
                                                                 scalar2=None, op0=ALU.mult), R=[bux[j], bconst], W=[bux[j]])
                        if t0 + NT == T:
                            P.op(dve, lambda e: e.memset(ux[j][:, :, NT + 1:NT + 2], 0.0), W=[bux[j]])
                        elif (t0 + NT) % SEG == 0:
                            P.op(dve, lambda e: e.tensor_scalar(out=ux[j][:, :, NT + 1:NT + 2], in0=ux[j][:, :, NT + 1:NT + 2],
                                                                 scalar1=segflag[:, 0:1], scalar2=None, op0=ALU.mult),
                                 R=[bux[j], bconst], W=[bux[j]])
                        for q in range(4):
                            ch = 2 + q
                            eng = dve if q % 2 == 0 else pool
                            wq = lambda k, ch=ch: spt[:, O_HCW + ch * 3 + k:O_HCW + ch * 3 + k + 1]
                            P.op(dve, lambda e, q=q, ch=ch, wq=wq: e.tensor_scalar(out=cv[:, q, :], in0=ux[j][:, q, 0:NT], scalar1=wq(0),
                                                                                  scalar2=spt[:, O_HCB + ch:O_HCB + ch + 1], op0=ALU.mult, op1=ALU.add),
                                 R=[bux[j], bsp], W=[bcv])
                            for k in (1, 2):
                                P.op(dve, lambda e, q=q, k=k, wq=wq: e.scalar_tensor_tensor(out=cv[:, q, :], in0=ux[j][:, q, k:k + NT], scalar=wq(k),
                                                                                           in1=cv[:, q, :], op0=ALU.mult, op1=ALU.add),
                                     R=[bux[j], bsp, bcv], W=[bcv])
                        for cc in range(2 if _DBG_X >= 1 else 0):
                            P.op(dve, lambda e, cc=cc: e.tensor_tensor(out=vrev[j][:, cc, ::-1], in0=cv[:, cc, :], in1=cv[:, 2 + cc, :], op=ALU.mult),
                                 R=[bcv], W=[bvrev[j]])
                        P.op(dve, lambda e: e.memset(dummy[:], 0.0), W=[bvrev[j]])
                        for cc in range(2 if _DBG_X >= 2 else 0):
                            pbi = cc

                            def tr(e, cc=cc, pbi=pbi):
                                for bq in range(4):
                                    i_ = e.transpose(out=psb[pbi][:, bq * 128:(bq + 1) * 128], in_=vrev[j][:, cc, bq * 128:(bq + 1) * 128],
                                                     identity=ident_bf[:])
                                return i_
                            P.op(pe, tr, R=[bvrev[j], bconst], W=[bpsb[pbi]])
                            B0 = t0 // 128
                            dst = vT[:, B0 + 3::-1, cc * 128:(cc + 1) * 128] if B0 == 0 else vT[:, B0 + 3:B0 - 1:-1, cc * 128:(cc + 1) * 128]
                            P.op(dve, lambda e, dst=dst, pbi=pbi: e.tensor_copy(out=dst, in_=psb[pbi][:, 0:512].rearrange("p (b c) -> p b c", c=128)),
                                 R=[bpsb[pbi]], W=[bvT])
                with contextlib.ExitStack() as p2:
                    P.phase_end()
                    BS = NB // NSEG
                    yT = sb("hy_yT", [128, NB, 128], F32, p2)
                    byT = Buf("hy_yT", multi=True)
                    NDP = 32
                    NE = 2 * NB - 1
                    npieces = (NE + NDP - 1) // NDP
                    NPB = 3
                    pc_t = [sb(f"hy_pc{i}", [128, NDP * 128], BF16, p2) for i in range(NPB)]
                    bpc = [Buf(f"hy_pc{i}") for i in range(NPB)]
                    pcn_t = [sb(f"hy_pcn{i}", [128, NDP * 128], BF16, p2) for i in range(2)]
                    bpcn = [Buf(f"hy_pcn{i}") for i in range(2)]
                    sfm1 = sb("hy_sfm1", [128, 1], F32, p2)
                    P.op(dve, lambda e: e.tensor_scalar(out=sfm1[:], in0=segflag[:], scalar1=-1.0, scalar2=None, op0=ALU.add), R=[bconst], W=[bconst])
                    NT = 512
                    u0 = [sb(f"hy_u0{i}", [128, NT + 2], F32, p2) for i in range(2)]
                    bu0 = [Buf(f"hy_u0{i}") for i in range(2)]
                    c0 = sb("hy_c0", [128, NT], F32, p2)
                    bc0 = Buf("hy_c0")
                    yfo = [sb(f"hy_yfo{i}", [128, NT], F32, p2) for i in range(2)]
                    byfo = [Buf(f"hy_yfo{i}") for i in range(2)]
                    pcount = 0
                    ncount = 0
                    e_mid = NB - 1
                    p_mid = e_mid // NDP
                    order = [p_mid] + [p for p in range(npieces) if p != p_mid]
                    for cc in range(2 if _DBG_SUB >= 3 else 0):
                        for ci in range(128):
                            c = cc * 128 + ci
                            pb = ci % 4
                            ybank = psf[pb]
                            nmm = 0
                            total = NE + sum(3 for ee in range(NE) if 0 < abs(ee - (NB - 1)) <= BS)
                            for p_ in order:
                                e0 = p_ * NDP
                                e1 = min(e0 + NDP, NE)
                                W_ = (e1 - e0) * 128
                                base = (T - 1) + (e0 - (NB - 1)) * 128 - 127
                                k = pcount % NPB
                                pcount += 1
                                src = bass.AP(tensor=KF.tensor, offset=c * 2 * T + base, ap=[[1, 128], [1, W_]])
                                P.dma(pc_t[k][:, 0:W_], src, R=[bKF], W=[bpc[k]], via=bpc[k])
                                es_ = list(range(e0, e1))
                                if p_ == p_mid:
                                    es_ = [e_mid] + [e for e in es_ if e != e_mid]
                                cross = [ee for ee in es_ if 0 < abs(ee - (NB - 1)) <= BS]
                                kn = None
                                if cross:
                                    kn = ncount % 2
                                    ncount += 1
                                    ca, cb_ = min(cross) - e0, max(cross) - e0 + 1
                                    if ncount % 2 == 0:
                                        P.op(dve, lambda e, k=k, kn=kn, ca=ca, cb_=cb_: e.tensor_scalar(
                                            out=pcn_t[kn][:, ca * 128:cb_ * 128], in0=pc_t[k][:, ca * 128:cb_ * 128], scalar1=sfm1[:, 0:1], scalar2=None,
                                            op0=ALU.mult), R=[bpc[k], bconst], W=[bpcn[kn]])
                                    else:
                                        P.op(act, lambda e, k=k, kn=kn, ca=ca, cb_=cb_: e.activation(
                                            out=pcn_t[kn][:, ca * 128:cb_ * 128], in_=pc_t[k][:, ca * 128:cb_ * 128], func=AF.Copy, scale=sfm1[:, 0:1]),
                                            R=[bpc[k], bconst], W=[bpcn[kn]])

                                def mm(e, es_=es_, e0=e0, k=k, kn=kn, ci=ci, cc=cc, ybank=ybank, nmm=nmm, total=total):
                                    i_ = None
                                    n_ = nmm
                                    for ee in es_:
                                        d = ee - (NB - 1)
                                        i0 = max(0, d)
                                        i1 = min(NB - 1, NB - 1 + d)
                                        i_ = e.matmul(ybank[:, i0:i1 + 1], lhsT=pc_t[k][:, (ee - e0) * 128:(ee - e0 + 1) * 128],
                                                      rhs=vT[:, i0 - d:i1 - d + 1, cc * 128 + ci], start=(n_ == 0), stop=(n_ == total - 1),
                                                      skip_group_check=True)
                                        n_ += 1
                                        if 0 < abs(d) <= BS:
                                            for s_ in range(1, NSEG):
                                                if d > 0:
                                                    a0, a1 = s_ * BS, s_ * BS + d - 1
                                                else:
                                                    a0, a1 = s_ * BS + d, s_ * BS - 1
                                                i_ = e.matmul(ybank[:, a0:a1 + 1], lhsT=pcn_t[kn][:, (ee - e0) * 128:(ee - e0 + 1) * 128],
                                                              rhs=vT[:, a0 - d:a1 - d + 1, cc * 128 + ci], start=False, stop=(n_ == total - 1),
                                                              skip_group_check=True)
                                                n_ += 1
                                    return i_
                                rl = [bpc[k], bvT] + ([bpcn[kn]] if kn is not None else [])
                                P.op(pe, mm, R=rl, W=[bpsf[pb]])
                                nmm += len(es_) + 3 * len(cross)
                            if ci % 2 == 0:
                                P.op(act, lambda e, ci=ci, ybank=ybank: e.activation(out=yT[:, :, ci], in_=ybank[:, 0:NB], func=AF.Copy),
                                     R=[bpsf[pb]], W=[byT])
                            else:
                                P.op(dve, lambda e, ci=ci, ybank=ybank: e.tensor_copy(out=yT[:, :, ci], in_=ybank[:, 0:NB]),
                                     R=[bpsf[pb]], W=[byT])
                        if stop and cc == 0:
                            P.dma(DBGY[:, :], yT[:].rearrange("p b c -> p (b c)"), R=[byT], W=[bDBG], via=byT)
                            P.dma(DBGV[:, :], vT[:].rearrange("p b c -> p (b c)"), R=[bvT], W=[bDBG], via=bvT)
                        for i in range(T // NT if _DBG_SUB >= 4 else 0):
                            j = i % 2
                            t0 = i * NT
                            lo = max(t0 - 1, 0)
                            hi = min(t0 + NT + 1, T)
                            P.dma(u0[j][:, lo - (t0 - 1):hi - (t0 - 1)], U[cc * 128:(cc + 1) * 128, lo:hi], R=[bU], W=[bu0[j]], via=bu0[j])
                            if t0 == 0:
                                P.op(dve, lambda e: e.memset(u0[j][:, 0:1], 0.0), W=[bu0[j]])
                            elif t0 % SEG == 0:
                                P.op(dve, lambda e: e.tensor_scalar(out=u0[j][:, 0:1], in0=u0[j][:, 0:1], scalar1=segflag[:, 0:1],
                                                                    scalar2=None, op0=ALU.mult), R=[bu0[j], bconst], W=[bu0[j]])
                            if t0 + NT == T:
                                P.op(dve, lambda e: e.memset(u0[j][:, NT + 1:NT + 2], 0.0), W=[bu0[j]])
                            elif (t0 + NT) % SEG == 0:
                                P.op(dve, lambda e: e.tensor_scalar(out=u0[j][:, NT + 1:NT + 2], in0=u0[j][:, NT + 1:NT + 2],
                                                                    scalar1=segflag[:, 0:1], scalar2=None, op0=ALU.mult),
                                     R=[bu0[j], bconst], W=[bu0[j]])
                            wq = lambda k_, cc=cc: spt[:, O_HCW + cc * 3 + k_:O_HCW + cc * 3 + k_ + 1]
                            P.op(dve, lambda e, wq=wq: e.tensor_scalar(out=c0[:], in0=u0[j][:, 0:NT], scalar1=wq(0),
                                                                      scalar2=spt[:, O_HCB + cc:O_HCB + cc + 1], op0=ALU.mult, op1=ALU.add),
                                 R=[bu0[j], bsp], W=[bc0])
                            for k_ in (1, 2):
                                P.op(dve, lambda e, k_=k_, wq=wq: e.scalar_tensor_tensor(out=c0[:], in0=u0[j][:, k_:k_ + NT], scalar=wq(k_),
                                                                                        in1=c0[:], op0=ALU.mult, op1=ALU.add),
                                     R=[bu0[j], bsp, bc0], W=[bc0])
                            pb = 4 + i % 2

                            def tr(e, pb=pb, t0=t0):
                                for bq in range(4):
                                    i_ = e.transpose(out=psf[pb][:, bq * 128:(bq + 1) * 128], in_=yT[:, t0 // 128 + bq, :], identity=ident_f[:])
                                return i_
                            P.op(pe, tr, R=[byT, bconst], W=[bpsf[pb]])
                            P.op(dve, lambda e, pb=pb: e.tensor_tensor(out=yfo[j][:], in0=psf[pb][:], in1=c0[:], op=ALU.mult),
                                 R=[bpsf[pb], bc0], W=[byfo[j]])
                            P.dma(HF[cc * 128:(cc + 1) * 128, t0:t0 + NT], yfo[j][:], R=[byfo[j]], W=[bHF], via=byfo[j])
                with contextlib.ExitStack() as p3:
                    P.phase_end()
                    NT = 512
                    yf = [sb(f"hy_yf{i}", [128, 2, NT], F32, p3) for i in range(2)]
                    byf = [Buf(f"hy_yf{i}") for i in range(2)]
                    sq = sb("hy_sq", [128, 2, NT], BF16, p3)
                    bsq = Buf("hy_sq")
                    rstd = sb("hy_rstd", [128, NT], F32, p3)
                    brstd = Buf("hy_rstd")
                    yo = [sb(f"hy_yo{i}", [128, 2, NT], BF16, p3) for i in range(2)]
                    byo = [Buf(f"hy_yo{i}") for i in range(2)]
                    HFv = HF.rearrange("(q p) t -> p q t", p=128)
                    YM0 = YM[0:256, :].rearrange("(q p) t -> p q t", p=128)
                    for i in range(T // NT if _DBG_SUB >= 4 else 0):
                        j = i % 2
                        t0 = i * NT
                        P.dma(yf[j][:], HFv[:, :, t0:t0 + NT], R=[bHF], W=[byf[j]], via=byf[j])
                        rms_rstd(yf[j][:], 2, NT, sq, rstd, byf[j], bsq, brstd, psf[2], bpsf[2], 1.0 / 256)
                        for q in range(2):
                            P.op(dve, lambda e, q=q: e.scalar_tensor_tensor(out=yo[j][:, q, :], in0=yf[j][:, q, :],
                                                                          scalar=spt[:, O_ON + q:O_ON + q + 1], in1=rstd[:],
                                                                          op0=ALU.mult, op1=ALU.mult),
                                 R=[byf[j], brstd, bsp], W=[byo[j]])
                        P.dma(YM0[:, :, t0:t0 + NT], yo[j][:], R=[byo[j]], W=[bYM], via=byo[j])

            dbg(f"omlt_c{l}", omlt[:].rearrange("p a b -> p (a b)"), blay, [128, 8])
            if lvl < 5:
                break
            with contextlib.ExitStack() as ph:
                P.phase_end()
                NT = 512
                NCH = NT // 64
                KW = [16, 32, 48, 64]
                KO = [0, NCH * 16, NCH * 48, NCH * 96]
                Ug = U[768:768 + 2560, :].rearrange("(g h p) t -> p g h t", g=5, h=4)
                OFv = OF.rearrange("(h p) t -> p h t", p=128)
                YM1 = YM[256:768, :].rearrange("(h p) t -> p h t", p=128)
                mask4 = sb("hg_mask4", [64, NCH, 4, 16], F32, ph)
                bmask4 = Buf("hg_mask4")
                P.op(dve, lambda e: e.memset(mask4[:], 1.0), W=[bmask4])
                P.op(pool, lambda e: e.affine_select(out=mask4[:], in_=mask4[:], pattern=[[0, NCH], [16, 4], [1, 16]],
                                                     compare_op=ALU.is_ge, fill=0.0, base=0, channel_multiplier=-1),
                     R=[bmask4], W=[bmask4])
                qv = [sb(f"hg_qv{i}", [128, 2, NT], F32, ph) for i in range(2)]
                bqv = [Buf(f"hg_qv{i}") for i in range(2)]
                zt = [sb(f"hg_z{i}", [128, NT], F32, ph) for i in range(2)]
                bzt = [Buf(f"hg_z{i}") for i in range(2)]
                gt = [sb(f"hg_g{i}", [128, NT], F32, ph) for i in range(2)]
                bgt = [Buf(f"hg_g{i}") for i in range(2)]
                oft = [sb(f"hg_of{i}", [128, NT], F32, ph) for i in range(2)]
                boft = [Buf(f"hg_of{i}") for i in range(2)]
                qr = sb("hg_qr", [128, 2, NT], F32, ph)
                bqr = Buf("hg_qr")
                fA = sb("hg_fA", [128, NT], F32, ph)
                fB = sb("hg_fB", [128, NT], F32, ph)
                fC = sb("hg_fC", [128, NT], F32, ph)
                fD = sb("hg_fD", [128, NT], F32, ph)
                fK = sb("hg_fK", [128, NCH * 160], F32, ph)
                bfA, bfB, bfC, bfD, bfK = Buf("fA"), Buf("fB"), Buf("fC"), Buf("fD"), Buf("fK")
                rr = sb("hg_rr", [128, NCH, 4], F32, ph)
                brr_ = Buf("hg_rr")
                P.op(dve, lambda e: e.memset(rr[:], 0.0), W=[brr_])
                ebl = sb("hg_ebl", [128, NCH], F32, ph)
                bebl = Buf("hg_ebl")
                Qh = sb("hg_Qh", [128, NT], BF16, ph)
                Kh = sb("hg_Kh", [128, NT], BF16, ph)
                Qs = sb("hg_Qs", [128, NT], BF16, ph)
                KI = sb("hg_KI", [128, NCH * 160], BF16, ph)
                vbf = sb("hg_vbf", [128, NT], BF16, ph)
                bQh, bKh, bQs, bKI, bvbf = Buf("Qh"), Buf("Kh"), Buf("Qs"), Buf("KI"), Buf("vbf")
                Ktok = sb("hg_Ktok", [64, NCH, 128], BF16, ph)
                Vtok = sb("hg_Vtok", [64, NCH, 128], BF16, ph)
                bKtok, bVtok = Buf("Ktok"), Buf("Vtok")
                Am = sb("hg_Am", [64, NCH, 4, 16], BF16, ph)
                bAm = Buf("Am")
                S = [sb(f"hg_S{h}", [128, 128], F32, ph) for h in range(4)]
                bS = [Buf(f"hg_S{h}") for h in range(4)]
                Spb = [sb(f"hg_Spb{i}", [128, 128], BF16, ph) for i in range(2)]
                bSpb = [Buf(f"hg_Spb{i}") for i in range(2)]
                sqh = sb("hg_sq", [128, 1, NT], BF16, ph)
                bsqh = Buf("hg_sq")
                rstdh = sb("hg_rstd", [128, NT], F32, ph)
                brstdh = Buf("hg_rstd")
                yo = [sb(f"hg_yo{i}", [128, NT], BF16, ph) for i in range(2)]
                byo = [Buf(f"hg_yo{i}") for i in range(2)]
                it = 0
                spcount = 0
                for dr in range(2):
                    tiles = list(range(T // NT)) if dr == 0 else list(range(T // NT - 1, -1, -1))
                    for h in range(4):
                        P.op(dve, lambda e, h=h: e.memset(S[h][:], 0.0), W=[bS[h]])
                    for i in tiles:
                        t0 = i * NT
                        for h in range(4):
                            j = it % 2
                            it += 1
                            P.dma(qv[j][:], Ug[:, 0:2, h, t0:t0 + NT], R=[bU], W=[bqv[j]], via=bqv[j])
                            P.dma(zt[j][:], Ug[:, 2 + dr, h, t0:t0 + NT], R=[bU], W=[bzt[j]], via=bzt[j])
                            if dr == 1:
                                P.dma(gt[j][:], Ug[:, 4, h, t0:t0 + NT], R=[bU], W=[bgt[j]], via=bgt[j])
                                P.dma(oft[j][:], OFv[:, h, t0:t0 + NT], R=[bOF], W=[boft[j]], via=boft[j])
                            at_seg = (t0 % SEG == 0) if dr == 0 else ((t0 + NT) % SEG == 0)
                            if at_seg:
                                P.op(dve, lambda e, h=h: e.tensor_scalar(out=S[h][:], in0=S[h][:], scalar1=segflag[:, 0:1], scalar2=None,
                                                                        op0=ALU.mult), R=[bS[h], bconst], W=[bS[h]])
                            if dr == 0:
                                P.op(act, lambda e: e.activation(out=fA[:], in_=zt[j][:], func=AF.Sigmoid), R=[bzt[j]], W=[bfA])
                                P.op(pool, lambda e: e.tensor_copy(out=qr[:], in_=qv[j][:]), R=[bqv[j]], W=[bqr])
                            else:
                                P.op(dve, lambda e: e.tensor_copy(out=fD[:], in_=zt[j][:, ::-1]), R=[bzt[j]], W=[bfD])
                                P.op(act, lambda e: e.activation(out=fA[:], in_=fD[:], func=AF.Sigmoid), R=[bfD], W=[bfA])
                                for g_ in range(2):
                                    P.op(dve, lambda e, g_=g_: e.tensor_copy(out=qr[:, g_, :], in_=qv[j][:, g_, ::-1]), R=[bqv[j]], W=[bqr])
                            P.op(dve, lambda e, h=h: e.tensor_scalar(out=fA[:], in0=fA[:], scalar1=omlt[:, dr, h:h + 1], scalar2=lbt[:, dr, h:h + 1],
                                                                     op0=ALU.mult, op1=ALU.add), R=[bfA, blay], W=[bfA])
                            P.op(act, lambda e: e.activation(out=fB[:], in_=fA[:], func=AF.Ln), R=[bfA], W=[bfB])
                            P.op(pool, lambda e: e.tensor_scalar(out=fA[:], in0=fA[:], scalar1=-1.0, scalar2=1.0, op0=ALU.mult, op1=ALU.add),
                                 R=[bfA], W=[bfA])
                            P.op(dve, lambda e: e.tensor_tensor_scan(out=fC[:], data0=cm_f[:], data1=fB[:], initial=0.0, op0=ALU.mult, op1=ALU.add),
                                 R=[bfB, bconst], W=[bfC])
                            b3 = fC[:].rearrange("p (c t) -> p c t", t=64)
                            b4 = fC[:].rearrange("p (c i t) -> p c i t", i=4, t=16)
                            kk3 = fA[:].rearrange("p (c t) -> p c t", t=64)
                            P.op(act, lambda e, b3=b3: e.activation(out=ebl[:], in_=b3[:, :, 63], func=AF.Exp), R=[bfC], W=[bebl])
                            P.op(act, lambda e: e.activation(out=fB[:], in_=fC[:], func=AF.Exp), R=[bfC], W=[bfB])
                            P.op(dve, lambda e: e.tensor_tensor(out=Qh[:], in0=qr[:, 0, :], in1=fB[:], op=ALU.mult), R=[bqr, bfB], W=[bQh])
                            P.op(dve, lambda e, b3=b3: e.tensor_tensor(out=fD[:].rearrange("p (c t) -> p c t", t=64),
                                                                      in0=b3[:, :, 63:64].broadcast_to([128, NCH, 64]), in1=b3, op=ALU.subtract),
                                 R=[bfC], W=[bfD])
                            P.op(act, lambda e: e.activation(out=fD[:], in_=fD[:], func=AF.Exp), R=[bfD], W=[bfD])
                            P.op(dve, lambda e: e.tensor_tensor(out=Kh[:], in0=fA[:], in1=fD[:], op=ALU.mult), R=[bfA, bfD], W=[bKh])
                            P.op(dve, lambda e, b3=b3: e.tensor_copy(out=rr[:, :, 1:4], in_=b3[:, :, 15:48:16]), R=[bfC], W=[brr_])
                            P.op(dve, lambda e, b4=b4: e.tensor_tensor(out=fB[:].rearrange("p (c i t) -> p c i t", i=4, t=16), in0=b4,
                                                                      in1=rr[:].unsqueeze(3).broadcast_to([128, NCH, 4, 16]), op=ALU.subtract),
                                 R=[bfC, brr_, bQh], W=[bfB])
                            P.op(act, lambda e: e.activation(out=fB[:], in_=fB[:], func=AF.Exp), R=[bfB], W=[bfB])
                            P.op(dve, lambda e: e.tensor_tensor(out=Qs[:], in0=qr[:, 0, :], in1=fB[:], op=ALU.mult), R=[bqr, bfB], W=[bQs])
                            for I in range(4):
                                P.op(dve, lambda e, I=I, b3=b3: e.tensor_tensor(
                                    out=fK[:, KO[I]:KO[I] + NCH * KW[I]].rearrange("p (c t) -> p c t", t=KW[I]),
                                    in0=rr[:, :, I:I + 1].broadcast_to([128, NCH, KW[I]]), in1=b3[:, :, 0:KW[I]], op=ALU.subtract),
                                    R=[bfC, brr_], W=[bfK])
                            P.op(act, lambda e: e.activation(out=fK[:], in_=fK[:], func=AF.Exp), R=[bfK], W=[bfK])
                            for I in range(4):
                                P.op(dve, lambda e, I=I, kk3=kk3: e.tensor_tensor(
                                    out=KI[:, KO[I]:KO[I] + NCH * KW[I]].rearrange("p (c t) -> p c t", t=KW[I]),
                                    in0=fK[:, KO[I]:KO[I] + NCH * KW[I]].rearrange("p (c t) -> p c t", t=KW[I]), in1=kk3[:, :, 0:KW[I]], op=ALU.mult),
                                    R=[bfK, bfA], W=[bKI])
                            P.op(act, lambda e: e.activation(out=vbf[:], in_=qr[:, 1, :], func=AF.Copy), R=[bqr], W=[bvbf])

                            def trK(e):
                                for c in range(NCH):
                                    i_ = e.transpose(out=psb[0][0:64, c * 128:(c + 1) * 128], in_=Kh[:, c * 64:(c + 1) * 64], identity=ident_bf[:])
                                return i_
                            P.op(pe, trK, R=[bKh, bconst], W=[bpsb[0]])
                            P.op(act, lambda e: e.activation(out=Ktok[:], in_=psb[0][0:64, :].rearrange("p (c k) -> p c k", k=128), func=AF.Copy),
                                 R=[bpsb[0]], W=[bKtok])

                            def trV(e):
                                for c in range(NCH):
                                    i_ = e.transpose(out=psb[1][0:64, c * 128:(c + 1) * 128], in_=vbf[:, c * 64:(c + 1) * 64], identity=ident_bf[:])
                                return i_
                            P.op(pe, trV, R=[bvbf, bconst], W=[bpsb[1]])
                            P.op(dve, lambda e: e.tensor_copy(out=Vtok[:], in_=psb[1][0:64, :].rearrange("p (c k) -> p c k", k=128)),
                                 R=[bpsb[1]], W=[bVtok])

                            def mmA(e):
                                for c in range(NCH):
                                    for I in range(4):
                                        col = (c * 4 + I) * 16
                                        i_ = e.matmul(psf[0][0:KW[I], col:col + 16], lhsT=KI[:, KO[I] + c * KW[I]:KO[I] + (c + 1) * KW[I]],
                                                      rhs=Qs[:, c * 64 + I * 16:c * 64 + (I + 1) * 16], start=True, stop=True, skip_group_check=True)
                                return i_
                            P.op(pe, mmA, R=[bKI, bQs], W=[bpsf[0]])
                            P.op(dve, lambda e: e.tensor_tensor(out=Am[:], in0=psf[0][0:64, :].rearrange("p (c i t) -> p c i t", i=4, t=16), in1=mask4[:],
                                                                op=ALU.mult), R=[bpsf[0], bmask4], W=[bAm])
                            ob = 1 + (it % 2)
                            for c in range(NCH):
                                sp_i = spcount % 2
                                spcount += 1
                                sbank = 3 + (spcount % 2)
                                P.op(act, lambda e, sp_i=sp_i, h=h: e.activation(out=Spb[sp_i][:], in_=S[h][:], func=AF.Copy), R=[bS[h]], W=[bSpb[sp_i]])

                                def mmo(e, c=c, sp_i=sp_i, ob=ob):
                                    e.matmul(psf[ob][:, c * 64:(c + 1) * 64], lhsT=Spb[sp_i][:], rhs=Qh[:, c * 64:(c + 1) * 64], start=True, stop=False,
                                             skip_group_check=True)
                                    for I in range(4):
                                        i_ = e.matmul(psf[ob][:, c * 64 + I * 16:c * 64 + (I + 1) * 16], lhsT=Vtok[0:KW[I], c, :], rhs=Am[0:KW[I], c, I, :],
                                                      start=False, stop=(I == 3), skip_group_check=True)
                                    return i_
                                P.op(pe, mmo, R=[bVtok, bAm, bSpb[sp_i], bQh], W=[bpsf[ob]])
                                P.op(pe, lambda e, c=c, sbank=sbank: e.matmul(psf[sbank][:, 0:128], lhsT=Ktok[:, c, :], rhs=Vtok[:, c, :], start=True, stop=True,
                                                                              skip_group_check=True), R=[bKtok, bVtok], W=[bpsf[sbank]])
                                P.op(dve, lambda e, c=c, sbank=sbank, h=h: e.scalar_tensor_tensor(out=S[h][:], in0=S[h][:], scalar=ebl[:, c:c + 1],
                                                                                                 in1=psf[sbank][:, 0:128], op0=ALU.mult, op1=ALU.add),
                                     R=[bpsf[sbank], bebl, bS[h]], W=[bS[h]])
                            if dr == 0:
                                P.op(act, lambda e, ob=ob: e.activation(out=oft[j][:], in_=psf[ob][:], func=AF.Copy), R=[bpsf[ob]], W=[boft[j]])
                                P.dma(OFv[:, h, t0:t0 + NT], oft[j][:], R=[boft[j]], W=[bOF], via=boft[j])
                            else:
                                P.op(act, lambda e, ob=ob: e.activation(out=fD[:], in_=psf[ob][:], func=AF.Copy), R=[bpsf[ob], bKh], W=[bfD])
                                P.op(dve, lambda e: e.tensor_tensor(out=oft[j][:], in0=fD[:, ::-1], in1=oft[j][:], op=ALU.add),
                                     R=[bfD, boft[j]], W=[boft[j]])
                                rms_rstd(oft[j][:].unsqueeze(1), 1, NT, sqh, rstdh, boft[j], bsqh, brstdh, psf[5], bpsf[5], 1.0 / 128)
                                P.op(dve, lambda e, h=h: e.scalar_tensor_tensor(out=oft[j][:], in0=oft[j][:], scalar=spt[:, O_ON + 2 + h:O_ON + 3 + h],
                                                                               in1=rstdh[:], op0=ALU.mult, op1=ALU.mult),
                                     R=[boft[j], brstdh, bsp], W=[boft[j]])
                                P.op(act, lambda e: e.activation(out=gt[j][:], in_=gt[j][:], func=AF.Silu), R=[bgt[j]], W=[bgt[j]])
                                P.op(dve, lambda e: e.tensor_tensor(out=yo[j][:], in0=oft[j][:], in1=gt[j][:], op=ALU.mult),
                                     R=[boft[j], bgt[j]], W=[byo[j]])
                                P.dma(YM1[:, h, t0:t0 + NT], yo[j][:], R=[byo[j]], W=[bYM], via=byo[j])

            if lvl < 6:
                break
            with contextlib.ExitStack() as ph:
                P.phase_end()
                NT = 512
                Ur = U[3328:3840, :].rearrange("(g c p) t -> p g c t", g=2, c=2)
                HFv = HF.rearrange("(c p) t -> p c t", p=128)
                YM2 = YM[768:1024, :].rearrange("(c p) t -> p c t", p=128)
                wa = sb("rg_wa", [128, 2, 2, 128], F32, ph)
                wx = sb("rg_wx", [128, 2, 2, 128], F32, ph)
                wab = sb("rg_wab", [128, 2, 2, 128], BF16, ph)
                wxb = sb("rg_wxb", [128, 2, 2, 128], BF16, ph)
                bwr = Buf("rg_w")
                P.op(dve, lambda e: e.memset(wa[:], 0.0), W=[bwr])
                P.op(dve, lambda e: e.memset(wx[:], 0.0), W=[bwr])
                for d_ in range(2):
                    for hd in range(4):
                        cc, hh_ = hd // 2, hd % 2
                        P.dma(wa[hh_ * 64:(hh_ + 1) * 64, d_, cc, hh_ * 64:(hh_ + 1) * 64], rwa_d[l, d_, hd], R=[bIN], W=[bwr], via=bwr, nowaw=(d_ + hd > 0))
                        P.dma(wx[hh_ * 64:(hh_ + 1) * 64, d_, cc, hh_ * 64:(hh_ + 1) * 64], rwx_d[l, d_, hd], R=[bIN], W=[bwr], via=bwr, nowaw=True)
                P.op(dve, lambda e: e.tensor_copy(out=wab[:], in_=wa[:]), R=[bwr], W=[bwr])
                P.op(dve, lambda e: e.tensor_copy(out=wxb[:], in_=wx[:]), R=[bwr], W=[bwr])
                ur = [sb(f"rg_u{i}", [128, 2, NT + 3], F32, ph) for i in range(2)]
                bur = [Buf(f"rg_u{i}") for i in range(2)]
                gr = [sb(f"rg_g{i}", [128, 2, NT], F32, ph) for i in range(2)]
                bgr = [Buf(f"rg_g{i}") for i in range(2)]
                hf = [sb(f"rg_hf{i}", [128, 2, NT], F32, ph) for i in range(2)]
                bhf = [Buf(f"rg_hf{i}") for i in range(2)]
                xr = sb("rg_xr", [128, 2, NT], F32, ph)
                xrb = sb("rg_xrb", [128, 2, NT], BF16, ph)
                bxr = Buf("rg_xr")
                rr = sb("rg_r", [128, 2, NT], F32, ph)
                ig = sb("rg_ig", [128, 2, NT], F32, ph)
                aa = sb("rg_a", [128, 2, NT], F32, ph)
                bb = sb("rg_b", [128, 2, NT], F32, ph)
                brr, big, baa, bbb = Buf("rg_r"), Buf("rg_ig"), Buf("rg_a"), Buf("rg_b")
                hst = sb("rg_hst", [128, 2], F32, ph)
                bhst = Buf("rg_hst")
                ga = sb("rg_ga", [128, 2, NT], F32, ph)
                bga = Buf("rg_ga")
                sq = sb("rg_sq", [128, 2, NT], BF16, ph)
                bsq = Buf("rg_sq")
                rstd = sb("rg_rstd", [128, NT], F32, ph)
                brstd = Buf("rg_rstd")
                yo = [sb(f"rg_yo{i}", [128, 2, NT], BF16, ph) for i in range(2)]
                byo = [Buf(f"rg_yo{i}") for i in range(2)]
                it = 0
                for dr in range(2):
                    tiles = list(range(T // NT)) if dr == 0 else list(range(T // NT - 1, -1, -1))
                    P.op(dve, lambda e: e.memset(hst[:], 0.0), W=[bhst])
                    for i in tiles:
                        j = it % 2
                        it += 1
                        t0 = i * NT
                        lo = max(t0 - 2, 0)
                        hi = min(t0 + NT + 1, T)
                        P.dma(ur[j][:, :, lo - (t0 - 2):hi - (t0 - 2)], Ur[:, 0, :, lo:hi], R=[bU], W=[bur[j]], via=bur[j])
                        if dr == 1:
                            P.dma(gr[j][:], Ur[:, 1, :, t0:t0 + NT], R=[bU], W=[bgr[j]], via=bgr[j])
                            P.dma(hf[j][:], HFv[:, :, t0:t0 + NT], R=[bHF], W=[bhf[j]], via=bhf[j])
                        if t0 == 0:
                            P.op(dve, lambda e: e.memset(ur[j][:, :, 0:2], 0.0), W=[bur[j]])
                        elif t0 % SEG == 0:
                            P.op(dve, lambda e: e.tensor_scalar(out=ur[j][:, :, 0:2], in0=ur[j][:, :, 0:2], scalar1=segflag[:, 0:1], scalar2=None,
                                                                 op0=ALU.mult), R=[bur[j], bconst], W=[bur[j]])
                        if t0 + NT == T:
                            P.op(dve, lambda e: e.memset(ur[j][:, :, NT + 2:NT + 3], 0.0), W=[bur[j]])
                        elif (t0 + NT) % SEG == 0:
                            P.op(dve, lambda e: e.tensor_scalar(out=ur[j][:, :, NT + 2:NT + 3], in0=ur[j][:, :, NT + 2:NT + 3], scalar1=segflag[:, 0:1],
                                                                 scalar2=None, op0=ALU.mult), R=[bur[j], bconst], W=[bur[j]])
                        at_seg = (t0 % SEG == 0) if dr == 0 else ((t0 + NT) % SEG == 0)
                        if at_seg:
                            P.op(dve, lambda e: e.tensor_scalar(out=hst[:], in0=hst[:], scalar1=segflag[:, 0:1], scalar2=None, op0=ALU.mult),
                                 R=[bhst, bconst], W=[bhst])
                        for cc in range(2):
                            wq = lambda k, cc=cc: spt[:, O_RCW + cc * 4 + k:O_RCW + cc * 4 + k + 1]
                            P.op(dve, lambda e, cc=cc, wq=wq: e.tensor_scalar(out=xr[:, cc, :], in0=ur[j][:, cc, 0:NT], scalar1=wq(0),
                                                                             scalar2=spt[:, O_RCB + cc:O_RCB + cc + 1], op0=ALU.mult, op1=ALU.add),
                                 R=[bur[j], bsp], W=[bxr])
                            for k in (1, 2, 3):
                                P.op(dve, lambda e, cc=cc, k=k, wq=wq: e.scalar_tensor_tensor(out=xr[:, cc, :], in0=ur[j][:, cc, k:k + NT], scalar=wq(k),
                                                                                             in1=xr[:, cc, :], op0=ALU.mult, op1=ALU.add),
                                     R=[bur[j], bsp, bxr], W=[bxr])
                        P.op(act, lambda e: e.activation(out=xrb[:], in_=xr[:], func=AF.Copy), R=[bxr], W=[bxr])
                        for cc in range(2):
                            P.op(pe, lambda e, cc=cc: e.matmul(psf[cc][:], lhsT=wab[:, dr, cc, :], rhs=xrb[:, cc, :], start=True, stop=True),
                                 R=[bwr, bxr], W=[bpsf[cc]])
                            P.op(act, lambda e, cc=cc: e.activation(out=rr[:, cc, :], in_=psf[cc][:], func=AF.Sigmoid,
                                                                    bias=spt[:, O_RBA + dr * 2 + cc:O_RBA + dr * 2 + cc + 1]),
                                 R=[bpsf[cc], bsp], W=[brr])
                            P.op(pe, lambda e, cc=cc: e.matmul(psf[2 + cc][:], lhsT=wxb[:, dr, cc, :], rhs=xrb[:, cc, :], start=True, stop=True),
                                 R=[bwr, bxr], W=[bpsf[2 + cc]])
                            P.op(act, lambda e, cc=cc: e.activation(out=ig[:, cc, :], in_=psf[2 + cc][:], func=AF.Sigmoid,
                                                                    bias=spt[:, O_RBX + dr * 2 + cc:O_RBX + dr * 2 + cc + 1]),
                                 R=[bpsf[2 + cc], bsp], W=[big])
                        for cc in range(2):
                            P.op(act, lambda e, cc=cc: e.activation(out=aa[:, cc, :], in_=rr[:, cc, :], func=AF.Exp,
                                                                    scale=clam[:, dr * 2 + cc:dr * 2 + cc + 1]), R=[brr, blay], W=[baa])
                            P.op(act, lambda e, cc=cc: e.activation(out=bb[:, cc, :], in_=rr[:, cc, :], func=AF.Exp,
                                                                    scale=clam2[:, dr * 2 + cc:dr * 2 + cc + 1]), R=[brr, blay], W=[bbb])
                        P.op(dve, lambda e: e.tensor_scalar(out=bb[:], in0=bb[:], scalar1=-1.0, scalar2=1.0, op0=ALU.mult, op1=ALU.add), R=[bbb], W=[bbb])
                        P.op(dve, lambda e: e.tensor_scalar(out=bb[:], in0=bb[:], scalar1=0.0, scalar2=None, op0=ALU.max), R=[bbb], W=[bbb])
                        P.op(act, lambda e: e.activation(out=bb[:], in_=bb[:], func=AF.Sqrt), R=[bbb], W=[bbb])
                        P.op(pool, lambda e: e.tensor_tensor(out=ig[:], in0=ig[:], in1=xr[:], op=ALU.mult), R=[big, bxr], W=[big])
                        P.op(dve, lambda e: e.tensor_tensor(out=bb[:], in0=bb[:], in1=ig[:], op=ALU.mult), R=[bbb, big], W=[bbb])
                        for cc in range(2):
                            if dr == 0:
                                P.op(dve, lambda e, cc=cc: e.tensor_tensor_scan(out=hf[j][:, cc, :], data0=aa[:, cc, :], data1=bb[:, cc, :],
                                                                                initial=hst[:, cc:cc + 1], op0=ALU.mult, op1=ALU.add),
                                     R=[baa, bbb, bhst], W=[bhf[j]])
                                P.op(dve, lambda e, cc=cc: e.tensor_copy(out=hst[:, cc:cc + 1], in_=hf[j][:, cc, NT - 1:NT]), R=[bhf[j]], W=[bhst])
                            else:
                                P.op(dve, lambda e, cc=cc: e.tensor_tensor_scan(out=rr[:, cc, ::-1], data0=aa[:, cc, ::-1], data1=bb[:, cc, ::-1],
                                                                                initial=hst[:, cc:cc + 1], op0=ALU.mult, op1=ALU.add),
                                     R=[baa, bbb, bhst, brr], W=[brr])
                                P.op(dve, lambda e, cc=cc: e.tensor_copy(out=hst[:, cc:cc + 1], in_=rr[:, cc, 0:1]), R=[brr], W=[bhst])
                        if dr == 0:
                            P.dma(HFv[:, :, t0:t0 + NT], hf[j][:], R=[bhf[j]], W=[bHF], via=bhf[j])
                        else:
                            P.op(pool, lambda e: e.tensor_tensor(out=hf[j][:], in0=hf[j][:], in1=rr[:], op=ALU.add), R=[bhf[j], brr], W=[bhf[j]])
                            P.op(pool, lambda e: e.tensor_tensor(out=ga[:], in0=gr[j][:], in1=gr[j][:], op=ALU.mult), R=[bgr[j]], W=[bga])
                            P.op(pool, lambda e: e.tensor_scalar(out=ga[:], in0=ga[:], scalar1=0.044715, scalar2=1.0, op0=ALU.mult, op1=ALU.add),
                                 R=[bga], W=[bga])
                            P.op(pool, lambda e: e.tensor_tensor(out=ga[:], in0=ga[:], in1=gr[j][:], op=ALU.mult), R=[bga, bgr[j]], W=[bga])
                            P.op(act, lambda e: e.activation(out=ga[:], in_=ga[:], func=AF.Sigmoid, scale=1.5957691216), R=[bga], W=[bga])
                            P.op(pool, lambda e: e.tensor_tensor(out=ga[:], in0=ga[:], in1=gr[j][:], op=ALU.mult), R=[bga, bgr[j]], W=[bga])
                            P.op(dve, lambda e: e.tensor_tensor(out=hf[j][:], in0=hf[j][:], in1=ga[:], op=ALU.mult), R=[bhf[j], bga], W=[bhf[j]])
                            rms_rstd(hf[j][:], 2, NT, sq, rstd, bhf[j], bsq, brstd, psf[4], bpsf[4], 1.0 / 256)
                            for cc in range(2):
                                P.op(dve, lambda e, cc=cc: e.scalar_tensor_tensor(out=yo[j][:, cc, :], in0=hf[j][:, cc, :],
                                                                                scalar=spt[:, O_ON + 6 + cc:O_ON + 7 + cc], in1=rstd[:],
                                                                                op0=ALU.mult, op1=ALU.mult), R=[bhf[j], brstd, bsp], W=[byo[j]])
                            P.dma(YM2[:, :, t0:t0 + NT], yo[j][:], R=[byo[j]], W=[bYM], via=byo[j])

            if lvl < 7:
                break
            with contextlib.ExitStack() as ph:
                P.phase_end()
                NT = 256
                NFC = DFF // 128
                wo = sb("p3wo", [128, 8, D], BF16, ph)
                wg = sb("p3wg", [128, 8, DFF], BF16, ph)
                wu = sb("p3wu", [128, 8, DFF], BF16, ph)
                wd = sb("p3wd", [128, NFC, D], BF16, ph)
                bwo, bwg, bwu, bwd = (Buf(n, multi=True) for n in ("p3wo", "p3wg", "p3wu", "p3wd"))
                with contextlib.ExitStack() as wl:
                    P.phase_end()
                    load_cast(wl, wo, w_out_d[l].rearrange("(kc p) n -> p kc n", p=128), 8, D, 256, bwo, "wo")
                    load_cast(wl, wg, wg_d[l].rearrange("(kc p) n -> p kc n", p=128), 8, DFF, 256, bwg, "wg")
                    load_cast(wl, wu, wu_d[l].rearrange("(kc p) n -> p kc n", p=128), 8, DFF, 256, bwu, "wu")
                with contextlib.ExitStack() as wl:
                    P.phase_end()
                    load_cast(wl, wd, wd_d[l].rearrange("(fc p) n -> p fc n", p=128), NFC, D, 64, bwd, "wd")
                P.phase_end()
                xt = [sb(f"p3x{i}", [128, 8, NT], F32, ph) for i in range(2)]
                bxt = [Buf(f"p3x{i}") for i in range(2)]
                ym = [sb(f"p3ym{i}", [128, 8, NT], BF16, ph) for i in range(2)]
                bym = [Buf(f"p3ym{i}") for i in range(2)]
                hb = sb("p3h", [128, 8, NT], BF16, ph)
                bhb = Buf("p3h")
                tmpn = [sb(f"p3tmp{i}", [128, NT], F32, ph) for i in range(2)]
                btmpn = [Buf(f"p3tmp{i}") for i in range(2)]
                sq = sb("p3sq", [128, 8, NT], BF16, ph)
                bsq = Buf("p3sq")
                rstd = sb("p3rstd", [128, NT], F32, ph)
                brstd = Buf("p3rstd")
                av = sb("p3a", [128, NFC, NT], BF16, ph)
                bav = Buf("p3a", multi=True)
                sg = [sb(f"p3sg{i}", [128, NT], F32, ph) for i in range(2)]
                bsg = [Buf(f"p3sg{i}") for i in range(2)]
                XTv = XT.rearrange("(kc p) t -> p kc t", p=128)
                YMv = YM.rearrange("(kc p) t -> p kc t", p=128)
                for i in range(T // NT):
                    j = i % 2
                    t0 = i * NT
                    seg = t0 // SEG
                    P.dma(xt[j][:], XTv[:, :, t0:t0 + NT], R=[bXT], W=[bxt[j]], via=bxt[j])
                    P.dma(ym[j][:], YMv[:, :, t0:t0 + NT], R=[bYM], W=[bym[j]], via=bym[j])
                    for mc in range(8):
                        pb = mc % 2

                        def mm(e, mc=mc, pb=pb):
                            for kc in range(8):
                                i_ = e.matmul(psf[pb][:, 0:NT], lhsT=wo[:, kc, mc * 128:(mc + 1) * 128], rhs=ym[j][:, kc, :], start=(kc == 0), stop=(kc == 7))
                            return i_
                        P.op(pe, mm, R=[bwo, bym[j]], W=[bpsf[pb]])
                        P.op(dve, lambda e, mc=mc, pb=pb: e.scalar_tensor_tensor(out=xt[j][:, mc, :], in0=psf[pb][:, 0:NT], scalar=modG[0][:, mc, seg:seg + 1],
                                                                                in1=xt[j][:, mc, :], op0=ALU.mult, op1=ALU.add),
                             R=[bpsf[pb], bmod, bxt[j]], W=[bxt[j]])
                    rms_rstd(xt[j][:], 8, NT, sq, rstd, bxt[j], bsq, brstd, psf[2], bpsf[2], 1.0 / D)
                    for kc in range(8):
                        tb = kc % 2
                        P.op(pool, lambda e, kc=kc, tb=tb: e.tensor_tensor(out=tmpn[tb][:], in0=xt[j][:, kc, :], in1=rstd[:], op=ALU.mult),
                             R=[bxt[j], brstd], W=[btmpn[tb]])
                        P.op(act, lambda e, kc=kc, tb=tb: e.activation(out=hb[:, kc, :], in_=tmpn[tb][:], func=AF.Identity,
                                                                       scale=modA[1][:, kc, seg:seg + 1], bias=modB[1][:, kc, seg:seg + 1]),
                             R=[btmpn[tb], bmod], W=[bhb])
                    for fc in range(NFC):
                        pb = 3 + fc % 3

                        def mm(e, fc=fc, pb=pb):
                            for kc in range(8):
                                e.matmul(psf[pb][:, 0:NT], lhsT=wg[:, kc, fc * 128:(fc + 1) * 128], rhs=hb[:, kc, :], start=(kc == 0), stop=(kc == 7),
                                         skip_group_check=True)
                            for kc in range(8):
                                i_ = e.matmul(psf[pb][:, NT:2 * NT], lhsT=wu[:, kc, fc * 128:(fc + 1) * 128], rhs=hb[:, kc, :], start=(kc == 0), stop=(kc == 7),
                                              skip_group_check=True)
                            return i_
                        P.op(pe, mm, R=[bwg, bwu, bhb], W=[bpsf[pb]])
                        s = fc % 2
                        P.op(act, lambda e, pb=pb, s=s: e.activation(out=sg[s][:], in_=psf[pb][:, 0:NT], func=AF.Silu), R=[bpsf[pb]], W=[bsg[s]])
                        P.op(dve, lambda e, pb=pb, s=s, fc=fc: e.tensor_tensor(out=av[:, fc, :], in0=sg[s][:], in1=psf[pb][:, NT:2 * NT], op=ALU.mult),
                             R=[bsg[s], bpsf[pb]], W=[bav])
                    for mc in range(8):
                        pb = mc % 2

                        def mm(e, mc=mc, pb=pb):
                            for fc in range(NFC):
                                i_ = e.matmul(psf[pb][:, 0:NT], lhsT=wd[:, fc, mc * 128:(mc + 1) * 128], rhs=av[:, fc, :], start=(fc == 0), stop=(fc == NFC - 1))
                            return i_
                        P.op(pe, mm, R=[bwd, bav], W=[bpsf[pb]])
                        P.op(dve, lambda e, mc=mc, pb=pb: e.scalar_tensor_tensor(out=xt[j][:, mc, :], in0=psf[pb][:, 0:NT], scalar=modG[1][:, mc, seg:seg + 1],
                                                                                in1=xt[j][:, mc, :], op0=ALU.mult, op1=ALU.add),
                             R=[bpsf[pb], bmod, bxt[j]], W=[bxt[j]])
                    P.dma(XTv[:, :, t0:t0 + NT], xt[j][:], R=[bxt[j]], W=[bXT], via=bxt[j])

        with contextlib.ExitStack() as ph:
            P.phase_end()
            NT = 512
            xt = [sb(f"p4x{i}", [128, 8, NT], F32, ph) for i in range(2)]
            bxt = [Buf(f"p4x{i}") for i in range(2)]
            sq = sb("p4sq", [128, 8, NT], BF16, ph)
            bsq = Buf("p4sq")
            rstd = sb("p4rstd", [128, NT], F32, ph)
            brstd = Buf("p4rstd")
            yo = [sb(f"p4y{i}", [128, D], F32, ph) for i in range(2)]
            byo = [Buf(f"p4y{i}") for i in range(2)]
            XTv = XT.rearrange("(kc p) t -> p kc t", p=128)
            oc = 0
            for i in range(T // NT if lvl >= 2 else 0):
                j = i % 2
                t0 = i * NT
                P.dma(xt[j][:], XTv[:, :, t0:t0 + NT], R=[bXT], W=[bxt[j]], via=bxt[j])
                rms_rstd(xt[j][:], 8, NT, sq, rstd, bxt[j], bsq, brstd, psf[0], bpsf[0], 1.0 / D)
                P.op(dve, lambda e: e.tensor_tensor(out=xt[j][:], in0=xt[j][:], in1=rstd[:].unsqueeze(1).broadcast_to([128, 8, NT]), op=ALU.mult),
                     R=[bxt[j], brstd], W=[bxt[j]])
                P.op(dve, lambda e: e.tensor_tensor(out=xt[j][:], in0=xt[j][:],
                                                    in1=spt[:, O_FN:O_FN + 8].unsqueeze(2).broadcast_to([128, 8, NT]), op=ALU.mult),
                     R=[bxt[j], bsp], W=[bxt[j]])
                for bq in range(NT // 128):
                    o = oc % 2
                    oc += 1
                    for half in range(2):
                        pb = 1 + (2 * oc + half) % 4

                        def tr(e, half=half, pb=pb, bq=bq):
                            for q in range(4):
                                kc = half * 4 + q
                                i_ = e.transpose(out=psf[pb][:, q * 128:(q + 1) * 128], in_=xt[j][:, kc, bq * 128:(bq + 1) * 128], identity=ident_f[:])
                            return i_
                        P.op(pe, tr, R=[bxt[j], bconst], W=[bpsf[pb]])
                        if half == 0:
                            P.op(act, lambda e, pb=pb, o=o: e.activation(out=yo[o][:, 0:512], in_=psf[pb][:], func=AF.Copy), R=[bpsf[pb]], W=[byo[o]])
                        else:
                            P.op(dve, lambda e, pb=pb, o=o: e.tensor_copy(out=yo[o][:, 512:1024], in_=psf[pb][:]), R=[bpsf[pb]], W=[byo[o]])
                    P.dma(y_out[t0 + bq * 128:t0 + (bq + 1) * 128, :], yo[o][:], R=[byo[o]], W=[bY], via=byo[o])
        P.finish()
    return nc


def _col(v):
    v = np.asarray(v, np.float32)
    return np.ascontiguousarray(v.reshape(-1, 128).T)


def pack_small(inp):
    out = np.zeros((DEPTH, 128, NSP), np.float32)
    deltas = np.abs(np.linspace(math.log(1e-2) / 1.5, math.log(1e-2) / 0.3, 256, dtype=np.float32)).astype(np.float32)
    for l in range(DEPTH):
        o = out[l]
        o[:, O_N1:O_N1 + 8] = _col(inp["norm1_w"][l])
        o[:, O_N2:O_N2 + 8] = _col(inp["norm2_w"][l])
        o[:, O_ON:O_ON + 8] = _col(inp["out_norm_w"][l])
        o[:, O_ADAB:O_ADAB + 48] = _col(inp["ada_b"][l])
        cw = np.asarray(inp["hy_conv_w"][l], np.float32)
        o[:, O_HCW:O_HCW + 18] = cw.T.reshape(6, 128, 3).transpose(1, 0, 2).reshape(128, 18)
        o[:, O_HCB:O_HCB + 6] = _col(inp["hy_conv_b"][l])
        o[:, O_HBIAS:O_HBIAS + 2] = _col(inp["hy_bias"][l])
        rw = np.asarray(inp["rg_conv_w"][l], np.float32)
        o[:, O_RCW:O_RCW + 8] = rw.T.reshape(2, 128, 4).transpose(1, 0, 2).reshape(128, 8)
        o[:, O_RCB:O_RCB + 2] = _col(inp["rg_conv_b"][l])
        for nm, off in (("rg_ba", O_RBA), ("rg_bx", O_RBX), ("rg_lambda", O_RLAM)):
            a = np.asarray(inp[nm][l], np.float32)
            o[:, off:off + 4] = a.reshape(2, 2, 128).transpose(2, 0, 1).reshape(128, 4)
        lb = np.asarray(inp["hg_lb_logits"], np.float32)
        o[:, O_LB:O_LB + 16] = lb.reshape(2, DEPTH, 4, 128).transpose(3, 0, 1, 2).reshape(128, 16)
        o[0:64, O_HYB + 0] = inp["hy_b1"][l]
        o[0:64, O_HYB + 1] = inp["hy_b2"][l]
        o[0:64, O_HYB + 2] = inp["hy_b3"][l]
        o[0:64, O_HYB + 3] = inp["hy_freq"][l]
        o[:, O_FN:O_FN + 8] = _col(inp["final_norm_w"])
        o[:, O_ND:O_ND + 2] = -_col(deltas)
    return out


def pos_tables(L, T):
    f32 = np.float32
    t = np.linspace(0.0, 1.0, L, dtype=f32)
    n = np.arange(L, dtype=f32)
    fb = np.linspace(1e-4, 15.0, 16, dtype=f32)
    ang = (fb[None, :] * n[:, None]) * f32(2.0 * math.pi / L)
    z = np.concatenate([t[:, None], np.cos(ang), -np.sin(ang)], axis=-1).astype(f32)
    zfeat = np.zeros((HY_EMB, T), f32)
    zfeat[:, :L] = z.T
    trow = np.full((1, T), 1e4, f32)
    trow[0, :L] = t
    return zfeat, trow


_CACHE = {}


def _get_prog(T):
    if T not in _CACHE:
        _CACHE[T] = build_program(T)
    return _CACHE[T]


def run_cores(core_specs, inp, T):
    nc = _get_prog(T)
    sp = pack_small(inp)
    shared = {k: np.ascontiguousarray(np.asarray(inp[k], np.float32)) for k in
              ("w_in", "w_out", "ada_w", "ffn_wg", "ffn_wu", "ffn_wd", "hy_w1", "hy_w2", "hy_w3", "hy_w4", "rg_wa", "rg_wx")}
    in_maps = []
    tabs = {}
    for (x, c_rows, L) in core_specs:
        if L not in tabs:
            tabs[L] = pos_tables(L, T)
        zfeat, trow = tabs[L]
        m = dict(shared)
        m["x"] = np.ascontiguousarray(x, dtype=np.float32)
        m["cT"] = np.ascontiguousarray(np.asarray(c_rows, np.float32).T)
        m["segflag"] = np.full((128, 1), 1.0 if L == T else 0.0, np.float32)
        m["zfeat"] = zfeat
        m["trow"] = trow
        m["sp"] = sp
        in_maps.append(m)
    res = run_bass_kernel_spmd(nc, in_maps, core_ids=list(range(len(in_maps))))
    return [r["y"] for r in res.results]


def kernel(**inp):
    T = 16384
    xp = np.asarray(inp["x_prompt"], np.float32)
    xs = np.asarray(inp["x_sample"], np.float32)
    cp = np.asarray(inp["c_prompt"], np.float32)
    cs = np.asarray(inp["c_sample"], np.float32)
    specs = []
    for s in range(2):
        specs.append((xs[s], np.repeat(cs[s:s + 1], NSEG, axis=0), T))
    for g in range(2):
        specs.append((xp[4 * g:4 * g + 4].reshape(T, D), cp[4 * g:4 * g + 4], T // NSEG))
    for _ in range(4):
        specs.append((np.zeros((T, D), np.float32), np.zeros((NSEG, D), np.float32), T // NSEG))
    ys = run_cores(specs, inp, T)
    y_sample = np.stack([ys[0], ys[1]], axis=0)
    y_prompt = np.concatenate([ys[2].reshape(4, T // NSEG, D), ys[3].reshape(4, T // NSEG, D)], axis=0)
    return (y_prompt, y_sample)
```

```python
import contextlib
import math
import numpy as np
import ml_dtypes
import concourse.bass as bass
import concourse.mybir as mybir
from concourse.bass_utils import run_bass_kernel_spmd

F32 = mybir.dt.float32
BF16 = mybir.dt.bfloat16
AF = mybir.ActivationFunctionType
ALU = mybir.AluOpType
PI = math.pi

D = 1024
DIN = 3840
DFF = 2816
DEPTH = 2
NSEG = 4
EPS = 1e-6
HY_EMB = 33
O_N1, O_N2, O_ON, O_ADAB, O_HCW, O_HCB, O_HBIAS, O_RCW, O_RCB, O_RBA, O_RBX, O_RLAM, O_LB, O_HYB, O_FN, O_ND = (
    0, 8, 16, 24, 72, 90, 96, 98, 106, 108, 112, 116, 120, 136, 140, 148)
NSP = 150


class Buf:
    def __init__(self, name, multi=False, excl=False):
        self.name = name
        self.multi = multi
        self.excl = excl
        self.w = {}
        self.r = {}
        self.slot = None


class Eng:
    def __init__(self, prog, e, name):
        self.e = e
        self.name = name
        self.sem = prog.sem("e_" + name)
        self.cnt = 0
        self.waited = {}


class Prog:
    def __init__(self, nc, es):
        self.nc = nc
        self.es = es
        self.nsem = 0
        self.pe = Eng(self, nc.tensor, "pe")
        self.act = Eng(self, nc.scalar, "act")
        self.dve = Eng(self, nc.vector, "dve")
        self.pool = Eng(self, nc.gpsimd, "pool")
        self.sp = Eng(self, nc.sync, "sp")
        self.slots = []
        self.slot_pool = []
        self.phase_bufs = []

    def sem(self, name):
        h = self.es.enter_context(self.nc.semaphore(f"{name}_{self.nsem}"))
        self.nsem += 1
        return h

    def _wait(self, eng, R, W, nowaw=False):
        deps = {}
        war = {}
        for b in R:
            for k, v in b.w.items():
                if deps.get(k, 0) < v:
                    deps[k] = v
        for b in W:
            if not (b.multi or nowaw):
                for k, v in b.w.items():
                    if deps.get(k, 0) < v:
                        deps[k] = v
            for k, v in b.r.items():
                if war.get(k, 0) < v:
                    war[k] = v
        for sem, val in war.items():
            if sem is eng.sem:
                continue
            if deps.get(sem, 0) < val:
                deps[sem] = val
        for sem, val in deps.items():
            if sem is eng.sem and eng is self.pe:
                continue
            if eng.waited.get(sem, 0) >= val:
                continue
            eng.e.wait_ge(sem, val)
            eng.waited[sem] = val

    def _mark(self, R, W, sem, val, nowaw=False):
        for b in W:
            if b.multi or nowaw:
                b.w[sem] = val
            else:
                b.w = {sem: val}
                b.r = {}
        for b in R:
            if b.r.get(sem, 0) < val:
                b.r[sem] = val

    def op(self, eng, fn, R=(), W=()):
        ex = [b for b in R if b.excl]
        if ex:
            R = [b for b in R if not b.excl]
            W = list(W) + ex
        self._wait(eng, R, W)
        inst = fn(eng.e)
        eng.cnt += 1
        inst.then_inc(eng.sem, 1)
        self._mark(R, W, eng.sem, eng.cnt)

    def dma(self, out, in_, R=(), W=(), via=None, nowaw=False, **kw):
        self._wait(self.sp, R, W, nowaw)
        if via.slot is None:
            if self.slot_pool:
                via.slot = self.slot_pool.pop(0)
            else:
                via.slot = [self.sem("dma"), 0]
                self.slots.append(via.slot)
            self.phase_bufs.append(via)
        sl = via.slot
        inst = self.nc.sync.dma_start(out=out, in_=in_, **kw)
        sl[1] += 16
        inst.then_inc(sl[0], 16)
        self._mark(R, W, sl[0], sl[1], nowaw)
        if _SYNC_DMA:
            self.nc.sync.wait_ge(sl[0], sl[1])
            self.sp.waited[sl[0]] = sl[1]

    def barrier(self):
        engs = [self.pe, self.act, self.dve, self.pool, self.sp]
        for E in engs:
            for F in engs:
                if F is E or F.cnt == 0:
                    continue
                if E.waited.get(F.sem, 0) < F.cnt:
                    E.e.wait_ge(F.sem, F.cnt)
                    E.waited[F.sem] = F.cnt
            for sl in self.slots:
                if sl[1] > 0 and E.waited.get(sl[0], 0) < sl[1]:
                    E.e.wait_ge(sl[0], sl[1])
                    E.waited[sl[0]] = sl[1]

    def phase_end(self):
        self.barrier()
        for b in self.phase_bufs:
            self.slot_pool.append(b.slot)
            b.slot = None
        self.phase_bufs = []

    def finish(self):
        for sl in self.slots:
            if self.sp.waited.get(sl[0], 0) < sl[1]:
                self.nc.sync.wait_ge(sl[0], sl[1])


_DBG_STOP = None
_SYNC_DMA = False
_DBG_SUB = 99
_DBG_X = 99
_STAGES = ["c0", "c1", "p0", "p1", "hy", "hg", "rg", "p3"]


def build_program(T):
    SEG = T // NSEG
    stop = _DBG_STOP
    lvl = _STAGES.index(stop) if stop else 99
    NB = T // 128
    nc = bass.Bass("TRN2", target_bir_lowering=False)
    dt_in = lambda name, shape, dt=F32: nc.dram_tensor(name, shape, dt, kind="ExternalInput").ap()
    x_in = dt_in("x", [T, D])
    cT_in = dt_in("cT", [D, NSEG])
    segflag_in = dt_in("segflag", [128, 1])
    zfeat_in = dt_in("zfeat", [HY_EMB, T])
    trow_in = dt_in("trow", [1, T])
    w_in_d = dt_in("w_in", [DEPTH, D, DIN])
    w_out_d = dt_in("w_out", [DEPTH, D, D])
    ada_w_d = dt_in("ada_w", [DEPTH, D, 6 * D])
    wg_d = dt_in("ffn_wg", [DEPTH, D, DFF])
    wu_d = dt_in("ffn_wu", [DEPTH, D, DFF])
    wd_d = dt_in("ffn_wd", [DEPTH, DFF, D])
    sp_d = dt_in("sp", [DEPTH, 128, NSP])
    hw1_d = dt_in("hy_w1", [DEPTH, HY_EMB, 64])
    hw2_d = dt_in("hy_w2", [DEPTH, 64, 64])
    hw3_d = dt_in("hy_w3", [DEPTH, 64, 64])
    hw4_d = dt_in("hy_w4", [DEPTH, 64, 512])
    rwa_d = dt_in("rg_wa", [DEPTH, 2, 4, 64, 64])
    rwx_d = dt_in("rg_wx", [DEPTH, 2, 4, 64, 64])
    y_out = nc.dram_tensor("y", [T, D], F32, kind="ExternalOutput").ap()
    scr = lambda name, shape, dt=F32: nc.dram_tensor(name, shape, dt, kind=("ExternalOutput" if stop else "Internal")).ap()
    XT = scr("XT", [D, T])
    U = scr("U", [DIN, T])
    YM = scr("YM", [D, T], BF16)
    HF = scr("HF", [256, T])
    OF = scr("OF", [512, T])
    KF = scr("KF", [256, 2 * T], BF16)
    DBGY = scr("DBGY", [128, NB * 128]) if stop else None
    DBGV = scr("DBGV", [128, NB * 256], BF16) if stop else None
    bDBG = Buf("DBG", multi=True)
    bXT, bU, bYM, bHF, bOF, bKF, bY = (Buf(n, multi=True) for n in ("XT", "U", "YM", "HF", "OF", "KF", "Y"))
    bIN = Buf("inputs")

    with contextlib.ExitStack() as es:
        P = Prog(nc, es)
        pe, act, dve, pool = P.pe, P.act, P.dve, P.pool

        uniq = [0]

        def sb(name, shape, dt=F32, stack=es):
            uniq[0] += 1
            return stack.enter_context(nc.sbuf_tensor(f"{name}_u{uniq[0]}", shape, dt))

        dbg_n = [0]

        def dbg(name, ap, buf, shape, dt=F32):
            if not stop:
                return
            dbg_n[0] += 1
            dd = nc.dram_tensor(f"DBG_{name}", shape, dt, kind="ExternalOutput").ap()
            P.dma(dd, ap, R=[buf], W=[bDBG], via=buf)

        psf = [es.enter_context(nc.psum_tensor(f"psf{i}", [128, 512], F32)) for i in range(7)]
        psb = [es.enter_context(nc.psum_tensor(f"psb{i}", [128, 1024], BF16)) for i in range(1)]
        bpsf = [Buf(f"psf{i}", excl=True) for i in range(7)]
        bpsb = [Buf(f"psb{i}", excl=True) for i in range(1)]

        ones_bf = sb("ones_bf", [128, 128], BF16)
        ident_bf = sb("ident_bf", [128, 128], BF16)
        ident_f = sb("ident_f", [128, 128], F32)
        mask_lo = sb("mask_lo", [64, 8, 64], F32)
        mask_hi = sb("mask_hi", [64, 8, 64], F32)
        cm_f = sb("cm_f", [128, 512], F32)
        cm_b = sb("cm_b", [128, 512], F32)
        epsb = sb("epsb", [128, 1], F32)
        segflag = sb("segflag_t", [128, 1], F32)
        zero_c = sb("zero_c", [128, 1], F32)
        dummy = sb("dummy", [128, 8], F32)
        bconst = Buf("const")
        P.op(dve, lambda e: e.memset(ones_bf[:], 1.0), W=[bconst])
        P.op(dve, lambda e: e.memset(ident_bf[:], 0.0), W=[bconst])
        P.op(dve, lambda e: e.memset(ident_f[:], 0.0), W=[bconst])
        P.op(dve, lambda e: e.memset(mask_lo[:], 1.0), W=[bconst])
        P.op(dve, lambda e: e.memset(mask_hi[:], 1.0), W=[bconst])
        P.op(dve, lambda e: e.memset(cm_f[:], 1.0), W=[bconst])
        P.op(dve, lambda e: e.memset(cm_b[:], 1.0), W=[bconst])
        P.op(dve, lambda e: e.memset(epsb[:], EPS), W=[bconst])
        P.op(dve, lambda e: e.memset(zero_c[:], 0.0), W=[bconst])
        P.op(pool, lambda e: e.affine_select(out=ident_bf[:], in_=ident_bf[:], pattern=[[-1, 128]],
                                             compare_op=ALU.not_equal, fill=1.0, base=0, channel_multiplier=1),
             R=[bconst], W=[bconst])
        P.op(pool, lambda e: e.affine_select(out=ident_f[:], in_=ident_f[:], pattern=[[-1, 128]],
                                             compare_op=ALU.not_equal, fill=1.0, base=0, channel_multiplier=1),
             R=[bconst], W=[bconst])
        P.op(pool, lambda e: e.affine_select(out=mask_lo[:], in_=mask_lo[:], pattern=[[0, 8], [1, 64]],
                                             compare_op=ALU.is_ge, fill=0.0, base=0, channel_multiplier=-1),
             R=[bconst], W=[bconst])
        P.op(pool, lambda e: e.affine_select(out=mask_hi[:], in_=mask_hi[:], pattern=[[0, 8], [-1, 64]],
                                             compare_op=ALU.is_ge, fill=0.0, base=0, channel_multiplier=1),
             R=[bconst], W=[bconst])
        cmf3 = cm_f[:].rearrange("p (c t) -> p c t", t=64)
        cmb3 = cm_b[:].rearrange("p (c t) -> p c t", t=64)
        P.op(dve, lambda e: e.memset(cmf3[:, :, 0:1], 0.0), R=[bconst], W=[bconst])
        P.op(dve, lambda e: e.memset(cmb3[:, :, 63:64], 0.0), R=[bconst], W=[bconst])
        P.dma(segflag[:], segflag_in[:, :], R=[bIN], W=[bconst], via=bconst)

        spt = sb("spt", [128, NSP], F32)
        bsp = Buf("spt")
        modA = [sb(f"modA{i}", [128, 8, NSEG], F32) for i in range(2)]
        modB = [sb(f"modB{i}", [128, 8, NSEG], F32) for i in range(2)]
        modG = [sb(f"modG{i}", [128, 8, NSEG], F32) for i in range(2)]
        bmod = Buf("mod")
        lbt = sb("lbt", [128, 2, 4], F32)
        omlt = sb("omlt", [128, 2, 4], F32)
        clam = sb("clam", [128, 4], F32)
        clam2 = sb("clam2", [128, 4], F32)
        blay = Buf("layer_small")

        def load_cast(stack, dst, src3, K, N, piece, bdst, tag):
            stg = [sb(f"wst_{tag}{i}", [128, K, piece], F32, stack) for i in range(2)]
            bst = [Buf(f"wst_{tag}{i}") for i in range(2)]
            n0 = 0
            i = 0
            while n0 < N:
                w = min(piece, N - n0)
                P.dma(stg[i % 2][:, :, 0:w], src3[:, :, n0:n0 + w], R=[bIN], W=[bst[i % 2]], via=bst[i % 2])
                if i % 2 == 0:
                    P.op(dve, lambda e, i=i, n0=n0, w=w: e.tensor_copy(out=dst[:, :, n0:n0 + w], in_=stg[i % 2][:, :, 0:w]),
                         R=[bst[i % 2]], W=[bdst])
                else:
                    P.op(act, lambda e, i=i, n0=n0, w=w: e.activation(out=dst[:, :, n0:n0 + w], in_=stg[i % 2][:, :, 0:w], func=AF.Copy),
                         R=[bst[i % 2]], W=[bdst])
                n0 += w
                i += 1

        def rms_rstd(xt, K, NT, sq, rstd, bx, bsq, brstd, ps, bps, inv_n):
            P.op(act, lambda e: e.activation(out=sq[:, 0:K, 0:NT], in_=xt, func=AF.Square), R=[bx], W=[bsq])

            def mm(e):
                for kc in range(K):
                    i = e.matmul(ps[:, 0:NT], lhsT=ones_bf[:], rhs=sq[:, kc, 0:NT], start=(kc == 0), stop=(kc == K - 1))
                return i
            P.op(pe, mm, R=[bsq, bconst], W=[bps])
            P.op(act, lambda e: e.activation(out=rstd[:, 0:NT], in_=ps[:, 0:NT], func=AF.Sqrt, scale=inv_n, bias=epsb[:, 0:1]),
                 R=[bps, bconst], W=[brstd])
            P.op(dve, lambda e: e.reciprocal(out=rstd[:, 0:NT], in_=rstd[:, 0:NT]), R=[brstd], W=[brstd])

        with contextlib.ExitStack() as ph:
            P.phase_end()
            xin = [sb(f"p0x{i}", [128, D], F32, ph) for i in range(2)]
            xo = [sb(f"p0o{i}", [128, 8, 128], F32, ph) for i in range(2)]
            bxin = [Buf(f"p0x{i}") for i in range(2)]
            bxo = [Buf(f"p0o{i}") for i in range(2)]
            XTv = XT.rearrange("(kc p) t -> p kc t", p=128)
            for b in range(NB if lvl >= 1 else 0):
                j = b % 2
                P.dma(xin[j][:], x_in[b * 128:(b + 1) * 128, :], R=[bIN], W=[bxin[j]], via=bxin[j])
                for half in range(2):
                    pb = (2 * b + half) % 4

                    def tr(e, half=half, pb=pb):
                        for q in range(4):
                            kc = half * 4 + q
                            i = e.transpose(out=psf[pb][:, q * 128:(q + 1) * 128], in_=xin[j][:, kc * 128:(kc + 1) * 128],
                                            identity=ident_f[:])
                        return i
                    P.op(pe, tr, R=[bxin[j], bconst], W=[bpsf[pb]])
                    eng = act if half == 0 else dve
                    if half == 0:
                        P.op(act, lambda e, pb=pb: e.activation(out=xo[j][:, 0:4, :], in_=psf[pb][:].rearrange("p (q t) -> p q t", t=128),
                                                                func=AF.Copy), R=[bpsf[pb]], W=[bxo[j]])
                    else:
                        P.op(dve, lambda e, pb=pb: e.tensor_copy(out=xo[j][:, 4:8, :], in_=psf[pb][:].rearrange("p (q t) -> p q t", t=128)),
                             R=[bpsf[pb]], W=[bxo[j]])
                P.dma(XTv[:, :, b * 128:(b + 1) * 128], xo[j][:], R=[bxo[j]], W=[bXT], via=bxo[j])

        for l in range(DEPTH):
            if lvl < 3 or (l > 0 and lvl < 99):
                break
            P.dma(spt[:], sp_d[l], R=[bIN], W=[bsp], via=bsp)
            with contextlib.ExitStack() as ph:
                P.phase_end()
                ct = sb("ada_c", [128, 8, NSEG], F32, ph)
                bct = Buf("ada_c")
                modT = sb("ada_mod", [128, 48, NSEG], F32, ph)
                bmodT = Buf("ada_mod")
                P.dma(ct[:], cT_in.rearrange("(kc p) s -> p kc s", p=128), R=[bIN], W=[bct], via=bct)
                P.op(act, lambda e: e.activation(out=ct[:], in_=ct[:], func=AF.Silu), R=[bct], W=[bct])
                aw = [sb(f"ada_w{i}", [128, 8, 512], F32, ph) for i in range(2)]
                baw = [Buf(f"ada_w{i}") for i in range(2)]
                awv = ada_w_d[l].rearrange("(kc p) n -> p kc n", p=128)
                for pc in range(12):
                    j = pc % 2
                    P.dma(aw[j][:], awv[:, :, pc * 512:(pc + 1) * 512], R=[bIN], W=[baw[j]], via=baw[j])
                    pb = pc % 2

                    def mm(e, j=j, pb=pb):
                        for m in range(4):
                            for kc in range(8):
                                i = e.matmul(psf[pb][:, m * NSEG:(m + 1) * NSEG], lhsT=aw[j][:, kc, m * 128:(m + 1) * 128],
                                             rhs=ct[:, kc, :], start=(kc == 0), stop=(kc == 7))
                        return i
                    P.op(pe, mm, R=[baw[j], bct], W=[bpsf[pb]])
                    P.op(dve, lambda e, pc=pc, pb=pb: e.tensor_tensor(
                        out=modT[:, pc * 4:(pc + 1) * 4, :], in0=psf[pb][:, 0:4 * NSEG].rearrange("p (m s) -> p m s", s=NSEG),
                        in1=spt[:, O_ADAB + pc * 4:O_ADAB + (pc + 1) * 4].unsqueeze(2).broadcast_to([128, 4, NSEG]), op=ALU.add),
                        R=[bpsf[pb], bsp], W=[bmodT])
                for i2, (osh, osc, og, onw) in enumerate(((0, 8, 16, O_N1), (24, 32, 40, O_N2))):
                    P.op(dve, lambda e, osc=osc, onw=onw, i2=i2: e.scalar_tensor_tensor(
                        out=modA[i2][:], in0=modT[:, osc:osc + 8, :], scalar=1.0,
                        in1=spt[:, onw:onw + 8].unsqueeze(2).broadcast_to([128, 8, NSEG]), op0=ALU.add, op1=ALU.mult),
                        R=[bmodT, bsp], W=[bmod])
                    P.op(dve, lambda e, osh=osh, i2=i2: e.tensor_copy(out=modB[i2][:], in_=modT[:, osh:osh + 8, :]), R=[bmodT], W=[bmod])
                    P.op(dve, lambda e, og=og, i2=i2: e.tensor_copy(out=modG[i2][:], in_=modT[:, og:og + 8, :]), R=[bmodT], W=[bmod])
                lbv = spt[:, O_LB:O_LB + 16].rearrange("p (d l h) -> p d l h", d=2, l=2)
                if l == 0:
                    P.op(dve, lambda e: e.memset(lbt[:], 0.0), W=[blay])
                else:
                    P.op(dve, lambda e: e.tensor_tensor(out=lbt[:], in0=lbv[:, :, 1, :], in1=lbv[:, :, 0, :], op=ALU.subtract),
                         R=[bsp], W=[blay])
                    P.op(act, lambda e: e.activation(out=lbt[:], in_=lbt[:], func=AF.Sigmoid), R=[blay], W=[blay])
                P.op(dve, lambda e: e.tensor_scalar(out=omlt[:], in0=lbt[:], scalar1=-1.0, scalar2=1.0, op0=ALU.mult, op1=ALU.add),
                     R=[blay], W=[blay])
                dbg(f"omlt_a{l}", omlt[:].rearrange("p a b -> p (a b)"), blay, [128, 8])
                P.op(act, lambda e: e.activation(out=clam[:], in_=spt[:, O_RLAM:O_RLAM + 4], func=AF.Exp, scale=-1.0), R=[bsp], W=[blay])
                P.op(dve, lambda e: e.tensor_scalar(out=clam[:], in0=clam[:], scalar1=1.0, scalar2=None, op0=ALU.add), R=[blay], W=[blay])
                P.op(act, lambda e: e.activation(out=clam[:], in_=clam[:], func=AF.Ln), R=[blay], W=[blay])
                P.op(dve, lambda e: e.tensor_scalar(out=clam2[:], in0=clam[:], scalar1=-16.0, scalar2=None, op0=ALU.mult), R=[blay], W=[blay])
                P.op(dve, lambda e: e.tensor_scalar(out=clam[:], in0=clam[:], scalar1=-8.0, scalar2=None, op0=ALU.mult), R=[blay], W=[blay])

            with contextlib.ExitStack() as ph:
                P.phase_end()
                NT = 512
                wbf = sb("p1w", [128, 8, DIN], BF16, ph)
                bw = Buf("p1w", multi=True)
                xt = [sb(f"p1x{i}", [128, 8, NT], F32, ph) for i in range(2)]
                bxt = [Buf(f"p1x{i}") for i in range(2)]
                hb = [sb(f"p1h{i}", [128, 8, NT], BF16, ph) for i in range(2)]
                bhb = [Buf(f"p1h{i}") for i in range(2)]
                sq = sb("p1sq", [128, 8, NT], BF16, ph)
                bsq = Buf("p1sq")
                rstd = sb("p1rstd", [128, NT], F32, ph)
                brstd = Buf("p1rstd")
                stg = [sb(f"p1st{i}", [128, 3, NT], F32, ph) for i in range(2)]
                bstg = [Buf(f"p1st{i}") for i in range(2)]
                with contextlib.ExitStack() as wl:
                    P.phase_end()
                    load_cast(wl, wbf, w_in_d[l].rearrange("(kc p) n -> p kc n", p=128), 8, DIN, 240, bw, "p1")
                XTv = XT.rearrange("(kc p) t -> p kc t", p=128)
                Uv = U.rearrange("(c p) t -> p c t", p=128)
                for i in range(T // NT):
                    j = i % 2
                    t0 = i * NT
                    seg = t0 // SEG
                    P.dma(xt[j][:], XTv[:, :, t0:t0 + NT], R=[bXT], W=[bxt[j]], via=bxt[j])
                    rms_rstd(xt[j][:], 8, NT, sq, rstd, bxt[j], bsq, brstd, psf[0], bpsf[0], 1.0 / D)
                    P.op(dve, lambda e: e.tensor_tensor(out=xt[j][:], in0=xt[j][:], in1=rstd[:].unsqueeze(1).broadcast_to([128, 8, NT]),
                                                        op=ALU.mult), R=[bxt[j], brstd], W=[bxt[j]])
                    P.op(dve, lambda e: e.tensor_tensor(out=xt[j][:], in0=xt[j][:],
                                                        in1=modA[0][:, :, seg:seg + 1].broadcast_to([128, 8, NT]), op=ALU.mult),
                         R=[bxt[j], bmod], W=[bxt[j]])
                    P.op(dve, lambda e: e.tensor_tensor(out=hb[j][:], in0=xt[j][:],
                                                        in1=modB[0][:, :, seg:seg + 1].broadcast_to([128, 8, NT]), op=ALU.add),
                         R=[bxt[j], bmod], W=[bhb[j]])
                    for mc in range(DIN // 128):
                        pb = 1 + mc % 4

                        def mm(e, mc=mc, pb=pb):
                            for kc in range(8):
                                i_ = e.matmul(psf[pb][:], lhsT=wbf[:, kc, mc * 128:(mc + 1) * 128], rhs=hb[j][:, kc, :],
                                              start=(kc == 0), stop=(kc == 7))
                            return i_
                        P.op(pe, mm, R=[bw, bhb[j]], W=[bpsf[pb]])
                        s = (mc // 3) % 2
                        if mc % 2 == 0:
                            P.op(act, lambda e, pb=pb, s=s, mc=mc: e.activation(out=stg[s][:, mc % 3, :], in_=psf[pb][:], func=AF.Copy),
                                 R=[bpsf[pb]], W=[bstg[s]])
                        else:
                            P.op(dve, lambda e, pb=pb, s=s, mc=mc: e.tensor_copy(out=stg[s][:, mc % 3, :], in_=psf[pb][:]),
                                 R=[bpsf[pb]], W=[bstg[s]])
                        if mc % 3 == 2:
                            P.dma(Uv[:, mc - 2:mc + 1, t0:t0 + NT], stg[s][:], R=[bstg[s]], W=[bU], via=bstg[s])

            dbg(f"omlt_b{l}", omlt[:].rearrange("p a b -> p (a b)"), blay, [128, 8])
            if lvl < 4:
                break
            with contextlib.ExitStack() as ph:
                P.phase_end()
                vT = sb("hy_vT", [128, NB, 256], BF16, ph)
                bvT = Buf("hy_vT", multi=True)
                with contextlib.ExitStack() as fg:
                    P.phase_end()
                    NF = 512
                    w1 = sb("hy_w1", [HY_EMB, 64], F32, fg)
                    w2 = sb("hy_w2", [64, 64], F32, fg)
                    w3 = sb("hy_w3", [64, 64], F32, fg)
                    w4 = sb("hy_w4", [64, 512], F32, fg)
                    bwf = Buf("hy_wf")
                    for tl, src in ((w1, hw1_d), (w2, hw2_d), (w3, hw3_d), (w4, hw4_d)):
                        P.dma(tl[:], src[l], R=[bIN], W=[bwf], via=bwf, nowaw=True)
                    zf = [sb(f"hy_zf{i}", [HY_EMB, NF], F32, fg) for i in range(2)]
                    bzf = [Buf(f"hy_zf{i}") for i in range(2)]
                    tr_ = [sb(f"hy_tr{i}", [128, NF], F32, fg) for i in range(2)]
                    btr = [Buf(f"hy_tr{i}") for i in range(2)]
                    hh = sb("hy_h", [64, NF], F32, fg)
                    bhh = Buf("hy_h")
                    tmpa = sb("hy_tmpa", [64, NF], F32, fg)
                    tmpb = sb("hy_tmpb", [64, NF], F32, fg)
                    btmp = Buf("hy_tmp")
                    dec = sb("hy_dec", [128, 2, NF], F32, fg)
                    bdec = Buf("hy_dec")
                    kt = sb("hy_kt", [128, NF], F32, fg)
                    bkt = Buf("hy_kt")
                    kst = [sb(f"hy_kst{i}", [128, NF], BF16, fg) for i in range(4)]
                    bkst = [Buf(f"hy_kst{i}") for i in range(4)]
                    kcount = 0
                    for n0 in range(0, T if _DBG_SUB >= 1 else 0, NF):
                        j = (n0 // NF) % 2
                        P.dma(zf[j][:], zfeat_in[:, n0:n0 + NF], R=[bIN], W=[bzf[j]], via=bzf[j])
                        P.dma(tr_[j][:], trow_in[0:1, n0:n0 + NF].broadcast_to([128, NF]), R=[bIN], W=[btr[j]], via=btr[j])
                        src_t, bsrc, Kd = zf[j], bzf[j], HY_EMB
                        for li, wl_ in enumerate((w1, w2, w3)):
                            P.op(pe, lambda e, wl_=wl_, src_t=src_t, Kd=Kd: e.matmul(psf[0][0:64, :], lhsT=wl_[0:Kd, :], rhs=src_t[0:Kd, :],
                                                                                   start=True, stop=True),
                                 R=[bwf, bsrc], W=[bpsf[0]])
                            P.op(dve, lambda e, li=li: e.tensor_scalar(out=tmpa[:], in0=psf[0][0:64, :], scalar1=spt[0:64, O_HYB + li:O_HYB + li + 1],
                                                                       scalar2=spt[0:64, O_HYB + 3:O_HYB + 4], op0=ALU.add, op1=ALU.mult),
                                 R=[bpsf[0], bsp], W=[btmp])
                            P.op(dve, lambda e: e.tensor_scalar(out=tmpb[:], in0=tmpa[:], scalar1=PI, scalar2=-2 * PI, op0=ALU.is_gt, op1=ALU.mult),
                                 R=[btmp], W=[btmp])
                            P.op(dve, lambda e: e.tensor_tensor(out=tmpb[:], in0=tmpb[:], in1=tmpa[:], op=ALU.add), R=[btmp], W=[btmp])
                            P.op(dve, lambda e: e.tensor_scalar(out=tmpa[:], in0=tmpa[:], scalar1=-PI, scalar2=2 * PI, op0=ALU.is_lt, op1=ALU.mult),
                                 R=[btmp], W=[btmp])
                            P.op(dve, lambda e: e.tensor_tensor(out=tmpb[:], in0=tmpb[:], in1=tmpa[:], op=ALU.add), R=[btmp], W=[btmp])
                            P.op(act, lambda e: e.activation(out=hh[:], in_=tmpb[:], func=AF.Sin), R=[btmp], W=[bhh])
                            src_t, bsrc, Kd = hh, bhh, 64
                        for cc in range(2):
                            P.op(act, lambda e, cc=cc: e.activation(out=dec[:, cc, :], in_=tr_[j][:], func=AF.Exp,
                                                                    scale=spt[:, O_ND + cc:O_ND + cc + 1]), R=[btr[j], bsp], W=[bdec])
                        for q in range(4):
                            cc = q % 2
                            pb = 1 + q % 2
                            P.op(pe, lambda e, q=q, pb=pb: e.matmul(psf[pb][:], lhsT=w4[:, q * 128:(q + 1) * 128], rhs=hh[:], start=True, stop=True),
                                 R=[bwf, bhh], W=[bpsf[pb]])
                            P.op(dve, lambda e, pb=pb, cc=cc: e.tensor_tensor(out=kt[:], in0=psf[pb][:], in1=dec[:, cc, :], op=ALU.mult),
                                 R=[bpsf[pb], bdec], W=[bkt])
                            ks = kcount % 4
                            kcount += 1
                            if q < 2:
                                if n0 == 0:
                                    P.op(dve, lambda e, cc=cc: e.tensor_tensor(out=kt[:, 0:1], in0=kt[:, 0:1],
                                                                               in1=spt[:, O_HBIAS + cc:O_HBIAS + cc + 1], op=ALU.add),
                                         R=[bkt, bsp], W=[bkt])
                                P.op(act, lambda e, ks=ks: e.activation(out=kst[ks][:], in_=kt[:], func=AF.Copy), R=[bkt], W=[bkst[ks]])
                                P.dma(KF[cc * 128:(cc + 1) * 128, T - 1 + n0:T - 1 + n0 + NF], kst[ks][:], R=[bkst[ks]], W=[bKF], via=bkst[ks])
                            else:
                                P.op(dve, lambda e, ks=ks: e.tensor_copy(out=kst[ks][:, ::-1], in_=kt[:]), R=[bkt], W=[bkst[ks]])
                                if n0 == 0:
                                    P.dma(KF[cc * 128:(cc + 1) * 128, T - NF:T - 1], kst[ks][:, 0:NF - 1], R=[bkst[ks]], W=[bKF], via=bkst[ks])
                                else:
                                    P.dma(KF[cc * 128:(cc + 1) * 128, T - n0 - NF:T - n0], kst[ks][:], R=[bkst[ks]], W=[bKF], via=bkst[ks])
                with contextlib.ExitStack() as p1:
                    P.phase_end()
                    NT = 512
                    ux = [sb(f"hy_ux{i}", [128, 4, NT + 2], F32, p1) for i in range(2)]
                    bux = [Buf(f"hy_ux{i}") for i in range(2)]
                    cv = sb("hy_cv", [128, 4, NT], F32, p1)
                    bcv = Buf("hy_cv")
                    vrev = [sb(f"hy_vrev{i}", [128, 2, NT], BF16, p1) for i in range(2)]
                    bvrev = [Buf(f"hy_vrev{i}") for i in range(2)]
                    Uq = U[256:768, :].rearrange("(q p) t -> p q t", p=128)
                    for i in range(T // NT if _DBG_SUB >= 2 else 0):
                        j = i % 2
                        t0 = i * NT
                        lo = max(t0 - 1, 0)
                        hi = min(t0 + NT + 1, T)
                        P.dma(ux[j][:, :, lo - (t0 - 1):hi - (t0 - 1)], Uq[:, :, lo:hi], R=[bU], W=[bux[j]], via=bux[j])
                        if t0 == 0:
                            P.op(dve, lambda e: e.memset(ux[j][:, :, 0:1], 0.0), W=[bux[j]])
                        elif t0 % SEG == 0:
                            P.op(dve, lambda e: e.tensor_scalar(out=ux[j][:, :, 0:1], in0=ux[j][:, :, 0:1], scalar1=segflag[:, 0:1],
                                                                 scalar2=None, op0=ALU.mult), R=[bux[j], bconst], W=[bux[j]])
                        if t0 + NT == T:
                            P.op(dve, lambda e: e.memset(ux[j][:, :, NT + 1:NT + 2], 0.0), W=[bux[j]])
                        elif (t0 + NT) % SEG == 0:
                            P.op(dve, lambda e: e.tensor_scalar(out=ux[j][:, :, NT + 1:NT + 2], in0=ux[j][:, :, NT + 1:NT + 2],
                                                                 scalar1=segflag[:, 0:1], scalar2=None, op0=ALU.mult),
                                 R=[bux[j], bconst], W=[bux[j]])
                        for q in range(4):
                            ch = 2 + q
                            eng = dve if q % 2 == 0 else pool
                            wq = lambda k, ch=ch: spt[:, O_HCW + ch * 3 + k:O_HCW + ch * 3 + k + 1]
                            P.op(dve, lambda e, q=q, ch=ch, wq=wq: e.tensor_scalar(out=cv[:, q, :], in0=ux[j][:, q, 0:NT], scalar1=wq(0),
                                                                                  scalar2=spt[:, O_HCB + ch:O_HCB + ch + 1], op0=ALU.mult, op1=ALU.add),
                                 R=[bux[j], bsp], W=[bcv])
                            for k in (1, 2):
                                P.op(dve, lambda e, q=q, k=k, wq=wq: e.scalar_tensor_tensor(out=cv[:, q, :], in0=ux[j][:, q, k:k + NT], scalar=wq(k),
                                                                                           in1=cv[:, q, :], op0=ALU.mult, op1=ALU.add),
                                     R=[bux[j], bsp, bcv], W=[bcv])
                        for cc in range(2 if _DBG_X >= 1 else 0):
                            P.op(dve, lambda e, cc=cc: e.tensor_tensor(out=vrev[j][:, cc, ::-1], in0=cv[:, cc, :], in1=cv[:, 2 + cc, :], op=ALU.mult),
                                 R=[bcv], W=[bvrev[j]])
                        P.op(dve, lambda e: e.memset(dummy[:], 0.0), W=[bvrev[j]])
                        for cc in range(2 if _DBG_X >= 2 else 0):
                            pbi = 0

                            def tr(e, cc=cc, pbi=pbi):
                                for bq in range(4):
                                    i_ = e.transpose(out=psb[pbi][:, bq * 128:(bq + 1) * 128], in_=vrev[j][:, cc, bq * 128:(bq + 1) * 128],
                                                     identity=ident_bf[:])
                                return i_
                            P.op(pe, tr, R=[bvrev[j], bconst], W=[bpsb[pbi]])
                            B0 = t0 // 128
                            dst = vT[:, B0 + 3::-1, cc * 128:(cc + 1) * 128] if B0 == 0 else vT[:, B0 + 3:B0 - 1:-1, cc * 128:(cc + 1) * 128]
                            P.op(dve, lambda e, dst=dst, pbi=pbi: e.tensor_copy(out=dst, in_=psb[pbi][:, 0:512].rearrange("p (b c) -> p b c", c=128)),
                                 R=[bpsb[pbi]], W=[bvT])
                with contextlib.ExitStack() as p2:
                    P.phase_end()
                    BS = NB // NSEG
                    yT = sb("hy_yT", [128, NB, 128], F32, p2)
                    byT = Buf("hy_yT", multi=True)
                    NDP = 32
                    NE = 2 * NB - 1
                    npieces = (NE + NDP - 1) // NDP
                    NPB = 3
                    pc_t = [sb(f"hy_pc{i}", [128, NDP * 128], BF16, p2) for i in range(NPB)]
                    bpc = [Buf(f"hy_pc{i}") for i in range(NPB)]
                    pcn_t = [sb(f"hy_pcn{i}", [128, NDP * 128], BF16, p2) for i in range(2)]
                    bpcn = [Buf(f"hy_pcn{i}") for i in range(2)]
                    sfm1 = sb("hy_sfm1", [128, 1], F32, p2)
                    P.op(dve, lambda e: e.tensor_scalar(out=sfm1[:], in0=segflag[:], scalar1=-1.0, scalar2=None, op0=ALU.add), R=[bconst], W=[bconst])
                    NT = 512
                    u0 = [sb(f"hy_u0{i}", [128, NT + 2], F32, p2) for i in range(2)]
                    bu0 = [Buf(f"hy_u0{i}") for i in range(2)]
                    c0 = sb("hy_c0", [128, NT], F32, p2)
                    bc0 = Buf("hy_c0")
                    yfo = [sb(f"hy_yfo{i}", [128, NT], F32, p2) for i in range(2)]
                    byfo = [Buf(f"hy_yfo{i}") for i in range(2)]
                    pcount = 0
                    ncount = 0
                    e_mid = NB - 1
                    p_mid = e_mid // NDP
                    order = [p_mid] + [p for p in range(npieces) if p != p_mid]
                    for cc in range(2 if _DBG_SUB >= 3 else 0):
                        for ci in range(128):
                            c = cc * 128 + ci
                            pb = ci % 4
                            ybank = psf[pb]
                            nmm = 0
                            total = NE + sum(3 for ee in range(NE) if 0 < abs(ee - (NB - 1)) <= BS)
                            for p_ in order:
                                e0 = p_ * NDP
                                e1 = min(e0 + NDP, NE)
                                W_ = (e1 - e0) * 128
                                base = (T - 1) + (e0 - (NB - 1)) * 128 - 127
                                k = pcount % NPB
                                pcount += 1
                                src = bass.AP(tensor=KF.tensor, offset=c * 2 * T + base, ap=[[1, 128], [1, W_]])
                                P.dma(pc_t[k][:, 0:W_], src, R=[bKF], W=[bpc[k]], via=bpc[k])
                                es_ = list(range(e0, e1))
                                if p_ == p_mid:
                                    es_ = [e_mid] + [e for e in es_ if e != e_mid]
                                cross = [ee for ee in es_ if 0 < abs(ee - (NB - 1)) <= BS]
                                kn = None
                                if cross:
                                    kn = ncount % 2
                                    ncount += 1
                                    ca, cb_ = min(cross) - e0, max(cross) - e0 + 1
                                    if ncount % 2 == 0:
                                        P.op(dve, lambda e, k=k, kn=kn, ca=ca, cb_=cb_: e.tensor_scalar(
                                            out=pcn_t[kn][:, ca * 128:cb_ * 128], in0=pc_t[k][:, ca * 128:cb_ * 128], scalar1=sfm1[:, 0:1], scalar2=None,
                                            op0=ALU.mult), R=[bpc[k], bconst], W=[bpcn[kn]])
                                    else:
                                        P.op(act, lambda e, k=k, kn=kn, ca=ca, cb_=cb_: e.activation(
                                            out=pcn_t[kn][:, ca * 128:cb_ * 128], in_=pc_t[k][:, ca * 128:cb_ * 128], func=AF.Copy, scale=sfm1[:, 0:1]),
                                            R=[bpc[k], bconst], W=[bpcn[kn]])

                                def mm(e, es_=es_, e0=e0, k=k, kn=kn, ci=ci, cc=cc, ybank=ybank, nmm=nmm, total=total):
                                    i_ = None
                                    n_ = nmm
                                    for ee in es_:
                                        d = ee - (NB - 1)
                                        i0 = max(0, d)
                                        i1 = min(NB - 1, NB - 1 + d)
                                        i_ = e.matmul(ybank[:, i0:i1 + 1], lhsT=pc_t[k][:, (ee - e0) * 128:(ee - e0 + 1) * 128],
                                                      rhs=vT[:, i0 - d:i1 - d + 1, cc * 128 + ci], start=(n_ == 0), stop=(n_ == total - 1),
                                                      skip_group_check=True)
                                        n_ += 1
                                        if 0 < abs(d) <= BS:
                                            for s_ in range(1, NSEG):
                                                if d > 0:
                                                    a0, a1 = s_ * BS, s_ * BS + d - 1
                                                else:
                                                    a0, a1 = s_ * BS + d, s_ * BS - 1
                                                i_ = e.matmul(ybank[:, a0:a1 + 1], lhsT=pcn_t[kn][:, (ee - e0) * 128:(ee - e0 + 1) * 128],
                                                              rhs=vT[:, a0 - d:a1 - d + 1, cc * 128 + ci], start=False, stop=(n_ == total - 1),
                                                              skip_group_check=True)
                                                n_ += 1
                                    return i_
                                rl = [bpc[k], bvT] + ([bpcn[kn]] if kn is not None else [])
                                P.op(pe, mm, R=rl, W=[bpsf[pb]])
                                nmm += len(es_) + 3 * len(cross)
                            if ci % 2 == 0:
                                P.op(act, lambda e, ci=ci, ybank=ybank: e.activation(out=yT[:, :, ci], in_=ybank[:, 0:NB], func=AF.Copy),
                                     R=[bpsf[pb]], W=[byT])
                            else:
                                P.op(dve, lambda e, ci=ci, ybank=ybank: e.tensor_copy(out=yT[:, :, ci], in_=ybank[:, 0:NB]),
                                     R=[bpsf[pb]], W=[byT])
                        if stop and cc == 0:
                            P.dma(DBGY[:, :], yT[:].rearrange("p b c -> p (b c)"), R=[byT], W=[bDBG], via=byT)
                            P.dma(DBGV[:, :], vT[:].rearrange("p b c -> p (b c)"), R=[bvT], W=[bDBG], via=bvT)
                        for i in range(T // NT if _DBG_SUB >= 4 else 0):
                            j = i % 2
                            t0 = i * NT
                            lo = max(t0 - 1, 0)
                            hi = min(t0 + NT + 1, T)
                            P.dma(u0[j][:, lo - (t0 - 1):hi - (t0 - 1)], U[cc * 128:(cc + 1) * 128, lo:hi], R=[bU], W=[bu0[j]], via=bu0[j])
                            if t0 == 0:
                                P.op(dve, lambda e: e.memset(u0[j][:, 0:1], 0.0), W=[bu0[j]])
                            elif t0 % SEG == 0:
                                P.op(dve, lambda e: e.tensor_scalar(out=u0[j][:, 0:1], in0=u0[j][:, 0:1], scalar1=segflag[:, 0:1],
                                                                    scalar2=None, op0=ALU.mult), R=[bu0[j], bconst], W=[bu0[j]])
                            if t0 + NT == T:
                                P.op(dve, lambda e: e.memset(u0[j][:, NT + 1:NT + 2], 0.0), W=[bu0[j]])
                            elif (t0 + NT) % SEG == 0:
                                P.op(dve, lambda e: e.tensor_scalar(out=u0[j][:, NT + 1:NT + 2], in0=u0[j][:, NT + 1:NT + 2],
                                                                    scalar1=segflag[:, 0:1], scalar2=None, op0=ALU.mult),
                                     R=[bu0[j], bconst], W=[bu0[j]])
                            wq = lambda k_, cc=cc: spt[:, O_HCW + cc * 3 + k_:O_HCW + cc * 3 + k_ + 1]
                            P.op(dve, lambda e, wq=wq: e.tensor_scalar(out=c0[:], in0=u0[j][:, 0:NT], scalar1=wq(0),
                                                                      scalar2=spt[:, O_HCB + cc:O_HCB + cc + 1], op0=ALU.mult, op1=ALU.add),
                                 R=[bu0[j], bsp], W=[bc0])
                            for k_ in (1, 2):
                                P.op(dve, lambda e, k_=k_, wq=wq: e.scalar_tensor_tensor(out=c0[:], in0=u0[j][:, k_:k_ + NT], scalar=wq(k_),
                                                                                        in1=c0[:], op0=ALU.mult, op1=ALU.add),
                                     R=[bu0[j], bsp, bc0], W=[bc0])
                            pb = 4 + i % 2

                            def tr(e, pb=pb, t0=t0):
                                for bq in range(4):
                                    i_ = e.transpose(out=psf[pb][:, bq * 128:(bq + 1) * 128], in_=yT[:, t0 // 128 + bq, :], identity=ident_f[:])
                                return i_
                            P.op(pe, tr, R=[byT, bconst], W=[bpsf[pb]])
                            P.op(dve, lambda e, pb=pb: e.tensor_tensor(out=yfo[j][:], in0=psf[pb][:], in1=c0[:], op=ALU.mult),
                                 R=[bpsf[pb], bc0], W=[byfo[j]])
                            P.dma(HF[cc * 128:(cc + 1) * 128, t0:t0 + NT], yfo[j][:], R=[byfo[j]], W=[bHF], via=byfo[j])
                with contextlib.ExitStack() as p3:
                    P.phase_end()
                    NT = 512
                    yf = [sb(f"hy_yf{i}", [128, 2, NT], F32, p3) for i in range(2)]
                    byf = [Buf(f"hy_yf{i}") for i in range(2)]
                    sq = sb("hy_sq", [128, 2, NT], BF16, p3)
                    bsq = Buf("hy_sq")
                    rstd = sb("hy_rstd", [128, NT], F32, p3)
                    brstd = Buf("hy_rstd")
                    yo = [sb(f"hy_yo{i}", [128, 2, NT], BF16, p3) for i in range(2)]
                    byo = [Buf(f"hy_yo{i}") for i in range(2)]
                    HFv = HF.rearrange("(q p) t -> p q t", p=128)
                    YM0 = YM[0:256, :].rearrange("(q p) t -> p q t", p=128)
                    for i in range(T // NT if _DBG_SUB >= 4 else 0):
                        j = i % 2
                        t0 = i * NT
                        P.dma(yf[j][:], HFv[:, :, t0:t0 + NT], R=[bHF], W=[byf[j]], via=byf[j])
                        rms_rstd(yf[j][:], 2, NT, sq, rstd, byf[j], bsq, brstd, psf[2], bpsf[2], 1.0 / 256)
                        for q in range(2):
                            P.op(dve, lambda e, q=q: e.scalar_tensor_tensor(out=yo[j][:, q, :], in0=yf[j][:, q, :],
                                                                          scalar=spt[:, O_ON + q:O_ON + q + 1], in1=rstd[:],
                                                                          op0=ALU.mult, op1=ALU.mult),
                                 R=[byf[j], brstd, bsp], W=[byo[j]])
                        P.dma(YM0[:, :, t0:t0 + NT], yo[j][:], R=[byo[j]], W=[bYM], via=byo[j])

            dbg(f"omlt_c{l}", omlt[:].rearrange("p a b -> p (a b)"), blay, [128, 8])
            if lvl < 5:
                break
            with contextlib.ExitStack() as ph:
                P.phase_end()
                NT = 512
                NCH = NT // 64
                KW = [16, 32, 48, 64]
                KO = [0, NCH * 16, NCH * 48, NCH * 96]
                Ug = U[768:768 + 2560, :].rearrange("(g h p) t -> p g h t", g=5, h=4)
                OFv = OF.rearrange("(h p) t -> p h t", p=128)
                YM1 = YM[256:768, :].rearrange("(h p) t -> p h t", p=128)
                mask4 = sb("hg_mask4", [64, NCH, 4, 16], F32, ph)
                bmask4 = Buf("hg_mask4")
                P.op(dve, lambda e: e.memset(mask4[:], 1.0), W=[bmask4])
                P.op(pool, lambda e: e.affine_select(out=mask4[:], in_=mask4[:], pattern=[[0, NCH], [16, 4], [1, 16]],
                                                     compare_op=ALU.is_ge, fill=0.0, base=0, channel_multiplier=-1),
                     R=[bmask4], W=[bmask4])
                sqh = sb("hg_sq", [128, 1, NT], BF16, ph)
                bsqh = Buf("hg_sq")
                rstdh = sb("hg_rstd", [128, NT], F32, ph)
                brstdh = Buf("hg_rstd")
                S = [sb(f"hg_S{h}", [128, 128], F32, ph) for h in range(4)]
                bS = [Buf(f"hg_S{h}") for h in range(4)]

                class HeadBufs:
                    pass
                HB = []
                for h in range(4):
                    B = HeadBufs()
                    for nm, shape, dt_ in (("qv", [128, 2, NT], F32), ("zt", [128, NT], F32), ("gt", [128, NT], F32), ("oft", [128, NT], F32),
                                           ("qr", [128, 2, NT], F32), ("fA", [128, NT], F32), ("fB", [128, NT], F32), ("fC", [128, NT], F32),
                                           ("fD", [128, NT], F32), ("fK", [128, NCH * 160], F32), ("rr", [128, NCH, 4], F32), ("ebl", [128, NCH], F32),
                                           ("Qh", [128, NT], BF16), ("Kh", [128, NT], BF16), ("Qs", [128, NT], BF16), ("KI", [128, NCH * 160], BF16),
                                           ("vbf", [128, NT], BF16), ("Ktok", [64, NCH, 128], BF16), ("Vtok", [64, NCH, 128], BF16),
                                           ("Am", [64, NCH, 4, 16], BF16), ("Spb0", [128, 128], BF16), ("Spb1", [128, 128], BF16), ("yo", [128, NT], BF16)):
                        setattr(B, nm, sb(f"hg{h}_{nm}", shape, dt_, ph))
                        setattr(B, "b_" + nm, Buf(f"hg{h}_{nm}"))
                    P.op(dve, lambda e, B=B: e.memset(B.rr[:], 0.0), W=[B.b_rr])
                    HB.append(B)

                def unit(dr, i, h):
                    B = HB[h]
                    t0 = i * NT
                    ob = 1 + h
                    sbank = 5 + (h % 2)
                    P.dma(B.qv[:], Ug[:, 0:2, h, t0:t0 + NT], R=[bU], W=[B.b_qv], via=B.b_qv)
                    P.dma(B.zt[:], Ug[:, 2 + dr, h, t0:t0 + NT], R=[bU], W=[B.b_zt], via=B.b_zt)
                    if dr == 1:
                        P.dma(B.gt[:], Ug[:, 4, h, t0:t0 + NT], R=[bU], W=[B.b_gt], via=B.b_gt)
                        P.dma(B.oft[:], OFv[:, h, t0:t0 + NT], R=[bOF], W=[B.b_oft], via=B.b_oft)
                    at_seg = (t0 % SEG == 0) if dr == 0 else ((t0 + NT) % SEG == 0)
                    if at_seg:
                        P.op(dve, lambda e: e.tensor_scalar(out=S[h][:], in0=S[h][:], scalar1=segflag[:, 0:1], scalar2=None,
                                                            op0=ALU.mult), R=[bS[h], bconst], W=[bS[h]])
                    yield
                    if dr == 0:
                        P.op(act, lambda e: e.activation(out=B.fA[:], in_=B.zt[:], func=AF.Sigmoid), R=[B.b_zt], W=[B.b_fA])
                        P.op(pool, lambda e: e.tensor_copy(out=B.qr[:], in_=B.qv[:]), R=[B.b_qv], W=[B.b_qr])
                    else:
                        P.op(dve, lambda e: e.tensor_copy(out=B.fD[:], in_=B.zt[:, ::-1]), R=[B.b_zt], W=[B.b_fD])
                        yield
                        P.op(act, lambda e: e.activation(out=B.fA[:], in_=B.fD[:], func=AF.Sigmoid), R=[B.b_fD], W=[B.b_fA])
                        for g_ in range(2):
                            P.op(dve, lambda e, g_=g_: e.tensor_copy(out=B.qr[:, g_, :], in_=B.qv[:, g_, ::-1]), R=[B.b_qv], W=[B.b_qr])
                    yield
                    P.op(dve, lambda e: e.tensor_scalar(out=B.fA[:], in0=B.fA[:], scalar1=omlt[:, dr, h:h + 1], scalar2=lbt[:, dr, h:h + 1],
                                                        op0=ALU.mult, op1=ALU.add), R=[B.b_fA, blay], W=[B.b_fA])
                    yield
                    P.op(act, lambda e: e.activation(out=B.fB[:], in_=B.fA[:], func=AF.Ln), R=[B.b_fA], W=[B.b_fB])
                    P.op(act, lambda e: e.activation(out=B.vbf[:], in_=B.qr[:, 1, :], func=AF.Copy), R=[B.b_qr], W=[B.b_vbf])
                    yield
                    P.op(pool, lambda e: e.tensor_scalar(out=B.fA[:], in0=B.fA[:], scalar1=-1.0, scalar2=1.0, op0=ALU.mult, op1=ALU.add),
                         R=[B.b_fA], W=[B.b_fA])
                    P.op(dve, lambda e: e.tensor_tensor_scan(out=B.fC[:], data0=cm_f[:], data1=B.fB[:], initial=0.0, op0=ALU.mult, op1=ALU.add),
                         R=[B.b_fB, bconst], W=[B.b_fC])

                    def trV(e):
                        for c in range(NCH):
                            i_ = e.transpose(out=psb[0][0:64, c * 128:(c + 1) * 128], in_=B.vbf[:, c * 64:(c + 1) * 64], identity=ident_bf[:])
                        return i_
                    P.op(pe, trV, R=[B.b_vbf, bconst], W=[bpsb[0]])
                    P.op(act, lambda e: e.activation(out=B.Vtok[:], in_=psb[0][0:64, :].rearrange("p (c k) -> p c k", k=128), func=AF.Copy),
                         R=[bpsb[0]], W=[B.b_Vtok])
                    yield
                    b3 = B.fC[:].rearrange("p (c t) -> p c t", t=64)
                    b4 = B.fC[:].rearrange("p (c i t) -> p c i t", i=4, t=16)
                    kk3 = B.fA[:].rearrange("p (c t) -> p c t", t=64)
                    P.op(act, lambda e: e.activation(out=B.ebl[:], in_=b3[:, :, 63], func=AF.Exp), R=[B.b_fC], W=[B.b_ebl])
                    P.op(act, lambda e: e.activation(out=B.fB[:], in_=B.fC[:], func=AF.Exp), R=[B.b_fC], W=[B.b_fB])
                    P.op(dve, lambda e: e.tensor_tensor(out=B.fD[:].rearrange("p (c t) -> p c t", t=64),
                                                        in0=b3[:, :, 63:64].broadcast_to([128, NCH, 64]), in1=b3, op=ALU.subtract),
                         R=[B.b_fC], W=[B.b_fD])
                    P.op(dve, lambda e: e.tensor_copy(out=B.rr[:, :, 1:4], in_=b3[:, :, 15:48:16]), R=[B.b_fC], W=[B.b_rr])
                    yield
                    P.op(dve, lambda e: e.tensor_tensor(out=B.Qh[:], in0=B.qr[:, 0, :], in1=B.fB[:], op=ALU.mult), R=[B.b_qr, B.b_fB], W=[B.b_Qh])
                    P.op(act, lambda e: e.activation(out=B.fD[:], in_=B.fD[:], func=AF.Exp), R=[B.b_fD], W=[B.b_fD])
                    for I in range(4):
                        P.op(dve, lambda e, I=I: e.tensor_tensor(
                            out=B.fK[:, KO[I]:KO[I] + NCH * KW[I]].rearrange("p (c t) -> p c t", t=KW[I]),
                            in0=B.rr[:, :, I:I + 1].broadcast_to([128, NCH, KW[I]]), in1=b3[:, :, 0:KW[I]], op=ALU.subtract),
                            R=[B.b_fC, B.b_rr], W=[B.b_fK])
                    yield
                    P.op(act, lambda e: e.activation(out=B.fK[:], in_=B.fK[:], func=AF.Exp), R=[B.b_fK], W=[B.b_fK])
                    P.op(dve, lambda e: e.tensor_tensor(out=B.Kh[:], in0=B.fA[:], in1=B.fD[:], op=ALU.mult), R=[B.b_fA, B.b_fD], W=[B.b_Kh])
                    P.op(dve, lambda e: e.tensor_tensor(out=B.fB[:].rearrange("p (c i t) -> p c i t", i=4, t=16), in0=b4,
                                                        in1=B.rr[:].unsqueeze(3).broadcast_to([128, NCH, 4, 16]), op=ALU.subtract),
                         R=[B.b_fC, B.b_rr, B.b_Qh], W=[B.b_fB])
                    yield
                    P.op(act, lambda e: e.activation(out=B.fB[:], in_=B.fB[:], func=AF.Exp), R=[B.b_fB], W=[B.b_fB])

                    def trK(e):
                        for c in range(NCH):
                            i_ = e.transpose(out=psb[0][0:64, c * 128:(c + 1) * 128], in_=B.Kh[:, c * 64:(c + 1) * 64], identity=ident_bf[:])
                        return i_
                    P.op(pe, trK, R=[B.b_Kh, bconst], W=[bpsb[0]])
                    P.op(act, lambda e: e.activation(out=B.Ktok[:], in_=psb[0][0:64, :].rearrange("p (c k) -> p c k", k=128), func=AF.Copy),
                         R=[bpsb[0]], W=[B.b_Ktok])
                    for I in range(4):
                        P.op(dve, lambda e, I=I: e.tensor_tensor(
                            out=B.KI[:, KO[I]:KO[I] + NCH * KW[I]].rearrange("p (c t) -> p c t", t=KW[I]),
                            in0=B.fK[:, KO[I]:KO[I] + NCH * KW[I]].rearrange("p (c t) -> p c t", t=KW[I]), in1=kk3[:, :, 0:KW[I]], op=ALU.mult),
                            R=[B.b_fK, B.b_fA], W=[B.b_KI])
                    yield
                    P.op(dve, lambda e: e.tensor_tensor(out=B.Qs[:], in0=B.qr[:, 0, :], in1=B.fB[:], op=ALU.mult), R=[B.b_qr, B.b_fB], W=[B.b_Qs])
                    yield

                    def mmA(e):
                        for c in range(NCH):
                            for I in range(4):
                                col = (c * 4 + I) * 16
                                i_ = e.matmul(psf[0][0:KW[I], col:col + 16], lhsT=B.KI[:, KO[I] + c * KW[I]:KO[I] + (c + 1) * KW[I]],
                                              rhs=B.Qs[:, c * 64 + I * 16:c * 64 + (I + 1) * 16], start=True, stop=True, skip_group_check=True)
                        return i_
                    P.op(pe, mmA, R=[B.b_KI, B.b_Qs], W=[bpsf[0]])
                    P.op(dve, lambda e: e.tensor_tensor(out=B.Am[:], in0=psf[0][0:64, :].rearrange("p (c i t) -> p c i t", i=4, t=16), in1=mask4[:],
                                                        op=ALU.mult), R=[bpsf[0], bmask4], W=[B.b_Am])
                    yield
                    for c in range(NCH):
                        Spb_, bSpb_ = (B.Spb0, B.b_Spb0) if c % 2 == 0 else (B.Spb1, B.b_Spb1)
                        P.op(act, lambda e, Spb_=Spb_: e.activation(out=Spb_[:], in_=S[h][:], func=AF.Copy), R=[bS[h]], W=[bSpb_])

                        def mmo(e, c=c, Spb_=Spb_):
                            e.matmul(psf[ob][:, c * 64:(c + 1) * 64], lhsT=Spb_[:], rhs=B.Qh[:, c * 64:(c + 1) * 64], start=True, stop=False,
                                     skip_group_check=True)
                            for I in range(4):
                                i_ = e.matmul(psf[ob][:, c * 64 + I * 16:c * 64 + (I + 1) * 16], lhsT=B.Vtok[0:KW[I], c, :], rhs=B.Am[0:KW[I], c, I, :],
                                              start=False, stop=(I == 3), skip_group_check=True)
                            return i_
                        P.op(pe, mmo, R=[B.b_Vtok, B.b_Am, bSpb_, B.b_Qh], W=[bpsf[ob]])
                        P.op(pe, lambda e, c=c: e.matmul(psf[sbank][:, 0:128], lhsT=B.Ktok[:, c, :], rhs=B.Vtok[:, c, :], start=True, stop=True,
                                                         skip_group_check=True), R=[B.b_Ktok, B.b_Vtok], W=[bpsf[sbank]])
                        P.op(dve, lambda e, c=c: e.scalar_tensor_tensor(out=S[h][:], in0=S[h][:], scalar=B.ebl[:, c:c + 1],
                                                                       in1=psf[sbank][:, 0:128], op0=ALU.mult, op1=ALU.add),
                             R=[bpsf[sbank], B.b_ebl, bS[h]], W=[bS[h]])
                        yield
                    if dr == 0:
                        P.op(act, lambda e: e.activation(out=B.oft[:], in_=psf[ob][:], func=AF.Copy), R=[bpsf[ob]], W=[B.b_oft])
                        P.dma(OFv[:, h, t0:t0 + NT], B.oft[:], R=[B.b_oft], W=[bOF], via=B.b_oft)
                    else:
                        P.op(act, lambda e: e.activation(out=B.fD[:], in_=psf[ob][:], func=AF.Copy), R=[bpsf[ob], B.b_Kh], W=[B.b_fD])
                        yield
                        P.op(dve, lambda e: e.tensor_tensor(out=B.oft[:], in0=B.fD[:, ::-1], in1=B.oft[:], op=ALU.add),
                             R=[B.b_fD, B.b_oft], W=[B.b_oft])
                        rms_rstd(B.oft[:].unsqueeze(1), 1, NT, sqh, rstdh, B.b_oft, bsqh, brstdh, psf[0], bpsf[0], 1.0 / 128)
                        P.op(dve, lambda e: e.scalar_tensor_tensor(out=B.oft[:], in0=B.oft[:], scalar=spt[:, O_ON + 2 + h:O_ON + 3 + h],
                                                                  in1=rstdh[:], op0=ALU.mult, op1=ALU.mult),
                             R=[B.b_oft, brstdh, bsp], W=[B.b_oft])
                        P.op(act, lambda e: e.activation(out=B.gt[:], in_=B.gt[:], func=AF.Silu), R=[B.b_gt], W=[B.b_gt])
                        yield
                        P.op(dve, lambda e: e.tensor_tensor(out=B.yo[:], in0=B.oft[:], in1=B.gt[:], op=ALU.mult),
                             R=[B.b_oft, B.b_gt], W=[B.b_yo])
                        P.dma(YM1[:, h, t0:t0 + NT], B.yo[:], R=[B.b_yo], W=[bYM], via=B.b_yo)

                for dr in range(2):
                    tiles = list(range(T // NT)) if dr == 0 else list(range(T // NT - 1, -1, -1))
                    for h in range(4):
                        P.op(dve, lambda e, h=h: e.memset(S[h][:], 0.0), W=[bS[h]])
                    for i in tiles:
                        alive = [unit(dr, i, h) for h in range(4)]
                        while alive:
                            for g_ in list(alive):
                                try:
                                    next(g_)
                                except StopIteration:
                                    alive.remove(g_)

            if lvl < 6:
                break
            with contextlib.ExitStack() as ph:
                P.phase_end()
                NT = 512
                Ur = U[3328:3840, :].rearrange("(g c p) t -> p g c t", g=2, c=2)
                HFv = HF.rearrange("(c p) t -> p c t", p=128)
                YM2 = YM[768:1024, :].rearrange("(c p) t -> p c t", p=128)
                wa = sb("rg_wa", [128, 2, 2, 128], F32, ph)
                wx = sb("rg_wx", [128, 2, 2, 128], F32, ph)
                wab = sb("rg_wab", [128, 2, 2, 128], BF16, ph)
                wxb = sb("rg_wxb", [128, 2, 2, 128], BF16, ph)
                bwr = Buf("rg_w")
                P.op(dve, lambda e: e.memset(wa[:], 0.0), W=[bwr])
                P.op(dve, lambda e: e.memset(wx[:], 0.0), W=[bwr])
                for d_ in range(2):
                    for hd in range(4):
                        cc, hh_ = hd // 2, hd % 2
                        P.dma(wa[hh_ * 64:(hh_ + 1) * 64, d_, cc, hh_ * 64:(hh_ + 1) * 64], rwa_d[l, d_, hd], R=[bIN], W=[bwr], via=bwr, nowaw=(d_ + hd > 0))
                        P.dma(wx[hh_ * 64:(hh_ + 1) * 64, d_, cc, hh_ * 64:(hh_ + 1) * 64], rwx_d[l, d_, hd], R=[bIN], W=[bwr], via=bwr, nowaw=True)
                P.op(dve, lambda e: e.tensor_copy(out=wab[:], in_=wa[:]), R=[bwr], W=[bwr])
                P.op(dve, lambda e: e.tensor_copy(out=wxb[:], in_=wx[:]), R=[bwr], W=[bwr])
                ur = [sb(f"rg_u{i}", [128, 2, NT + 3], F32, ph) for i in range(2)]
                bur = [Buf(f"rg_u{i}") for i in range(2)]
                gr = [sb(f"rg_g{i}", [128, 2, NT], F32, ph) for i in range(2)]
                bgr = [Buf(f"rg_g{i}") for i in range(2)]
                hf = [sb(f"rg_hf{i}", [128, 2, NT], F32, ph) for i in range(2)]
                bhf = [Buf(f"rg_hf{i}") for i in range(2)]
                xr = sb("rg_xr", [128, 2, NT], F32, ph)
                xrb = sb("rg_xrb", [128, 2, NT], BF16, ph)
                bxr = Buf("rg_xr")
                rr = sb("rg_r", [128, 2, NT], F32, ph)
                ig = sb("rg_ig", [128, 2, NT], F32, ph)
                aa = sb("rg_a", [128, 2, NT], F32, ph)
                bb = sb("rg_b", [128, 2, NT], F32, ph)
                brr, big, baa, bbb = Buf("rg_r"), Buf("rg_ig"), Buf("rg_a"), Buf("rg_b")
                hst = sb("rg_hst", [128, 2], F32, ph)
                bhst = Buf("rg_hst")
                ga = sb("rg_ga", [128, 2, NT], F32, ph)
                bga = Buf("rg_ga")
                sq = sb("rg_sq", [128, 2, NT], BF16, ph)
                bsq = Buf("rg_sq")
                rstd = sb("rg_rstd", [128, NT], F32, ph)
                brstd = Buf("rg_rstd")
                yo = [sb(f"rg_yo{i}", [128, 2, NT], BF16, ph) for i in range(2)]
                byo = [Buf(f"rg_yo{i}") for i in range(2)]
                it = 0
                for dr in range(2):
                    tiles = list(range(T // NT)) if dr == 0 else list(range(T // NT - 1, -1, -1))
                    P.op(dve, lambda e: e.memset(hst[:], 0.0), W=[bhst])
                    for i in tiles:
                        j = it % 2
                        it += 1
                        t0 = i * NT
                        lo = max(t0 - 2, 0)
                        hi = min(t0 + NT + 1, T)
                        P.dma(ur[j][:, :, lo - (t0 - 2):hi - (t0 - 2)], Ur[:, 0, :, lo:hi], R=[bU], W=[bur[j]], via=bur[j])
                        if dr == 1:
                            P.dma(gr[j][:], Ur[:, 1, :, t0:t0 + NT], R=[bU], W=[bgr[j]], via=bgr[j])
                            P.dma(hf[j][:], HFv[:, :, t0:t0 + NT], R=[bHF], W=[bhf[j]], via=bhf[j])
                        if t0 == 0:
                            P.op(dve, lambda e: e.memset(ur[j][:, :, 0:2], 0.0), W=[bur[j]])
                        elif t0 % SEG == 0:
                            P.op(dve, lambda e: e.tensor_scalar(out=ur[j][:, :, 0:2], in0=ur[j][:, :, 0:2], scalar1=segflag[:, 0:1], scalar2=None,
                                                                 op0=ALU.mult), R=[bur[j], bconst], W=[bur[j]])
                        if t0 + NT == T:
                            P.op(dve, lambda e: e.memset(ur[j][:, :, NT + 2:NT + 3], 0.0), W=[bur[j]])
                        elif (t0 + NT) % SEG == 0:
                            P.op(dve, lambda e: e.tensor_scalar(out=ur[j][:, :, NT + 2:NT + 3], in0=ur[j][:, :, NT + 2:NT + 3], scalar1=segflag[:, 0:1],
                                                                 scalar2=None, op0=ALU.mult), R=[bur[j], bconst], W=[bur[j]])
                        at_seg = (t0 % SEG == 0) if dr == 0 else ((t0 + NT) % SEG == 0)
                        if at_seg:
                            P.op(dve, lambda e: e.tensor_scalar(out=hst[:], in0=hst[:], scalar1=segflag[:, 0:1], scalar2=None, op0=ALU.mult),
                                 R=[bhst, bconst], W=[bhst])
                        for cc in range(2):
                            wq = lambda k, cc=cc: spt[:, O_RCW + cc * 4 + k:O_RCW + cc * 4 + k + 1]
                            P.op(dve, lambda e, cc=cc, wq=wq: e.tensor_scalar(out=xr[:, cc, :], in0=ur[j][:, cc, 0:NT], scalar1=wq(0),
                                                                             scalar2=spt[:, O_RCB + cc:O_RCB + cc + 1], op0=ALU.mult, op1=ALU.add),
                                 R=[bur[j], bsp], W=[bxr])
                            for k in (1, 2, 3):
                                P.op(dve, lambda e, cc=cc, k=k, wq=wq: e.scalar_tensor_tensor(out=xr[:, cc, :], in0=ur[j][:, cc, k:k + NT], scalar=wq(k),
                                                                                             in1=xr[:, cc, :], op0=ALU.mult, op1=ALU.add),
                                     R=[bur[j], bsp, bxr], W=[bxr])
                        P.op(act, lambda e: e.activation(out=xrb[:], in_=xr[:], func=AF.Copy), R=[bxr], W=[bxr])
                        for cc in range(2):
                            P.op(pe, lambda e, cc=cc: e.matmul(psf[cc][:], lhsT=wab[:, dr, cc, :], rhs=xrb[:, cc, :], start=True, stop=True),
                                 R=[bwr, bxr], W=[bpsf[cc]])
                            P.op(act, lambda e, cc=cc: e.activation(out=rr[:, cc, :], in_=psf[cc][:], func=AF.Sigmoid,
                                                                    bias=spt[:, O_RBA + dr * 2 + cc:O_RBA + dr * 2 + cc + 1]),
                                 R=[bpsf[cc], bsp], W=[brr])
                            P.op(pe, lambda e, cc=cc: e.matmul(psf[2 + cc][:], lhsT=wxb[:, dr, cc, :], rhs=xrb[:, cc, :], start=True, stop=True),
                                 R=[bwr, bxr], W=[bpsf[2 + cc]])
                            P.op(act, lambda e, cc=cc: e.activation(out=ig[:, cc, :], in_=psf[2 + cc][:], func=AF.Sigmoid,
                                                                    bias=spt[:, O_RBX + dr * 2 + cc:O_RBX + dr * 2 + cc + 1]),
                                 R=[bpsf[2 + cc], bsp], W=[big])
                        for cc in range(2):
                            P.op(act, lambda e, cc=cc: e.activation(out=aa[:, cc, :], in_=rr[:, cc, :], func=AF.Exp,
                                                                    scale=clam[:, dr * 2 + cc:dr * 2 + cc + 1]), R=[brr, blay], W=[baa])
                            P.op(act, lambda e, cc=cc: e.activation(out=bb[:, cc, :], in_=rr[:, cc, :], func=AF.Exp,
                                                                    scale=clam2[:, dr * 2 + cc:dr * 2 + cc + 1]), R=[brr, blay], W=[bbb])
                        P.op(dve, lambda e: e.tensor_scalar(out=bb[:], in0=bb[:], scalar1=-1.0, scalar2=1.0, op0=ALU.mult, op1=ALU.add), R=[bbb], W=[bbb])
                        P.op(dve, lambda e: e.tensor_scalar(out=bb[:], in0=bb[:], scalar1=0.0, scalar2=None, op0=ALU.max), R=[bbb], W=[bbb])
                        P.op(act, lambda e: e.activation(out=bb[:], in_=bb[:], func=AF.Sqrt), R=[bbb], W=[bbb])
                        P.op(pool, lambda e: e.tensor_tensor(out=ig[:], in0=ig[:], in1=xr[:], op=ALU.mult), R=[big, bxr], W=[big])
                        P.op(dve, lambda e: e.tensor_tensor(out=bb[:], in0=bb[:], in1=ig[:], op=ALU.mult), R=[bbb, big], W=[bbb])
                        for cc in range(2):
                            if dr == 0:
                                P.op(dve, lambda e, cc=cc: e.tensor_tensor_scan(out=hf[j][:, cc, :], data0=aa[:, cc, :], data1=bb[:, cc, :],
                                                                                initial=hst[:, cc:cc + 1], op0=ALU.mult, op1=ALU.add),
                                     R=[baa, bbb, bhst], W=[bhf[j]])
                                P.op(dve, lambda e, cc=cc: e.tensor_copy(out=hst[:, cc:cc + 1], in_=hf[j][:, cc, NT - 1:NT]), R=[bhf[j]], W=[bhst])
                            else:
                                P.op(dve, lambda e, cc=cc: e.tensor_tensor_scan(out=rr[:, cc, ::-1], data0=aa[:, cc, ::-1], data1=bb[:, cc, ::-1],
                                                                                initial=hst[:, cc:cc + 1], op0=ALU.mult, op1=ALU.add),
                                     R=[baa, bbb, bhst, brr], W=[brr])
                                P.op(dve, lambda e, cc=cc: e.tensor_copy(out=hst[:, cc:cc + 1], in_=rr[:, cc, 0:1]), R=[brr], W=[bhst])
                        if dr == 0:
                            P.dma(HFv[:, :, t0:t0 + NT], hf[j][:], R=[bhf[j]], W=[bHF], via=bhf[j])
                        else:
                            P.op(pool, lambda e: e.tensor_tensor(out=hf[j][:], in0=hf[j][:], in1=rr[:], op=ALU.add), R=[bhf[j], brr], W=[bhf[j]])
                            P.op(pool, lambda e: e.tensor_tensor(out=ga[:], in0=gr[j][:], in1=gr[j][:], op=ALU.mult), R=[bgr[j]], W=[bga])
                            P.op(pool, lambda e: e.tensor_scalar(out=ga[:], in0=ga[:], scalar1=0.044715, scalar2=1.0, op0=ALU.mult, op1=ALU.add),
                                 R=[bga], W=[bga])
                            P.op(pool, lambda e: e.tensor_tensor(out=ga[:], in0=ga[:], in1=gr[j][:], op=ALU.mult), R=[bga, bgr[j]], W=[bga])
                            P.op(act, lambda e: e.activation(out=ga[:], in_=ga[:], func=AF.Sigmoid, scale=1.5957691216), R=[bga], W=[bga])
                            P.op(pool, lambda e: e.tensor_tensor(out=ga[:], in0=ga[:], in1=gr[j][:], op=ALU.mult), R=[bga, bgr[j]], W=[bga])
                            P.op(dve, lambda e: e.tensor_tensor(out=hf[j][:], in0=hf[j][:], in1=ga[:], op=ALU.mult), R=[bhf[j], bga], W=[bhf[j]])
                            rms_rstd(hf[j][:], 2, NT, sq, rstd, bhf[j], bsq, brstd, psf[4], bpsf[4], 1.0 / 256)
                            for cc in range(2):
                                P.op(dve, lambda e, cc=cc: e.scalar_tensor_tensor(out=yo[j][:, cc, :], in0=hf[j][:, cc, :],
                                                                                scalar=spt[:, O_ON + 6 + cc:O_ON + 7 + cc], in1=rstd[:],
                                                                                op0=ALU.mult, op1=ALU.mult), R=[bhf[j], brstd, bsp], W=[byo[j]])
                            P.dma(YM2[:, :, t0:t0 + NT], yo[j][:], R=[byo[j]], W=[bYM], via=byo[j])

            if lvl < 7:
                break
            with contextlib.ExitStack() as ph:
                P.phase_end()
                NT = 256
                NFC = DFF // 128
                wo = sb("p3wo", [128, 8, D], BF16, ph)
                wg = sb("p3wg", [128, 8, DFF], BF16, ph)
                wu = sb("p3wu", [128, 8, DFF], BF16, ph)
                wd = sb("p3wd", [128, NFC, D], BF16, ph)
                bwo, bwg, bwu, bwd = (Buf(n, multi=True) for n in ("p3wo", "p3wg", "p3wu", "p3wd"))
                with contextlib.ExitStack() as wl:
                    P.phase_end()
                    load_cast(wl, wo, w_out_d[l].rearrange("(kc p) n -> p kc n", p=128), 8, D, 256, bwo, "wo")
                    load_cast(wl, wg, wg_d[l].rearrange("(kc p) n -> p kc n", p=128), 8, DFF, 256, bwg, "wg")
                    load_cast(wl, wu, wu_d[l].rearrange("(kc p) n -> p kc n", p=128), 8, DFF, 256, bwu, "wu")
                with contextlib.ExitStack() as wl:
                    P.phase_end()
                    load_cast(wl, wd, wd_d[l].rearrange("(fc p) n -> p fc n", p=128), NFC, D, 64, bwd, "wd")
                P.phase_end()
                xt = [sb(f"p3x{i}", [128, 8, NT], F32, ph) for i in range(2)]
                bxt = [Buf(f"p3x{i}") for i in range(2)]
                ym = [sb(f"p3ym{i}", [128, 8, NT], BF16, ph) for i in range(2)]
                bym = [Buf(f"p3ym{i}") for i in range(2)]
                hb = sb("p3h", [128, 8, NT], BF16, ph)
                bhb = Buf("p3h")
                tmpn = [sb(f"p3tmp{i}", [128, NT], F32, ph) for i in range(2)]
                btmpn = [Buf(f"p3tmp{i}") for i in range(2)]
                sq = sb("p3sq", [128, 8, NT], BF16, ph)
                bsq = Buf("p3sq")
                rstd = sb("p3rstd", [128, NT], F32, ph)
                brstd = Buf("p3rstd")
                av = sb("p3a", [128, NFC, NT], BF16, ph)
                bav = Buf("p3a", multi=True)
                sg = [sb(f"p3sg{i}", [128, NT], F32, ph) for i in range(2)]
                bsg = [Buf(f"p3sg{i}") for i in range(2)]
                XTv = XT.rearrange("(kc p) t -> p kc t", p=128)
                YMv = YM.rearrange("(kc p) t -> p kc t", p=128)
                for i in range(T // NT):
                    j = i % 2
                    t0 = i * NT
                    seg = t0 // SEG
                    P.dma(xt[j][:], XTv[:, :, t0:t0 + NT], R=[bXT], W=[bxt[j]], via=bxt[j])
                    P.dma(ym[j][:], YMv[:, :, t0:t0 + NT], R=[bYM], W=[bym[j]], via=bym[j])
                    for mc in range(8):
                        pb = mc % 2

                        def mm(e, mc=mc, pb=pb):
                            for kc in range(8):
                                i_ = e.matmul(psf[pb][:, 0:NT], lhsT=wo[:, kc, mc * 128:(mc + 1) * 128], rhs=ym[j][:, kc, :], start=(kc == 0), stop=(kc == 7))
                            return i_
                        P.op(pe, mm, R=[bwo, bym[j]], W=[bpsf[pb]])
                        P.op(dve, lambda e, mc=mc, pb=pb: e.scalar_tensor_tensor(out=xt[j][:, mc, :], in0=psf[pb][:, 0:NT], scalar=modG[0][:, mc, seg:seg + 1],
                                                                                in1=xt[j][:, mc, :], op0=ALU.mult, op1=ALU.add),
                             R=[bpsf[pb], bmod, bxt[j]], W=[bxt[j]])
                    rms_rstd(xt[j][:], 8, NT, sq, rstd, bxt[j], bsq, brstd, psf[2], bpsf[2], 1.0 / D)
                    for kc in range(8):
                        tb = kc % 2
                        P.op(pool, lambda e, kc=kc, tb=tb: e.tensor_tensor(out=tmpn[tb][:], in0=xt[j][:, kc, :], in1=rstd[:], op=ALU.mult),
                             R=[bxt[j], brstd], W=[btmpn[tb]])
                        P.op(act, lambda e, kc=kc, tb=tb: e.activation(out=hb[:, kc, :], in_=tmpn[tb][:], func=AF.Identity,
                                                                       scale=modA[1][:, kc, seg:seg + 1], bias=modB[1][:, kc, seg:seg + 1]),
                             R=[btmpn[tb], bmod], W=[bhb])
                    for fc in range(NFC):
                        pb = 3 + fc % 3

                        def mm(e, fc=fc, pb=pb):
                            for kc in range(8):
                                e.matmul(psf[pb][:, 0:NT], lhsT=wg[:, kc, fc * 128:(fc + 1) * 128], rhs=hb[:, kc, :], start=(kc == 0), stop=(kc == 7),
                                         skip_group_check=True)
                            for kc in range(8):
                                i_ = e.matmul(psf[pb][:, NT:2 * NT], lhsT=wu[:, kc, fc * 128:(fc + 1) * 128], rhs=hb[:, kc, :], start=(kc == 0), stop=(kc == 7),
                                              skip_group_check=True)
                            return i_
                        P.op(pe, mm, R=[bwg, bwu, bhb], W=[bpsf[pb]])
                        s = fc % 2
                        P.op(act, lambda e, pb=pb, s=s: e.activation(out=sg[s][:], in_=psf[pb][:, 0:NT], func=AF.Silu), R=[bpsf[pb]], W=[bsg[s]])
                        P.op(dve, lambda e, pb=pb, s=s, fc=fc: e.tensor_tensor(out=av[:, fc, :], in0=sg[s][:], in1=psf[pb][:, NT:2 * NT], op=ALU.mult),
                             R=[bsg[s], bpsf[pb]], W=[bav])
                    for mc in range(8):
                        pb = mc % 2

                        def mm(e, mc=mc, pb=pb):
                            for fc in range(NFC):
                                i_ = e.matmul(psf[pb][:, 0:NT], lhsT=wd[:, fc, mc * 128:(mc + 1) * 128], rhs=av[:, fc, :], start=(fc == 0), stop=(fc == NFC - 1))
                            return i_
                        P.op(pe, mm, R=[bwd, bav], W=[bpsf[pb]])
                        P.op(dve, lambda e, mc=mc, pb=pb: e.scalar_tensor_tensor(out=xt[j][:, mc, :], in0=psf[pb][:, 0:NT], scalar=modG[1][:, mc, seg:seg + 1],
                                                                                in1=xt[j][:, mc, :], op0=ALU.mult, op1=ALU.add),
                             R=[bpsf[pb], bmod, bxt[j]], W=[bxt[j]])
                    P.dma(XTv[:, :, t0:t0 + NT], xt[j][:], R=[bxt[j]], W=[bXT], via=bxt[j])

        with contextlib.ExitStack() as ph:
            P.phase_end()
            NT = 512
            xt = [sb(f"p4x{i}", [128, 8, NT], F32, ph) for i in range(2)]
            bxt = [Buf(f"p4x{i}") for i in range(2)]
            sq = sb("p4sq", [128, 8, NT], BF16, ph)
            bsq = Buf("p4sq")
            rstd = sb("p4rstd", [128, NT], F32, ph)
            brstd = Buf("p4rstd")
            yo = [sb(f"p4y{i}", [128, D], F32, ph) for i in range(2)]
            byo = [Buf(f"p4y{i}") for i in range(2)]
            XTv = XT.rearrange("(kc p) t -> p kc t", p=128)
            oc = 0
            for i in range(T // NT if lvl >= 2 else 0):
                j = i % 2
                t0 = i * NT
                P.dma(xt[j][:], XTv[:, :, t0:t0 + NT], R=[bXT], W=[bxt[j]], via=bxt[j])
                rms_rstd(xt[j][:], 8, NT, sq, rstd, bxt[j], bsq, brstd, psf[0], bpsf[0], 1.0 / D)
                P.op(dve, lambda e: e.tensor_tensor(out=xt[j][:], in0=xt[j][:], in1=rstd[:].unsqueeze(1).broadcast_to([128, 8, NT]), op=ALU.mult),
                     R=[bxt[j], brstd], W=[bxt[j]])
                P.op(dve, lambda e: e.tensor_tensor(out=xt[j][:], in0=xt[j][:],
                                                    in1=spt[:, O_FN:O_FN + 8].unsqueeze(2).broadcast_to([128, 8, NT]), op=ALU.mult),
                     R=[bxt[j], bsp], W=[bxt[j]])
                for bq in range(NT // 128):
                    o = oc % 2
                    oc += 1
                    for half in range(2):
                        pb = 1 + (2 * oc + half) % 4

                        def tr(e, half=half, pb=pb, bq=bq):
                            for q in range(4):
                                kc = half * 4 + q
                                i_ = e.transpose(out=psf[pb][:, q * 128:(q + 1) * 128], in_=xt[j][:, kc, bq * 128:(bq + 1) * 128], identity=ident_f[:])
                            return i_
                        P.op(pe, tr, R=[bxt[j], bconst], W=[bpsf[pb]])
                        if half == 0:
                            P.op(act, lambda e, pb=pb, o=o: e.activation(out=yo[o][:, 0:512], in_=psf[pb][:], func=AF.Copy), R=[bpsf[pb]], W=[byo[o]])
                        else:
                            P.op(dve, lambda e, pb=pb, o=o: e.tensor_copy(out=yo[o][:, 512:1024], in_=psf[pb][:]), R=[bpsf[pb]], W=[byo[o]])
                    P.dma(y_out[t0 + bq * 128:t0 + (bq + 1) * 128, :], yo[o][:], R=[byo[o]], W=[bY], via=byo[o])
        P.finish()
    return nc


def _col(v):
    v = np.asarray(v, np.float32)
    return np.ascontiguousarray(v.reshape(-1, 128).T)


def pack_small(inp):
    out = np.zeros((DEPTH, 128, NSP), np.float32)
    deltas = np.abs(np.linspace(math.log(1e-2) / 1.5, math.log(1e-2) / 0.3, 256, dtype=np.float32)).astype(np.float32)
    for l in range(DEPTH):
        o = out[l]
        o[:, O_N1:O_N1 + 8] = _col(inp["norm1_w"][l])
        o[:, O_N2:O_N2 + 8] = _col(inp["norm2_w"][l])
        o[:, O_ON:O_ON + 8] = _col(inp["out_norm_w"][l])
        o[:, O_ADAB:O_ADAB + 48] = _col(inp["ada_b"][l])
        cw = np.asarray(inp["hy_conv_w"][l], np.float32)
        o[:, O_HCW:O_HCW + 18] = cw.T.reshape(6, 128, 3).transpose(1, 0, 2).reshape(128, 18)
        o[:, O_HCB:O_HCB + 6] = _col(inp["hy_conv_b"][l])
        o[:, O_HBIAS:O_HBIAS + 2] = _col(inp["hy_bias"][l])
        rw = np.asarray(inp["rg_conv_w"][l], np.float32)
        o[:, O_RCW:O_RCW + 8] = rw.T.reshape(2, 128, 4).transpose(1, 0, 2).reshape(128, 8)
        o[:, O_RCB:O_RCB + 2] = _col(inp["rg_conv_b"][l])
        for nm, off in (("rg_ba", O_RBA), ("rg_bx", O_RBX), ("rg_lambda", O_RLAM)):
            a = np.asarray(inp[nm][l], np.float32)
            o[:, off:off + 4] = a.reshape(2, 2, 128).transpose(2, 0, 1).reshape(128, 4)
        lb = np.asarray(inp["hg_lb_logits"], np.float32)
        o[:, O_LB:O_LB + 16] = lb.reshape(2, DEPTH, 4, 128).transpose(3, 0, 1, 2).reshape(128, 16)
        o[0:64, O_HYB + 0] = inp["hy_b1"][l]
        o[0:64, O_HYB + 1] = inp["hy_b2"][l]
        o[0:64, O_HYB + 2] = inp["hy_b3"][l]
        o[0:64, O_HYB + 3] = inp["hy_freq"][l]
        o[:, O_FN:O_FN + 8] = _col(inp["final_norm_w"])
        o[:, O_ND:O_ND + 2] = -_col(deltas)
    return out


def pos_tables(L, T):
    f32 = np.float32
    t = np.linspace(0.0, 1.0, L, dtype=f32)
    n = np.arange(L, dtype=f32)
    fb = np.linspace(1e-4, 15.0, 16, dtype=f32)
    ang = (fb[None, :] * n[:, None]) * f32(2.0 * math.pi / L)
    z = np.concatenate([t[:, None], np.cos(ang), -np.sin(ang)], axis=-1).astype(f32)
    zfeat = np.zeros((HY_EMB, T), f32)
    zfeat[:, :L] = z.T
    trow = np.full((1, T), 1e4, f32)
    trow[0, :L] = t
    return zfeat, trow


_CACHE = {}


def _get_prog(T):
    if T not in _CACHE:
        _CACHE[T] = build_program(T)
    return _CACHE[T]


def run_cores(core_specs, inp, T):
    nc = _get_prog(T)
    sp = pack_small(inp)
    shared = {k: np.ascontiguousarray(np.asarray(inp[k], np.float32)) for k in
              ("w_in", "w_out", "ada_w", "ffn_wg", "ffn_wu", "ffn_wd", "hy_w1", "hy_w2", "hy_w3", "hy_w4", "rg_wa", "rg_wx")}
    in_maps = []
    tabs = {}
    for (x, c_rows, L) in core_specs:
        if L not in tabs:
            tabs[L] = pos_tables(L, T)
        zfeat, trow = tabs[L]
        m = dict(shared)
        m["x"] = np.ascontiguousarray(x, dtype=np.float32)
        m["cT"] = np.ascontiguousarray(np.asarray(c_rows, np.float32).T)
        m["segflag"] = np.full((128, 1), 1.0 if L == T else 0.0, np.float32)
        m["zfeat"] = zfeat
        m["trow"] = trow
        m["sp"] = sp
        in_maps.append(m)
    res = run_bass_kernel_spmd(nc, in_maps, core_ids=list(range(len(in_maps))))
    return [r["y"] for r in res.results]


def kernel(**inp):
    T = 16384
    xp = np.asarray(inp["x_prompt"], np.float32)
    xs = np.asarray(inp["x_sample"], np.float32)
    cp = np.asarray(inp["c_prompt"], np.float32)
    cs = np.asarray(inp["c_sample"], np.float32)
    specs = []
    for s in range(2):
        specs.append((xs[s], np.repeat(cs[s:s + 1], NSEG, axis=0), T))
    for g in range(2):
        specs.append((xp[4 * g:4 * g + 4].reshape(T, D), cp[4 * g:4 * g + 4], T // NSEG))
    for _ in range(4):
        specs.append((np.zeros((T, D), np.float32), np.zeros((NSEG, D), np.float32), T // NSEG))
    ys = run_cores(specs, inp, T)
    y_sample = np.stack([ys[0], ys[1]], axis=0)
    y_prompt = np.concatenate([ys[2].reshape(4, T // NSEG, D), ys[3].reshape(4, T // NSEG, D)], axis=0)
    return (y_prompt, y_sample)
```

```python
import contextlib
import math
import numpy as np
import ml_dtypes
import concourse.bass as bass
import concourse.mybir as mybir
from concourse.bass_utils import run_bass_kernel_spmd

F32 = mybir.dt.float32
BF16 = mybir.dt.bfloat16
AF = mybir.ActivationFunctionType
ALU = mybir.AluOpType
PI = math.pi

D = 1024
DIN = 3840
DFF = 2816
DEPTH = 2
NSEG = 4
EPS = 1e-6
HY_EMB = 33
O_N1, O_N2, O_ON, O_ADAB, O_HCW, O_HCB, O_HBIAS, O_RCW, O_RCB, O_RBA, O_RBX, O_RLAM, O_LB, O_HYB, O_FN, O_ND = (
    0, 8, 16, 24, 72, 90, 96, 98, 106, 108, 112, 116, 120, 136, 140, 148)
NSP = 150


class Buf:
    def __init__(self, name, multi=False, excl=False):
        self.name = name
        self.multi = multi
        self.excl = excl
        self.w = {}
        self.r = {}
        self.slot = None


class Eng:
    def __init__(self, prog, e, name):
        self.e = e
        self.name = name
        self.sem = prog.sem("e_" + name)
        self.cnt = 0
        self.waited = {}


class Prog:
    def __init__(self, nc, es):
        self.nc = nc
        self.es = es
        self.nsem = 0
        self.pe = Eng(self, nc.tensor, "pe")
        self.act = Eng(self, nc.scalar, "act")
        self.dve = Eng(self, nc.vector, "dve")
        self.pool = Eng(self, nc.gpsimd, "pool")
        self.sp = Eng(self, nc.sync, "sp")
        self.slots = []
        self.slot_pool = []
        self.phase_bufs = []

    def sem(self, name):
        h = self.es.enter_context(self.nc.semaphore(f"{name}_{self.nsem}"))
        self.nsem += 1
        return h

    def _wait(self, eng, R, W, nowaw=False):
        deps = {}
        war = {}
        for b in R:
            for k, v in b.w.items():
                if deps.get(k, 0) < v:
                    deps[k] = v
        for b in W:
            if not (b.multi or nowaw):
                for k, v in b.w.items():
                    if deps.get(k, 0) < v:
                        deps[k] = v
            for k, v in b.r.items():
                if war.get(k, 0) < v:
                    war[k] = v
        for sem, val in war.items():
            if sem is eng.sem:
                continue
            if deps.get(sem, 0) < val:
                deps[sem] = val
        for sem, val in deps.items():
            if sem is eng.sem and eng is self.pe:
                continue
            if eng.waited.get(sem, 0) >= val:
                continue
            eng.e.wait_ge(sem, val)
            eng.waited[sem] = val

    def _mark(self, R, W, sem, val, nowaw=False):
        for b in W:
            if b.multi or nowaw:
                b.w[sem] = val
            else:
                b.w = {sem: val}
                b.r = {}
        for b in R:
            if b.r.get(sem, 0) < val:
                b.r[sem] = val

    def op(self, eng, fn, R=(), W=()):
        ex = [b for b in R if b.excl]
        if ex:
            R = [b for b in R if not b.excl]
            W = list(W) + ex
        self._wait(eng, R, W)
        inst = fn(eng.e)
        eng.cnt += 1
        inst.then_inc(eng.sem, 1)
        self._mark(R, W, eng.sem, eng.cnt)

    def dma(self, out, in_, R=(), W=(), via=None, nowaw=False, **kw):
        self._wait(self.sp, R, W, nowaw)
        if via.slot is None:
            if self.slot_pool:
                via.slot = self.slot_pool.pop(0)
            else:
                via.slot = [self.sem("dma"), 0]
                self.slots.append(via.slot)
            self.phase_bufs.append(via)
        sl = via.slot
        inst = self.nc.sync.dma_start(out=out, in_=in_, **kw)
        sl[1] += 16
        inst.then_inc(sl[0], 16)
        self._mark(R, W, sl[0], sl[1], nowaw)
        if _SYNC_DMA:
            self.nc.sync.wait_ge(sl[0], sl[1])
            self.sp.waited[sl[0]] = sl[1]

    def barrier(self):
        engs = [self.pe, self.act, self.dve, self.pool, self.sp]
        for E in engs:
            for F in engs:
                if F is E or F.cnt == 0:
                    continue
                if E.waited.get(F.sem, 0) < F.cnt:
                    E.e.wait_ge(F.sem, F.cnt)
                    E.waited[F.sem] = F.cnt
            for sl in self.slots:
                if sl[1] > 0 and E.waited.get(sl[0], 0) < sl[1]:
                    E.e.wait_ge(sl[0], sl[1])
                    E.waited[sl[0]] = sl[1]

    def phase_end(self):
        self.barrier()
        for b in self.phase_bufs:
            self.slot_pool.append(b.slot)
            b.slot = None
        self.phase_bufs = []

    def finish(self):
        for sl in self.slots:
            if self.sp.waited.get(sl[0], 0) < sl[1]:
                self.nc.sync.wait_ge(sl[0], sl[1])


_DBG_STOP = None
_SYNC_DMA = False
_DBG_SUB = 99
_DBG_X = 99
_STAGES = ["c0", "c1", "p0", "p1", "hy", "hg", "rg", "p3"]


def build_program(T):
    SEG = T // NSEG
    stop = _DBG_STOP
    lvl = _STAGES.index(stop) if stop else 99
    NB = T // 128
    nc = bass.Bass("TRN2", target_bir_lowering=False)
    dt_in = lambda name, shape, dt=F32: nc.dram_tensor(name, shape, dt, kind="ExternalInput").ap()
    x_in = dt_in("x", [T, D])
    cT_in = dt_in("cT", [D, NSEG])
    segflag_in = dt_in("segflag", [128, 1])
    zfeat_in = dt_in("zfeat", [HY_EMB, T])
    trow_in = dt_in("trow", [1, T])
    w_in_d = dt_in("w_in", [DEPTH, D, DIN])
    w_out_d = dt_in("w_out", [DEPTH, D, D])
    ada_w_d = dt_in("ada_w", [DEPTH, D, 6 * D])
    wg_d = dt_in("ffn_wg", [DEPTH, D, DFF])
    wu_d = dt_in("ffn_wu", [DEPTH, D, DFF])
    wd_d = dt_in("ffn_wd", [DEPTH, DFF, D])
    sp_d = dt_in("sp", [DEPTH, 128, NSP])
    hw1_d = dt_in("hy_w1", [DEPTH, HY_EMB, 64])
    hw2_d = dt_in("hy_w2", [DEPTH, 64, 64])
    hw3_d = dt_in("hy_w3", [DEPTH, 64, 64])
    hw4_d = dt_in("hy_w4", [DEPTH, 64, 512])
    rwa_d = dt_in("rg_wa", [DEPTH, 2, 4, 64, 64])
    rwx_d = dt_in("rg_wx", [DEPTH, 2, 4, 64, 64])
    y_out = nc.dram_tensor("y", [T, D], F32, kind="ExternalOutput").ap()
    scr = lambda name, shape, dt=F32: nc.dram_tensor(name, shape, dt, kind=("ExternalOutput" if stop else "Internal")).ap()
    XT = scr("XT", [D, T])
    U = scr("U", [DIN, T])
    YM = scr("YM", [D, T], BF16)
    HF = scr("HF", [256, T])
    OF = scr("OF", [512, T])
    KF = scr("KF", [256, 2 * T], BF16)
    DBGY = scr("DBGY", [128, NB * 128]) if stop else None
    DBGV = scr("DBGV", [128, NB * 256], BF16) if stop else None
    bDBG = Buf("DBG", multi=True)
    bXT, bU, bYM, bHF, bOF, bKF, bY = (Buf(n, multi=True) for n in ("XT", "U", "YM", "HF", "OF", "KF", "Y"))
    bIN = Buf("inputs")

    with contextlib.ExitStack() as es:
        P = Prog(nc, es)
        pe, act, dve, pool = P.pe, P.act, P.dve, P.pool

        uniq = [0]

        def sb(name, shape, dt=F32, stack=es):
            uniq[0] += 1
            return stack.enter_context(nc.sbuf_tensor(f"{name}_u{uniq[0]}", shape, dt))

        dbg_n = [0]

        def dbg(name, ap, buf, shape, dt=F32):
            if not stop:
                return
            dbg_n[0] += 1
            dd = nc.dram_tensor(f"DBG_{name}", shape, dt, kind="ExternalOutput").ap()
            P.dma(dd, ap, R=[buf], W=[bDBG], via=buf)

        psf = [es.enter_context(nc.psum_tensor(f"psf{i}", [128, 512], F32)) for i in range(7)]
        psb = [es.enter_context(nc.psum_tensor(f"psb{i}", [128, 1024], BF16)) for i in range(1)]
        bpsf = [Buf(f"psf{i}", excl=True) for i in range(7)]
        bpsb = [Buf(f"psb{i}", excl=True) for i in range(1)]

        ones_bf = sb("ones_bf", [128, 128], BF16)
        ident_bf = sb("ident_bf", [128, 128], BF16)
        ident_f = sb("ident_f", [128, 128], F32)
        mask_lo = sb("mask_lo", [64, 8, 64], F32)
        mask_hi = sb("mask_hi", [64, 8, 64], F32)
        cm_f = sb("cm_f", [128, 512], F32)
        cm_b = sb("cm_b", [128, 512], F32)
        epsb = sb("epsb", [128, 1], F32)
        segflag = sb("segflag_t", [128, 1], F32)
        zero_c = sb("zero_c", [128, 1], F32)
        dummy = sb("dummy", [128, 8], F32)
        bconst = Buf("const")
        P.op(dve, lambda e: e.memset(ones_bf[:], 1.0), W=[bconst])
        P.op(dve, lambda e: e.memset(ident_bf[:], 0.0), W=[bconst])
        P.op(dve, lambda e: e.memset(ident_f[:], 0.0), W=[bconst])
        P.op(dve, lambda e: e.memset(mask_lo[:], 1.0), W=[bconst])
        P.op(dve, lambda e: e.memset(mask_hi[:], 1.0), W=[bconst])
        P.op(dve, lambda e: e.memset(cm_f[:], 1.0), W=[bconst])
        P.op(dve, lambda e: e.memset(cm_b[:], 1.0), W=[bconst])
        P.op(dve, lambda e: e.memset(epsb[:], EPS), W=[bconst])
        P.op(dve, lambda e: e.memset(zero_c[:], 0.0), W=[bconst])
        P.op(pool, lambda e: e.affine_select(out=ident_bf[:], in_=ident_bf[:], pattern=[[-1, 128]],
                                             compare_op=ALU.not_equal, fill=1.0, base=0, channel_multiplier=1),
             R=[bconst], W=[bconst])
        P.op(pool, lambda e: e.affine_select(out=ident_f[:], in_=ident_f[:], pattern=[[-1, 128]],
                                             compare_op=ALU.not_equal, fill=1.0, base=0, channel_multiplier=1),
             R=[bconst], W=[bconst])
        P.op(pool, lambda e: e.affine_select(out=mask_lo[:], in_=mask_lo[:], pattern=[[0, 8], [1, 64]],
                                             compare_op=ALU.is_ge, fill=0.0, base=0, channel_multiplier=-1),
             R=[bconst], W=[bconst])
        P.op(pool, lambda e: e.affine_select(out=mask_hi[:], in_=mask_hi[:], pattern=[[0, 8], [-1, 64]],
                                             compare_op=ALU.is_ge, fill=0.0, base=0, channel_multiplier=1),
             R=[bconst], W=[bconst])
        cmf3 = cm_f[:].rearrange("p (c t) -> p c t", t=64)
        cmb3 = cm_b[:].rearrange("p (c t) -> p c t", t=64)
        P.op(dve, lambda e: e.memset(cmf3[:, :, 0:1], 0.0), R=[bconst], W=[bconst])
        P.op(dve, lambda e: e.memset(cmb3[:, :, 63:64], 0.0), R=[bconst], W=[bconst])
        P.dma(segflag[:], segflag_in[:, :], R=[bIN], W=[bconst], via=bconst)

        spt = sb("spt", [128, NSP], F32)
        bsp = Buf("spt")
        modA = [sb(f"modA{i}", [128, 8, NSEG], F32) for i in range(2)]
        modB = [sb(f"modB{i}", [128, 8, NSEG], F32) for i in range(2)]
        modG = [sb(f"modG{i}", [128, 8, NSEG], F32) for i in range(2)]
        bmod = Buf("mod")
        lbt = sb("lbt", [128, 2, 4], F32)
        omlt = sb("omlt", [128, 2, 4], F32)
        clam = sb("clam", [128, 4], F32)
        clam2 = sb("clam2", [128, 4], F32)
        blay = Buf("layer_small")

        def load_cast(stack, dst, src3, K, N, piece, bdst, tag):
            stg = [sb(f"wst_{tag}{i}", [128, K, piece], F32, stack) for i in range(2)]
            bst = [Buf(f"wst_{tag}{i}") for i in range(2)]
            n0 = 0
            i = 0
            while n0 < N:
                w = min(piece, N - n0)
                P.dma(stg[i % 2][:, :, 0:w], src3[:, :, n0:n0 + w], R=[bIN], W=[bst[i % 2]], via=bst[i % 2])
                if i % 2 == 0:
                    P.op(dve, lambda e, i=i, n0=n0, w=w: e.tensor_copy(out=dst[:, :, n0:n0 + w], in_=stg[i % 2][:, :, 0:w]),
                         R=[bst[i % 2]], W=[bdst])
                else:
                    P.op(act, lambda e, i=i, n0=n0, w=w: e.activation(out=dst[:, :, n0:n0 + w], in_=stg[i % 2][:, :, 0:w], func=AF.Copy),
                         R=[bst[i % 2]], W=[bdst])
                n0 += w
                i += 1

        def rms_rstd(xt, K, NT, sq, rstd, bx, bsq, brstd, ps, bps, inv_n):
            P.op(act, lambda e: e.activation(out=sq[:, 0:K, 0:NT], in_=xt, func=AF.Square), R=[bx], W=[bsq])

            def mm(e):
                for kc in range(K):
                    i = e.matmul(ps[:, 0:NT], lhsT=ones_bf[:], rhs=sq[:, kc, 0:NT], start=(kc == 0), stop=(kc == K - 1))
                return i
            P.op(pe, mm, R=[bsq, bconst], W=[bps])
            P.op(act, lambda e: e.activation(out=rstd[:, 0:NT], in_=ps[:, 0:NT], func=AF.Sqrt, scale=inv_n, bias=epsb[:, 0:1]),
                 R=[bps, bconst], W=[brstd])
            P.op(dve, lambda e: e.reciprocal(out=rstd[:, 0:NT], in_=rstd[:, 0:NT]), R=[brstd], W=[brstd])

        with contextlib.ExitStack() as ph:
            P.phase_end()
            xin = [sb(f"p0x{i}", [128, D], F32, ph) for i in range(2)]
            xo = [sb(f"p0o{i}", [128, 8, 128], F32, ph) for i in range(2)]
            bxin = [Buf(f"p0x{i}") for i in range(2)]
            bxo = [Buf(f"p0o{i}") for i in range(2)]
            XTv = XT.rearrange("(kc p) t -> p kc t", p=128)
            for b in range(NB if lvl >= 1 else 0):
                j = b % 2
                P.dma(xin[j][:], x_in[b * 128:(b + 1) * 128, :], R=[bIN], W=[bxin[j]], via=bxin[j])
                for half in range(2):
                    pb = (2 * b + half) % 4

                    def tr(e, half=half, pb=pb):
                        for q in range(4):
                            kc = half * 4 + q
                            i = e.transpose(out=psf[pb][:, q * 128:(q + 1) * 128], in_=xin[j][:, kc * 128:(kc + 1) * 128],
                                            identity=ident_f[:])
                        return i
                    P.op(pe, tr, R=[bxin[j], bconst], W=[bpsf[pb]])
                    eng = act if half == 0 else dve
                    if half == 0:
                        P.op(act, lambda e, pb=pb: e.activation(out=xo[j][:, 0:4, :], in_=psf[pb][:].rearrange("p (q t) -> p q t", t=128),
                                                                func=AF.Copy), R=[bpsf[pb]], W=[bxo[j]])
                    else:
                        P.op(dve, lambda e, pb=pb: e.tensor_copy(out=xo[j][:, 4:8, :], in_=psf[pb][:].rearrange("p (q t) -> p q t", t=128)),
                             R=[bpsf[pb]], W=[bxo[j]])
                P.dma(XTv[:, :, b * 128:(b + 1) * 128], xo[j][:], R=[bxo[j]], W=[bXT], via=bxo[j])

        for l in range(DEPTH):
            if lvl < 3 or (l > 0 and lvl < 99):
                break
            P.dma(spt[:], sp_d[l], R=[bIN], W=[bsp], via=bsp)
            with contextlib.ExitStack() as ph:
                P.phase_end()
                ct = sb("ada_c", [128, 8, NSEG], F32, ph)
                bct = Buf("ada_c")
                modT = sb("ada_mod", [128, 48, NSEG], F32, ph)
                bmodT = Buf("ada_mod")
                P.dma(ct[:], cT_in.rearrange("(kc p) s -> p kc s", p=128), R=[bIN], W=[bct], via=bct)
                P.op(act, lambda e: e.activation(out=ct[:], in_=ct[:], func=AF.Silu), R=[bct], W=[bct])
                aw = [sb(f"ada_w{i}", [128, 8, 512], F32, ph) for i in range(2)]
                baw = [Buf(f"ada_w{i}") for i in range(2)]
                awv = ada_w_d[l].rearrange("(kc p) n -> p kc n", p=128)
                for pc in range(12):
                    j = pc % 2
                    P.dma(aw[j][:], awv[:, :, pc * 512:(pc + 1) * 512], R=[bIN], W=[baw[j]], via=baw[j])
                    pb = pc % 2

                    def mm(e, j=j, pb=pb):
                        for m in range(4):
                            for kc in range(8):
                                i = e.matmul(psf[pb][:, m * NSEG:(m + 1) * NSEG], lhsT=aw[j][:, kc, m * 128:(m + 1) * 128],
                                             rhs=ct[:, kc, :], start=(kc == 0), stop=(kc == 7))
                        return i
                    P.op(pe, mm, R=[baw[j], bct], W=[bpsf[pb]])
                    P.op(dve, lambda e, pc=pc, pb=pb: e.tensor_tensor(
                        out=modT[:, pc * 4:(pc + 1) * 4, :], in0=psf[pb][:, 0:4 * NSEG].rearrange("p (m s) -> p m s", s=NSEG),
                        in1=spt[:, O_ADAB + pc * 4:O_ADAB + (pc + 1) * 4].unsqueeze(2).broadcast_to([128, 4, NSEG]), op=ALU.add),
                        R=[bpsf[pb], bsp], W=[bmodT])
                for i2, (osh, osc, og, onw) in enumerate(((0, 8, 16, O_N1), (24, 32, 40, O_N2))):
                    P.op(dve, lambda e, osc=osc, onw=onw, i2=i2: e.scalar_tensor_tensor(
                        out=modA[i2][:], in0=modT[:, osc:osc + 8, :], scalar=1.0,
                        in1=spt[:, onw:onw + 8].unsqueeze(2).broadcast_to([128, 8, NSEG]), op0=ALU.add, op1=ALU.mult),
                        R=[bmodT, bsp], W=[bmod])
                    P.op(dve, lambda e, osh=osh, i2=i2: e.tensor_copy(out=modB[i2][:], in_=modT[:, osh:osh + 8, :]), R=[bmodT], W=[bmod])
                    P.op(dve, lambda e, og=og, i2=i2: e.tensor_copy(out=modG[i2][:], in_=modT[:, og:og + 8, :]), R=[bmodT], W=[bmod])
                lbv = spt[:, O_LB:O_LB + 16].rearrange("p (d l h) -> p d l h", d=2, l=2)
                if l == 0:
                    P.op(dve, lambda e: e.memset(lbt[:], 0.0), W=[blay])
                else:
                    P.op(dve, lambda e: e.tensor_tensor(out=lbt[:], in0=lbv[:, :, 1, :], in1=lbv[:, :, 0, :], op=ALU.subtract),
                         R=[bsp], W=[blay])
                    P.op(act, lambda e: e.activation(out=lbt[:], in_=lbt[:], func=AF.Sigmoid), R=[blay], W=[blay])
                P.op(dve, lambda e: e.tensor_scalar(out=omlt[:], in0=lbt[:], scalar1=-1.0, scalar2=1.0, op0=ALU.mult, op1=ALU.add),
                     R=[blay], W=[blay])
                dbg(f"omlt_a{l}", omlt[:].rearrange("p a b -> p (a b)"), blay, [128, 8])
                P.op(act, lambda e: e.activation(out=clam[:], in_=spt[:, O_RLAM:O_RLAM + 4], func=AF.Exp, scale=-1.0), R=[bsp], W=[blay])
                P.op(dve, lambda e: e.tensor_scalar(out=clam[:], in0=clam[:], scalar1=1.0, scalar2=None, op0=ALU.add), R=[blay], W=[blay])
                P.op(act, lambda e: e.activation(out=clam[:], in_=clam[:], func=AF.Ln), R=[blay], W=[blay])
                P.op(dve, lambda e: e.tensor_scalar(out=clam2[:], in0=clam[:], scalar1=-16.0, scalar2=None, op0=ALU.mult), R=[blay], W=[blay])
                P.op(dve, lambda e: e.tensor_scalar(out=clam[:], in0=clam[:], scalar1=-8.0, scalar2=None, op0=ALU.mult), R=[blay], W=[blay])

            with contextlib.ExitStack() as ph:
                P.phase_end()
                NT = 512
                wbf = sb("p1w", [128, 8, DIN], BF16, ph)
                bw = Buf("p1w", multi=True)
                xt = [sb(f"p1x{i}", [128, 8, NT], F32, ph) for i in range(2)]
                bxt = [Buf(f"p1x{i}") for i in range(2)]
                hb = [sb(f"p1h{i}", [128, 8, NT], BF16, ph) for i in range(2)]
                bhb = [Buf(f"p1h{i}") for i in range(2)]
                sq = sb("p1sq", [128, 8, NT], BF16, ph)
                bsq = Buf("p1sq")
                rstd = sb("p1rstd", [128, NT], F32, ph)
                brstd = Buf("p1rstd")
                stg = [sb(f"p1st{i}", [128, 3, NT], F32, ph) for i in range(2)]
                bstg = [Buf(f"p1st{i}") for i in range(2)]
                with contextlib.ExitStack() as wl:
                    P.phase_end()
                    load_cast(wl, wbf, w_in_d[l].rearrange("(kc p) n -> p kc n", p=128), 8, DIN, 240, bw, "p1")
                XTv = XT.rearrange("(kc p) t -> p kc t", p=128)
                Uv = U.rearrange("(c p) t -> p c t", p=128)
                for i in range(T // NT):
                    j = i % 2
                    t0 = i * NT
                    seg = t0 // SEG
                    P.dma(xt[j][:], XTv[:, :, t0:t0 + NT], R=[bXT], W=[bxt[j]], via=bxt[j])
                    rms_rstd(xt[j][:], 8, NT, sq, rstd, bxt[j], bsq, brstd, psf[0], bpsf[0], 1.0 / D)
                    P.op(dve, lambda e: e.tensor_tensor(out=xt[j][:], in0=xt[j][:], in1=rstd[:].unsqueeze(1).broadcast_to([128, 8, NT]),
                                                        op=ALU.mult), R=[bxt[j], brstd], W=[bxt[j]])
                    P.op(dve, lambda e: e.tensor_tensor(out=xt[j][:], in0=xt[j][:],
                                                        in1=modA[0][:, :, seg:seg + 1].broadcast_to([128, 8, NT]), op=ALU.mult),
                         R=[bxt[j], bmod], W=[bxt[j]])
                    P.op(dve, lambda e: e.tensor_tensor(out=hb[j][:], in0=xt[j][:],
                                                        in1=modB[0][:, :, seg:seg + 1].broadcast_to([128, 8, NT]), op=ALU.add),
                         R=[bxt[j], bmod], W=[bhb[j]])
                    for mc in range(DIN // 128):
                        pb = 1 + mc % 4

                        def mm(e, mc=mc, pb=pb):
                            for kc in range(8):
                                i_ = e.matmul(psf[pb][:], lhsT=wbf[:, kc, mc * 128:(mc + 1) * 128], rhs=hb[j][:, kc, :],
                                              start=(kc == 0), stop=(kc == 7))
                            return i_
                        P.op(pe, mm, R=[bw, bhb[j]], W=[bpsf[pb]])
                        s = (mc // 3) % 2
                        if mc % 2 == 0:
                            P.op(act, lambda e, pb=pb, s=s, mc=mc: e.activation(out=stg[s][:, mc % 3, :], in_=psf[pb][:], func=AF.Copy),
                                 R=[bpsf[pb]], W=[bstg[s]])
                        else:
                            P.op(dve, lambda e, pb=pb, s=s, mc=mc: e.tensor_copy(out=stg[s][:, mc % 3, :], in_=psf[pb][:]),
                                 R=[bpsf[pb]], W=[bstg[s]])
                        if mc % 3 == 2:
                            P.dma(Uv[:, mc - 2:mc + 1, t0:t0 + NT], stg[s][:], R=[bstg[s]], W=[bU], via=bstg[s])

            dbg(f"omlt_b{l}", omlt[:].rearrange("p a b -> p (a b)"), blay, [128, 8])
            if lvl < 4:
                break
            with contextlib.ExitStack() as ph:
                P.phase_end()
                vT = sb("hy_vT", [128, NB, 256], BF16, ph)
                bvT = Buf("hy_vT", multi=True)
                with contextlib.ExitStack() as fg:
                    P.phase_end()
                    NF = 512
                    w1 = sb("hy_w1", [HY_EMB, 64], F32, fg)
                    w2 = sb("hy_w2", [64, 64], F32, fg)
                    w3 = sb("hy_w3", [64, 64], F32, fg)
                    w4 = sb("hy_w4", [64, 512], F32, fg)
                    bwf = Buf("hy_wf")
                    for tl, src in ((w1, hw1_d), (w2, hw2_d), (w3, hw3_d), (w4, hw4_d)):
                        P.dma(tl[:], src[l], R=[bIN], W=[bwf], via=bwf, nowaw=True)
                    zf = [sb(f"hy_zf{i}", [HY_EMB, NF], F32, fg) for i in range(2)]
                    bzf = [Buf(f"hy_zf{i}") for i in range(2)]
                    tr_ = [sb(f"hy_tr{i}", [128, NF], F32, fg) for i in range(2)]
                    btr = [Buf(f"hy_tr{i}") for i in range(2)]
                    hh = sb("hy_h", [64, NF], F32, fg)
                    bhh = Buf("hy_h")
                    tmpa = sb("hy_tmpa", [64, NF], F32, fg)
                    tmpb = sb("hy_tmpb", [64, NF], F32, fg)
                    btmp = Buf("hy_tmp")
                    dec = sb("hy_dec", [128, 2, NF], F32, fg)
                    bdec = Buf("hy_dec")
                    kt = sb("hy_kt", [128, NF], F32, fg)
                    bkt = Buf("hy_kt")
                    kst = [sb(f"hy_kst{i}", [128, NF], BF16, fg) for i in range(4)]
                    bkst = [Buf(f"hy_kst{i}") for i in range(4)]
                    kcount = 0
                    for n0 in range(0, T if _DBG_SUB >= 1 else 0, NF):
                        j = (n0 // NF) % 2
                        P.dma(zf[j][:], zfeat_in[:, n0:n0 + NF], R=[bIN], W=[bzf[j]], via=bzf[j])
                        P.dma(tr_[j][:], trow_in[0:1, n0:n0 + NF].broadcast_to([128, NF]), R=[bIN], W=[btr[j]], via=btr[j])
                        src_t, bsrc, Kd = zf[j], bzf[j], HY_EMB
                        for li, wl_ in enumerate((w1, w2, w3)):
                            P.op(pe, lambda e, wl_=wl_, src_t=src_t, Kd=Kd: e.matmul(psf[0][0:64, :], lhsT=wl_[0:Kd, :], rhs=src_t[0:Kd, :],
                                                                                   start=True, stop=True),
                                 R=[bwf, bsrc], W=[bpsf[0]])
                            P.op(dve, lambda e, li=li: e.tensor_scalar(out=tmpa[:], in0=psf[0][0:64, :], scalar1=spt[0:64, O_HYB + li:O_HYB + li + 1],
                                                                       scalar2=spt[0:64, O_HYB + 3:O_HYB + 4], op0=ALU.add, op1=ALU.mult),
                                 R=[bpsf[0], bsp], W=[btmp])
                            P.op(dve, lambda e: e.tensor_scalar(out=tmpb[:], in0=tmpa[:], scalar1=PI, scalar2=-2 * PI, op0=ALU.is_gt, op1=ALU.mult),
                                 R=[btmp], W=[btmp])
                            P.op(dve, lambda e: e.tensor_tensor(out=tmpb[:], in0=tmpb[:], in1=tmpa[:], op=ALU.add), R=[btmp], W=[btmp])
                            P.op(dve, lambda e: e.tensor_scalar(out=tmpa[:], in0=tmpa[:], scalar1=-PI, scalar2=2 * PI, op0=ALU.is_lt, op1=ALU.mult),
                                 R=[btmp], W=[btmp])
                            P.op(dve, lambda e: e.tensor_tensor(out=tmpb[:], in0=tmpb[:], in1=tmpa[:], op=ALU.add), R=[btmp], W=[btmp])
                            P.op(act, lambda e: e.activation(out=hh[:], in_=tmpb[:], func=AF.Sin), R=[btmp], W=[bhh])
                            src_t, bsrc, Kd = hh, bhh, 64
                        for cc in range(2):
                            P.op(act, lambda e, cc=cc: e.activation(out=dec[:, cc, :], in_=tr_[j][:], func=AF.Exp,
                                                                    scale=spt[:, O_ND + cc:O_ND + cc + 1]), R=[btr[j], bsp], W=[bdec])
                        for q in range(4):
                            cc = q % 2
                            pb = 1 + q % 2
                            P.op(pe, lambda e, q=q, pb=pb: e.matmul(psf[pb][:], lhsT=w4[:, q * 128:(q + 1) * 128], rhs=hh[:], start=True, stop=True),
                                 R=[bwf, bhh], W=[bpsf[pb]])
                            P.op(dve, lambda e, pb=pb, cc=cc: e.tensor_tensor(out=kt[:], in0=psf[pb][:], in1=dec[:, cc, :], op=ALU.mult),
                                 R=[bpsf[pb], bdec], W=[bkt])
                            ks = kcount % 4
                            kcount += 1
                            if q < 2:
                                if n0 == 0:
                                    P.op(dve, lambda e, cc=cc: e.tensor_tensor(out=kt[:, 0:1], in0=kt[:, 0:1],
                                                                               in1=spt[:, O_HBIAS + cc:O_HBIAS + cc + 1], op=ALU.add),
                                         R=[bkt, bsp], W=[bkt])
                                P.op(act, lambda e, ks=ks: e.activation(out=kst[ks][:], in_=kt[:], func=AF.Copy), R=[bkt], W=[bkst[ks]])
                                P.dma(KF[cc * 128:(cc + 1) * 128, T - 1 + n0:T - 1 + n0 + NF], kst[ks][:], R=[bkst[ks]], W=[bKF], via=bkst[ks])
                            else:
                                P.op(dve, lambda e, ks=ks: e.tensor_copy(out=kst[ks][:, ::-1], in_=kt[:]), R=[bkt], W=[bkst[ks]])
                                if n0 == 0:
                                    P.dma(KF[cc * 128:(cc + 1) * 128, T - NF:T - 1], kst[ks][:, 0:NF - 1], R=[bkst[ks]], W=[bKF], via=bkst[ks])
                                else:
                                    P.dma(KF[cc * 128:(cc + 1) * 128, T - n0 - NF:T - n0], kst[ks][:], R=[bkst[ks]], W=[bKF], via=bkst[ks])
                with contextlib.ExitStack() as p1:
                    P.phase_end()
                    NT = 512
                    ux = [sb(f"hy_ux{i}", [128, 4, NT + 2], F32, p1) for i in range(2)]
                    bux = [Buf(f"hy_ux{i}") for i in range(2)]
                    cv = sb("hy_cv", [128, 4, NT], F32, p1)
                    bcv = Buf("hy_cv")
                    vrev = [sb(f"hy_vrev{i}", [128, 2, NT], BF16, p1) for i in range(2)]
                    bvrev = [Buf(f"hy_vrev{i}") for i in range(2)]
                    Uq = U[256:768, :].rearrange("(q p) t -> p q t", p=128)
                    for i in range(T // NT if _DBG_SUB >= 2 else 0):
                        j = i % 2
                        t0 = i * NT
                        lo = max(t0 - 1, 0)
                        hi = min(t0 + NT + 1, T)
                        P.dma(ux[j][:, :, lo - (t0 - 1):hi - (t0 - 1)], Uq[:, :, lo:hi], R=[bU], W=[bux[j]], via=bux[j])
                        if t0 == 0:
                            P.op(dve, lambda e: e.memset(ux[j][:, :, 0:1], 0.0), W=[bux[j]])
                        elif t0 % SEG == 0:
                            P.op(dve, lambda e: e.tensor_scalar(out=ux[j][:, :, 0:1], in0=ux[j][:, :, 0:1], scalar1=segflag[:, 0:1],
                                                                 scalar2=None, op0=ALU.mult), R=[bux[j], bconst], W=[bux[j]])
                        if t0 + NT == T:
                            P.op(dve, lambda e: e.memset(ux[j][:, :, NT + 1:NT + 2], 0.0), W=[bux[j]])
                        elif (t0 + NT) % SEG == 0:
                            P.op(dve, lambda e: e.tensor_scalar(out=ux[j][:, :, NT + 1:NT + 2], in0=ux[j][:, :, NT + 1:NT + 2],
                                                                 scalar1=segflag[:, 0:1], scalar2=None, op0=ALU.mult),
                                 R=[bux[j], bconst], W=[bux[j]])
                        for q in range(4):
                            ch = 2 + q
                            eng = dve if q % 2 == 0 else pool
                            wq = lambda k, ch=ch: spt[:, O_HCW + ch * 3 + k:O_HCW + ch * 3 + k + 1]
                            P.op(dve, lambda e, q=q, ch=ch, wq=wq: e.tensor_scalar(out=cv[:, q, :], in0=ux[j][:, q, 0:NT], scalar1=wq(0),
                                                                                  scalar2=spt[:, O_HCB + ch:O_HCB + ch + 1], op0=ALU.mult, op1=ALU.add),
                                 R=[bux[j], bsp], W=[bcv])
                            for k in (1, 2):
                                P.op(dve, lambda e, q=q, k=k, wq=wq: e.scalar_tensor_tensor(out=cv[:, q, :], in0=ux[j][:, q, k:k + NT], scalar=wq(k),
                                                                                           in1=cv[:, q, :], op0=ALU.mult, op1=ALU.add),
                                     R=[bux[j], bsp, bcv], W=[bcv])
                        for cc in range(2 if _DBG_X >= 1 else 0):
                            P.op(dve, lambda e, cc=cc: e.tensor_tensor(out=vrev[j][:, cc, ::-1], in0=cv[:, cc, :], in1=cv[:, 2 + cc, :], op=ALU.mult),
                                 R=[bcv], W=[bvrev[j]])
                        P.op(dve, lambda e: e.memset(dummy[:], 0.0), W=[bvrev[j]])
                        for cc in range(2 if _DBG_X >= 2 else 0):
                            pbi = 0

                            def tr(e, cc=cc, pbi=pbi):
                                for bq in range(4):
                                    i_ = e.transpose(out=psb[pbi][:, bq * 128:(bq + 1) * 128], in_=vrev[j][:, cc, bq * 128:(bq + 1) * 128],
                                                     identity=ident_bf[:])
                                return i_
                            P.op(pe, tr, R=[bvrev[j], bconst], W=[bpsb[pbi]])
                            B0 = t0 // 128
                            dst = vT[:, B0 + 3::-1, cc * 128:(cc + 1) * 128] if B0 == 0 else vT[:, B0 + 3:B0 - 1:-1, cc * 128:(cc + 1) * 128]
                            P.op(dve, lambda e, dst=dst, pbi=pbi: e.tensor_copy(out=dst, in_=psb[pbi][:, 0:512].rearrange("p (b c) -> p b c", c=128)),
                                 R=[bpsb[pbi]], W=[bvT])
                with contextlib.ExitStack() as p2:
                    P.phase_end()
                    BS = NB // NSEG
                    yT = sb("hy_yT", [128, NB, 128], F32, p2)
                    byT = Buf("hy_yT", multi=True)
                    NDP = 32
                    NE = 2 * NB - 1
                    npieces = (NE + NDP - 1) // NDP
                    NPB = 3
                    pc_t = [sb(f"hy_pc{i}", [128, NDP * 128], BF16, p2) for i in range(NPB)]
                    bpc = [Buf(f"hy_pc{i}") for i in range(NPB)]
                    pcn_t = [sb(f"hy_pcn{i}", [128, NDP * 128], BF16, p2) for i in range(2)]
                    bpcn = [Buf(f"hy_pcn{i}") for i in range(2)]
                    sfm1 = sb("hy_sfm1", [128, 1], F32, p2)
                    P.op(dve, lambda e: e.tensor_scalar(out=sfm1[:], in0=segflag[:], scalar1=-1.0, scalar2=None, op0=ALU.add), R=[bconst], W=[bconst])
                    NT = 512
                    u0 = [sb(f"hy_u0{i}", [128, NT + 2], F32, p2) for i in range(2)]
                    bu0 = [Buf(f"hy_u0{i}") for i in range(2)]
                    c0 = sb("hy_c0", [128, NT], F32, p2)
                    bc0 = Buf("hy_c0")
                    yfo = [sb(f"hy_yfo{i}", [128, NT], F32, p2) for i in range(2)]
                    byfo = [Buf(f"hy_yfo{i}") for i in range(2)]
                    pcount = 0
                    ncount = 0
                    e_mid = NB - 1
                    p_mid = e_mid // NDP
                    order = [p_mid] + [p for p in range(npieces) if p != p_mid]
                    for cc in range(2 if _DBG_SUB >= 3 else 0):
                        for ci in range(128):
                            c = cc * 128 + ci
                            pb = ci % 4
                            ybank = psf[pb]
                            nmm = 0
                            total = NE + sum(1 for ee in range(NE) if 0 < abs(ee - (NB - 1)) <= BS)
                            for p_ in order:
                                e0 = p_ * NDP
                                e1 = min(e0 + NDP, NE)
                                W_ = (e1 - e0) * 128
                                base = (T - 1) + (e0 - (NB - 1)) * 128 - 127
                                k = pcount % NPB
                                pcount += 1
                                src = bass.AP(tensor=KF.tensor, offset=c * 2 * T + base, ap=[[1, 128], [1, W_]])
                                P.dma(pc_t[k][:, 0:W_], src, R=[bKF], W=[bpc[k]], via=bpc[k])
                                es_ = list(range(e0, e1))
                                if p_ == p_mid:
                                    es_ = [e_mid] + [e for e in es_ if e != e_mid]
                                cross = [ee for ee in es_ if 0 < abs(ee - (NB - 1)) <= BS]
                                kn = None
                                if cross:
                                    kn = ncount % 2
                                    ncount += 1
                                    ca, cb_ = min(cross) - e0, max(cross) - e0 + 1
                                    if ncount % 2 == 0:
                                        P.op(dve, lambda e, k=k, kn=kn, ca=ca, cb_=cb_: e.tensor_scalar(
                                            out=pcn_t[kn][:, ca * 128:cb_ * 128], in0=pc_t[k][:, ca * 128:cb_ * 128], scalar1=sfm1[:, 0:1], scalar2=None,
                                            op0=ALU.mult), R=[bpc[k], bconst], W=[bpcn[kn]])
                                    else:
                                        P.op(act, lambda e, k=k, kn=kn, ca=ca, cb_=cb_: e.activation(
                                            out=pcn_t[kn][:, ca * 128:cb_ * 128], in_=pc_t[k][:, ca * 128:cb_ * 128], func=AF.Copy, scale=sfm1[:, 0:1]),
                                            R=[bpc[k], bconst], W=[bpcn[kn]])

                                def mm(e, es_=es_, e0=e0, k=k, kn=kn, ci=ci, cc=cc, ybank=ybank, nmm=nmm, total=total):
                                    i_ = None
                                    n_ = nmm
                                    for ee in es_:
                                        d = ee - (NB - 1)
                                        i0 = max(0, d)
                                        i1 = min(NB - 1, NB - 1 + d)
                                        i_ = e.matmul(ybank[:, i0:i1 + 1], lhsT=pc_t[k][:, (ee - e0) * 128:(ee - e0 + 1) * 128],
                                                      rhs=vT[:, i0 - d:i1 - d + 1, cc * 128 + ci], start=(n_ == 0), stop=(n_ == total - 1),
                                                      skip_group_check=True)
                                        n_ += 1
                                        if 0 < abs(d) <= BS:
                                            ad = abs(d)
                                            o_off = BS if d > 0 else BS - ad
                                            r_off = BS - ad if d > 0 else BS
                                            ycol = ybank[:, 0:1]
                                            vcol = vT[:, 0:1, cc * 128 + ci]
                                            o_ap = bass.AP(tensor=ycol.tensor, offset=ycol.offset + o_off, ap=[list(ycol.ap[0]), [BS, NSEG - 1], [1, ad]])
                                            r_ap = bass.AP(tensor=vcol.tensor, offset=vcol.offset + r_off * 256, ap=[list(vcol.ap[0]), [BS * 256, NSEG - 1], [256, ad]])
                                            i_ = e.matmul(o_ap, lhsT=pcn_t[kn][:, (ee - e0) * 128:(ee - e0 + 1) * 128], rhs=r_ap,
                                                          start=False, stop=(n_ == total - 1), skip_group_check=True)
                                            n_ += 1
                                    return i_
                                rl = [bpc[k], bvT] + ([bpcn[kn]] if kn is not None else [])
                                P.op(pe, mm, R=rl, W=[bpsf[pb]])
                                nmm += len(es_) + len(cross)
                            if ci % 2 == 0:
                                P.op(act, lambda e, ci=ci, ybank=ybank: e.activation(out=yT[:, :, ci], in_=ybank[:, 0:NB], func=AF.Copy),
                                     R=[bpsf[pb]], W=[byT])
                            else:
                                P.op(dve, lambda e, ci=ci, ybank=ybank: e.tensor_copy(out=yT[:, :, ci], in_=ybank[:, 0:NB]),
                                     R=[bpsf[pb]], W=[byT])
                        if stop and cc == 0:
                            P.dma(DBGY[:, :], yT[:].rearrange("p b c -> p (b c)"), R=[byT], W=[bDBG], via=byT)
                            P.dma(DBGV[:, :], vT[:].rearrange("p b c -> p (b c)"), R=[bvT], W=[bDBG], via=bvT)
                        for i in range(T // NT if _DBG_SUB >= 4 else 0):
                            j = i % 2
                            t0 = i * NT
                            lo = max(t0 - 1, 0)
                            hi = min(t0 + NT + 1, T)
                            P.dma(u0[j][:, lo - (t0 - 1):hi - (t0 - 1)], U[cc * 128:(cc + 1) * 128, lo:hi], R=[bU], W=[bu0[j]], via=bu0[j])
                            if t0 == 0:
                                P.op(dve, lambda e: e.memset(u0[j][:, 0:1], 0.0), W=[bu0[j]])
                            elif t0 % SEG == 0:
                                P.op(dve, lambda e: e.tensor_scalar(out=u0[j][:, 0:1], in0=u0[j][:, 0:1], scalar1=segflag[:, 0:1],
                                                                    scalar2=None, op0=ALU.mult), R=[bu0[j], bconst], W=[bu0[j]])
                            if t0 + NT == T:
                                P.op(dve, lambda e: e.memset(u0[j][:, NT + 1:NT + 2], 0.0), W=[bu0[j]])
                            elif (t0 + NT) % SEG == 0:
                                P.op(dve, lambda e: e.tensor_scalar(out=u0[j][:, NT + 1:NT + 2], in0=u0[j][:, NT + 1:NT + 2],
                                                                    scalar1=segflag[:, 0:1], scalar2=None, op0=ALU.mult),
                                     R=[bu0[j], bconst], W=[bu0[j]])
                            wq = lambda k_, cc=cc: spt[:, O_HCW + cc * 3 + k_:O_HCW + cc * 3 + k_ + 1]
                            P.op(dve, lambda e, wq=wq: e.tensor_scalar(out=c0[:], in0=u0[j][:, 0:NT], scalar1=wq(0),
                                                                      scalar2=spt[:, O_HCB + cc:O_HCB + cc + 1], op0=ALU.mult, op1=ALU.add),
                                 R=[bu0[j], bsp], W=[bc0])
                            for k_ in (1, 2):
                                P.op(dve, lambda e, k_=k_, wq=wq: e.scalar_tensor_tensor(out=c0[:], in0=u0[j][:, k_:k_ + NT], scalar=wq(k_),
                                                                                        in1=c0[:], op0=ALU.mult, op1=ALU.add),
                                     R=[bu0[j], bsp, bc0], W=[bc0])
                            pb = 4 + i % 2

                            def tr(e, pb=pb, t0=t0):
                                for bq in range(4):
                                    i_ = e.transpose(out=psf[pb][:, bq * 128:(bq + 1) * 128], in_=yT[:, t0 // 128 + bq, :], identity=ident_f[:])
                                return i_
                            P.op(pe, tr, R=[byT, bconst], W=[bpsf[pb]])
                            P.op(dve, lambda e, pb=pb: e.tensor_tensor(out=yfo[j][:], in0=psf[pb][:], in1=c0[:], op=ALU.mult),
                                 R=[bpsf[pb], bc0], W=[byfo[j]])
                            P.dma(HF[cc * 128:(cc + 1) * 128, t0:t0 + NT], yfo[j][:], R=[byfo[j]], W=[bHF], via=byfo[j])
                with contextlib.ExitStack() as p3:
                    P.phase_end()
                    NT = 512
                    yf = [sb(f"hy_yf{i}", [128, 2, NT], F32, p3) for i in range(2)]
                    byf = [Buf(f"hy_yf{i}") for i in range(2)]
                    sq = sb("hy_sq", [128, 2, NT], BF16, p3)
                    bsq = Buf("hy_sq")
                    rstd = sb("hy_rstd", [128, NT], F32, p3)
                    brstd = Buf("hy_rstd")
                    yo = [sb(f"hy_yo{i}", [128, 2, NT], BF16, p3) for i in range(2)]
                    byo = [Buf(f"hy_yo{i}") for i in range(2)]
                    HFv = HF.rearrange("(q p) t -> p q t", p=128)
                    YM0 = YM[0:256, :].rearrange("(q p) t -> p q t", p=128)
                    for i in range(T // NT if _DBG_SUB >= 4 else 0):
                        j = i % 2
                        t0 = i * NT
                        P.dma(yf[j][:], HFv[:, :, t0:t0 + NT], R=[bHF], W=[byf[j]], via=byf[j])
                        rms_rstd(yf[j][:], 2, NT, sq, rstd, byf[j], bsq, brstd, psf[2], bpsf[2], 1.0 / 256)
                        for q in range(2):
                            P.op(dve, lambda e, q=q: e.scalar_tensor_tensor(out=yo[j][:, q, :], in0=yf[j][:, q, :],
                                                                          scalar=spt[:, O_ON + q:O_ON + q + 1], in1=rstd[:],
                                                                          op0=ALU.mult, op1=ALU.mult),
                                 R=[byf[j], brstd, bsp], W=[byo[j]])
                        P.dma(YM0[:, :, t0:t0 + NT], yo[j][:], R=[byo[j]], W=[bYM], via=byo[j])

            dbg(f"omlt_c{l}", omlt[:].rearrange("p a b -> p (a b)"), blay, [128, 8])
            if lvl < 5:
                break
            with contextlib.ExitStack() as ph:
                P.phase_end()
                NT = 512
                NCH = NT // 64
                KW = [16, 32, 48, 64]
                KO = [0, NCH * 16, NCH * 48, NCH * 96]
                Ug = U[768:768 + 2560, :].rearrange("(g h p) t -> p g h t", g=5, h=4)
                OFv = OF.rearrange("(h p) t -> p h t", p=128)
                YM1 = YM[256:768, :].rearrange("(h p) t -> p h t", p=128)
                mask4 = sb("hg_mask4", [64, NCH, 4, 16], F32, ph)
                bmask4 = Buf("hg_mask4")
                P.op(dve, lambda e: e.memset(mask4[:], 1.0), W=[bmask4])
                P.op(pool, lambda e: e.affine_select(out=mask4[:], in_=mask4[:], pattern=[[0, NCH], [16, 4], [1, 16]],
                                                     compare_op=ALU.is_ge, fill=0.0, base=0, channel_multiplier=-1),
                     R=[bmask4], W=[bmask4])
                sqh = sb("hg_sq", [128, 1, NT], BF16, ph)
                bsqh = Buf("hg_sq")
                rstdh = sb("hg_rstd", [128, NT], F32, ph)
                brstdh = Buf("hg_rstd")
                S = [sb(f"hg_S{h}", [128, 128], F32, ph) for h in range(4)]
                bS = [Buf(f"hg_S{h}") for h in range(4)]

                class HeadBufs:
                    pass
                HB = []
                for h in range(4):
                    B = HeadBufs()
                    for nm, shape, dt_ in (("qv", [128, 2, NT], F32), ("zt", [128, NT], F32), ("gt", [128, NT], F32), ("oft", [128, NT], F32),
                                           ("qr", [128, 2, NT], F32), ("fA", [128, NT], F32), ("fB", [128, NT], F32), ("fC", [128, NT], F32),
                                           ("fD", [128, NT], F32), ("fK", [128, NCH * 160], F32), ("rr", [128, NCH, 4], F32), ("ebl", [128, NCH], F32),
                                           ("Qh", [128, NT], BF16), ("Kh", [128, NT], BF16), ("Qs", [128, NT], BF16), ("KI", [128, NCH * 160], BF16),
                                           ("vbf", [128, NT], BF16), ("Ktok", [64, NCH, 128], BF16), ("Vtok", [64, NCH, 128], BF16),
                                           ("Am", [64, NCH, 4, 16], BF16), ("Spb0", [128, 128], BF16), ("Spb1", [128, 128], BF16), ("yo", [128, NT], BF16)):
                        setattr(B, nm, sb(f"hg{h}_{nm}", shape, dt_, ph))
                        setattr(B, "b_" + nm, Buf(f"hg{h}_{nm}"))
                    P.op(dve, lambda e, B=B: e.memset(B.rr[:], 0.0), W=[B.b_rr])
                    HB.append(B)

                def unit(dr, i, h):
                    B = HB[h]
                    t0 = i * NT
                    ob = 1 + h
                    sbank = 5 + (h % 2)
                    P.dma(B.qv[:], Ug[:, 0:2, h, t0:t0 + NT], R=[bU], W=[B.b_qv], via=B.b_qv)
                    P.dma(B.zt[:], Ug[:, 2 + dr, h, t0:t0 + NT], R=[bU], W=[B.b_zt], via=B.b_zt)
                    if dr == 1:
                        P.dma(B.gt[:], Ug[:, 4, h, t0:t0 + NT], R=[bU], W=[B.b_gt], via=B.b_gt)
                        P.dma(B.oft[:], OFv[:, h, t0:t0 + NT], R=[bOF], W=[B.b_oft], via=B.b_oft)
                    at_seg = (t0 % SEG == 0) if dr == 0 else ((t0 + NT) % SEG == 0)
                    if at_seg:
                        P.op(dve, lambda e: e.tensor_scalar(out=S[h][:], in0=S[h][:], scalar1=segflag[:, 0:1], scalar2=None,
                                                            op0=ALU.mult), R=[bS[h], bconst], W=[bS[h]])
                    yield
                    if dr == 0:
                        P.op(act, lambda e: e.activation(out=B.fA[:], in_=B.zt[:], func=AF.Sigmoid), R=[B.b_zt], W=[B.b_fA])
                        P.op(pool, lambda e: e.tensor_copy(out=B.qr[:], in_=B.qv[:]), R=[B.b_qv], W=[B.b_qr])
                    else:
                        P.op(dve, lambda e: e.tensor_copy(out=B.fD[:], in_=B.zt[:, ::-1]), R=[B.b_zt], W=[B.b_fD])
                        yield
                        P.op(act, lambda e: e.activation(out=B.fA[:], in_=B.fD[:], func=AF.Sigmoid), R=[B.b_fD], W=[B.b_fA])
                        for g_ in range(2):
                            P.op(dve, lambda e, g_=g_: e.tensor_copy(out=B.qr[:, g_, :], in_=B.qv[:, g_, ::-1]), R=[B.b_qv], W=[B.b_qr])
                    yield
                    P.op(dve, lambda e: e.tensor_scalar(out=B.fA[:], in0=B.fA[:], scalar1=omlt[:, dr, h:h + 1], scalar2=lbt[:, dr, h:h + 1],
                                                        op0=ALU.mult, op1=ALU.add), R=[B.b_fA, blay], W=[B.b_fA])
                    yield
                    P.op(act, lambda e: e.activation(out=B.fB[:], in_=B.fA[:], func=AF.Ln), R=[B.b_fA], W=[B.b_fB])
                    P.op(act, lambda e: e.activation(out=B.vbf[:], in_=B.qr[:, 1, :], func=AF.Copy), R=[B.b_qr], W=[B.b_vbf])
                    yield
                    P.op(pool, lambda e: e.tensor_scalar(out=B.fA[:], in0=B.fA[:], scalar1=-1.0, scalar2=1.0, op0=ALU.mult, op1=ALU.add),
                         R=[B.b_fA], W=[B.b_fA])
                    P.op(dve, lambda e: e.tensor_tensor_scan(out=B.fC[:], data0=cm_f[:], data1=B.fB[:], initial=0.0, op0=ALU.mult, op1=ALU.add),
                         R=[B.b_fB, bconst], W=[B.b_fC])

                    def trV(e):
                        for c in range(NCH):
                            i_ = e.transpose(out=psb[0][0:64, c * 128:(c + 1) * 128], in_=B.vbf[:, c * 64:(c + 1) * 64], identity=ident_bf[:])
                        return i_
                    P.op(pe, trV, R=[B.b_vbf, bconst], W=[bpsb[0]])
                    P.op(act, lambda e: e.activation(out=B.Vtok[:], in_=psb[0][0:64, :].rearrange("p (c k) -> p c k", k=128), func=AF.Copy),
                         R=[bpsb[0]], W=[B.b_Vtok])
                    yield
                    b3 = B.fC[:].rearrange("p (c t) -> p c t", t=64)
                    b4 = B.fC[:].rearrange("p (c i t) -> p c i t", i=4, t=16)
                    kk3 = B.fA[:].rearrange("p (c t) -> p c t", t=64)
                    P.op(act, lambda e: e.activation(out=B.ebl[:], in_=b3[:, :, 63], func=AF.Exp), R=[B.b_fC], W=[B.b_ebl])
                    P.op(act, lambda e: e.activation(out=B.fB[:], in_=B.fC[:], func=AF.Exp), R=[B.b_fC], W=[B.b_fB])
                    P.op(dve, lambda e: e.tensor_tensor(out=B.fD[:].rearrange("p (c t) -> p c t", t=64),
                                                        in0=b3[:, :, 63:64].broadcast_to([128, NCH, 64]), in1=b3, op=ALU.subtract),
                         R=[B.b_fC], W=[B.b_fD])
                    P.op(dve, lambda e: e.tensor_copy(out=B.rr[:, :, 1:4], in_=b3[:, :, 15:48:16]), R=[B.b_fC], W=[B.b_rr])
                    yield
                    P.op(dve, lambda e: e.tensor_tensor(out=B.Qh[:], in0=B.qr[:, 0, :], in1=B.fB[:], op=ALU.mult), R=[B.b_qr, B.b_fB], W=[B.b_Qh])
                    P.op(act, lambda e: e.activation(out=B.fD[:], in_=B.fD[:], func=AF.Exp), R=[B.b_fD], W=[B.b_fD])
                    for I in range(4):
                        P.op(dve, lambda e, I=I: e.tensor_tensor(
                            out=B.fK[:, KO[I]:KO[I] + NCH * KW[I]].rearrange("p (c t) -> p c t", t=KW[I]),
                            in0=B.rr[:, :, I:I + 1].broadcast_to([128, NCH, KW[I]]), in1=b3[:, :, 0:KW[I]], op=ALU.subtract),
                            R=[B.b_fC, B.b_rr], W=[B.b_fK])
                    yield
                    P.op(act, lambda e: e.activation(out=B.fK[:], in_=B.fK[:], func=AF.Exp), R=[B.b_fK], W=[B.b_fK])
                    P.op(dve, lambda e: e.tensor_tensor(out=B.Kh[:], in0=B.fA[:], in1=B.fD[:], op=ALU.mult), R=[B.b_fA, B.b_fD], W=[B.b_Kh])
                    P.op(dve, lambda e: e.tensor_tensor(out=B.fB[:].rearrange("p (c i t) -> p c i t", i=4, t=16), in0=b4,
                                                        in1=B.rr[:].unsqueeze(3).broadcast_to([128, NCH, 4, 16]), op=ALU.subtract),
                         R=[B.b_fC, B.b_rr, B.b_Qh], W=[B.b_fB])
                    yield
                    P.op(act, lambda e: e.activation(out=B.fB[:], in_=B.fB[:], func=AF.Exp), R=[B.b_fB], W=[B.b_fB])

                    def trK(e):
                        for c in range(NCH):
                            i_ = e.transpose(out=psb[0][0:64, c * 128:(c + 1) * 128], in_=B.Kh[:, c * 64:(c + 1) * 64], identity=ident_bf[:])
                        return i_
                    P.op(pe, trK, R=[B.b_Kh, bconst], W=[bpsb[0]])
                    P.op(act, lambda e: e.activation(out=B.Ktok[:], in_=psb[0][0:64, :].rearrange("p (c k) -> p c k", k=128), func=AF.Copy),
                         R=[bpsb[0]], W=[B.b_Ktok])
                    for I in range(4):
                        P.op(dve, lambda e, I=I: e.tensor_tensor(
                            out=B.KI[:, KO[I]:KO[I] + NCH * KW[I]].rearrange("p (c t) -> p c t", t=KW[I]),
                            in0=B.fK[:, KO[I]:KO[I] + NCH * KW[I]].rearrange("p (c t) -> p c t", t=KW[I]), in1=kk3[:, :, 0:KW[I]], op=ALU.mult),
                            R=[B.b_fK, B.b_fA], W=[B.b_KI])
                    yield
                    P.op(dve, lambda e: e.tensor_tensor(out=B.Qs[:], in0=B.qr[:, 0, :], in1=B.fB[:], op=ALU.mult), R=[B.b_qr, B.b_fB], W=[B.b_Qs])
                    yield

                    def mmA(e):
                        for c in range(NCH):
                            for I in range(4):
                                col = (c * 4 + I) * 16
                                i_ = e.matmul(psf[0][0:KW[I], col:col + 16], lhsT=B.KI[:, KO[I] + c * KW[I]:KO[I] + (c + 1) * KW[I]],
                                              rhs=B.Qs[:, c * 64 + I * 16:c * 64 + (I + 1) * 16], start=True, stop=True, skip_group_check=True)
                        return i_
                    P.op(pe, mmA, R=[B.b_KI, B.b_Qs], W=[bpsf[0]])
                    P.op(dve, lambda e: e.tensor_tensor(out=B.Am[:], in0=psf[0][0:64, :].rearrange("p (c i t) -> p c i t", i=4, t=16), in1=mask4[:],
                                                        op=ALU.mult), R=[bpsf[0], bmask4], W=[B.b_Am])
                    yield
                    for c in range(NCH):
                        Spb_, bSpb_ = (B.Spb0, B.b_Spb0) if c % 2 == 0 else (B.Spb1, B.b_Spb1)
                        P.op(act, lambda e, Spb_=Spb_: e.activation(out=Spb_[:], in_=S[h][:], func=AF.Copy), R=[bS[h]], W=[bSpb_])

                        def mmo(e, c=c, Spb_=Spb_):
                            e.matmul(psf[ob][:, c * 64:(c + 1) * 64], lhsT=Spb_[:], rhs=B.Qh[:, c * 64:(c + 1) * 64], start=True, stop=False,
                                     skip_group_check=True)
                            for I in range(4):
                                i_ = e.matmul(psf[ob][:, c * 64 + I * 16:c * 64 + (I + 1) * 16], lhsT=B.Vtok[0:KW[I], c, :], rhs=B.Am[0:KW[I], c, I, :],
                                              start=False, stop=(I == 3), skip_group_check=True)
                            return i_
                        P.op(pe, mmo, R=[B.b_Vtok, B.b_Am, bSpb_, B.b_Qh], W=[bpsf[ob]])
                        P.op(pe, lambda e, c=c: e.matmul(psf[sbank][:, 0:128], lhsT=B.Ktok[:, c, :], rhs=B.Vtok[:, c, :], start=True, stop=True,
                                                         skip_group_check=True), R=[B.b_Ktok, B.b_Vtok], W=[bpsf[sbank]])
                        P.op(dve, lambda e, c=c: e.scalar_tensor_tensor(out=S[h][:], in0=S[h][:], scalar=B.ebl[:, c:c + 1],
                                                                       in1=psf[sbank][:, 0:128], op0=ALU.mult, op1=ALU.add),
                             R=[bpsf[sbank], B.b_ebl, bS[h]], W=[bS[h]])
                        yield
                    if dr == 0:
                        P.op(act, lambda e: e.activation(out=B.oft[:], in_=psf[ob][:], func=AF.Copy), R=[bpsf[ob]], W=[B.b_oft])
                        P.dma(OFv[:, h, t0:t0 + NT], B.oft[:], R=[B.b_oft], W=[bOF], via=B.b_oft)
                    else:
                        P.op(act, lambda e: e.activation(out=B.fD[:], in_=psf[ob][:], func=AF.Copy), R=[bpsf[ob], B.b_Kh], W=[B.b_fD])
                        yield
                        P.op(dve, lambda e: e.tensor_tensor(out=B.oft[:], in0=B.fD[:, ::-1], in1=B.oft[:], op=ALU.add),
                             R=[B.b_fD, B.b_oft], W=[B.b_oft])
                        rms_rstd(B.oft[:].unsqueeze(1), 1, NT, sqh, rstdh, B.b_oft, bsqh, brstdh, psf[0], bpsf[0], 1.0 / 128)
                        P.op(dve, lambda e: e.scalar_tensor_tensor(out=B.oft[:], in0=B.oft[:], scalar=spt[:, O_ON + 2 + h:O_ON + 3 + h],
                                                                  in1=rstdh[:], op0=ALU.mult, op1=ALU.mult),
                             R=[B.b_oft, brstdh, bsp], W=[B.b_oft])
                        P.op(act, lambda e: e.activation(out=B.gt[:], in_=B.gt[:], func=AF.Silu), R=[B.b_gt], W=[B.b_gt])
                        yield
                        P.op(dve, lambda e: e.tensor_tensor(out=B.yo[:], in0=B.oft[:], in1=B.gt[:], op=ALU.mult),
                             R=[B.b_oft, B.b_gt], W=[B.b_yo])
                        P.dma(YM1[:, h, t0:t0 + NT], B.yo[:], R=[B.b_yo], W=[bYM], via=B.b_yo)

                for dr in range(2):
                    tiles = list(range(T // NT)) if dr == 0 else list(range(T // NT - 1, -1, -1))
                    for h in range(4):
                        P.op(dve, lambda e, h=h: e.memset(S[h][:], 0.0), W=[bS[h]])
                    for i in tiles:
                        alive = [unit(dr, i, h) for h in range(4)]
                        while alive:
                            for g_ in list(alive):
                                try:
                                    next(g_)
                                except StopIteration:
                                    alive.remove(g_)

            if lvl < 6:
                break
            with contextlib.ExitStack() as ph:
                P.phase_end()
                NT = 512
                Ur = U[3328:3840, :].rearrange("(g c p) t -> p g c t", g=2, c=2)
                HFv = HF.rearrange("(c p) t -> p c t", p=128)
                YM2 = YM[768:1024, :].rearrange("(c p) t -> p c t", p=128)
                wa = sb("rg_wa", [128, 2, 2, 128], F32, ph)
                wx = sb("rg_wx", [128, 2, 2, 128], F32, ph)
                wab = sb("rg_wab", [128, 2, 2, 128], BF16, ph)
                wxb = sb("rg_wxb", [128, 2, 2, 128], BF16, ph)
                bwr = Buf("rg_w")
                P.op(dve, lambda e: e.memset(wa[:], 0.0), W=[bwr])
                P.op(dve, lambda e: e.memset(wx[:], 0.0), W=[bwr])
                for d_ in range(2):
                    for hd in range(4):
                        cc, hh_ = hd // 2, hd % 2
                        P.dma(wa[hh_ * 64:(hh_ + 1) * 64, d_, cc, hh_ * 64:(hh_ + 1) * 64], rwa_d[l, d_, hd], R=[bIN], W=[bwr], via=bwr, nowaw=(d_ + hd > 0))
                        P.dma(wx[hh_ * 64:(hh_ + 1) * 64, d_, cc, hh_ * 64:(hh_ + 1) * 64], rwx_d[l, d_, hd], R=[bIN], W=[bwr], via=bwr, nowaw=True)
                P.op(dve, lambda e: e.tensor_copy(out=wab[:], in_=wa[:]), R=[bwr], W=[bwr])
                P.op(dve, lambda e: e.tensor_copy(out=wxb[:], in_=wx[:]), R=[bwr], W=[bwr])
                ur = [sb(f"rg_u{i}", [128, 2, NT + 3], F32, ph) for i in range(2)]
                bur = [Buf(f"rg_u{i}") for i in range(2)]
                gr = [sb(f"rg_g{i}", [128, 2, NT], F32, ph) for i in range(2)]
                bgr = [Buf(f"rg_g{i}") for i in range(2)]
                hf = [sb(f"rg_hf{i}", [128, 2, NT], F32, ph) for i in range(2)]
                bhf = [Buf(f"rg_hf{i}") for i in range(2)]
                xr = sb("rg_xr", [128, 2, NT], F32, ph)
                xrb = sb("rg_xrb", [128, 2, NT], BF16, ph)
                bxr = Buf("rg_xr")
                rr = sb("rg_r", [128, 2, NT], F32, ph)
                ig = sb("rg_ig", [128, 2, NT], F32, ph)
                aa = sb("rg_a", [128, 2, NT], F32, ph)
                bb = sb("rg_b", [128, 2, NT], F32, ph)
                brr, big, baa, bbb = Buf("rg_r"), Buf("rg_ig"), Buf("rg_a"), Buf("rg_b")
                hst = sb("rg_hst", [128, 2], F32, ph)
                bhst = Buf("rg_hst")
                ga = sb("rg_ga", [128, 2, NT], F32, ph)
                bga = Buf("rg_ga")
                sq = sb("rg_sq", [128, 2, NT], BF16, ph)
                bsq = Buf("rg_sq")
                rstd = sb("rg_rstd", [128, NT], F32, ph)
                brstd = Buf("rg_rstd")
                yo = [sb(f"rg_yo{i}", [128, 2, NT], BF16, ph) for i in range(2)]
                byo = [Buf(f"rg_yo{i}") for i in range(2)]
                it = 0
                for dr in range(2):
                    tiles = list(range(T // NT)) if dr == 0 else list(range(T // NT - 1, -1, -1))
                    P.op(dve, lambda e: e.memset(hst[:], 0.0), W=[bhst])
                    for i in tiles:
                        j = it % 2
                        it += 1
                        t0 = i * NT
                        lo = max(t0 - 2, 0)
                        hi = min(t0 + NT + 1, T)
                        P.dma(ur[j][:, :, lo - (t0 - 2):hi - (t0 - 2)], Ur[:, 0, :, lo:hi], R=[bU], W=[bur[j]], via=bur[j])
                        if dr == 1:
                            P.dma(gr[j][:], Ur[:, 1, :, t0:t0 + NT], R=[bU], W=[bgr[j]], via=bgr[j])
                            P.dma(hf[j][:], HFv[:, :, t0:t0 + NT], R=[bHF], W=[bhf[j]], via=bhf[j])
                        if t0 == 0:
                            P.op(dve, lambda e: e.memset(ur[j][:, :, 0:2], 0.0), W=[bur[j]])
                        elif t0 % SEG == 0:
                            P.op(dve, lambda e: e.tensor_scalar(out=ur[j][:, :, 0:2], in0=ur[j][:, :, 0:2], scalar1=segflag[:, 0:1], scalar2=None,
                                                                 op0=ALU.mult), R=[bur[j], bconst], W=[bur[j]])
                        if t0 + NT == T:
                            P.op(dve, lambda e: e.memset(ur[j][:, :, NT + 2:NT + 3], 0.0), W=[bur[j]])
                        elif (t0 + NT) % SEG == 0:
                            P.op(dve, lambda e: e.tensor_scalar(out=ur[j][:, :, NT + 2:NT + 3], in0=ur[j][:, :, NT + 2:NT + 3], scalar1=segflag[:, 0:1],
                                                                 scalar2=None, op0=ALU.mult), R=[bur[j], bconst], W=[bur[j]])
                        at_seg = (t0 % SEG == 0) if dr == 0 else ((t0 + NT) % SEG == 0)
                        if at_seg:
                            P.op(dve, lambda e: e.tensor_scalar(out=hst[:], in0=hst[:], scalar1=segflag[:, 0:1], scalar2=None, op0=ALU.mult),
                                 R=[bhst, bconst], W=[bhst])
                        for cc in range(2):
                            wq = lambda k, cc=cc: spt[:, O_RCW + cc * 4 + k:O_RCW + cc * 4 + k + 1]
                            P.op(dve, lambda e, cc=cc, wq=wq: e.tensor_scalar(out=xr[:, cc, :], in0=ur[j][:, cc, 0:NT], scalar1=wq(0),
                                                                             scalar2=spt[:, O_RCB + cc:O_RCB + cc + 1], op0=ALU.mult, op1=ALU.add),
                                 R=[bur[j], bsp], W=[bxr])
                            for k in (1, 2, 3):
                                P.op(dve, lambda e, cc=cc, k=k, wq=wq: e.scalar_tensor_tensor(out=xr[:, cc, :], in0=ur[j][:, cc, k:k + NT], scalar=wq(k),
                                                                                             in1=xr[:, cc, :], op0=ALU.mult, op1=ALU.add),
                                     R=[bur[j], bsp, bxr], W=[bxr])
                        P.op(act, lambda e: e.activation(out=xrb[:], in_=xr[:], func=AF.Copy), R=[bxr], W=[bxr])
                        for cc in range(2):
                            P.op(pe, lambda e, cc=cc: e.matmul(psf[cc][:], lhsT=wab[:, dr, cc, :], rhs=xrb[:, cc, :], start=True, stop=True),
                                 R=[bwr, bxr], W=[bpsf[cc]])
                            P.op(act, lambda e, cc=cc: e.activation(out=rr[:, cc, :], in_=psf[cc][:], func=AF.Sigmoid,
                                                                    bias=spt[:, O_RBA + dr * 2 + cc:O_RBA + dr * 2 + cc + 1]),
                                 R=[bpsf[cc], bsp], W=[brr])
                            P.op(pe, lambda e, cc=cc: e.matmul(psf[2 + cc][:], lhsT=wxb[:, dr, cc, :], rhs=xrb[:, cc, :], start=True, stop=True),
                                 R=[bwr, bxr], W=[bpsf[2 + cc]])
                            P.op(act, lambda e, cc=cc: e.activation(out=ig[:, cc, :], in_=psf[2 + cc][:], func=AF.Sigmoid,
                                                                    bias=spt[:, O_RBX + dr * 2 + cc:O_RBX + dr * 2 + cc + 1]),
                                 R=[bpsf[2 + cc], bsp], W=[big])
                        for cc in range(2):
                            P.op(act, lambda e, cc=cc: e.activation(out=aa[:, cc, :], in_=rr[:, cc, :], func=AF.Exp,
                                                                    scale=clam[:, dr * 2 + cc:dr * 2 + cc + 1]), R=[brr, blay], W=[baa])
                            P.op(act, lambda e, cc=cc: e.activation(out=bb[:, cc, :], in_=rr[:, cc, :], func=AF.Exp,
                                                                    scale=clam2[:, dr * 2 + cc:dr * 2 + cc + 1]), R=[brr, blay], W=[bbb])
                        P.op(dve, lambda e: e.tensor_scalar(out=bb[:], in0=bb[:], scalar1=-1.0, scalar2=1.0, op0=ALU.mult, op1=ALU.add), R=[bbb], W=[bbb])
                        P.op(dve, lambda e: e.tensor_scalar(out=bb[:], in0=bb[:], scalar1=0.0, scalar2=None, op0=ALU.max), R=[bbb], W=[bbb])
                        P.op(act, lambda e: e.activation(out=bb[:], in_=bb[:], func=AF.Sqrt), R=[bbb], W=[bbb])
                        P.op(pool, lambda e: e.tensor_tensor(out=ig[:], in0=ig[:], in1=xr[:], op=ALU.mult), R=[big, bxr], W=[big])
                        P.op(dve, lambda e: e.tensor_tensor(out=bb[:], in0=bb[:], in1=ig[:], op=ALU.mult), R=[bbb, big], W=[bbb])
                        for cc in range(2):
                            if dr == 0:
                                P.op(dve, lambda e, cc=cc: e.tensor_tensor_scan(out=hf[j][:, cc, :], data0=aa[:, cc, :], data1=bb[:, cc, :],
                                                                                initial=hst[:, cc:cc + 1], op0=ALU.mult, op1=ALU.add),
                                     R=[baa, bbb, bhst], W=[bhf[j]])
                                P.op(dve, lambda e, cc=cc: e.tensor_copy(out=hst[:, cc:cc + 1], in_=hf[j][:, cc, NT - 1:NT]), R=[bhf[j]], W=[bhst])
                            else:
                                P.op(dve, lambda e, cc=cc: e.tensor_tensor_scan(out=rr[:, cc, ::-1], data0=aa[:, cc, ::-1], data1=bb[:, cc, ::-1],
                                                                                initial=hst[:, cc:cc + 1], op0=ALU.mult, op1=ALU.add),
                                     R=[baa, bbb, bhst, brr], W=[brr])
                                P.op(dve, lambda e, cc=cc: e.tensor_copy(out=hst[:, cc:cc + 1], in_=rr[:, cc, 0:1]), R=[brr], W=[bhst])
                        if dr == 0:
                            P.dma(HFv[:, :, t0:t0 + NT], hf[j][:], R=[bhf[j]], W=[bHF], via=bhf[j])
                        else:
                            P.op(pool, lambda e: e.tensor_tensor(out=hf[j][:], in0=hf[j][:], in1=rr[:], op=ALU.add), R=[bhf[j], brr], W=[bhf[j]])
                            P.op(pool, lambda e: e.tensor_tensor(out=ga[:], in0=gr[j][:], in1=gr[j][:], op=ALU.mult), R=[bgr[j]], W=[bga])
                            P.op(pool, lambda e: e.tensor_scalar(out=ga[:], in0=ga[:], scalar1=0.044715, scalar2=1.0, op0=ALU.mult, op1=ALU.add),
                                 R=[bga], W=[bga])
                            P.op(pool, lambda e: e.tensor_tensor(out=ga[:], in0=ga[:], in1=gr[j][:], op=ALU.mult), R=[bga, bgr[j]], W=[bga])
                            P.op(act, lambda e: e.activation(out=ga[:], in_=ga[:], func=AF.Sigmoid, scale=1.5957691216), R=[bga], W=[bga])
                            P.op(pool, lambda e: e.tensor_tensor(out=ga[:], in0=ga[:], in1=gr[j][:], op=ALU.mult), R=[bga, bgr[j]], W=[bga])
                            P.op(dve, lambda e: e.tensor_tensor(out=hf[j][:], in0=hf[j][:], in1=ga[:], op=ALU.mult), R=[bhf[j], bga], W=[bhf[j]])
                            rms_rstd(hf[j][:], 2, NT, sq, rstd, bhf[j], bsq, brstd, psf[4], bpsf[4], 1.0 / 256)
                            for cc in range(2):
                                P.op(dve, lambda e, cc=cc: e.scalar_tensor_tensor(out=yo[j][:, cc, :], in0=hf[j][:, cc, :],
                                                                                scalar=spt[:, O_ON + 6 + cc:O_ON + 7 + cc], in1=rstd[:],
                                                                                op0=ALU.mult, op1=ALU.mult), R=[bhf[j], brstd, bsp], W=[byo[j]])
                            P.dma(YM2[:, :, t0:t0 + NT], yo[j][:], R=[byo[j]], W=[bYM], via=byo[j])

            if lvl < 7:
                break
            with contextlib.ExitStack() as ph:
                P.phase_end()
                NT = 256
                NFC = DFF // 128
                wo = sb("p3wo", [128, 8, D], BF16, ph)
                wg = sb("p3wg", [128, 8, DFF], BF16, ph)
                wu = sb("p3wu", [128, 8, DFF], BF16, ph)
                wd = sb("p3wd", [128, NFC, D], BF16, ph)
                bwo, bwg, bwu, bwd = (Buf(n, multi=True) for n in ("p3wo", "p3wg", "p3wu", "p3wd"))
                with contextlib.ExitStack() as wl:
                    P.phase_end()
                    load_cast(wl, wo, w_out_d[l].rearrange("(kc p) n -> p kc n", p=128), 8, D, 256, bwo, "wo")
                    load_cast(wl, wg, wg_d[l].rearrange("(kc p) n -> p kc n", p=128), 8, DFF, 256, bwg, "wg")
                    load_cast(wl, wu, wu_d[l].rearrange("(kc p) n -> p kc n", p=128), 8, DFF, 256, bwu, "wu")
                with contextlib.ExitStack() as wl:
                    P.phase_end()
                    load_cast(wl, wd, wd_d[l].rearrange("(fc p) n -> p fc n", p=128), NFC, D, 64, bwd, "wd")
                P.phase_end()
                xt = [sb(f"p3x{i}", [128, 8, NT], F32, ph) for i in range(2)]
                bxt = [Buf(f"p3x{i}") for i in range(2)]
                ym = [sb(f"p3ym{i}", [128, 8, NT], BF16, ph) for i in range(2)]
                bym = [Buf(f"p3ym{i}") for i in range(2)]
                hb = sb("p3h", [128, 8, NT], BF16, ph)
                bhb = Buf("p3h")
                tmpn = [sb(f"p3tmp{i}", [128, NT], F32, ph) for i in range(2)]
                btmpn = [Buf(f"p3tmp{i}") for i in range(2)]
                sq = sb("p3sq", [128, 8, NT], BF16, ph)
                bsq = Buf("p3sq")
                rstd = sb("p3rstd", [128, NT], F32, ph)
                brstd = Buf("p3rstd")
                av = sb("p3a", [128, NFC, NT], BF16, ph)
                bav = Buf("p3a", multi=True)
                sg = [sb(f"p3sg{i}", [128, NT], F32, ph) for i in range(2)]
                bsg = [Buf(f"p3sg{i}") for i in range(2)]
                XTv = XT.rearrange("(kc p) t -> p kc t", p=128)
                YMv = YM.rearrange("(kc p) t -> p kc t", p=128)
                for i in range(T // NT):
                    j = i % 2
                    t0 = i * NT
                    seg = t0 // SEG
                    P.dma(xt[j][:], XTv[:, :, t0:t0 + NT], R=[bXT], W=[bxt[j]], via=bxt[j])
                    P.dma(ym[j][:], YMv[:, :, t0:t0 + NT], R=[bYM], W=[bym[j]], via=bym[j])
                    for mc in range(8):
                        pb = mc % 2

                        def mm(e, mc=mc, pb=pb):
                            for kc in range(8):
                                i_ = e.matmul(psf[pb][:, 0:NT], lhsT=wo[:, kc, mc * 128:(mc + 1) * 128], rhs=ym[j][:, kc, :], start=(kc == 0), stop=(kc == 7))
                            return i_
                        P.op(pe, mm, R=[bwo, bym[j]], W=[bpsf[pb]])
                        P.op(dve, lambda e, mc=mc, pb=pb: e.scalar_tensor_tensor(out=xt[j][:, mc, :], in0=psf[pb][:, 0:NT], scalar=modG[0][:, mc, seg:seg + 1],
                                                                                in1=xt[j][:, mc, :], op0=ALU.mult, op1=ALU.add),
                             R=[bpsf[pb], bmod, bxt[j]], W=[bxt[j]])
                    rms_rstd(xt[j][:], 8, NT, sq, rstd, bxt[j], bsq, brstd, psf[2], bpsf[2], 1.0 / D)
                    for kc in range(8):
                        tb = kc % 2
                        P.op(pool, lambda e, kc=kc, tb=tb: e.tensor_tensor(out=tmpn[tb][:], in0=xt[j][:, kc, :], in1=rstd[:], op=ALU.mult),
                             R=[bxt[j], brstd], W=[btmpn[tb]])
                        P.op(act, lambda e, kc=kc, tb=tb: e.activation(out=hb[:, kc, :], in_=tmpn[tb][:], func=AF.Identity,
                                                                       scale=modA[1][:, kc, seg:seg + 1], bias=modB[1][:, kc, seg:seg + 1]),
                             R=[btmpn[tb], bmod], W=[bhb])
                    for fc in range(NFC):
                        pb = 3 + fc % 3

                        def mm(e, fc=fc, pb=pb):
                            for kc in range(8):
                                e.matmul(psf[pb][:, 0:NT], lhsT=wg[:, kc, fc * 128:(fc + 1) * 128], rhs=hb[:, kc, :], start=(kc == 0), stop=(kc == 7),
                                         skip_group_check=True)
                            for kc in range(8):
                                i_ = e.matmul(psf[pb][:, NT:2 * NT], lhsT=wu[:, kc, fc * 128:(fc + 1) * 128], rhs=hb[:, kc, :], start=(kc == 0), stop=(kc == 7),
                                              skip_group_check=True)
                            return i_
                        P.op(pe, mm, R=[bwg, bwu, bhb], W=[bpsf[pb]])
                        s = fc % 2
                        P.op(act, lambda e, pb=pb, s=s: e.activation(out=sg[s][:], in_=psf[pb][:, 0:NT], func=AF.Silu), R=[bpsf[pb]], W=[bsg[s]])
                        P.op(dve, lambda e, pb=pb, s=s, fc=fc: e.tensor_tensor(out=av[:, fc, :], in0=sg[s][:], in1=psf[pb][:, NT:2 * NT], op=ALU.mult),
                             R=[bsg[s], bpsf[pb]], W=[bav])
                    for mc in range(8):
                        pb = mc % 2

                        def mm(e, mc=mc, pb=pb):
                            for fc in range(NFC):
                                i_ = e.matmul(psf[pb][:, 0:NT], lhsT=wd[:, fc, mc * 128:(mc + 1) * 128], rhs=av[:, fc, :], start=(fc == 0), stop=(fc == NFC - 1))
                            return i_
                        P.op(pe, mm, R=[bwd, bav], W=[bpsf[pb]])
                        P.op(dve, lambda e, mc=mc, pb=pb: e.scalar_tensor_tensor(out=xt[j][:, mc, :], in0=psf[pb][:, 0:NT], scalar=modG[1][:, mc, seg:seg + 1],
                                                                                in1=xt[j][:, mc, :], op0=ALU.mult, op1=ALU.add),
                             R=[bpsf[pb], bmod, bxt[j]], W=[bxt[j]])
                    P.dma(XTv[:, :, t0:t0 + NT], xt[j][:], R=[bxt[j]], W=[bXT], via=bxt[j])

        with contextlib.ExitStack() as ph:
            P.phase_end()
            NT = 512
            xt = [sb(f"p4x{i}", [128, 8, NT], F32, ph) for i in range(2)]
            bxt = [Buf(f"p4x{i}") for i in range(2)]
            sq = sb("p4sq", [128, 8, NT], BF16, ph)
            bsq = Buf("p4sq")
            rstd = sb("p4rstd", [128, NT], F32, ph)
            brstd = Buf("p4rstd")
            yo = [sb(f"p4y{i}", [128, D], F32, ph) for i in range(2)]
            byo = [Buf(f"p4y{i}") for i in range(2)]
            XTv = XT.rearrange("(kc p) t -> p kc t", p=128)
            oc = 0
            for i in range(T // NT if lvl >= 2 else 0):
                j = i % 2
                t0 = i * NT
                P.dma(xt[j][:], XTv[:, :, t0:t0 + NT], R=[bXT], W=[bxt[j]], via=bxt[j])
                rms_rstd(xt[j][:], 8, NT, sq, rstd, bxt[j], bsq, brstd, psf[0], bpsf[0], 1.0 / D)
                P.op(dve, lambda e: e.tensor_tensor(out=xt[j][:], in0=xt[j][:], in1=rstd[:].unsqueeze(1).broadcast_to([128, 8, NT]), op=ALU.mult),
                     R=[bxt[j], brstd], W=[bxt[j]])
                P.op(dve, lambda e: e.tensor_tensor(out=xt[j][:], in0=xt[j][:],
                                                    in1=spt[:, O_FN:O_FN + 8].unsqueeze(2).broadcast_to([128, 8, NT]), op=ALU.mult),
                     R=[bxt[j], bsp], W=[bxt[j]])
                for bq in range(NT // 128):
                    o = oc % 2
                    oc += 1
                    for half in range(2):
                        pb = 1 + (2 * oc + half) % 4

                        def tr(e, half=half, pb=pb, bq=bq):
                            for q in range(4):
                                kc = half * 4 + q
                                i_ = e.transpose(out=psf[pb][:, q * 128:(q + 1) * 128], in_=xt[j][:, kc, bq * 128:(bq + 1) * 128], identity=ident_f[:])
                            return i_
                        P.op(pe, tr, R=[bxt[j], bconst], W=[bpsf[pb]])
                        if half == 0:
                            P.op(act, lambda e, pb=pb, o=o: e.activation(out=yo[o][:, 0:512], in_=psf[pb][:], func=AF.Copy), R=[bpsf[pb]], W=[byo[o]])
                        else:
                            P.op(dve, lambda e, pb=pb, o=o: e.tensor_copy(out=yo[o][:, 512:1024], in_=psf[pb][:]), R=[bpsf[pb]], W=[byo[o]])
                    P.dma(y_out[t0 + bq * 128:t0 + (bq + 1) * 128, :], yo[o][:], R=[byo[o]], W=[bY], via=byo[o])
        P.finish()
    return nc


def _col(v):
    v = np.asarray(v, np.float32)
    return np.ascontiguousarray(v.reshape(-1, 128).T)


def pack_small(inp):
    out = np.zeros((DEPTH, 128, NSP), np.float32)
    deltas = np.abs(np.linspace(math.log(1e-2) / 1.5, math.log(1e-2) / 0.3, 256, dtype=np.float32)).astype(np.float32)
    for l in range(DEPTH):
        o = out[l]
        o[:, O_N1:O_N1 + 8] = _col(inp["norm1_w"][l])
        o[:, O_N2:O_N2 + 8] = _col(inp["norm2_w"][l])
        o[:, O_ON:O_ON + 8] = _col(inp["out_norm_w"][l])
        o[:, O_ADAB:O_ADAB + 48] = _col(inp["ada_b"][l])
        cw = np.asarray(inp["hy_conv_w"][l], np.float32)
        o[:, O_HCW:O_HCW + 18] = cw.T.reshape(6, 128, 3).transpose(1, 0, 2).reshape(128, 18)
        o[:, O_HCB:O_HCB + 6] = _col(inp["hy_conv_b"][l])
        o[:, O_HBIAS:O_HBIAS + 2] = _col(inp["hy_bias"][l])
        rw = np.asarray(inp["rg_conv_w"][l], np.float32)
        o[:, O_RCW:O_RCW + 8] = rw.T.reshape(2, 128, 4).transpose(1, 0, 2).reshape(128, 8)
        o[:, O_RCB:O_RCB + 2] = _col(inp["rg_conv_b"][l])
        for nm, off in (("rg_ba", O_RBA), ("rg_bx", O_RBX), ("rg_lambda", O_RLAM)):
            a = np.asarray(inp[nm][l], np.float32)
            o[:, off:off + 4] = a.reshape(2, 2, 128).transpose(2, 0, 1).reshape(128, 4)
        lb = np.asarray(inp["hg_lb_logits"], np.float32)
        o[:, O_LB:O_LB + 16] = lb.reshape(2, DEPTH, 4, 128).transpose(3, 0, 1, 2).reshape(128, 16)
        o[0:64, O_HYB + 0] = inp["hy_b1"][l]
        o[0:64, O_HYB + 1] = inp["hy_b2"][l]
        o[0:64, O_HYB + 2] = inp["hy_b3"][l]
        o[0:64, O_HYB + 3] = inp["hy_freq"][l]
        o[:, O_FN:O_FN + 8] = _col(inp["final_norm_w"])
        o[:, O_ND:O_ND + 2] = -_col(deltas)
    return out


def pos_tables(L, T):
    f32 = np.float32
    t = np.linspace(0.0, 1.0, L, dtype=f32)
    n = np.arange(L, dtype=f32)
    fb = np.linspace(1e-4, 15.0, 16, dtype=f32)
    ang = (fb[None, :] * n[:, None]) * f32(2.0 * math.pi / L)
    z = np.concatenate([t[:, None], np.cos(ang), -np.sin(ang)], axis=-1).astype(f32)
    zfeat = np.zeros((HY_EMB, T), f32)
    zfeat[:, :L] = z.T
    trow = np.full((1, T), 1e4, f32)
    trow[0, :L] = t
    return zfeat, trow


_CACHE = {}


def _get_prog(T):
    if T not in _CACHE:
        _CACHE[T] = build_program(T)
    return _CACHE[T]


def run_cores(core_specs, inp, T):
    nc = _get_prog(T)
    sp = pack_small(inp)
    shared = {k: np.ascontiguousarray(np.asarray(inp[k], np.float32)) for k in
              ("w_in", "w_out", "ada_w", "ffn_wg", "ffn_wu", "ffn_wd", "hy_w1", "hy_w2", "hy_w3", "hy_w4", "rg_wa", "rg_wx")}
    in_maps = []
    tabs = {}
    for (x, c_rows, L) in core_specs:
        if L not in tabs:
            tabs[L] = pos_tables(L, T)
        zfeat, trow = tabs[L]
        m = dict(shared)
        m["x"] = np.ascontiguousarray(x, dtype=np.float32)
        m["cT"] = np.ascontiguousarray(np.asarray(c_rows, np.float32).T)
        m["segflag"] = np.full((128, 1), 1.0 if L == T else 0.0, np.float32)
        m["zfeat"] = zfeat
        m["trow"] = trow
        m["sp"] = sp
        in_maps.append(m)
    res = run_bass_kernel_spmd(nc, in_maps, core_ids=list(range(len(in_maps))))
    return [r["y"] for r in res.results]


def kernel(**inp):
    T = 16384
    xp = np.asarray(inp["x_prompt"], np.float32)
    xs = np.asarray(inp["x_sample"], np.float32)
    cp = np.asarray(inp["c_prompt"], np.float32)
    cs = np.asarray(inp["c_sample"], np.float32)
    specs = []
    for s in range(2):
        specs.append((xs[s], np.repeat(cs[s:s + 1], NSEG, axis=0), T))
    for g in range(2):
        specs.append((xp[4 * g:4 * g + 4].reshape(T, D), cp[4 * g:4 * g + 4], T // NSEG))
    for _ in range(4):
        specs.append((np.zeros((T, D), np.float32), np.zeros((NSEG, D), np.float32), T // NSEG))
    ys = run_cores(specs, inp, T)
    y_sample = np.stack([ys[0], ys[1]], axis=0)
    y_prompt = np.concatenate([ys[2].reshape(4, T // NSEG, D), ys[3].reshape(4, T // NSEG, D)], axis=0)
    return (y_prompt, y_sample)
```

```python
import contextlib
import math
import numpy as np
import ml_dtypes
import concourse.bass as bass
import concourse.mybir as mybir
from concourse.bass_utils import run_bass_kernel_spmd

F32 = mybir.dt.float32
BF16 = mybir.dt.bfloat16
AF = mybir.ActivationFunctionType
ALU = mybir.AluOpType
PI = math.pi

D = 1024
DIN = 3840
DFF = 2816
DEPTH = 2
NSEG = 4
EPS = 1e-6
HY_EMB = 33
O_N1, O_N2, O_ON, O_ADAB, O_HCW, O_HCB, O_HBIAS, O_RCW, O_RCB, O_RBA, O_RBX, O_RLAM, O_LB, O_HYB, O_FN, O_ND = (
    0, 8, 16, 24, 72, 90, 96, 98, 106, 108, 112, 116, 120, 136, 140, 148)
NSP = 150


class Buf:
    def __init__(self, name, multi=False, excl=False):
        self.name = name
        self.multi = multi
        self.excl = excl
        self.w = {}
        self.r = {}
        self.slot = None


class Eng:
    def __init__(self, prog, e, name):
        self.e = e
        self.name = name
        self.sem = prog.sem("e_" + name)
        self.cnt = 0
        self.waited = {}


class Prog:
    def __init__(self, nc, es):
        self.nc = nc
        self.es = es
        self.nsem = 0
        self.pe = Eng(self, nc.tensor, "pe")
        self.act = Eng(self, nc.scalar, "act")
        self.dve = Eng(self, nc.vector, "dve")
        self.pool = Eng(self, nc.gpsimd, "pool")
        self.sp = Eng(self, nc.sync, "sp")
        self.slots = []
        self.slot_pool = []
        self.phase_bufs = []

    def sem(self, name):
        h = self.es.enter_context(self.nc.semaphore(f"{name}_{self.nsem}"))
        self.nsem += 1
        return h

    def _wait(self, eng, R, W, nowaw=False):
        deps = {}
        war = {}
        for b in R:
            for k, v in b.w.items():
                if deps.get(k, 0) < v:
                    deps[k] = v
        for b in W:
            if not (b.multi or nowaw):
                for k, v in b.w.items():
                    if deps.get(k, 0) < v:
                        deps[k] = v
            for k, v in b.r.items():
                if war.get(k, 0) < v:
                    war[k] = v
        for sem, val in war.items():
            if sem is eng.sem:
                continue
            if deps.get(sem, 0) < val:
                deps[sem] = val
        for sem, val in deps.items():
            if sem is eng.sem and eng is self.pe:
                continue
            if eng.waited.get(sem, 0) >= val:
                continue
            eng.e.wait_ge(sem, val)
            eng.waited[sem] = val

    def _mark(self, R, W, sem, val, nowaw=False):
        for b in W:
            if b.multi or nowaw:
                b.w[sem] = val
            else:
                b.w = {sem: val}
                b.r = {}
        for b in R:
            if b.r.get(sem, 0) < val:
                b.r[sem] = val

    def op(self, eng, fn, R=(), W=()):
        ex = [b for b in R if b.excl]
        if ex:
            R = [b for b in R if not b.excl]
            W = list(W) + ex
        self._wait(eng, R, W)
        inst = fn(eng.e)
        eng.cnt += 1
        inst.then_inc(eng.sem, 1)
        self._mark(R, W, eng.sem, eng.cnt)

    def dma(self, out, in_, R=(), W=(), via=None, nowaw=False, **kw):
        self._wait(self.sp, R, W, nowaw)
        if via.slot is None:
            if self.slot_pool:
                via.slot = self.slot_pool.pop(0)
            else:
                via.slot = [self.sem("dma"), 0]
                self.slots.append(via.slot)
            self.phase_bufs.append(via)
        sl = via.slot
        inst = self.nc.sync.dma_start(out=out, in_=in_, **kw)
        sl[1] += 16
        inst.then_inc(sl[0], 16)
        self._mark(R, W, sl[0], sl[1], nowaw)
        if _SYNC_DMA:
            self.nc.sync.wait_ge(sl[0], sl[1])
            self.sp.waited[sl[0]] = sl[1]

    def barrier(self):
        engs = [self.pe, self.act, self.dve, self.pool, self.sp]
        for E in engs:
            for F in engs:
                if F is E or F.cnt == 0:
                    continue
                if E.waited.get(F.sem, 0) < F.cnt:
                    E.e.wait_ge(F.sem, F.cnt)
                    E.waited[F.sem] = F.cnt
            for sl in self.slots:
                if sl[1] > 0 and E.waited.get(sl[0], 0) < sl[1]:
                    E.e.wait_ge(sl[0], sl[1])
                    E.waited[sl[0]] = sl[1]

    def phase_end(self):
        self.barrier()
        for b in self.phase_bufs:
            self.slot_pool.append(b.slot)
            b.slot = None
        self.phase_bufs = []

    def finish(self):
        for sl in self.slots:
            if self.sp.waited.get(sl[0], 0) < sl[1]:
                self.nc.sync.wait_ge(sl[0], sl[1])


_DBG_STOP = None
_DBG_NODUMP = False
_SYNC_DMA = False
_DBG_SUB = 99
_DBG_X = 99
_STAGES = ["c0", "c1", "p0", "p1", "hy", "hg", "rg", "p3"]


def build_program(T):
    SEG = T // NSEG
    stop = _DBG_STOP
    lvl = _STAGES.index(stop) if stop else 99
    NB = T // 128
    nc = bass.Bass("TRN2", target_bir_lowering=False)
    dt_in = lambda name, shape, dt=F32: nc.dram_tensor(name, shape, dt, kind="ExternalInput").ap()
    x_in = dt_in("x", [T, D])
    cT_in = dt_in("cT", [D, NSEG])
    segflag_in = dt_in("segflag", [128, 1])
    zfeat_in = dt_in("zfeat", [HY_EMB, T])
    trow_in = dt_in("trow", [1, T])
    w_in_d = dt_in("w_in", [DEPTH, D, DIN])
    w_out_d = dt_in("w_out", [DEPTH, D, D])
    ada_w_d = dt_in("ada_w", [DEPTH, D, 6 * D])
    wg_d = dt_in("ffn_wg", [DEPTH, D, DFF])
    wu_d = dt_in("ffn_wu", [DEPTH, D, DFF])
    wd_d = dt_in("ffn_wd", [DEPTH, DFF, D])
    sp_d = dt_in("sp", [DEPTH, 128, NSP])
    hw1_d = dt_in("hy_w1", [DEPTH, HY_EMB, 64])
    hw2_d = dt_in("hy_w2", [DEPTH, 64, 64])
    hw3_d = dt_in("hy_w3", [DEPTH, 64, 64])
    hw4_d = dt_in("hy_w4", [DEPTH, 64, 512])
    rwa_d = dt_in("rg_wa", [DEPTH, 2, 4, 64, 64])
    rwx_d = dt_in("rg_wx", [DEPTH, 2, 4, 64, 64])
    y_out = nc.dram_tensor("y", [T, D], F32, kind="ExternalOutput").ap()
    scr = lambda name, shape, dt=F32: nc.dram_tensor(name, shape, dt, kind=("ExternalOutput" if (stop and not _DBG_NODUMP) else "Internal")).ap()
    XT = scr("XT", [D, T])
    U = scr("U", [DIN, T])
    YM = scr("YM", [D, T], BF16)
    HF = scr("HF", [256, T])
    OF = scr("OF", [512, T])
    KF = scr("KF", [256, 2 * T], BF16)
    DBGY = scr("DBGY", [128, NB * 128]) if (stop and not _DBG_NODUMP) else None
    DBGV = scr("DBGV", [128, NB * 256], BF16) if (stop and not _DBG_NODUMP) else None
    bDBG = Buf("DBG", multi=True)
    bXT, bU, bYM, bHF, bOF, bKF, bY = (Buf(n, multi=True) for n in ("XT", "U", "YM", "HF", "OF", "KF", "Y"))
    bIN = Buf("inputs")

    with contextlib.ExitStack() as es:
        P = Prog(nc, es)
        pe, act, dve, pool = P.pe, P.act, P.dve, P.pool

        uniq = [0]

        def sb(name, shape, dt=F32, stack=es):
            uniq[0] += 1
            return stack.enter_context(nc.sbuf_tensor(f"{name}_u{uniq[0]}", shape, dt))

        dbg_n = [0]

        def dbg(name, ap, buf, shape, dt=F32):
            if not stop or _DBG_NODUMP:
                return
            dbg_n[0] += 1
            dd = nc.dram_tensor(f"DBG_{name}", shape, dt, kind="ExternalOutput").ap()
            P.dma(dd, ap, R=[buf], W=[bDBG], via=buf)

        psf = [es.enter_context(nc.psum_tensor(f"psf{i}", [128, 512], F32)) for i in range(7)]
        psb = [es.enter_context(nc.psum_tensor(f"psb{i}", [128, 1024], BF16)) for i in range(1)]
        bpsf = [Buf(f"psf{i}", excl=True) for i in range(7)]
        bpsb = [Buf(f"psb{i}", excl=True) for i in range(1)]

        ones_bf = sb("ones_bf", [128, 128], BF16)
        ident_bf = sb("ident_bf", [128, 128], BF16)
        ident_f = sb("ident_f", [128, 128], F32)
        mask_lo = sb("mask_lo", [64, 8, 64], F32)
        mask_hi = sb("mask_hi", [64, 8, 64], F32)
        cm_f = sb("cm_f", [128, 512], F32)
        cm_b = sb("cm_b", [128, 512], F32)
        epsb = sb("epsb", [128, 1], F32)
        segflag = sb("segflag_t", [128, 1], F32)
        zero_c = sb("zero_c", [128, 1], F32)
        dummy = sb("dummy", [128, 8], F32)
        bconst = Buf("const")
        P.op(dve, lambda e: e.memset(ones_bf[:], 1.0), W=[bconst])
        P.op(dve, lambda e: e.memset(ident_bf[:], 0.0), W=[bconst])
        P.op(dve, lambda e: e.memset(ident_f[:], 0.0), W=[bconst])
        P.op(dve, lambda e: e.memset(mask_lo[:], 1.0), W=[bconst])
        P.op(dve, lambda e: e.memset(mask_hi[:], 1.0), W=[bconst])
        P.op(dve, lambda e: e.memset(cm_f[:], 1.0), W=[bconst])
        P.op(dve, lambda e: e.memset(cm_b[:], 1.0), W=[bconst])
        P.op(dve, lambda e: e.memset(epsb[:], EPS), W=[bconst])
        P.op(dve, lambda e: e.memset(zero_c[:], 0.0), W=[bconst])
        P.op(pool, lambda e: e.affine_select(out=ident_bf[:], in_=ident_bf[:], pattern=[[-1, 128]],
                                             compare_op=ALU.not_equal, fill=1.0, base=0, channel_multiplier=1),
             R=[bconst], W=[bconst])
        P.op(pool, lambda e: e.affine_select(out=ident_f[:], in_=ident_f[:], pattern=[[-1, 128]],
                                             compare_op=ALU.not_equal, fill=1.0, base=0, channel_multiplier=1),
             R=[bconst], W=[bconst])
        P.op(pool, lambda e: e.affine_select(out=mask_lo[:], in_=mask_lo[:], pattern=[[0, 8], [1, 64]],
                                             compare_op=ALU.is_ge, fill=0.0, base=0, channel_multiplier=-1),
             R=[bconst], W=[bconst])
        P.op(pool, lambda e: e.affine_select(out=mask_hi[:], in_=mask_hi[:], pattern=[[0, 8], [-1, 64]],
                                             compare_op=ALU.is_ge, fill=0.0, base=0, channel_multiplier=1),
             R=[bconst], W=[bconst])
        cmf3 = cm_f[:].rearrange("p (c t) -> p c t", t=64)
        cmb3 = cm_b[:].rearrange("p (c t) -> p c t", t=64)
        P.op(dve, lambda e: e.memset(cmf3[:, :, 0:1], 0.0), R=[bconst], W=[bconst])
        P.op(dve, lambda e: e.memset(cmb3[:, :, 63:64], 0.0), R=[bconst], W=[bconst])
        P.dma(segflag[:], segflag_in[:, :], R=[bIN], W=[bconst], via=bconst)

        spt = sb("spt", [128, NSP], F32)
        bsp = Buf("spt")
        modA = [sb(f"modA{i}", [128, 8, NSEG], F32) for i in range(2)]
        modB = [sb(f"modB{i}", [128, 8, NSEG], F32) for i in range(2)]
        modG = [sb(f"modG{i}", [128, 8, NSEG], F32) for i in range(2)]
        bmod = Buf("mod")
        lbt = sb("lbt", [128, 2, 4], F32)
        omlt = sb("omlt", [128, 2, 4], F32)
        clam = sb("clam", [128, 4], F32)
        clam2 = sb("clam2", [128, 4], F32)
        blay = Buf("layer_small")

        def load_cast(stack, dst, src3, K, N, piece, bdst, tag):
            stg = [sb(f"wst_{tag}{i}", [128, K, piece], F32, stack) for i in range(2)]
            bst = [Buf(f"wst_{tag}{i}") for i in range(2)]
            n0 = 0
            i = 0
            while n0 < N:
                w = min(piece, N - n0)
                P.dma(stg[i % 2][:, :, 0:w], src3[:, :, n0:n0 + w], R=[bIN], W=[bst[i % 2]], via=bst[i % 2])
                if i % 2 == 0:
                    P.op(dve, lambda e, i=i, n0=n0, w=w: e.tensor_copy(out=dst[:, :, n0:n0 + w], in_=stg[i % 2][:, :, 0:w]),
                         R=[bst[i % 2]], W=[bdst])
                else:
                    P.op(act, lambda e, i=i, n0=n0, w=w: e.activation(out=dst[:, :, n0:n0 + w], in_=stg[i % 2][:, :, 0:w], func=AF.Copy),
                         R=[bst[i % 2]], W=[bdst])
                n0 += w
                i += 1

        def rms_rstd(xt, K, NT, sq, rstd, bx, bsq, brstd, ps, bps, inv_n):
            P.op(act, lambda e: e.activation(out=sq[:, 0:K, 0:NT], in_=xt, func=AF.Square), R=[bx], W=[bsq])

            def mm(e):
                for kc in range(K):
                    i = e.matmul(ps[:, 0:NT], lhsT=ones_bf[:], rhs=sq[:, kc, 0:NT], start=(kc == 0), stop=(kc == K - 1))
                return i
            P.op(pe, mm, R=[bsq, bconst], W=[bps])
            P.op(act, lambda e: e.activation(out=rstd[:, 0:NT], in_=ps[:, 0:NT], func=AF.Sqrt, scale=inv_n, bias=epsb[:, 0:1]),
                 R=[bps, bconst], W=[brstd])
            P.op(dve, lambda e: e.reciprocal(out=rstd[:, 0:NT], in_=rstd[:, 0:NT]), R=[brstd], W=[brstd])

        with contextlib.ExitStack() as ph:
            P.phase_end()
            xin = [sb(f"p0x{i}", [128, D], F32, ph) for i in range(2)]
            xo = [sb(f"p0o{i}", [128, 8, 128], F32, ph) for i in range(2)]
            bxin = [Buf(f"p0x{i}") for i in range(2)]
            bxo = [Buf(f"p0o{i}") for i in range(2)]
            XTv = XT.rearrange("(kc p) t -> p kc t", p=128)
            for b in range(NB if lvl >= 1 else 0):
                j = b % 2
                P.dma(xin[j][:], x_in[b * 128:(b + 1) * 128, :], R=[bIN], W=[bxin[j]], via=bxin[j])
                for half in range(2):
                    pb = (2 * b + half) % 4

                    def tr(e, half=half, pb=pb):
                        for q in range(4):
                            kc = half * 4 + q
                            i = e.transpose(out=psf[pb][:, q * 128:(q + 1) * 128], in_=xin[j][:, kc * 128:(kc + 1) * 128],
                                            identity=ident_f[:])
                        return i
                    P.op(pe, tr, R=[bxin[j], bconst], W=[bpsf[pb]])
                    eng = act if half == 0 else dve
                    if half == 0:
                        P.op(act, lambda e, pb=pb: e.activation(out=xo[j][:, 0:4, :], in_=psf[pb][:].rearrange("p (q t) -> p q t", t=128),
                                                                func=AF.Copy), R=[bpsf[pb]], W=[bxo[j]])
                    else:
                        P.op(dve, lambda e, pb=pb: e.tensor_copy(out=xo[j][:, 4:8, :], in_=psf[pb][:].rearrange("p (q t) -> p q t", t=128)),
                             R=[bpsf[pb]], W=[bxo[j]])
                P.dma(XTv[:, :, b * 128:(b + 1) * 128], xo[j][:], R=[bxo[j]], W=[bXT], via=bxo[j])

        for l in range(DEPTH):
            if lvl < 3 or (l > 0 and lvl < 99):
                break
            P.dma(spt[:], sp_d[l], R=[bIN], W=[bsp], via=bsp)
            with contextlib.ExitStack() as ph:
                P.phase_end()
                ct = sb("ada_c", [128, 8, NSEG], F32, ph)
                bct = Buf("ada_c")
                modT = sb("ada_mod", [128, 48, NSEG], F32, ph)
                bmodT = Buf("ada_mod")
                P.dma(ct[:], cT_in.rearrange("(kc p) s -> p kc s", p=128), R=[bIN], W=[bct], via=bct)
                P.op(act, lambda e: e.activation(out=ct[:], in_=ct[:], func=AF.Silu), R=[bct], W=[bct])
                aw = [sb(f"ada_w{i}", [128, 8, 512], F32, ph) for i in range(2)]
                baw = [Buf(f"ada_w{i}") for i in range(2)]
                awv = ada_w_d[l].rearrange("(kc p) n -> p kc n", p=128)
                for pc in range(12):
                    j = pc % 2
                    P.dma(aw[j][:], awv[:, :, pc * 512:(pc + 1) * 512], R=[bIN], W=[baw[j]], via=baw[j])
                    pb = pc % 2

                    def mm(e, j=j, pb=pb):
                        for m in range(4):
                            for kc in range(8):
                                i = e.matmul(psf[pb][:, m * NSEG:(m + 1) * NSEG], lhsT=aw[j][:, kc, m * 128:(m + 1) * 128],
                                             rhs=ct[:, kc, :], start=(kc == 0), stop=(kc == 7))
                        return i
                    P.op(pe, mm, R=[baw[j], bct], W=[bpsf[pb]])
                    P.op(dve, lambda e, pc=pc, pb=pb: e.tensor_tensor(
                        out=modT[:, pc * 4:(pc + 1) * 4, :], in0=psf[pb][:, 0:4 * NSEG].rearrange("p (m s) -> p m s", s=NSEG),
                        in1=spt[:, O_ADAB + pc * 4:O_ADAB + (pc + 1) * 4].unsqueeze(2).broadcast_to([128, 4, NSEG]), op=ALU.add),
                        R=[bpsf[pb], bsp], W=[bmodT])
                for i2, (osh, osc, og, onw) in enumerate(((0, 8, 16, O_N1), (24, 32, 40, O_N2))):
                    P.op(dve, lambda e, osc=osc, onw=onw, i2=i2: e.scalar_tensor_tensor(
                        out=modA[i2][:], in0=modT[:, osc:osc + 8, :], scalar=1.0,
                        in1=spt[:, onw:onw + 8].unsqueeze(2).broadcast_to([128, 8, NSEG]), op0=ALU.add, op1=ALU.mult),
                        R=[bmodT, bsp], W=[bmod])
                    P.op(dve, lambda e, osh=osh, i2=i2: e.tensor_copy(out=modB[i2][:], in_=modT[:, osh:osh + 8, :]), R=[bmodT], W=[bmod])
                    P.op(dve, lambda e, og=og, i2=i2: e.tensor_copy(out=modG[i2][:], in_=modT[:, og:og + 8, :]), R=[bmodT], W=[bmod])
                lbv = spt[:, O_LB:O_LB + 16].rearrange("p (d l h) -> p d l h", d=2, l=2)
                if l == 0:
                    P.op(dve, lambda e: e.memset(lbt[:], 0.0), W=[blay])
                else:
                    P.op(dve, lambda e: e.tensor_tensor(out=lbt[:], in0=lbv[:, :, 1, :], in1=lbv[:, :, 0, :], op=ALU.subtract),
                         R=[bsp], W=[blay])
                    P.op(act, lambda e: e.activation(out=lbt[:], in_=lbt[:], func=AF.Sigmoid), R=[blay], W=[blay])
                P.op(dve, lambda e: e.tensor_scalar(out=omlt[:], in0=lbt[:], scalar1=-1.0, scalar2=1.0, op0=ALU.mult, op1=ALU.add),
                     R=[blay], W=[blay])
                dbg(f"omlt_a{l}", omlt[:].rearrange("p a b -> p (a b)"), blay, [128, 8])
                P.op(act, lambda e: e.activation(out=clam[:], in_=spt[:, O_RLAM:O_RLAM + 4], func=AF.Exp, scale=-1.0), R=[bsp], W=[blay])
                P.op(dve, lambda e: e.tensor_scalar(out=clam[:], in0=clam[:], scalar1=1.0, scalar2=None, op0=ALU.add), R=[blay], W=[blay])
                P.op(act, lambda e: e.activation(out=clam[:], in_=clam[:], func=AF.Ln), R=[blay], W=[blay])
                P.op(dve, lambda e: e.tensor_scalar(out=clam2[:], in0=clam[:], scalar1=-16.0, scalar2=None, op0=ALU.mult), R=[blay], W=[blay])
                P.op(dve, lambda e: e.tensor_scalar(out=clam[:], in0=clam[:], scalar1=-8.0, scalar2=None, op0=ALU.mult), R=[blay], W=[blay])

            with contextlib.ExitStack() as ph:
                P.phase_end()
                NT = 512
                wbf = sb("p1w", [128, 8, DIN], BF16, ph)
                bw = Buf("p1w", multi=True)
                xt = [sb(f"p1x{i}", [128, 8, NT], F32, ph) for i in range(2)]
                bxt = [Buf(f"p1x{i}") for i in range(2)]
                hb = [sb(f"p1h{i}", [128, 8, NT], BF16, ph) for i in range(2)]
                bhb = [Buf(f"p1h{i}") for i in range(2)]
                sq = sb("p1sq", [128, 8, NT], BF16, ph)
                bsq = Buf("p1sq")
                rstd = sb("p1rstd", [128, NT], F32, ph)
                brstd = Buf("p1rstd")
                stg = [sb(f"p1st{i}", [128, 3, NT], F32, ph) for i in range(2)]
                bstg = [Buf(f"p1st{i}") for i in range(2)]
                with contextlib.ExitStack() as wl:
                    P.phase_end()
                    load_cast(wl, wbf, w_in_d[l].rearrange("(kc p) n -> p kc n", p=128), 8, DIN, 240, bw, "p1")
                XTv = XT.rearrange("(kc p) t -> p kc t", p=128)
                Uv = U.rearrange("(c p) t -> p c t", p=128)
                for i in range(T // NT):
                    j = i % 2
                    t0 = i * NT
                    seg = t0 // SEG
                    P.dma(xt[j][:], XTv[:, :, t0:t0 + NT], R=[bXT], W=[bxt[j]], via=bxt[j])
                    rms_rstd(xt[j][:], 8, NT, sq, rstd, bxt[j], bsq, brstd, psf[0], bpsf[0], 1.0 / D)
                    P.op(dve, lambda e: e.tensor_tensor(out=xt[j][:], in0=xt[j][:], in1=rstd[:].unsqueeze(1).broadcast_to([128, 8, NT]),
                                                        op=ALU.mult), R=[bxt[j], brstd], W=[bxt[j]])
                    P.op(dve, lambda e: e.tensor_tensor(out=xt[j][:], in0=xt[j][:],
                                                        in1=modA[0][:, :, seg:seg + 1].broadcast_to([128, 8, NT]), op=ALU.mult),
                         R=[bxt[j], bmod], W=[bxt[j]])
                    P.op(dve, lambda e: e.tensor_tensor(out=hb[j][:], in0=xt[j][:],
                                                        in1=modB[0][:, :, seg:seg + 1].broadcast_to([128, 8, NT]), op=ALU.add),
                         R=[bxt[j], bmod], W=[bhb[j]])
                    for mc in range(DIN // 128):
                        pb = 1 + mc % 4

                        def mm(e, mc=mc, pb=pb):
                            for kc in range(8):
                                i_ = e.matmul(psf[pb][:], lhsT=wbf[:, kc, mc * 128:(mc + 1) * 128], rhs=hb[j][:, kc, :],
                                              start=(kc == 0), stop=(kc == 7))
                            return i_
                        P.op(pe, mm, R=[bw, bhb[j]], W=[bpsf[pb]])
                        s = (mc // 3) % 2
                        if mc % 2 == 0:
                            P.op(act, lambda e, pb=pb, s=s, mc=mc: e.activation(out=stg[s][:, mc % 3, :], in_=psf[pb][:], func=AF.Copy),
                                 R=[bpsf[pb]], W=[bstg[s]])
                        else:
                            P.op(dve, lambda e, pb=pb, s=s, mc=mc: e.tensor_copy(out=stg[s][:, mc % 3, :], in_=psf[pb][:]),
                                 R=[bpsf[pb]], W=[bstg[s]])
                        if mc % 3 == 2:
                            P.dma(Uv[:, mc - 2:mc + 1, t0:t0 + NT], stg[s][:], R=[bstg[s]], W=[bU], via=bstg[s])

            dbg(f"omlt_b{l}", omlt[:].rearrange("p a b -> p (a b)"), blay, [128, 8])
            if lvl < 4:
                break
            with contextlib.ExitStack() as ph:
                P.phase_end()
                vT = sb("hy_vT", [128, 256, NB], BF16, ph)
                bvT = Buf("hy_vT", multi=True)
                with contextlib.ExitStack() as fg:
                    P.phase_end()
                    NF = 512
                    w1 = sb("hy_w1", [HY_EMB, 64], F32, fg)
                    w2 = sb("hy_w2", [64, 64], F32, fg)
                    w3 = sb("hy_w3", [64, 64], F32, fg)
                    w4 = sb("hy_w4", [64, 512], F32, fg)
                    bwf = Buf("hy_wf")
                    for tl, src in ((w1, hw1_d), (w2, hw2_d), (w3, hw3_d), (w4, hw4_d)):
                        P.dma(tl[:], src[l], R=[bIN], W=[bwf], via=bwf, nowaw=True)
                    zf = [sb(f"hy_zf{i}", [HY_EMB, NF], F32, fg) for i in range(2)]
                    bzf = [Buf(f"hy_zf{i}") for i in range(2)]
                    tr_ = [sb(f"hy_tr{i}", [128, NF], F32, fg) for i in range(2)]
                    btr = [Buf(f"hy_tr{i}") for i in range(2)]
                    hh = sb("hy_h", [64, NF], F32, fg)
                    bhh = Buf("hy_h")
                    tmpa = sb("hy_tmpa", [64, NF], F32, fg)
                    tmpb = sb("hy_tmpb", [64, NF], F32, fg)
                    btmp = Buf("hy_tmp")
                    dec = sb("hy_dec", [128, 2, NF], F32, fg)
                    bdec = Buf("hy_dec")
                    kt = sb("hy_kt", [128, NF], F32, fg)
                    bkt = Buf("hy_kt")
                    kst = [sb(f"hy_kst{i}", [128, NF], BF16, fg) for i in range(4)]
                    bkst = [Buf(f"hy_kst{i}") for i in range(4)]
                    kcount = 0
                    for n0 in range(0, T if _DBG_SUB >= 1 else 0, NF):
                        j = (n0 // NF) % 2
                        P.dma(zf[j][:], zfeat_in[:, n0:n0 + NF], R=[bIN], W=[bzf[j]], via=bzf[j])
                        P.dma(tr_[j][:], trow_in[0:1, n0:n0 + NF].broadcast_to([128, NF]), R=[bIN], W=[btr[j]], via=btr[j])
                        src_t, bsrc, Kd = zf[j], bzf[j], HY_EMB
                        for li, wl_ in enumerate((w1, w2, w3)):
                            P.op(pe, lambda e, wl_=wl_, src_t=src_t, Kd=Kd: e.matmul(psf[0][0:64, :], lhsT=wl_[0:Kd, :], rhs=src_t[0:Kd, :],
                                                                                   start=True, stop=True),
                                 R=[bwf, bsrc], W=[bpsf[0]])
                            P.op(dve, lambda e, li=li: e.tensor_scalar(out=tmpa[:], in0=psf[0][0:64, :], scalar1=spt[0:64, O_HYB + li:O_HYB + li + 1],
                                                                       scalar2=spt[0:64, O_HYB + 3:O_HYB + 4], op0=ALU.add, op1=ALU.mult),
                                 R=[bpsf[0], bsp], W=[btmp])
                            P.op(dve, lambda e: e.tensor_scalar(out=tmpb[:], in0=tmpa[:], scalar1=PI, scalar2=-2 * PI, op0=ALU.is_gt, op1=ALU.mult),
                                 R=[btmp], W=[btmp])
                            P.op(dve, lambda e: e.tensor_tensor(out=tmpb[:], in0=tmpb[:], in1=tmpa[:], op=ALU.add), R=[btmp], W=[btmp])
                            P.op(dve, lambda e: e.tensor_scalar(out=tmpa[:], in0=tmpa[:], scalar1=-PI, scalar2=2 * PI, op0=ALU.is_lt, op1=ALU.mult),
                                 R=[btmp], W=[btmp])
                            P.op(dve, lambda e: e.tensor_tensor(out=tmpb[:], in0=tmpb[:], in1=tmpa[:], op=ALU.add), R=[btmp], W=[btmp])
                            P.op(act, lambda e: e.activation(out=hh[:], in_=tmpb[:], func=AF.Sin), R=[btmp], W=[bhh])
                            src_t, bsrc, Kd = hh, bhh, 64
                        for cc in range(2):
                            P.op(act, lambda e, cc=cc: e.activation(out=dec[:, cc, :], in_=tr_[j][:], func=AF.Exp,
                                                                    scale=spt[:, O_ND + cc:O_ND + cc + 1]), R=[btr[j], bsp], W=[bdec])
                        for q in range(4):
                            cc = q % 2
                            pb = 1 + q % 2
                            P.op(pe, lambda e, q=q, pb=pb: e.matmul(psf[pb][:], lhsT=w4[:, q * 128:(q + 1) * 128], rhs=hh[:], start=True, stop=True),
                                 R=[bwf, bhh], W=[bpsf[pb]])
                            P.op(dve, lambda e, pb=pb, cc=cc: e.tensor_tensor(out=kt[:], in0=psf[pb][:], in1=dec[:, cc, :], op=ALU.mult),
                                 R=[bpsf[pb], bdec], W=[bkt])
                            ks = kcount % 4
                            kcount += 1
                            if q < 2:
                                if n0 == 0:
                                    P.op(dve, lambda e, cc=cc: e.tensor_tensor(out=kt[:, 0:1], in0=kt[:, 0:1],
                                                                               in1=spt[:, O_HBIAS + cc:O_HBIAS + cc + 1], op=ALU.add),
                                         R=[bkt, bsp], W=[bkt])
                                P.op(act, lambda e, ks=ks: e.activation(out=kst[ks][:], in_=kt[:], func=AF.Copy), R=[bkt], W=[bkst[ks]])
                                P.dma(KF[cc * 128:(cc + 1) * 128, T - 1 + n0:T - 1 + n0 + NF], kst[ks][:], R=[bkst[ks]], W=[bKF], via=bkst[ks])
                            else:
                                P.op(dve, lambda e, ks=ks: e.tensor_copy(out=kst[ks][:, ::-1], in_=kt[:]), R=[bkt], W=[bkst[ks]])
                                if n0 == 0:
                                    P.dma(KF[cc * 128:(cc + 1) * 128, T - NF:T - 1], kst[ks][:, 0:NF - 1], R=[bkst[ks]], W=[bKF], via=bkst[ks])
                                else:
                                    P.dma(KF[cc * 128:(cc + 1) * 128, T - n0 - NF:T - n0], kst[ks][:], R=[bkst[ks]], W=[bKF], via=bkst[ks])
                with contextlib.ExitStack() as p1:
                    P.phase_end()
                    NT = 512
                    ux = [sb(f"hy_ux{i}", [128, 4, NT + 2], F32, p1) for i in range(2)]
                    bux = [Buf(f"hy_ux{i}") for i in range(2)]
                    cv = sb("hy_cv", [128, 4, NT], F32, p1)
                    bcv = Buf("hy_cv")
                    vrev = [sb(f"hy_vrev{i}", [128, 2, NT], BF16, p1) for i in range(2)]
                    bvrev = [Buf(f"hy_vrev{i}") for i in range(2)]
                    Uq = U[256:768, :].rearrange("(q p) t -> p q t", p=128)
                    for i in range(T // NT if _DBG_SUB >= 2 else 0):
                        j = i % 2
                        t0 = i * NT
                        lo = max(t0 - 1, 0)
                        hi = min(t0 + NT + 1, T)
                        P.dma(ux[j][:, :, lo - (t0 - 1):hi - (t0 - 1)], Uq[:, :, lo:hi], R=[bU], W=[bux[j]], via=bux[j])
                        if t0 == 0:
                            P.op(dve, lambda e: e.memset(ux[j][:, :, 0:1], 0.0), W=[bux[j]])
                        elif t0 % SEG == 0:
                            P.op(dve, lambda e: e.tensor_scalar(out=ux[j][:, :, 0:1], in0=ux[j][:, :, 0:1], scalar1=segflag[:, 0:1],
                                                                 scalar2=None, op0=ALU.mult), R=[bux[j], bconst], W=[bux[j]])
                        if t0 + NT == T:
                            P.op(dve, lambda e: e.memset(ux[j][:, :, NT + 1:NT + 2], 0.0), W=[bux[j]])
                        elif (t0 + NT) % SEG == 0:
                            P.op(dve, lambda e: e.tensor_scalar(out=ux[j][:, :, NT + 1:NT + 2], in0=ux[j][:, :, NT + 1:NT + 2],
                                                                 scalar1=segflag[:, 0:1], scalar2=None, op0=ALU.mult),
                                 R=[bux[j], bconst], W=[bux[j]])
                        for q in range(4):
                            ch = 2 + q
                            eng = dve if q % 2 == 0 else pool
                            wq = lambda k, ch=ch: spt[:, O_HCW + ch * 3 + k:O_HCW + ch * 3 + k + 1]
                            P.op(dve, lambda e, q=q, ch=ch, wq=wq: e.tensor_scalar(out=cv[:, q, :], in0=ux[j][:, q, 0:NT], scalar1=wq(0),
                                                                                  scalar2=spt[:, O_HCB + ch:O_HCB + ch + 1], op0=ALU.mult, op1=ALU.add),
                                 R=[bux[j], bsp], W=[bcv])
                            for k in (1, 2):
                                P.op(dve, lambda e, q=q, k=k, wq=wq: e.scalar_tensor_tensor(out=cv[:, q, :], in0=ux[j][:, q, k:k + NT], scalar=wq(k),
                                                                                           in1=cv[:, q, :], op0=ALU.mult, op1=ALU.add),
                                     R=[bux[j], bsp, bcv], W=[bcv])
                        for cc in range(2 if _DBG_X >= 1 else 0):
                            P.op(dve, lambda e, cc=cc: e.tensor_tensor(out=vrev[j][:, cc, ::-1], in0=cv[:, cc, :], in1=cv[:, 2 + cc, :], op=ALU.mult),
                                 R=[bcv], W=[bvrev[j]])
                        P.op(dve, lambda e: e.memset(dummy[:], 0.0), W=[bvrev[j]])
                        for cc in range(2 if _DBG_X >= 2 else 0):
                            pbi = 0

                            def tr(e, cc=cc, pbi=pbi):
                                for bq in range(4):
                                    i_ = e.transpose(out=psb[pbi][:, bq * 128:(bq + 1) * 128], in_=vrev[j][:, cc, bq * 128:(bq + 1) * 128],
                                                     identity=ident_bf[:])
                                return i_
                            P.op(pe, tr, R=[bvrev[j], bconst], W=[bpsb[pbi]])
                            B0 = t0 // 128
                            dst = vT[:, cc * 128:(cc + 1) * 128, B0 + 3::-1] if B0 == 0 else vT[:, cc * 128:(cc + 1) * 128, B0 + 3:B0 - 1:-1]
                            P.op(dve, lambda e, dst=dst, pbi=pbi: e.tensor_copy(out=dst, in_=psb[pbi][:, 0:512].rearrange("p (b c) -> p c b", c=128)),
                                 R=[bpsb[pbi]], W=[bvT])
                with contextlib.ExitStack() as p2:
                    P.phase_end()
                    BS = NB // NSEG
                    yT = sb("hy_yT", [128, NB, 128], F32, p2)
                    byT = Buf("hy_yT", multi=True)
                    NDP = 16
                    NE = 2 * NB - 1
                    npieces = (NE + NDP - 1) // NDP
                    NPB = 6
                    pc_t = [sb(f"hy_pc{i}", [128, NDP * 128], BF16, p2) for i in range(NPB)]
                    bpc = [Buf(f"hy_pc{i}") for i in range(NPB)]
                    pcn_t = [sb(f"hy_pcn{i}", [128, NDP * 128], BF16, p2) for i in range(3)]
                    bpcn = [Buf(f"hy_pcn{i}") for i in range(3)]
                    sfm1 = sb("hy_sfm1", [128, 1], F32, p2)
                    P.op(dve, lambda e: e.tensor_scalar(out=sfm1[:], in0=segflag[:], scalar1=-1.0, scalar2=None, op0=ALU.add), R=[bconst], W=[bconst])
                    NT = 512
                    u0 = [sb(f"hy_u0{i}", [128, NT + 2], F32, p2) for i in range(2)]
                    bu0 = [Buf(f"hy_u0{i}") for i in range(2)]
                    c0 = sb("hy_c0", [128, NT], F32, p2)
                    bc0 = Buf("hy_c0")
                    yfo = [sb(f"hy_yfo{i}", [128, NT], F32, p2) for i in range(2)]
                    byfo = [Buf(f"hy_yfo{i}") for i in range(2)]
                    pcount = 0
                    ncount = 0
                    e_mid = NB - 1
                    p_mid = e_mid // NDP
                    order = [p_mid] + [p for p in range(npieces) if p != p_mid]
                    for cc in range(2 if _DBG_SUB >= 3 else 0):
                        for ci in range(128):
                            c = cc * 128 + ci
                            pb = ci % 4
                            ybank = psf[pb]
                            nmm = 0
                            total = NE + sum(1 for ee in range(NE) if 0 < abs(ee - (NB - 1)) <= BS)
                            for p_ in order:
                                e0 = p_ * NDP
                                e1 = min(e0 + NDP, NE)
                                W_ = (e1 - e0) * 128
                                base = (T - 1) + (e0 - (NB - 1)) * 128 - 127
                                k = pcount % NPB
                                pcount += 1
                                src = bass.AP(tensor=KF.tensor, offset=c * 2 * T + base, ap=[[1, 128], [1, W_]])
                                P.dma(pc_t[k][:, 0:W_], src, R=[bKF], W=[bpc[k]], via=bpc[k])
                                es_ = list(range(e0, e1))
                                if p_ == p_mid:
                                    es_ = [e_mid] + [e for e in es_ if e != e_mid]
                                cross = [ee for ee in es_ if 0 < abs(ee - (NB - 1)) <= BS]
                                kn = None
                                if cross:
                                    kn = ncount % 3
                                    ncount += 1
                                    ca, cb_ = min(cross) - e0, max(cross) - e0 + 1
                                    if ncount % 2 == 0:
                                        P.op(dve, lambda e, k=k, kn=kn, ca=ca, cb_=cb_: e.tensor_scalar(
                                            out=pcn_t[kn][:, ca * 128:cb_ * 128], in0=pc_t[k][:, ca * 128:cb_ * 128], scalar1=sfm1[:, 0:1], scalar2=None,
                                            op0=ALU.mult), R=[bpc[k], bconst], W=[bpcn[kn]])
                                    else:
                                        P.op(act, lambda e, k=k, kn=kn, ca=ca, cb_=cb_: e.activation(
                                            out=pcn_t[kn][:, ca * 128:cb_ * 128], in_=pc_t[k][:, ca * 128:cb_ * 128], func=AF.Copy, scale=sfm1[:, 0:1]),
                                            R=[bpc[k], bconst], W=[bpcn[kn]])

                                def mm(e, es_=es_, e0=e0, k=k, kn=kn, ci=ci, cc=cc, ybank=ybank, nmm=nmm, total=total):
                                    i_ = None
                                    n_ = nmm
                                    for ee in es_:
                                        d = ee - (NB - 1)
                                        i0 = max(0, d)
                                        i1 = min(NB - 1, NB - 1 + d)
                                        i_ = e.matmul(ybank[:, i0:i1 + 1], lhsT=pc_t[k][:, (ee - e0) * 128:(ee - e0 + 1) * 128],
                                                      rhs=vT[:, cc * 128 + ci, i0 - d:i1 - d + 1], start=(n_ == 0), stop=(n_ == total - 1),
                                                      skip_group_check=True)
                                        n_ += 1
                                        if 0 < abs(d) <= BS:
                                            ad = abs(d)
                                            o_off = BS if d > 0 else BS - ad
                                            r_off = BS - ad if d > 0 else BS
                                            ycol = ybank[:, 0:1]
                                            vcol = vT[:, cc * 128 + ci, 0:1]
                                            o_ap = bass.AP(tensor=ycol.tensor, offset=ycol.offset + o_off, ap=[list(ycol.ap[0]), [BS, NSEG - 1], [1, ad]])
                                            r_ap = bass.AP(tensor=vcol.tensor, offset=vcol.offset + r_off, ap=[list(vcol.ap[0]), [BS, NSEG - 1], [1, ad]])
                                            i_ = e.matmul(o_ap, lhsT=pcn_t[kn][:, (ee - e0) * 128:(ee - e0 + 1) * 128], rhs=r_ap,
                                                          start=False, stop=(n_ == total - 1), skip_group_check=True)
                                            n_ += 1
                                    return i_
                                rl = [bpc[k], bvT] + ([bpcn[kn]] if kn is not None else [])
                                P.op(pe, mm, R=rl, W=[bpsf[pb]])
                                nmm += len(es_) + len(cross)
                            if ci % 2 == 0:
                                P.op(act, lambda e, ci=ci, ybank=ybank: e.activation(out=yT[:, :, ci], in_=ybank[:, 0:NB], func=AF.Copy),
                                     R=[bpsf[pb]], W=[byT])
                            else:
                                P.op(dve, lambda e, ci=ci, ybank=ybank: e.tensor_copy(out=yT[:, :, ci], in_=ybank[:, 0:NB]),
                                     R=[bpsf[pb]], W=[byT])
                        if stop and cc == 0 and not _DBG_NODUMP:
                            P.dma(DBGY[:, :], yT[:].rearrange("p b c -> p (b c)"), R=[byT], W=[bDBG], via=byT)
                            P.dma(DBGV[:, :], vT[:].rearrange("p c b -> p (c b)"), R=[bvT], W=[bDBG], via=bvT)
                        for i in range(T // NT if _DBG_SUB >= 4 else 0):
                            j = i % 2
                            t0 = i * NT
                            lo = max(t0 - 1, 0)
                            hi = min(t0 + NT + 1, T)
                            P.dma(u0[j][:, lo - (t0 - 1):hi - (t0 - 1)], U[cc * 128:(cc + 1) * 128, lo:hi], R=[bU], W=[bu0[j]], via=bu0[j])
                            if t0 == 0:
                                P.op(dve, lambda e: e.memset(u0[j][:, 0:1], 0.0), W=[bu0[j]])
                            elif t0 % SEG == 0:
                                P.op(dve, lambda e: e.tensor_scalar(out=u0[j][:, 0:1], in0=u0[j][:, 0:1], scalar1=segflag[:, 0:1],
                                                                    scalar2=None, op0=ALU.mult), R=[bu0[j], bconst], W=[bu0[j]])
                            if t0 + NT == T:
                                P.op(dve, lambda e: e.memset(u0[j][:, NT + 1:NT + 2], 0.0), W=[bu0[j]])
                            elif (t0 + NT) % SEG == 0:
                                P.op(dve, lambda e: e.tensor_scalar(out=u0[j][:, NT + 1:NT + 2], in0=u0[j][:, NT + 1:NT + 2],
                                                                    scalar1=segflag[:, 0:1], scalar2=None, op0=ALU.mult),
                                     R=[bu0[j], bconst], W=[bu0[j]])
                            wq = lambda k_, cc=cc: spt[:, O_HCW + cc * 3 + k_:O_HCW + cc * 3 + k_ + 1]
                            P.op(dve, lambda e, wq=wq: e.tensor_scalar(out=c0[:], in0=u0[j][:, 0:NT], scalar1=wq(0),
                                                                      scalar2=spt[:, O_HCB + cc:O_HCB + cc + 1], op0=ALU.mult, op1=ALU.add),
                                 R=[bu0[j], bsp], W=[bc0])
                            for k_ in (1, 2):
                                P.op(dve, lambda e, k_=k_, wq=wq: e.scalar_tensor_tensor(out=c0[:], in0=u0[j][:, k_:k_ + NT], scalar=wq(k_),
                                                                                        in1=c0[:], op0=ALU.mult, op1=ALU.add),
                                     R=[bu0[j], bsp, bc0], W=[bc0])
                            pb = 4 + i % 2

                            def tr(e, pb=pb, t0=t0):
                                for bq in range(4):
                                    i_ = e.transpose(out=psf[pb][:, bq * 128:(bq + 1) * 128], in_=yT[:, t0 // 128 + bq, :], identity=ident_f[:])
                                return i_
                            P.op(pe, tr, R=[byT, bconst], W=[bpsf[pb]])
                            P.op(dve, lambda e, pb=pb: e.tensor_tensor(out=yfo[j][:], in0=psf[pb][:], in1=c0[:], op=ALU.mult),
                                 R=[bpsf[pb], bc0], W=[byfo[j]])
                            P.dma(HF[cc * 128:(cc + 1) * 128, t0:t0 + NT], yfo[j][:], R=[byfo[j]], W=[bHF], via=byfo[j])
                with contextlib.ExitStack() as p3:
                    P.phase_end()
                    NT = 512
                    yf = [sb(f"hy_yf{i}", [128, 2, NT], F32, p3) for i in range(2)]
                    byf = [Buf(f"hy_yf{i}") for i in range(2)]
                    sq = sb("hy_sq", [128, 2, NT], BF16, p3)
                    bsq = Buf("hy_sq")
                    rstd = sb("hy_rstd", [128, NT], F32, p3)
                    brstd = Buf("hy_rstd")
                    yo = [sb(f"hy_yo{i}", [128, 2, NT], BF16, p3) for i in range(2)]
                    byo = [Buf(f"hy_yo{i}") for i in range(2)]
                    HFv = HF.rearrange("(q p) t -> p q t", p=128)
                    YM0 = YM[0:256, :].rearrange("(q p) t -> p q t", p=128)
                    for i in range(T // NT if _DBG_SUB >= 4 else 0):
                        j = i % 2
                        t0 = i * NT
                        P.dma(yf[j][:], HFv[:, :, t0:t0 + NT], R=[bHF], W=[byf[j]], via=byf[j])
                        rms_rstd(yf[j][:], 2, NT, sq, rstd, byf[j], bsq, brstd, psf[2], bpsf[2], 1.0 / 256)
                        for q in range(2):
                            P.op(dve, lambda e, q=q: e.scalar_tensor_tensor(out=yo[j][:, q, :], in0=yf[j][:, q, :],
                                                                          scalar=spt[:, O_ON + q:O_ON + q + 1], in1=rstd[:],
                                                                          op0=ALU.mult, op1=ALU.mult),
                                 R=[byf[j], brstd, bsp], W=[byo[j]])
                        P.dma(YM0[:, :, t0:t0 + NT], yo[j][:], R=[byo[j]], W=[bYM], via=byo[j])

            dbg(f"omlt_c{l}", omlt[:].rearrange("p a b -> p (a b)"), blay, [128, 8])
            if lvl < 5:
                break
            with contextlib.ExitStack() as ph:
                P.phase_end()
                NT = 512
                NCH = NT // 64
                KW = [16, 32, 48, 64]
                KO = [0, NCH * 16, NCH * 48, NCH * 96]
                Ug = U[768:768 + 2560, :].rearrange("(g h p) t -> p g h t", g=5, h=4)
                OFv = OF.rearrange("(h p) t -> p h t", p=128)
                YM1 = YM[256:768, :].rearrange("(h p) t -> p h t", p=128)
                mask4 = sb("hg_mask4", [64, NCH, 4, 16], F32, ph)
                bmask4 = Buf("hg_mask4")
                P.op(dve, lambda e: e.memset(mask4[:], 1.0), W=[bmask4])
                P.op(pool, lambda e: e.affine_select(out=mask4[:], in_=mask4[:], pattern=[[0, NCH], [16, 4], [1, 16]],
                                                     compare_op=ALU.is_ge, fill=0.0, base=0, channel_multiplier=-1),
                     R=[bmask4], W=[bmask4])
                sqh = sb("hg_sq", [128, 1, NT], BF16, ph)
                bsqh = Buf("hg_sq")
                rstdh = sb("hg_rstd", [128, NT], F32, ph)
                brstdh = Buf("hg_rstd")
                S = [sb(f"hg_S{h}", [128, 128], F32, ph) for h in range(4)]
                bS = [Buf(f"hg_S{h}") for h in range(4)]

                class HeadBufs:
                    pass
                HB = []
                for h in range(4):
                    B = HeadBufs()
                    for nm, shape, dt_ in (("qv", [128, 2, NT], F32), ("zt", [128, NT], F32), ("gt", [128, NT], F32), ("oft", [128, NT], F32),
                                           ("qr", [128, 2, NT], F32), ("fA", [128, NT], F32), ("fB", [128, NT], F32), ("fC", [128, NT], F32),
                                           ("fD", [128, NT], F32), ("fK", [128, NCH * 160], F32), ("rr", [128, NCH, 4], F32), ("ebl", [128, NCH], F32),
                                           ("Qh", [128, NT], BF16), ("Kh", [128, NT], BF16), ("Qs", [128, NT], BF16), ("KI", [128, NCH * 160], BF16),
                                           ("vbf", [128, NT], BF16), ("Ktok", [64, NCH, 128], BF16), ("Vtok", [64, NCH, 128], BF16),
                                           ("Am", [64, NCH, 4, 16], BF16), ("Spb0", [128, 128], BF16), ("Spb1", [128, 128], BF16), ("yo", [128, NT], BF16)):
                        setattr(B, nm, sb(f"hg{h}_{nm}", shape, dt_, ph))
                        setattr(B, "b_" + nm, Buf(f"hg{h}_{nm}"))
                    P.op(dve, lambda e, B=B: e.memset(B.rr[:], 0.0), W=[B.b_rr])
                    HB.append(B)

                def unit(dr, i, h):
                    B = HB[h]
                    t0 = i * NT
                    ob = 1 + h
                    sbank = 5 + (h % 2)
                    P.dma(B.qv[:], Ug[:, 0:2, h, t0:t0 + NT], R=[bU], W=[B.b_qv], via=B.b_qv)
                    P.dma(B.zt[:], Ug[:, 2 + dr, h, t0:t0 + NT], R=[bU], W=[B.b_zt], via=B.b_zt)
                    if dr == 1:
                        P.dma(B.gt[:], Ug[:, 4, h, t0:t0 + NT], R=[bU], W=[B.b_gt], via=B.b_gt)
                        P.dma(B.oft[:], OFv[:, h, t0:t0 + NT], R=[bOF], W=[B.b_oft], via=B.b_oft)
                    at_seg = (t0 % SEG == 0) if dr == 0 else ((t0 + NT) % SEG == 0)
                    if at_seg:
                        P.op(dve, lambda e: e.tensor_scalar(out=S[h][:], in0=S[h][:], scalar1=segflag[:, 0:1], scalar2=None,
                                                            op0=ALU.mult), R=[bS[h], bconst], W=[bS[h]])
                    yield
                    if dr == 0:
                        P.op(act, lambda e: e.activation(out=B.fA[:], in_=B.zt[:], func=AF.Sigmoid), R=[B.b_zt], W=[B.b_fA])
                        P.op(pool, lambda e: e.tensor_copy(out=B.qr[:], in_=B.qv[:]), R=[B.b_qv], W=[B.b_qr])
                    else:
                        P.op(dve, lambda e: e.tensor_copy(out=B.fD[:], in_=B.zt[:, ::-1]), R=[B.b_zt], W=[B.b_fD])
                        yield
                        P.op(act, lambda e: e.activation(out=B.fA[:], in_=B.fD[:], func=AF.Sigmoid), R=[B.b_fD], W=[B.b_fA])
                        for g_ in range(2):
                            P.op(dve, lambda e, g_=g_: e.tensor_copy(out=B.qr[:, g_, :], in_=B.qv[:, g_, ::-1]), R=[B.b_qv], W=[B.b_qr])
                    yield
                    P.op(dve, lambda e: e.tensor_scalar(out=B.fA[:], in0=B.fA[:], scalar1=omlt[:, dr, h:h + 1], scalar2=lbt[:, dr, h:h + 1],
                                                        op0=ALU.mult, op1=ALU.add), R=[B.b_fA, blay], W=[B.b_fA])
                    yield
                    P.op(act, lambda e: e.activation(out=B.fB[:], in_=B.fA[:], func=AF.Ln), R=[B.b_fA], W=[B.b_fB])
                    P.op(act, lambda e: e.activation(out=B.vbf[:], in_=B.qr[:, 1, :], func=AF.Copy), R=[B.b_qr], W=[B.b_vbf])
                    yield
                    P.op(pool, lambda e: e.tensor_scalar(out=B.fA[:], in0=B.fA[:], scalar1=-1.0, scalar2=1.0, op0=ALU.mult, op1=ALU.add),
                         R=[B.b_fA], W=[B.b_fA])
                    P.op(dve, lambda e: e.tensor_tensor_scan(out=B.fC[:], data0=cm_f[:], data1=B.fB[:], initial=0.0, op0=ALU.mult, op1=ALU.add),
                         R=[B.b_fB, bconst], W=[B.b_fC])

                    def trV(e):
                        for c in range(NCH):
                            i_ = e.transpose(out=psb[0][0:64, c * 128:(c + 1) * 128], in_=B.vbf[:, c * 64:(c + 1) * 64], identity=ident_bf[:])
                        return i_
                    P.op(pe, trV, R=[B.b_vbf, bconst], W=[bpsb[0]])
                    P.op(act, lambda e: e.activation(out=B.Vtok[:], in_=psb[0][0:64, :].rearrange("p (c k) -> p c k", k=128), func=AF.Copy),
                         R=[bpsb[0]], W=[B.b_Vtok])
                    yield
                    b3 = B.fC[:].rearrange("p (c t) -> p c t", t=64)
                    b4 = B.fC[:].rearrange("p (c i t) -> p c i t", i=4, t=16)
                    kk3 = B.fA[:].rearrange("p (c t) -> p c t", t=64)
                    P.op(act, lambda e: e.activation(out=B.ebl[:], in_=b3[:, :, 63], func=AF.Exp), R=[B.b_fC], W=[B.b_ebl])
                    P.op(act, lambda e: e.activation(out=B.fB[:], in_=B.fC[:], func=AF.Exp), R=[B.b_fC], W=[B.b_fB])
                    P.op(dve, lambda e: e.tensor_tensor(out=B.fD[:].rearrange("p (c t) -> p c t", t=64),
                                                        in0=b3[:, :, 63:64].broadcast_to([128, NCH, 64]), in1=b3, op=ALU.subtract),
                         R=[B.b_fC], W=[B.b_fD])
                    P.op(dve, lambda e: e.tensor_copy(out=B.rr[:, :, 1:4], in_=b3[:, :, 15:48:16]), R=[B.b_fC], W=[B.b_rr])
                    yield
                    P.op(dve, lambda e: e.tensor_tensor(out=B.Qh[:], in0=B.qr[:, 0, :], in1=B.fB[:], op=ALU.mult), R=[B.b_qr, B.b_fB], W=[B.b_Qh])
                    P.op(act, lambda e: e.activation(out=B.fD[:], in_=B.fD[:], func=AF.Exp), R=[B.b_fD], W=[B.b_fD])
                    for I in range(4):
                        P.op(dve, lambda e, I=I: e.tensor_tensor(
                            out=B.fK[:, KO[I]:KO[I] + NCH * KW[I]].rearrange("p (c t) -> p c t", t=KW[I]),
                            in0=B.rr[:, :, I:I + 1].broadcast_to([128, NCH, KW[I]]), in1=b3[:, :, 0:KW[I]], op=ALU.subtract),
                            R=[B.b_fC, B.b_rr], W=[B.b_fK])
                    yield
                    P.op(act, lambda e: e.activation(out=B.fK[:], in_=B.fK[:], func=AF.Exp), R=[B.b_fK], W=[B.b_fK])
                    P.op(dve, lambda e: e.tensor_tensor(out=B.Kh[:], in0=B.fA[:], in1=B.fD[:], op=ALU.mult), R=[B.b_fA, B.b_fD], W=[B.b_Kh])
                    P.op(dve, lambda e: e.tensor_tensor(out=B.fB[:].rearrange("p (c i t) -> p c i t", i=4, t=16), in0=b4,
                                                        in1=B.rr[:].unsqueeze(3).broadcast_to([128, NCH, 4, 16]), op=ALU.subtract),
                         R=[B.b_fC, B.b_rr, B.b_Qh], W=[B.b_fB])
                    yield
                    P.op(act, lambda e: e.activation(out=B.fB[:], in_=B.fB[:], func=AF.Exp), R=[B.b_fB], W=[B.b_fB])

                    def trK(e):
                        for c in range(NCH):
                            i_ = e.transpose(out=psb[0][0:64, c * 128:(c + 1) * 128], in_=B.Kh[:, c * 64:(c + 1) * 64], identity=ident_bf[:])
                        return i_
                    P.op(pe, trK, R=[B.b_Kh, bconst], W=[bpsb[0]])
                    P.op(act, lambda e: e.activation(out=B.Ktok[:], in_=psb[0][0:64, :].rearrange("p (c k) -> p c k", k=128), func=AF.Copy),
                         R=[bpsb[0]], W=[B.b_Ktok])
                    for I in range(4):
                        P.op(dve, lambda e, I=I: e.tensor_tensor(
                            out=B.KI[:, KO[I]:KO[I] + NCH * KW[I]].rearrange("p (c t) -> p c t", t=KW[I]),
                            in0=B.fK[:, KO[I]:KO[I] + NCH * KW[I]].rearrange("p (c t) -> p c t", t=KW[I]), in1=kk3[:, :, 0:KW[I]], op=ALU.mult),
                            R=[B.b_fK, B.b_fA], W=[B.b_KI])
                    yield
                    P.op(dve, lambda e: e.tensor_tensor(out=B.Qs[:], in0=B.qr[:, 0, :], in1=B.fB[:], op=ALU.mult), R=[B.b_qr, B.b_fB], W=[B.b_Qs])
                    yield

                    def mmA(e):
                        for c in range(NCH):
                            for I in range(4):
                                col = (c * 4 + I) * 16
                                i_ = e.matmul(psf[0][0:KW[I], col:col + 16], lhsT=B.KI[:, KO[I] + c * KW[I]:KO[I] + (c + 1) * KW[I]],
                                              rhs=B.Qs[:, c * 64 + I * 16:c * 64 + (I + 1) * 16], start=True, stop=True, skip_group_check=True)
                        return i_
                    P.op(pe, mmA, R=[B.b_KI, B.b_Qs], W=[bpsf[0]])
                    P.op(dve, lambda e: e.tensor_tensor(out=B.Am[:], in0=psf[0][0:64, :].rearrange("p (c i t) -> p c i t", i=4, t=16), in1=mask4[:],
                                                        op=ALU.mult), R=[bpsf[0], bmask4], W=[B.b_Am])
                    yield
                    for c in range(NCH):
                        Spb_, bSpb_ = (B.Spb0, B.b_Spb0) if c % 2 == 0 else (B.Spb1, B.b_Spb1)
                        P.op(act, lambda e, Spb_=Spb_: e.activation(out=Spb_[:], in_=S[h][:], func=AF.Copy), R=[bS[h]], W=[bSpb_])

                        def mmo(e, c=c, Spb_=Spb_):
                            e.matmul(psf[ob][:, c * 64:(c + 1) * 64], lhsT=Spb_[:], rhs=B.Qh[:, c * 64:(c + 1) * 64], start=True, stop=False,
                                     skip_group_check=True)
                            for I in range(4):
                                i_ = e.matmul(psf[ob][:, c * 64 + I * 16:c * 64 + (I + 1) * 16], lhsT=B.Vtok[0:KW[I], c, :], rhs=B.Am[0:KW[I], c, I, :],
                                              start=False, stop=(I == 3), skip_group_check=True)
                            return i_
                        P.op(pe, mmo, R=[B.b_Vtok, B.b_Am, bSpb_, B.b_Qh], W=[bpsf[ob]])
                        P.op(pe, lambda e, c=c: e.matmul(psf[sbank][:, 0:128], lhsT=B.Ktok[:, c, :], rhs=B.Vtok[:, c, :], start=True, stop=True,
                                                         skip_group_check=True), R=[B.b_Ktok, B.b_Vtok], W=[bpsf[sbank]])
                        P.op(dve, lambda e, c=c: e.scalar_tensor_tensor(out=S[h][:], in0=S[h][:], scalar=B.ebl[:, c:c + 1],
                                                                       in1=psf[sbank][:, 0:128], op0=ALU.mult, op1=ALU.add),
                             R=[bpsf[sbank], B.b_ebl, bS[h]], W=[bS[h]])
                        yield
                    if dr == 0:
                        P.op(act, lambda e: e.activation(out=B.oft[:], in_=psf[ob][:], func=AF.Copy), R=[bpsf[ob]], W=[B.b_oft])
                        P.dma(OFv[:, h, t0:t0 + NT], B.oft[:], R=[B.b_oft], W=[bOF], via=B.b_oft)
                    else:
                        P.op(act, lambda e: e.activation(out=B.fD[:], in_=psf[ob][:], func=AF.Copy), R=[bpsf[ob], B.b_Kh], W=[B.b_fD])
                        yield
                        P.op(dve, lambda e: e.tensor_tensor(out=B.oft[:], in0=B.fD[:, ::-1], in1=B.oft[:], op=ALU.add),
                             R=[B.b_fD, B.b_oft], W=[B.b_oft])
                        rms_rstd(B.oft[:].unsqueeze(1), 1, NT, sqh, rstdh, B.b_oft, bsqh, brstdh, psf[0], bpsf[0], 1.0 / 128)
                        P.op(dve, lambda e: e.scalar_tensor_tensor(out=B.oft[:], in0=B.oft[:], scalar=spt[:, O_ON + 2 + h:O_ON + 3 + h],
                                                                  in1=rstdh[:], op0=ALU.mult, op1=ALU.mult),
                             R=[B.b_oft, brstdh, bsp], W=[B.b_oft])
                        P.op(act, lambda e: e.activation(out=B.gt[:], in_=B.gt[:], func=AF.Silu), R=[B.b_gt], W=[B.b_gt])
                        yield
                        P.op(dve, lambda e: e.tensor_tensor(out=B.yo[:], in0=B.oft[:], in1=B.gt[:], op=ALU.mult),
                             R=[B.b_oft, B.b_gt], W=[B.b_yo])
                        P.dma(YM1[:, h, t0:t0 + NT], B.yo[:], R=[B.b_yo], W=[bYM], via=B.b_yo)

                for dr in range(2):
                    tiles = list(range(T // NT)) if dr == 0 else list(range(T // NT - 1, -1, -1))
                    for h in range(4):
                        P.op(dve, lambda e, h=h: e.memset(S[h][:], 0.0), W=[bS[h]])
                    for i in tiles:
                        alive = [unit(dr, i, h) for h in range(4)]
                        while alive:
                            for g_ in list(alive):
                                try:
                                    next(g_)
                                except StopIteration:
                                    alive.remove(g_)

            if lvl < 6:
                break
            with contextlib.ExitStack() as ph:
                P.phase_end()
                NT = 512
                Ur = U[3328:3840, :].rearrange("(g c p) t -> p g c t", g=2, c=2)
                HFv = HF.rearrange("(c p) t -> p c t", p=128)
                YM2 = YM[768:1024, :].rearrange("(c p) t -> p c t", p=128)
                wa = sb("rg_wa", [128, 2, 2, 128], F32, ph)
                wx = sb("rg_wx", [128, 2, 2, 128], F32, ph)
                wab = sb("rg_wab", [128, 2, 2, 128], BF16, ph)
                wxb = sb("rg_wxb", [128, 2, 2, 128], BF16, ph)
                bwr = Buf("rg_w")
                P.op(dve, lambda e: e.memset(wa[:], 0.0), W=[bwr])
                P.op(dve, lambda e: e.memset(wx[:], 0.0), W=[bwr])
                for d_ in range(2):
                    for hd in range(4):
                        cc, hh_ = hd // 2, hd % 2
                        P.dma(wa[hh_ * 64:(hh_ + 1) * 64, d_, cc, hh_ * 64:(hh_ + 1) * 64], rwa_d[l, d_, hd], R=[bIN], W=[bwr], via=bwr, nowaw=(d_ + hd > 0))
                        P.dma(wx[hh_ * 64:(hh_ + 1) * 64, d_, cc, hh_ * 64:(hh_ + 1) * 64], rwx_d[l, d_, hd], R=[bIN], W=[bwr], via=bwr, nowaw=True)
                P.op(dve, lambda e: e.tensor_copy(out=wab[:], in_=wa[:]), R=[bwr], W=[bwr])
                P.op(dve, lambda e: e.tensor_copy(out=wxb[:], in_=wx[:]), R=[bwr], W=[bwr])
                ur = [sb(f"rg_u{i}", [128, 2, NT + 3], F32, ph) for i in range(2)]
                bur = [Buf(f"rg_u{i}") for i in range(2)]
                gr = [sb(f"rg_g{i}", [128, 2, NT], F32, ph) for i in range(2)]
                bgr = [Buf(f"rg_g{i}") for i in range(2)]
                hf = [sb(f"rg_hf{i}", [128, 2, NT], F32, ph) for i in range(2)]
                bhf = [Buf(f"rg_hf{i}") for i in range(2)]
                xr = sb("rg_xr", [128, 2, NT], F32, ph)
                xrb = sb("rg_xrb", [128, 2, NT], BF16, ph)
                bxr = Buf("rg_xr")
                rr = sb("rg_r", [128, 2, NT], F32, ph)
                ig = sb("rg_ig", [128, 2, NT], F32, ph)
                aa = sb("rg_a", [128, 2, NT], F32, ph)
                bb = sb("rg_b", [128, 2, NT], F32, ph)
                brr, big, baa, bbb = Buf("rg_r"), Buf("rg_ig"), Buf("rg_a"), Buf("rg_b")
                hst = sb("rg_hst", [128, 2], F32, ph)
                bhst = Buf("rg_hst")
                ga = sb("rg_ga", [128, 2, NT], F32, ph)
                bga = Buf("rg_ga")
                sq = sb("rg_sq", [128, 2, NT], BF16, ph)
                bsq = Buf("rg_sq")
                rstd = sb("rg_rstd", [128, NT], F32, ph)
                brstd = Buf("rg_rstd")
                yo = [sb(f"rg_yo{i}", [128, 2, NT], BF16, ph) for i in range(2)]
                byo = [Buf(f"rg_yo{i}") for i in range(2)]
                it = 0
                for dr in range(2):
                    tiles = list(range(T // NT)) if dr == 0 else list(range(T // NT - 1, -1, -1))
                    P.op(dve, lambda e: e.memset(hst[:], 0.0), W=[bhst])
                    for i in tiles:
                        j = it % 2
                        it += 1
                        t0 = i * NT
                        lo = max(t0 - 2, 0)
                        hi = min(t0 + NT + 1, T)
                        P.dma(ur[j][:, :, lo - (t0 - 2):hi - (t0 - 2)], Ur[:, 0, :, lo:hi], R=[bU], W=[bur[j]], via=bur[j])
                        if dr == 1:
                            P.dma(gr[j][:], Ur[:, 1, :, t0:t0 + NT], R=[bU], W=[bgr[j]], via=bgr[j])
                            P.dma(hf[j][:], HFv[:, :, t0:t0 + NT], R=[bHF], W=[bhf[j]], via=bhf[j])
                        if t0 == 0:
                            P.op(dve, lambda e: e.memset(ur[j][:, :, 0:2], 0.0), W=[bur[j]])
                        elif t0 % SEG == 0:
                            P.op(dve, lambda e: e.tensor_scalar(out=ur[j][:, :, 0:2], in0=ur[j][:, :, 0:2], scalar1=segflag[:, 0:1], scalar2=None,
                                                                 op0=ALU.mult), R=[bur[j], bconst], W=[bur[j]])
                        if t0 + NT == T:
                            P.op(dve, lambda e: e.memset(ur[j][:, :, NT + 2:NT + 3], 0.0), W=[bur[j]])
                        elif (t0 + NT) % SEG == 0:
                            P.op(dve, lambda e: e.tensor_scalar(out=ur[j][:, :, NT + 2:NT + 3], in0=ur[j][:, :, NT + 2:NT + 3], scalar1=segflag[:, 0:1],
                                                                 scalar2=None, op0=ALU.mult), R=[bur[j], bconst], W=[bur[j]])
                        at_seg = (t0 % SEG == 0) if dr == 0 else ((t0 + NT) % SEG == 0)
                        if at_seg:
                            P.op(dve, lambda e: e.tensor_scalar(out=hst[:], in0=hst[:], scalar1=segflag[:, 0:1], scalar2=None, op0=ALU.mult),
                                 R=[bhst, bconst], W=[bhst])
                        for cc in range(2):
                            wq = lambda k, cc=cc: spt[:, O_RCW + cc * 4 + k:O_RCW + cc * 4 + k + 1]
                            P.op(dve, lambda e, cc=cc, wq=wq: e.tensor_scalar(out=xr[:, cc, :], in0=ur[j][:, cc, 0:NT], scalar1=wq(0),
                                                                             scalar2=spt[:, O_RCB + cc:O_RCB + cc + 1], op0=ALU.mult, op1=ALU.add),
                                 R=[bur[j], bsp], W=[bxr])
                            for k in (1, 2, 3):
                                P.op(dve, lambda e, cc=cc, k=k, wq=wq: e.scalar_tensor_tensor(out=xr[:, cc, :], in0=ur[j][:, cc, k:k + NT], scalar=wq(k),
                                                                                             in1=xr[:, cc, :], op0=ALU.mult, op1=ALU.add),
                                     R=[bur[j], bsp, bxr], W=[bxr])
                        P.op(act, lambda e: e.activation(out=xrb[:], in_=xr[:], func=AF.Copy), R=[bxr], W=[bxr])
                        for cc in range(2):
                            P.op(pe, lambda e, cc=cc: e.matmul(psf[cc][:], lhsT=wab[:, dr, cc, :], rhs=xrb[:, cc, :], start=True, stop=True),
                                 R=[bwr, bxr], W=[bpsf[cc]])
                            P.op(act, lambda e, cc=cc: e.activation(out=rr[:, cc, :], in_=psf[cc][:], func=AF.Sigmoid,
                                                                    bias=spt[:, O_RBA + dr * 2 + cc:O_RBA + dr * 2 + cc + 1]),
                                 R=[bpsf[cc], bsp], W=[brr])
                            P.op(pe, lambda e, cc=cc: e.matmul(psf[2 + cc][:], lhsT=wxb[:, dr, cc, :], rhs=xrb[:, cc, :], start=True, stop=True),
                                 R=[bwr, bxr], W=[bpsf[2 + cc]])
                            P.op(act, lambda e, cc=cc: e.activation(out=ig[:, cc, :], in_=psf[2 + cc][:], func=AF.Sigmoid,
                                                                    bias=spt[:, O_RBX + dr * 2 + cc:O_RBX + dr * 2 + cc + 1]),
                                 R=[bpsf[2 + cc], bsp], W=[big])
                        for cc in range(2):
                            P.op(act, lambda e, cc=cc: e.activation(out=aa[:, cc, :], in_=rr[:, cc, :], func=AF.Exp,
                                                                    scale=clam[:, dr * 2 + cc:dr * 2 + cc + 1]), R=[brr, blay], W=[baa])
                            P.op(act, lambda e, cc=cc: e.activation(out=bb[:, cc, :], in_=rr[:, cc, :], func=AF.Exp,
                                                                    scale=clam2[:, dr * 2 + cc:dr * 2 + cc + 1]), R=[brr, blay], W=[bbb])
                        P.op(dve, lambda e: e.tensor_scalar(out=bb[:], in0=bb[:], scalar1=-1.0, scalar2=1.0, op0=ALU.mult, op1=ALU.add), R=[bbb], W=[bbb])
                        P.op(dve, lambda e: e.tensor_scalar(out=bb[:], in0=bb[:], scalar1=0.0, scalar2=None, op0=ALU.max), R=[bbb], W=[bbb])
                        P.op(act, lambda e: e.activation(out=bb[:], in_=bb[:], func=AF.Sqrt), R=[bbb], W=[bbb])
                        P.op(pool, lambda e: e.tensor_tensor(out=ig[:], in0=ig[:], in1=xr[:], op=ALU.mult), R=[big, bxr], W=[big])
                        P.op(dve, lambda e: e.tensor_tensor(out=bb[:], in0=bb[:], in1=ig[:], op=ALU.mult), R=[bbb, big], W=[bbb])
                        for cc in range(2):
                            if dr == 0:
                                P.op(dve, lambda e, cc=cc: e.tensor_tensor_scan(out=hf[j][:, cc, :], data0=aa[:, cc, :], data1=bb[:, cc, :],
                                                                                initial=hst[:, cc:cc + 1], op0=ALU.mult, op1=ALU.add),
                                     R=[baa, bbb, bhst], W=[bhf[j]])
                                P.op(dve, lambda e, cc=cc: e.tensor_copy(out=hst[:, cc:cc + 1], in_=hf[j][:, cc, NT - 1:NT]), R=[bhf[j]], W=[bhst])
                            else:
                                P.op(dve, lambda e, cc=cc: e.tensor_tensor_scan(out=rr[:, cc, ::-1], data0=aa[:, cc, ::-1], data1=bb[:, cc, ::-1],
                                                                                initial=hst[:, cc:cc + 1], op0=ALU.mult, op1=ALU.add),
                                     R=[baa, bbb, bhst, brr], W=[brr])
                                P.op(dve, lambda e, cc=cc: e.tensor_copy(out=hst[:, cc:cc + 1], in_=rr[:, cc, 0:1]), R=[brr], W=[bhst])
                        if dr == 0:
                            P.dma(HFv[:, :, t0:t0 + NT], hf[j][:], R=[bhf[j]], W=[bHF], via=bhf[j])
                        else:
                            P.op(pool, lambda e: e.tensor_tensor(out=hf[j][:], in0=hf[j][:], in1=rr[:], op=ALU.add), R=[bhf[j], brr], W=[bhf[j]])
                            P.op(pool, lambda e: e.tensor_tensor(out=ga[:], in0=gr[j][:], in1=gr[j][:], op=ALU.mult), R=[bgr[j]], W=[bga])
                            P.op(pool, lambda e: e.tensor_scalar(out=ga[:], in0=ga[:], scalar1=0.044715, scalar2=1.0, op0=ALU.mult, op1=ALU.add),
                                 R=[bga], W=[bga])
                            P.op(pool, lambda e: e.tensor_tensor(out=ga[:], in0=ga[:], in1=gr[j][:], op=ALU.mult), R=[bga, bgr[j]], W=[bga])
                            P.op(act, lambda e: e.activation(out=ga[:], in_=ga[:], func=AF.Sigmoid, scale=1.5957691216), R=[bga], W=[bga])
                            P.op(pool, lambda e: e.tensor_tensor(out=ga[:], in0=ga[:], in1=gr[j][:], op=ALU.mult), R=[bga, bgr[j]], W=[bga])
                            P.op(dve, lambda e: e.tensor_tensor(out=hf[j][:], in0=hf[j][:], in1=ga[:], op=ALU.mult), R=[bhf[j], bga], W=[bhf[j]])
                            rms_rstd(hf[j][:], 2, NT, sq, rstd, bhf[j], bsq, brstd, psf[4], bpsf[4], 1.0 / 256)
                            for cc in range(2):
                                P.op(dve, lambda e, cc=cc: e.scalar_tensor_tensor(out=yo[j][:, cc, :], in0=hf[j][:, cc, :],
                                                                                scalar=spt[:, O_ON + 6 + cc:O_ON + 7 + cc], in1=rstd[:],
                                                                                op0=ALU.mult, op1=ALU.mult), R=[bhf[j], brstd, bsp], W=[byo[j]])
                            P.dma(YM2[:, :, t0:t0 + NT], yo[j][:], R=[byo[j]], W=[bYM], via=byo[j])

            if lvl < 7:
                break
            with contextlib.ExitStack() as ph:
                P.phase_end()
                NT = 256
                NFC = DFF // 128
                wo = sb("p3wo", [128, 8, D], BF16, ph)
                wg = sb("p3wg", [128, 8, DFF], BF16, ph)
                wu = sb("p3wu", [128, 8, DFF], BF16, ph)
                wd = sb("p3wd", [128, NFC, D], BF16, ph)
                bwo, bwg, bwu, bwd = (Buf(n, multi=True) for n in ("p3wo", "p3wg", "p3wu", "p3wd"))
                with contextlib.ExitStack() as wl:
                    P.phase_end()
                    load_cast(wl, wo, w_out_d[l].rearrange("(kc p) n -> p kc n", p=128), 8, D, 256, bwo, "wo")
                    load_cast(wl, wg, wg_d[l].rearrange("(kc p) n -> p kc n", p=128), 8, DFF, 256, bwg, "wg")
                    load_cast(wl, wu, wu_d[l].rearrange("(kc p) n -> p kc n", p=128), 8, DFF, 256, bwu, "wu")
                with contextlib.ExitStack() as wl:
                    P.phase_end()
                    load_cast(wl, wd, wd_d[l].rearrange("(fc p) n -> p fc n", p=128), NFC, D, 64, bwd, "wd")
                P.phase_end()
                xt = [sb(f"p3x{i}", [128, 8, NT], F32, ph) for i in range(2)]
                bxt = [Buf(f"p3x{i}") for i in range(2)]
                ym = [sb(f"p3ym{i}", [128, 8, NT], BF16, ph) for i in range(2)]
                bym = [Buf(f"p3ym{i}") for i in range(2)]
                hb = sb("p3h", [128, 8, NT], BF16, ph)
                bhb = Buf("p3h")
                tmpn = [sb(f"p3tmp{i}", [128, NT], F32, ph) for i in range(2)]
                btmpn = [Buf(f"p3tmp{i}") for i in range(2)]
                sq = sb("p3sq", [128, 8, NT], BF16, ph)
                bsq = Buf("p3sq")
                rstd = sb("p3rstd", [128, NT], F32, ph)
                brstd = Buf("p3rstd")
                av = sb("p3a", [128, NFC, NT], BF16, ph)
                bav = Buf("p3a", multi=True)
                sg = [sb(f"p3sg{i}", [128, NT], F32, ph) for i in range(2)]
                bsg = [Buf(f"p3sg{i}") for i in range(2)]
                XTv = XT.rearrange("(kc p) t -> p kc t", p=128)
                YMv = YM.rearrange("(kc p) t -> p kc t", p=128)
                for i in range(T // NT):
                    j = i % 2
                    t0 = i * NT
                    seg = t0 // SEG
                    P.dma(xt[j][:], XTv[:, :, t0:t0 + NT], R=[bXT], W=[bxt[j]], via=bxt[j])
                    P.dma(ym[j][:], YMv[:, :, t0:t0 + NT], R=[bYM], W=[bym[j]], via=bym[j])
                    for mc in range(8):
                        pb = mc % 2

                        def mm(e, mc=mc, pb=pb):
                            for kc in range(8):
                                i_ = e.matmul(psf[pb][:, 0:NT], lhsT=wo[:, kc, mc * 128:(mc + 1) * 128], rhs=ym[j][:, kc, :], start=(kc == 0), stop=(kc == 7))
                            return i_
                        P.op(pe, mm, R=[bwo, bym[j]], W=[bpsf[pb]])
                        P.op(dve, lambda e, mc=mc, pb=pb: e.scalar_tensor_tensor(out=xt[j][:, mc, :], in0=psf[pb][:, 0:NT], scalar=modG[0][:, mc, seg:seg + 1],
                                                                                in1=xt[j][:, mc, :], op0=ALU.mult, op1=ALU.add),
                             R=[bpsf[pb], bmod, bxt[j]], W=[bxt[j]])
                    rms_rstd(xt[j][:], 8, NT, sq, rstd, bxt[j], bsq, brstd, psf[2], bpsf[2], 1.0 / D)
                    for kc in range(8):
                        tb = kc % 2
                        P.op(pool, lambda e, kc=kc, tb=tb: e.tensor_tensor(out=tmpn[tb][:], in0=xt[j][:, kc, :], in1=rstd[:], op=ALU.mult),
                             R=[bxt[j], brstd], W=[btmpn[tb]])
                        P.op(act, lambda e, kc=kc, tb=tb: e.activation(out=hb[:, kc, :], in_=tmpn[tb][:], func=AF.Identity,
                                                                       scale=modA[1][:, kc, seg:seg + 1], bias=modB[1][:, kc, seg:seg + 1]),
                             R=[btmpn[tb], bmod], W=[bhb])
                    for fc in range(NFC):
                        pb = 3 + fc % 3

                        def mm(e, fc=fc, pb=pb):
                            for kc in range(8):
                                e.matmul(psf[pb][:, 0:NT], lhsT=wg[:, kc, fc * 128:(fc + 1) * 128], rhs=hb[:, kc, :], start=(kc == 0), stop=(kc == 7),
                                         skip_group_check=True)
                            for kc in range(8):
                                i_ = e.matmul(psf[pb][:, NT:2 * NT], lhsT=wu[:, kc, fc * 128:(fc + 1) * 128], rhs=hb[:, kc, :], start=(kc == 0), stop=(kc == 7),
                                              skip_group_check=True)
                            return i_
                        P.op(pe, mm, R=[bwg, bwu, bhb], W=[bpsf[pb]])
                        s = fc % 2
                        P.op(act, lambda e, pb=pb, s=s: e.activation(out=sg[s][:], in_=psf[pb][:, 0:NT], func=AF.Silu), R=[bpsf[pb]], W=[bsg[s]])
                        P.op(dve, lambda e, pb=pb, s=s, fc=fc: e.tensor_tensor(out=av[:, fc, :], in0=sg[s][:], in1=psf[pb][:, NT:2 * NT], op=ALU.mult),
                             R=[bsg[s], bpsf[pb]], W=[bav])
                    for mc in range(8):
                        pb = mc % 2

                        def mm(e, mc=mc, pb=pb):
                            for fc in range(NFC):
                                i_ = e.matmul(psf[pb][:, 0:NT], lhsT=wd[:, fc, mc * 128:(mc + 1) * 128], rhs=av[:, fc, :], start=(fc == 0), stop=(fc == NFC - 1))
                            return i_
                        P.op(pe, mm, R=[bwd, bav], W=[bpsf[pb]])
                        P.op(dve, lambda e, mc=mc, pb=pb: e.scalar_tensor_tensor(out=xt[j][:, mc, :], in0=psf[pb][:, 0:NT], scalar=modG[1][:, mc, seg:seg + 1],
                                                                                in1=xt[j][:, mc, :], op0=ALU.mult, op1=ALU.add),
                             R=[bpsf[pb], bmod, bxt[j]], W=[bxt[j]])
                    P.dma(XTv[:, :, t0:t0 + NT], xt[j][:], R=[bxt[j]], W=[bXT], via=bxt[j])

        with contextlib.ExitStack() as ph:
            P.phase_end()
            NT = 512
            xt = [sb(f"p4x{i}", [128, 8, NT], F32, ph) for i in range(2)]
            bxt = [Buf(f"p4x{i}") for i in range(2)]
            sq = sb("p4sq", [128, 8, NT], BF16, ph)
            bsq = Buf("p4sq")
            rstd = sb("p4rstd", [128, NT], F32, ph)
            brstd = Buf("p4rstd")
            yo = [sb(f"p4y{i}", [128, D], F32, ph) for i in range(2)]
            byo = [Buf(f"p4y{i}") for i in range(2)]
            XTv = XT.rearrange("(kc p) t -> p kc t", p=128)
            oc = 0
            for i in range(T // NT if lvl >= 2 else 0):
                j = i % 2
                t0 = i * NT
                P.dma(xt[j][:], XTv[:, :, t0:t0 + NT], R=[bXT], W=[bxt[j]], via=bxt[j])
                rms_rstd(xt[j][:], 8, NT, sq, rstd, bxt[j], bsq, brstd, psf[0], bpsf[0], 1.0 / D)
                P.op(dve, lambda e: e.tensor_tensor(out=xt[j][:], in0=xt[j][:], in1=rstd[:].unsqueeze(1).broadcast_to([128, 8, NT]), op=ALU.mult),
                     R=[bxt[j], brstd], W=[bxt[j]])
                P.op(dve, lambda e: e.tensor_tensor(out=xt[j][:], in0=xt[j][:],
                                                    in1=spt[:, O_FN:O_FN + 8].unsqueeze(2).broadcast_to([128, 8, NT]), op=ALU.mult),
                     R=[bxt[j], bsp], W=[bxt[j]])
                for bq in range(NT // 128):
                    o = oc % 2
                    oc += 1
                    for half in range(2):
                        pb = 1 + (2 * oc + half) % 4

                        def tr(e, half=half, pb=pb, bq=bq):
                            for q in range(4):
                                kc = half * 4 + q
                                i_ = e.transpose(out=psf[pb][:, q * 128:(q + 1) * 128], in_=xt[j][:, kc, bq * 128:(bq + 1) * 128], identity=ident_f[:])
                            return i_
                        P.op(pe, tr, R=[bxt[j], bconst], W=[bpsf[pb]])
                        if half == 0:
                            P.op(act, lambda e, pb=pb, o=o: e.activation(out=yo[o][:, 0:512], in_=psf[pb][:], func=AF.Copy), R=[bpsf[pb]], W=[byo[o]])
                        else:
                            P.op(dve, lambda e, pb=pb, o=o: e.tensor_copy(out=yo[o][:, 512:1024], in_=psf[pb][:]), R=[bpsf[pb]], W=[byo[o]])
                    P.dma(y_out[t0 + bq * 128:t0 + (bq + 1) * 128, :], yo[o][:], R=[byo[o]], W=[bY], via=byo[o])
        P.finish()
    return nc


def _col(v):
    v = np.asarray(v, np.float32)
    return np.ascontiguousarray(v.reshape(-1, 128).T)


def pack_small(inp):
    out = np.zeros((DEPTH, 128, NSP), np.float32)
    deltas = np.abs(np.linspace(math.log(1e-2) / 1.5, math.log(1e-2) / 0.3, 256, dtype=np.float32)).astype(np.float32)
    for l in range(DEPTH):
        o = out[l]
        o[:, O_N1:O_N1 + 8] = _col(inp["norm1_w"][l])
        o[:, O_N2:O_N2 + 8] = _col(inp["norm2_w"][l])
        o[:, O_ON:O_ON + 8] = _col(inp["out_norm_w"][l])
        o[:, O_ADAB:O_ADAB + 48] = _col(inp["ada_b"][l])
        cw = np.asarray(inp["hy_conv_w"][l], np.float32)
        o[:, O_HCW:O_HCW + 18] = cw.T.reshape(6, 128, 3).transpose(1, 0, 2).reshape(128, 18)
        o[:, O_HCB:O_HCB + 6] = _col(inp["hy_conv_b"][l])
        o[:, O_HBIAS:O_HBIAS + 2] = _col(inp["hy_bias"][l])
        rw = np.asarray(inp["rg_conv_w"][l], np.float32)
        o[:, O_RCW:O_RCW + 8] = rw.T.reshape(2, 128, 4).transpose(1, 0, 2).reshape(128, 8)
        o[:, O_RCB:O_RCB + 2] = _col(inp["rg_conv_b"][l])
        for nm, off in (("rg_ba", O_RBA), ("rg_bx", O_RBX), ("rg_lambda", O_RLAM)):
            a = np.asarray(inp[nm][l], np.float32)
            o[:, off:off + 4] = a.reshape(2, 2, 128).transpose(2, 0, 1).reshape(128, 4)
        lb = np.asarray(inp["hg_lb_logits"], np.float32)
        o[:, O_LB:O_LB + 16] = lb.reshape(2, DEPTH, 4, 128).transpose(3, 0, 1, 2).reshape(128, 16)
        o[0:64, O_HYB + 0] = inp["hy_b1"][l]
        o[0:64, O_HYB + 1] = inp["hy_b2"][l]
        o[0:64, O_HYB + 2] = inp["hy_b3"][l]
        o[0:64, O_HYB + 3] = inp["hy_freq"][l]
        o[:, O_FN:O_FN + 8] = _col(inp["final_norm_w"])
        o[:, O_ND:O_ND + 2] = -_col(deltas)
    return out


def pos_tables(L, T):
    f32 = np.float32
    t = np.linspace(0.0, 1.0, L, dtype=f32)
    n = np.arange(L, dtype=f32)
    fb = np.linspace(1e-4, 15.0, 16, dtype=f32)
    ang = (fb[None, :] * n[:, None]) * f32(2.0 * math.pi / L)
    z = np.concatenate([t[:, None], np.cos(ang), -np.sin(ang)], axis=-1).astype(f32)
    zfeat = np.zeros((HY_EMB, T), f32)
    zfeat[:, :L] = z.T
    trow = np.full((1, T), 1e4, f32)
    trow[0, :L] = t
    return zfeat, trow


_CACHE = {}


def _get_prog(T):
    if T not in _CACHE:
        _CACHE[T] = build_program(T)
    return _CACHE[T]


def run_cores(core_specs, inp, T):
    nc = _get_prog(T)
    sp = pack_small(inp)
    shared = {k: np.ascontiguousarray(np.asarray(inp[k], np.float32)) for k in
              ("w_in", "w_out", "ada_w", "ffn_wg", "ffn_wu", "ffn_wd", "hy_w1", "hy_w2", "hy_w3", "hy_w4", "rg_wa", "rg_wx")}
    in_maps = []
    tabs = {}
    for (x, c_rows, L) in core_specs:
        if L not in tabs:
            tabs[L] = pos_tables(L, T)
        zfeat, trow = tabs[L]
        m = dict(shared)
        m["x"] = np.ascontiguousarray(x, dtype=np.float32)
        m["cT"] = np.ascontiguousarray(np.asarray(c_rows, np.float32).T)
        m["segflag"] = np.full((128, 1), 1.0 if L == T else 0.0, np.float32)
        m["zfeat"] = zfeat
        m["trow"] = trow
        m["sp"] = sp
        in_maps.append(m)
    res = run_bass_kernel_spmd(nc, in_maps, core_ids=list(range(len(in_maps))))
    return [r["y"] for r in res.results]


def kernel(**inp):
    T = 16384
    xp = np.asarray(inp["x_prompt"], np.float32)
    xs = np.asarray(inp["x_sample"], np.float32)
    cp = np.asarray(inp["c_prompt"], np.float32)
    cs = np.asarray(inp["c_sample"], np.float32)
    specs = []
    for s in range(2):
        specs.append((xs[s], np.repeat(cs[s:s + 1], NSEG, axis=0), T))
    for g in range(2):
        specs.append((xp[4 * g:4 * g + 4].reshape(T, D), cp[4 * g:4 * g + 4], T // NSEG))
    for _ in range(4):
        specs.append((np.zeros((T, D), np.float32), np.zeros((NSEG, D), np.float32), T // NSEG))
    ys = run_cores(specs, inp, T)
    y_sample = np.stack([ys[0], ys[1]], axis=0)
    y_prompt = np.concatenate([ys[2].reshape(4, T // NSEG, D), ys[3].reshape(4, T // NSEG, D)], axis=0)
    return (y_prompt, y_sample)
```

```python
import contextlib
import math
import numpy as np
import ml_dtypes
import concourse.bass as bass
import concourse.mybir as mybir
from concourse.bass_utils import run_bass_kernel_spmd

F32 = mybir.dt.float32
BF16 = mybir.dt.bfloat16
AF = mybir.ActivationFunctionType
ALU = mybir.AluOpType
PI = math.pi

D = 1024
DIN = 3840
DFF = 2816
DEPTH = 2
NSEG = 4
EPS = 1e-6
HY_EMB = 33
O_N1, O_N2, O_ON, O_ADAB, O_HCW, O_HCB, O_HBIAS, O_RCW, O_RCB, O_RBA, O_RBX, O_RLAM, O_LB, O_HYB, O_FN, O_ND = (
    0, 8, 16, 24, 72, 90, 96, 98, 106, 108, 112, 116, 120, 136, 140, 148)
NSP = 150


class Buf:
    def __init__(self, name, multi=False, excl=False):
        self.name = name
        self.multi = multi
        self.excl = excl
        self.w = {}
        self.r = {}
        self.slot = None


class Eng:
    def __init__(self, prog, e, name):
        self.e = e
        self.name = name
        self.sem = prog.sem("e_" + name)
        self.cnt = 0
        self.waited = {}


class Prog:
    def __init__(self, nc, es):
        self.nc = nc
        self.es = es
        self.nsem = 0
        self.pe = Eng(self, nc.tensor, "pe")
        self.act = Eng(self, nc.scalar, "act")
        self.dve = Eng(self, nc.vector, "dve")
        self.pool = Eng(self, nc.gpsimd, "pool")
        self.sp = Eng(self, nc.sync, "sp")
        self.slots = []
        self.slot_pool = []
        self.phase_bufs = []

    def sem(self, name):
        h = self.es.enter_context(self.nc.semaphore(f"{name}_{self.nsem}"))
        self.nsem += 1
        return h

    def _wait(self, eng, R, W, nowaw=False):
        deps = {}
        war = {}
        for b in R:
            for k, v in b.w.items():
                if deps.get(k, 0) < v:
                    deps[k] = v
        for b in W:
            if not (b.multi or nowaw):
                for k, v in b.w.items():
                    if deps.get(k, 0) < v:
                        deps[k] = v
            for k, v in b.r.items():
                if war.get(k, 0) < v:
                    war[k] = v
        for sem, val in war.items():
            if sem is eng.sem and not _STRICT_WAR:
                continue
            if deps.get(sem, 0) < val:
                deps[sem] = val
        for sem, val in deps.items():
            if sem is eng.sem and eng is self.pe:
                continue
            if eng.waited.get(sem, 0) >= val:
                continue
            eng.e.wait_ge(sem, val)
            eng.waited[sem] = val

    def _mark(self, R, W, sem, val, nowaw=False):
        for b in W:
            if b.multi or nowaw:
                b.w[sem] = val
            else:
                b.w = {sem: val}
                b.r = {}
        for b in R:
            if b.r.get(sem, 0) < val:
                b.r[sem] = val

    def op(self, eng, fn, R=(), W=()):
        ex = [b for b in R if b.excl]
        if ex:
            R = [b for b in R if not b.excl]
            W = list(W) + ex
        self._wait(eng, R, W)
        inst = fn(eng.e)
        eng.cnt += 1
        inst.then_inc(eng.sem, 1)
        self._mark(R, W, eng.sem, eng.cnt)

    def dma(self, out, in_, R=(), W=(), via=None, nowaw=False, **kw):
        self._wait(self.sp, R, W, nowaw)
        if via.slot is None:
            if self.slot_pool:
                via.slot = self.slot_pool.pop(0)
            else:
                via.slot = [self.sem("dma"), 0]
                self.slots.append(via.slot)
            self.phase_bufs.append(via)
        sl = via.slot
        inst = self.nc.sync.dma_start(out=out, in_=in_, **kw)
        sl[1] += 16
        inst.then_inc(sl[0], 16)
        self._mark(R, W, sl[0], sl[1], nowaw)
        if _SYNC_DMA:
            self.nc.sync.wait_ge(sl[0], sl[1])
            self.sp.waited[sl[0]] = sl[1]

    def barrier(self):
        engs = [self.pe, self.act, self.dve, self.pool, self.sp]
        for E in engs:
            for F in engs:
                if F is E or F.cnt == 0:
                    continue
                if E.waited.get(F.sem, 0) < F.cnt:
                    E.e.wait_ge(F.sem, F.cnt)
                    E.waited[F.sem] = F.cnt
            for sl in self.slots:
                if sl[1] > 0 and E.waited.get(sl[0], 0) < sl[1]:
                    E.e.wait_ge(sl[0], sl[1])
                    E.waited[sl[0]] = sl[1]

    def phase_end(self):
        self.barrier()
        for b in self.phase_bufs:
            self.slot_pool.append(b.slot)
            b.slot = None
        self.phase_bufs = []

    def finish(self):
        for sl in self.slots:
            if self.sp.waited.get(sl[0], 0) < sl[1]:
                self.nc.sync.wait_ge(sl[0], sl[1])


_DBG_STOP = None
_STRICT_WAR = True
_DBG_NODUMP = False
_SYNC_DMA = False
_DBG_SUB = 99
_DBG_X = 99
_STAGES = ["c0", "c1", "p0", "p1", "hy", "hg", "rg", "p3"]


def build_program(T):
    SEG = T // NSEG
    stop = _DBG_STOP
    lvl = _STAGES.index(stop) if stop else 99
    NB = T // 128
    nc = bass.Bass("TRN2", target_bir_lowering=False)
    dt_in = lambda name, shape, dt=F32: nc.dram_tensor(name, shape, dt, kind="ExternalInput").ap()
    x_in = dt_in("x", [T, D])
    cT_in = dt_in("cT", [D, NSEG])
    segflag_in = dt_in("segflag", [128, 1])
    zfeat_in = dt_in("zfeat", [HY_EMB, T])
    trow_in = dt_in("trow", [1, T])
    w_in_d = dt_in("w_in", [DEPTH, D, DIN])
    w_out_d = dt_in("w_out", [DEPTH, D, D])
    ada_w_d = dt_in("ada_w", [DEPTH, D, 6 * D])
    wg_d = dt_in("ffn_wg", [DEPTH, D, DFF])
    wu_d = dt_in("ffn_wu", [DEPTH, D, DFF])
    wd_d = dt_in("ffn_wd", [DEPTH, DFF, D])
    sp_d = dt_in("sp", [DEPTH, 128, NSP])
    hw1_d = dt_in("hy_w1", [DEPTH, HY_EMB, 64])
    hw2_d = dt_in("hy_w2", [DEPTH, 64, 64])
    hw3_d = dt_in("hy_w3", [DEPTH, 64, 64])
    hw4_d = dt_in("hy_w4", [DEPTH, 64, 512])
    rwa_d = dt_in("rg_wa", [DEPTH, 2, 4, 64, 64])
    rwx_d = dt_in("rg_wx", [DEPTH, 2, 4, 64, 64])
    y_out = nc.dram_tensor("y", [T, D], F32, kind="ExternalOutput").ap()
    scr = lambda name, shape, dt=F32: nc.dram_tensor(name, shape, dt, kind=("ExternalOutput" if (stop and not _DBG_NODUMP) else "Internal")).ap()
    XT = scr("XT", [D, T])
    U = scr("U", [DIN, T])
    YM = scr("YM", [D, T], BF16)
    HF = scr("HF", [256, T])
    OF = scr("OF", [512, T])
    KF = scr("KF", [256, 2 * T], BF16)
    DBGY = scr("DBGY", [128, NB * 128]) if (stop and not _DBG_NODUMP) else None
    DBGV = scr("DBGV", [128, NB * 256], BF16) if (stop and not _DBG_NODUMP) else None
    bDBG = Buf("DBG", multi=True)
    bXT, bU, bYM, bHF, bOF, bKF, bY = (Buf(n, multi=True) for n in ("XT", "U", "YM", "HF", "OF", "KF", "Y"))
    bIN = Buf("inputs")

    with contextlib.ExitStack() as es:
        P = Prog(nc, es)
        pe, act, dve, pool = P.pe, P.act, P.dve, P.pool

        uniq = [0]

        def sb(name, shape, dt=F32, stack=es):
            uniq[0] += 1
            return stack.enter_context(nc.sbuf_tensor(f"{name}_u{uniq[0]}", shape, dt))

        dbg_n = [0]

        def dbg(name, ap, buf, shape, dt=F32):
            if not stop or _DBG_NODUMP:
                return
            dbg_n[0] += 1
            dd = nc.dram_tensor(f"DBG_{name}", shape, dt, kind="ExternalOutput").ap()
            P.dma(dd, ap, R=[buf], W=[bDBG], via=buf)

        psf = [es.enter_context(nc.psum_tensor(f"psf{i}", [128, 512], F32)) for i in range(7)]
        psb = [es.enter_context(nc.psum_tensor(f"psb{i}", [128, 1024], BF16)) for i in range(1)]
        bpsf = [Buf(f"psf{i}", excl=True) for i in range(7)]
        bpsb = [Buf(f"psb{i}", excl=True) for i in range(1)]

        ones_bf = sb("ones_bf", [128, 128], BF16)
        ident_bf = sb("ident_bf", [128, 128], BF16)
        ident_f = sb("ident_f", [128, 128], F32)
        mask_lo = sb("mask_lo", [64, 8, 64], F32)
        mask_hi = sb("mask_hi", [64, 8, 64], F32)
        cm_f = sb("cm_f", [128, 512], F32)
        cm_b = sb("cm_b", [128, 512], F32)
        epsb = sb("epsb", [128, 1], F32)
        segflag = sb("segflag_t", [128, 1], F32)
        zero_c = sb("zero_c", [128, 1], F32)
        dummy = sb("dummy", [128, 8], F32)
        bconst = Buf("const")
        P.op(dve, lambda e: e.memset(ones_bf[:], 1.0), W=[bconst])
        P.op(dve, lambda e: e.memset(ident_bf[:], 0.0), W=[bconst])
        P.op(dve, lambda e: e.memset(ident_f[:], 0.0), W=[bconst])
        P.op(dve, lambda e: e.memset(mask_lo[:], 1.0), W=[bconst])
        P.op(dve, lambda e: e.memset(mask_hi[:], 1.0), W=[bconst])
        P.op(dve, lambda e: e.memset(cm_f[:], 1.0), W=[bconst])
        P.op(dve, lambda e: e.memset(cm_b[:], 1.0), W=[bconst])
        P.op(dve, lambda e: e.memset(epsb[:], EPS), W=[bconst])
        P.op(dve, lambda e: e.memset(zero_c[:], 0.0), W=[bconst])
        P.op(pool, lambda e: e.affine_select(out=ident_bf[:], in_=ident_bf[:], pattern=[[-1, 128]],
                                             compare_op=ALU.not_equal, fill=1.0, base=0, channel_multiplier=1),
             R=[bconst], W=[bconst])
        P.op(pool, lambda e: e.affine_select(out=ident_f[:], in_=ident_f[:], pattern=[[-1, 128]],
                                             compare_op=ALU.not_equal, fill=1.0, base=0, channel_multiplier=1),
             R=[bconst], W=[bconst])
        P.op(pool, lambda e: e.affine_select(out=mask_lo[:], in_=mask_lo[:], pattern=[[0, 8], [1, 64]],
                                             compare_op=ALU.is_ge, fill=0.0, base=0, channel_multiplier=-1),
             R=[bconst], W=[bconst])
        P.op(pool, lambda e: e.affine_select(out=mask_hi[:], in_=mask_hi[:], pattern=[[0, 8], [-1, 64]],
                                             compare_op=ALU.is_ge, fill=0.0, base=0, channel_multiplier=1),
             R=[bconst], W=[bconst])
        cmf3 = cm_f[:].rearrange("p (c t) -> p c t", t=64)
        cmb3 = cm_b[:].rearrange("p (c t) -> p c t", t=64)
        P.op(dve, lambda e: e.memset(cmf3[:, :, 0:1], 0.0), R=[bconst], W=[bconst])
        P.op(dve, lambda e: e.memset(cmb3[:, :, 63:64], 0.0), R=[bconst], W=[bconst])
        P.dma(segflag[:], segflag_in[:, :], R=[bIN], W=[bconst], via=bconst)

        spt = sb("spt", [128, NSP], F32)
        bsp = Buf("spt")
        modA = [sb(f"modA{i}", [128, 8, NSEG], F32) for i in range(2)]
        modB = [sb(f"modB{i}", [128, 8, NSEG], F32) for i in range(2)]
        modG = [sb(f"modG{i}", [128, 8, NSEG], F32) for i in range(2)]
        bmod = Buf("mod")
        lbt = sb("lbt", [128, 2, 4], F32)
        omlt = sb("omlt", [128, 2, 4], F32)
        clam = sb("clam", [128, 4], F32)
        clam2 = sb("clam2", [128, 4], F32)
        blay = Buf("layer_small")

        def load_cast(stack, dst, src3, K, N, piece, bdst, tag):
            stg = [sb(f"wst_{tag}{i}", [128, K, piece], F32, stack) for i in range(2)]
            bst = [Buf(f"wst_{tag}{i}") for i in range(2)]
            n0 = 0
            i = 0
            while n0 < N:
                w = min(piece, N - n0)
                P.dma(stg[i % 2][:, :, 0:w], src3[:, :, n0:n0 + w], R=[bIN], W=[bst[i % 2]], via=bst[i % 2])
                if i % 2 == 0:
                    P.op(dve, lambda e, i=i, n0=n0, w=w: e.tensor_copy(out=dst[:, :, n0:n0 + w], in_=stg[i % 2][:, :, 0:w]),
                         R=[bst[i % 2]], W=[bdst])
                else:
                    P.op(act, lambda e, i=i, n0=n0, w=w: e.activation(out=dst[:, :, n0:n0 + w], in_=stg[i % 2][:, :, 0:w], func=AF.Copy),
                         R=[bst[i % 2]], W=[bdst])
                n0 += w
                i += 1

        def rms_rstd(xt, K, NT, sq, rstd, bx, bsq, brstd, ps, bps, inv_n):
            P.op(act, lambda e: e.activation(out=sq[:, 0:K, 0:NT], in_=xt, func=AF.Square), R=[bx], W=[bsq])

            def mm(e):
                for kc in range(K):
                    i = e.matmul(ps[:, 0:NT], lhsT=ones_bf[:], rhs=sq[:, kc, 0:NT], start=(kc == 0), stop=(kc == K - 1))
                return i
            P.op(pe, mm, R=[bsq, bconst], W=[bps])
            P.op(act, lambda e: e.activation(out=rstd[:, 0:NT], in_=ps[:, 0:NT], func=AF.Sqrt, scale=inv_n, bias=epsb[:, 0:1]),
                 R=[bps, bconst], W=[brstd])
            P.op(dve, lambda e: e.reciprocal(out=rstd[:, 0:NT], in_=rstd[:, 0:NT]), R=[brstd], W=[brstd])

        with contextlib.ExitStack() as ph:
            P.phase_end()
            xin = [sb(f"p0x{i}", [128, D], F32, ph) for i in range(2)]
            xo = [sb(f"p0o{i}", [128, 8, 128], F32, ph) for i in range(2)]
            bxin = [Buf(f"p0x{i}") for i in range(2)]
            bxo = [Buf(f"p0o{i}") for i in range(2)]
            XTv = XT.rearrange("(kc p) t -> p kc t", p=128)
            for b in range(NB if lvl >= 1 else 0):
                j = b % 2
                P.dma(xin[j][:], x_in[b * 128:(b + 1) * 128, :], R=[bIN], W=[bxin[j]], via=bxin[j])
                for half in range(2):
                    pb = (2 * b + half) % 4

                    def tr(e, half=half, pb=pb):
                        for q in range(4):
                            kc = half * 4 + q
                            i = e.transpose(out=psf[pb][:, q * 128:(q + 1) * 128], in_=xin[j][:, kc * 128:(kc + 1) * 128],
                                            identity=ident_f[:])
                        return i
                    P.op(pe, tr, R=[bxin[j], bconst], W=[bpsf[pb]])
                    eng = act if half == 0 else dve
                    if half == 0:
                        P.op(act, lambda e, pb=pb: e.activation(out=xo[j][:, 0:4, :], in_=psf[pb][:].rearrange("p (q t) -> p q t", t=128),
                                                                func=AF.Copy), R=[bpsf[pb]], W=[bxo[j]])
                    else:
                        P.op(dve, lambda e, pb=pb: e.tensor_copy(out=xo[j][:, 4:8, :], in_=psf[pb][:].rearrange("p (q t) -> p q t", t=128)),
                             R=[bpsf[pb]], W=[bxo[j]])
                P.dma(XTv[:, :, b * 128:(b + 1) * 128], xo[j][:], R=[bxo[j]], W=[bXT], via=bxo[j])

        for l in range(DEPTH):
            if lvl < 3 or (l > 0 and lvl < 99):
                break
            P.dma(spt[:], sp_d[l], R=[bIN], W=[bsp], via=bsp)
            with contextlib.ExitStack() as ph:
                P.phase_end()
                ct = sb("ada_c", [128, 8, NSEG], F32, ph)
                bct = Buf("ada_c")
                modT = sb("ada_mod", [128, 48, NSEG], F32, ph)
                bmodT = Buf("ada_mod")
                P.dma(ct[:], cT_in.rearrange("(kc p) s -> p kc s", p=128), R=[bIN], W=[bct], via=bct)
                P.op(act, lambda e: e.activation(out=ct[:], in_=ct[:], func=AF.Silu), R=[bct], W=[bct])
                aw = [sb(f"ada_w{i}", [128, 8, 512], F32, ph) for i in range(2)]
                baw = [Buf(f"ada_w{i}") for i in range(2)]
                awv = ada_w_d[l].rearrange("(kc p) n -> p kc n", p=128)
                for pc in range(12):
                    j = pc % 2
                    P.dma(aw[j][:], awv[:, :, pc * 512:(pc + 1) * 512], R=[bIN], W=[baw[j]], via=baw[j])
                    pb = pc % 2

                    def mm(e, j=j, pb=pb):
                        for m in range(4):
                            for kc in range(8):
                                i = e.matmul(psf[pb][:, m * NSEG:(m + 1) * NSEG], lhsT=aw[j][:, kc, m * 128:(m + 1) * 128],
                                             rhs=ct[:, kc, :], start=(kc == 0), stop=(kc == 7))
                        return i
                    P.op(pe, mm, R=[baw[j], bct], W=[bpsf[pb]])
                    P.op(dve, lambda e, pc=pc, pb=pb: e.tensor_tensor(
                        out=modT[:, pc * 4:(pc + 1) * 4, :], in0=psf[pb][:, 0:4 * NSEG].rearrange("p (m s) -> p m s", s=NSEG),
                        in1=spt[:, O_ADAB + pc * 4:O_ADAB + (pc + 1) * 4].unsqueeze(2).broadcast_to([128, 4, NSEG]), op=ALU.add),
                        R=[bpsf[pb], bsp], W=[bmodT])
                for i2, (osh, osc, og, onw) in enumerate(((0, 8, 16, O_N1), (24, 32, 40, O_N2))):
                    P.op(dve, lambda e, osc=osc, onw=onw, i2=i2: e.scalar_tensor_tensor(
                        out=modA[i2][:], in0=modT[:, osc:osc + 8, :], scalar=1.0,
                        in1=spt[:, onw:onw + 8].unsqueeze(2).broadcast_to([128, 8, NSEG]), op0=ALU.add, op1=ALU.mult),
                        R=[bmodT, bsp], W=[bmod])
                    P.op(dve, lambda e, osh=osh, i2=i2: e.tensor_copy(out=modB[i2][:], in_=modT[:, osh:osh + 8, :]), R=[bmodT], W=[bmod])
                    P.op(dve, lambda e, og=og, i2=i2: e.tensor_copy(out=modG[i2][:], in_=modT[:, og:og + 8, :]), R=[bmodT], W=[bmod])
                lbv = spt[:, O_LB:O_LB + 16].rearrange("p (d l h) -> p d l h", d=2, l=2)
                if l == 0:
                    P.op(dve, lambda e: e.memset(lbt[:], 0.0), W=[blay])
                else:
                    P.op(dve, lambda e: e.tensor_tensor(out=lbt[:], in0=lbv[:, :, 1, :], in1=lbv[:, :, 0, :], op=ALU.subtract),
                         R=[bsp], W=[blay])
                    P.op(act, lambda e: e.activation(out=lbt[:], in_=lbt[:], func=AF.Sigmoid), R=[blay], W=[blay])
                P.op(dve, lambda e: e.tensor_scalar(out=omlt[:], in0=lbt[:], scalar1=-1.0, scalar2=1.0, op0=ALU.mult, op1=ALU.add),
                     R=[blay], W=[blay])
                dbg(f"omlt_a{l}", omlt[:].rearrange("p a b -> p (a b)"), blay, [128, 8])
                P.op(act, lambda e: e.activation(out=clam[:], in_=spt[:, O_RLAM:O_RLAM + 4], func=AF.Exp, scale=-1.0), R=[bsp], W=[blay])
                P.op(dve, lambda e: e.tensor_scalar(out=clam[:], in0=clam[:], scalar1=1.0, scalar2=None, op0=ALU.add), R=[blay], W=[blay])
                P.op(act, lambda e: e.activation(out=clam[:], in_=clam[:], func=AF.Ln), R=[blay], W=[blay])
                P.op(dve, lambda e: e.tensor_scalar(out=clam2[:], in0=clam[:], scalar1=-16.0, scalar2=None, op0=ALU.mult), R=[blay], W=[blay])
                P.op(dve, lambda e: e.tensor_scalar(out=clam[:], in0=clam[:], scalar1=-8.0, scalar2=None, op0=ALU.mult), R=[blay], W=[blay])

            with contextlib.ExitStack() as ph:
                P.phase_end()
                NT = 512
                wbf = sb("p1w", [128, 8, DIN], BF16, ph)
                bw = Buf("p1w", multi=True)
                xt = [sb(f"p1x{i}", [128, 8, NT], F32, ph) for i in range(2)]
                bxt = [Buf(f"p1x{i}") for i in range(2)]
                hb = [sb(f"p1h{i}", [128, 8, NT], BF16, ph) for i in range(2)]
                bhb = [Buf(f"p1h{i}") for i in range(2)]
                sq = sb("p1sq", [128, 8, NT], BF16, ph)
                bsq = Buf("p1sq")
                rstd = sb("p1rstd", [128, NT], F32, ph)
                brstd = Buf("p1rstd")
                stg = [sb(f"p1st{i}", [128, 3, NT], F32, ph) for i in range(2)]
                bstg = [Buf(f"p1st{i}") for i in range(2)]
                with contextlib.ExitStack() as wl:
                    P.phase_end()
                    load_cast(wl, wbf, w_in_d[l].rearrange("(kc p) n -> p kc n", p=128), 8, DIN, 240, bw, "p1")
                XTv = XT.rearrange("(kc p) t -> p kc t", p=128)
                Uv = U.rearrange("(c p) t -> p c t", p=128)
                for i in range(T // NT):
                    j = i % 2
                    t0 = i * NT
                    seg = t0 // SEG
                    P.dma(xt[j][:], XTv[:, :, t0:t0 + NT], R=[bXT], W=[bxt[j]], via=bxt[j])
                    rms_rstd(xt[j][:], 8, NT, sq, rstd, bxt[j], bsq, brstd, psf[0], bpsf[0], 1.0 / D)
                    P.op(dve, lambda e: e.tensor_tensor(out=xt[j][:], in0=xt[j][:], in1=rstd[:].unsqueeze(1).broadcast_to([128, 8, NT]),
                                                        op=ALU.mult), R=[bxt[j], brstd], W=[bxt[j]])
                    P.op(dve, lambda e: e.tensor_tensor(out=xt[j][:], in0=xt[j][:],
                                                        in1=modA[0][:, :, seg:seg + 1].broadcast_to([128, 8, NT]), op=ALU.mult),
                         R=[bxt[j], bmod], W=[bxt[j]])
                    P.op(dve, lambda e: e.tensor_tensor(out=hb[j][:], in0=xt[j][:],
                                                        in1=modB[0][:, :, seg:seg + 1].broadcast_to([128, 8, NT]), op=ALU.add),
                         R=[bxt[j], bmod], W=[bhb[j]])
                    for mc in range(DIN // 128):
                        pb = 1 + mc % 4

                        def mm(e, mc=mc, pb=pb):
                            for kc in range(8):
                                i_ = e.matmul(psf[pb][:], lhsT=wbf[:, kc, mc * 128:(mc + 1) * 128], rhs=hb[j][:, kc, :],
                                              start=(kc == 0), stop=(kc == 7))
                            return i_
                        P.op(pe, mm, R=[bw, bhb[j]], W=[bpsf[pb]])
                        s = (mc // 3) % 2
                        if mc % 2 == 0:
                            P.op(act, lambda e, pb=pb, s=s, mc=mc: e.activation(out=stg[s][:, mc % 3, :], in_=psf[pb][:], func=AF.Copy),
                                 R=[bpsf[pb]], W=[bstg[s]])
                        else:
                            P.op(dve, lambda e, pb=pb, s=s, mc=mc: e.tensor_copy(out=stg[s][:, mc % 3, :], in_=psf[pb][:]),
                                 R=[bpsf[pb]], W=[bstg[s]])
                        if mc % 3 == 2:
                            P.dma(Uv[:, mc - 2:mc + 1, t0:t0 + NT], stg[s][:], R=[bstg[s]], W=[bU], via=bstg[s])

            dbg(f"omlt_b{l}", omlt[:].rearrange("p a b -> p (a b)"), blay, [128, 8])
            if lvl < 4:
                break
            with contextlib.ExitStack() as ph:
                P.phase_end()
                vT = sb("hy_vT", [128, 256, NB], BF16, ph)
                bvT = Buf("hy_vT", multi=True)
                with contextlib.ExitStack() as fg:
                    P.phase_end()
                    NF = 512
                    w1 = sb("hy_w1", [HY_EMB, 64], F32, fg)
                    w2 = sb("hy_w2", [64, 64], F32, fg)
                    w3 = sb("hy_w3", [64, 64], F32, fg)
                    w4 = sb("hy_w4", [64, 512], F32, fg)
                    bwf = Buf("hy_wf")
                    for tl, src in ((w1, hw1_d), (w2, hw2_d), (w3, hw3_d), (w4, hw4_d)):
                        P.dma(tl[:], src[l], R=[bIN], W=[bwf], via=bwf, nowaw=True)
                    zf = [sb(f"hy_zf{i}", [HY_EMB, NF], F32, fg) for i in range(2)]
                    bzf = [Buf(f"hy_zf{i}") for i in range(2)]
                    tr_ = [sb(f"hy_tr{i}", [128, NF], F32, fg) for i in range(2)]
                    btr = [Buf(f"hy_tr{i}") for i in range(2)]
                    hh = sb("hy_h", [64, NF], F32, fg)
                    bhh = Buf("hy_h")
                    tmpa = sb("hy_tmpa", [64, NF], F32, fg)
                    tmpb = sb("hy_tmpb", [64, NF], F32, fg)
                    btmp = Buf("hy_tmp")
                    dec = sb("hy_dec", [128, 2, NF], F32, fg)
                    bdec = Buf("hy_dec")
                    kt = sb("hy_kt", [128, NF], F32, fg)
                    bkt = Buf("hy_kt")
                    kst = [sb(f"hy_kst{i}", [128, NF], BF16, fg) for i in range(4)]
                    bkst = [Buf(f"hy_kst{i}") for i in range(4)]
                    kcount = 0
                    for n0 in range(0, T if _DBG_SUB >= 1 else 0, NF):
                        j = (n0 // NF) % 2
                        P.dma(zf[j][:], zfeat_in[:, n0:n0 + NF], R=[bIN], W=[bzf[j]], via=bzf[j])
                        P.dma(tr_[j][:], trow_in[0:1, n0:n0 + NF].broadcast_to([128, NF]), R=[bIN], W=[btr[j]], via=btr[j])
                        src_t, bsrc, Kd = zf[j], bzf[j], HY_EMB
                        for li, wl_ in enumerate((w1, w2, w3)):
                            P.op(pe, lambda e, wl_=wl_, src_t=src_t, Kd=Kd: e.matmul(psf[0][0:64, :], lhsT=wl_[0:Kd, :], rhs=src_t[0:Kd, :],
                                                                                   start=True, stop=True),
                                 R=[bwf, bsrc], W=[bpsf[0]])
                            P.op(dve, lambda e, li=li: e.tensor_scalar(out=tmpa[:], in0=psf[0][0:64, :], scalar1=spt[0:64, O_HYB + li:O_HYB + li + 1],
                                                                       scalar2=spt[0:64, O_HYB + 3:O_HYB + 4], op0=ALU.add, op1=ALU.mult),
                                 R=[bpsf[0], bsp], W=[btmp])
                            P.op(dve, lambda e: e.tensor_scalar(out=tmpb[:], in0=tmpa[:], scalar1=PI, scalar2=-2 * PI, op0=ALU.is_gt, op1=ALU.mult),
                                 R=[btmp], W=[btmp])
                            P.op(dve, lambda e: e.tensor_tensor(out=tmpb[:], in0=tmpb[:], in1=tmpa[:], op=ALU.add), R=[btmp], W=[btmp])
                            P.op(dve, lambda e: e.tensor_scalar(out=tmpa[:], in0=tmpa[:], scalar1=-PI, scalar2=2 * PI, op0=ALU.is_lt, op1=ALU.mult),
                                 R=[btmp], W=[btmp])
                            P.op(dve, lambda e: e.tensor_tensor(out=tmpb[:], in0=tmpb[:], in1=tmpa[:], op=ALU.add), R=[btmp], W=[btmp])
                            P.op(act, lambda e: e.activation(out=hh[:], in_=tmpb[:], func=AF.Sin), R=[btmp], W=[bhh])
                            src_t, bsrc, Kd = hh, bhh, 64
                        for cc in range(2):
                            P.op(act, lambda e, cc=cc: e.activation(out=dec[:, cc, :], in_=tr_[j][:], func=AF.Exp,
                                                                    scale=spt[:, O_ND + cc:O_ND + cc + 1]), R=[btr[j], bsp], W=[bdec])
                        for q in range(4):
                            cc = q % 2
                            pb = 1 + q % 2
                            P.op(pe, lambda e, q=q, pb=pb: e.matmul(psf[pb][:], lhsT=w4[:, q * 128:(q + 1) * 128], rhs=hh[:], start=True, stop=True),
                                 R=[bwf, bhh], W=[bpsf[pb]])
                            P.op(dve, lambda e, pb=pb, cc=cc: e.tensor_tensor(out=kt[:], in0=psf[pb][:], in1=dec[:, cc, :], op=ALU.mult),
                                 R=[bpsf[pb], bdec], W=[bkt])
                            ks = kcount % 4
                            kcount += 1
                            if q < 2:
                                if n0 == 0:
                                    P.op(dve, lambda e, cc=cc: e.tensor_tensor(out=kt[:, 0:1], in0=kt[:, 0:1],
                                                                               in1=spt[:, O_HBIAS + cc:O_HBIAS + cc + 1], op=ALU.add),
                                         R=[bkt, bsp], W=[bkt])
                                P.op(act, lambda e, ks=ks: e.activation(out=kst[ks][:], in_=kt[:], func=AF.Copy), R=[bkt], W=[bkst[ks]])
                                P.dma(KF[cc * 128:(cc + 1) * 128, T - 1 + n0:T - 1 + n0 + NF], kst[ks][:], R=[bkst[ks]], W=[bKF], via=bkst[ks])
                            else:
                                P.op(dve, lambda e, ks=ks: e.tensor_copy(out=kst[ks][:, ::-1], in_=kt[:]), R=[bkt], W=[bkst[ks]])
                                if n0 == 0:
                                    P.dma(KF[cc * 128:(cc + 1) * 128, T - NF:T - 1], kst[ks][:, 0:NF - 1], R=[bkst[ks]], W=[bKF], via=bkst[ks])
                                else:
                                    P.dma(KF[cc * 128:(cc + 1) * 128, T - n0 - NF:T - n0], kst[ks][:], R=[bkst[ks]], W=[bKF], via=bkst[ks])
                with contextlib.ExitStack() as p1:
                    P.phase_end()
                    NT = 512
                    ux = [sb(f"hy_ux{i}", [128, 4, NT + 2], F32, p1) for i in range(2)]
                    bux = [Buf(f"hy_ux{i}") for i in range(2)]
                    cv = sb("hy_cv", [128, 4, NT], F32, p1)
                    bcv = Buf("hy_cv")
                    vrev = [sb(f"hy_vrev{i}", [128, 2, NT], BF16, p1) for i in range(2)]
                    bvrev = [Buf(f"hy_vrev{i}") for i in range(2)]
                    Uq = U[256:768, :].rearrange("(q p) t -> p q t", p=128)
                    for i in range(T // NT if _DBG_SUB >= 2 else 0):
                        j = i % 2
                        t0 = i * NT
                        lo = max(t0 - 1, 0)
                        hi = min(t0 + NT + 1, T)
                        P.dma(ux[j][:, :, lo - (t0 - 1):hi - (t0 - 1)], Uq[:, :, lo:hi], R=[bU], W=[bux[j]], via=bux[j])
                        if t0 == 0:
                            P.op(dve, lambda e: e.memset(ux[j][:, :, 0:1], 0.0), W=[bux[j]])
                        elif t0 % SEG == 0:
                            P.op(dve, lambda e: e.tensor_scalar(out=ux[j][:, :, 0:1], in0=ux[j][:, :, 0:1], scalar1=segflag[:, 0:1],
                                                                 scalar2=None, op0=ALU.mult), R=[bux[j], bconst], W=[bux[j]])
                        if t0 + NT == T:
                            P.op(dve, lambda e: e.memset(ux[j][:, :, NT + 1:NT + 2], 0.0), W=[bux[j]])
                        elif (t0 + NT) % SEG == 0:
                            P.op(dve, lambda e: e.tensor_scalar(out=ux[j][:, :, NT + 1:NT + 2], in0=ux[j][:, :, NT + 1:NT + 2],
                                                                 scalar1=segflag[:, 0:1], scalar2=None, op0=ALU.mult),
                                 R=[bux[j], bconst], W=[bux[j]])
                        for q in range(4):
                            ch = 2 + q
                            eng = dve if q % 2 == 0 else pool
                            wq = lambda k, ch=ch: spt[:, O_HCW + ch * 3 + k:O_HCW + ch * 3 + k + 1]
                            P.op(dve, lambda e, q=q, ch=ch, wq=wq: e.tensor_scalar(out=cv[:, q, :], in0=ux[j][:, q, 0:NT], scalar1=wq(0),
                                                                                  scalar2=spt[:, O_HCB + ch:O_HCB + ch + 1], op0=ALU.mult, op1=ALU.add),
                                 R=[bux[j], bsp], W=[bcv])
                            for k in (1, 2):
                                P.op(dve, lambda e, q=q, k=k, wq=wq: e.scalar_tensor_tensor(out=cv[:, q, :], in0=ux[j][:, q, k:k + NT], scalar=wq(k),
                                                                                           in1=cv[:, q, :], op0=ALU.mult, op1=ALU.add),
                                     R=[bux[j], bsp, bcv], W=[bcv])
                        for cc in range(2 if _DBG_X >= 1 else 0):
                            P.op(dve, lambda e, cc=cc: e.tensor_tensor(out=vrev[j][:, cc, ::-1], in0=cv[:, cc, :], in1=cv[:, 2 + cc, :], op=ALU.mult),
                                 R=[bcv], W=[bvrev[j]])
                        P.op(dve, lambda e: e.memset(dummy[:], 0.0), W=[bvrev[j]])
                        for cc in range(2 if _DBG_X >= 2 else 0):
                            pbi = 0

                            def tr(e, cc=cc, pbi=pbi):
                                for bq in range(4):
                                    i_ = e.transpose(out=psb[pbi][:, bq * 128:(bq + 1) * 128], in_=vrev[j][:, cc, bq * 128:(bq + 1) * 128],
                                                     identity=ident_bf[:])
                                return i_
                            P.op(pe, tr, R=[bvrev[j], bconst], W=[bpsb[pbi]])
                            B0 = t0 // 128
                            dst = vT[:, cc * 128:(cc + 1) * 128, B0 + 3::-1] if B0 == 0 else vT[:, cc * 128:(cc + 1) * 128, B0 + 3:B0 - 1:-1]
                            P.op(dve, lambda e, dst=dst, pbi=pbi: e.tensor_copy(out=dst, in_=psb[pbi][:, 0:512].rearrange("p (b c) -> p c b", c=128)),
                                 R=[bpsb[pbi]], W=[bvT])
                with contextlib.ExitStack() as p2:
                    P.phase_end()
                    BS = NB // NSEG
                    yT = sb("hy_yT", [128, NB, 128], F32, p2)
                    byT = Buf("hy_yT", multi=True)
                    NDP = 16
                    NE = 2 * NB - 1
                    npieces = (NE + NDP - 1) // NDP
                    NPB = 6
                    pc_t = [sb(f"hy_pc{i}", [128, NDP * 128], BF16, p2) for i in range(NPB)]
                    bpc = [Buf(f"hy_pc{i}") for i in range(NPB)]
                    pcn_t = [sb(f"hy_pcn{i}", [128, NDP * 128], BF16, p2) for i in range(3)]
                    bpcn = [Buf(f"hy_pcn{i}") for i in range(3)]
                    sfm1 = sb("hy_sfm1", [128, 1], F32, p2)
                    P.op(dve, lambda e: e.tensor_scalar(out=sfm1[:], in0=segflag[:], scalar1=-1.0, scalar2=None, op0=ALU.add), R=[bconst], W=[bconst])
                    NT = 512
                    u0 = [sb(f"hy_u0{i}", [128, NT + 2], F32, p2) for i in range(2)]
                    bu0 = [Buf(f"hy_u0{i}") for i in range(2)]
                    c0 = sb("hy_c0", [128, NT], F32, p2)
                    bc0 = Buf("hy_c0")
                    yfo = [sb(f"hy_yfo{i}", [128, NT], F32, p2) for i in range(2)]
                    byfo = [Buf(f"hy_yfo{i}") for i in range(2)]
                    pcount = 0
                    ncount = 0
                    e_mid = NB - 1
                    p_mid = e_mid // NDP
                    order = [p_mid] + [p for p in range(npieces) if p != p_mid]
                    for cc in range(2 if _DBG_SUB >= 3 else 0):
                        for ci in range(128):
                            c = cc * 128 + ci
                            pb = ci % 4
                            ybank = psf[pb]
                            nmm = 0
                            total = NE + sum(1 for ee in range(NE) if 0 < abs(ee - (NB - 1)) <= BS)
                            for p_ in order:
                                e0 = p_ * NDP
                                e1 = min(e0 + NDP, NE)
                                W_ = (e1 - e0) * 128
                                base = (T - 1) + (e0 - (NB - 1)) * 128 - 127
                                k = pcount % NPB
                                pcount += 1
                                src = bass.AP(tensor=KF.tensor, offset=c * 2 * T + base, ap=[[1, 128], [1, W_]])
                                P.dma(pc_t[k][:, 0:W_], src, R=[bKF], W=[bpc[k]], via=bpc[k])
                                es_ = list(range(e0, e1))
                                if p_ == p_mid:
                                    es_ = [e_mid] + [e for e in es_ if e != e_mid]
                                cross = [ee for ee in es_ if 0 < abs(ee - (NB - 1)) <= BS]
                                kn = None
                                if cross:
                                    kn = ncount % 3
                                    ncount += 1
                                    ca, cb_ = min(cross) - e0, max(cross) - e0 + 1
                                    if ncount % 2 == 0:
                                        P.op(dve, lambda e, k=k, kn=kn, ca=ca, cb_=cb_: e.tensor_scalar(
                                            out=pcn_t[kn][:, ca * 128:cb_ * 128], in0=pc_t[k][:, ca * 128:cb_ * 128], scalar1=sfm1[:, 0:1], scalar2=None,
                                            op0=ALU.mult), R=[bpc[k], bconst], W=[bpcn[kn]])
                                    else:
                                        P.op(act, lambda e, k=k, kn=kn, ca=ca, cb_=cb_: e.activation(
                                            out=pcn_t[kn][:, ca * 128:cb_ * 128], in_=pc_t[k][:, ca * 128:cb_ * 128], func=AF.Copy, scale=sfm1[:, 0:1]),
                                            R=[bpc[k], bconst], W=[bpcn[kn]])

                                def mm(e, es_=es_, e0=e0, k=k, kn=kn, ci=ci, cc=cc, ybank=ybank, nmm=nmm, total=total):
                                    i_ = None
                                    n_ = nmm
                                    for ee in es_:
                                        d = ee - (NB - 1)
                                        i0 = max(0, d)
                                        i1 = min(NB - 1, NB - 1 + d)
                                        i_ = e.matmul(ybank[:, i0:i1 + 1], lhsT=pc_t[k][:, (ee - e0) * 128:(ee - e0 + 1) * 128],
                                                      rhs=vT[:, cc * 128 + ci, i0 - d:i1 - d + 1], start=(n_ == 0), stop=(n_ == total - 1),
                                                      skip_group_check=True)
                                        n_ += 1
                                        if 0 < abs(d) <= BS:
                                            ad = abs(d)
                                            o_off = BS if d > 0 else BS - ad
                                            r_off = BS - ad if d > 0 else BS
                                            ycol = ybank[:, 0:1]
                                            vcol = vT[:, cc * 128 + ci, 0:1]
                                            o_ap = bass.AP(tensor=ycol.tensor, offset=ycol.offset + o_off, ap=[list(ycol.ap[0]), [BS, NSEG - 1], [1, ad]])
                                            r_ap = bass.AP(tensor=vcol.tensor, offset=vcol.offset + r_off, ap=[list(vcol.ap[0]), [BS, NSEG - 1], [1, ad]])
                                            i_ = e.matmul(o_ap, lhsT=pcn_t[kn][:, (ee - e0) * 128:(ee - e0 + 1) * 128], rhs=r_ap,
                                                          start=False, stop=(n_ == total - 1), skip_group_check=True)
                                            n_ += 1
                                    return i_
                                rl = [bpc[k], bvT] + ([bpcn[kn]] if kn is not None else [])
                                P.op(pe, mm, R=rl, W=[bpsf[pb]])
                                nmm += len(es_) + len(cross)
                            if ci % 2 == 0:
                                P.op(act, lambda e, ci=ci, ybank=ybank: e.activation(out=yT[:, :, ci], in_=ybank[:, 0:NB], func=AF.Copy),
                                     R=[bpsf[pb]], W=[byT])
                            else:
                                P.op(dve, lambda e, ci=ci, ybank=ybank: e.tensor_copy(out=yT[:, :, ci], in_=ybank[:, 0:NB]),
                                     R=[bpsf[pb]], W=[byT])
                        if stop and cc == 0 and not _DBG_NODUMP:
                            P.dma(DBGY[:, :], yT[:].rearrange("p b c -> p (b c)"), R=[byT], W=[bDBG], via=byT)
                            P.dma(DBGV[:, :], vT[:].rearrange("p c b -> p (c b)"), R=[bvT], W=[bDBG], via=bvT)
                        for i in range(T // NT if _DBG_SUB >= 4 else 0):
                            j = i % 2
                            t0 = i * NT
                            lo = max(t0 - 1, 0)
                            hi = min(t0 + NT + 1, T)
                            P.dma(u0[j][:, lo - (t0 - 1):hi - (t0 - 1)], U[cc * 128:(cc + 1) * 128, lo:hi], R=[bU], W=[bu0[j]], via=bu0[j])
                            if t0 == 0:
                                P.op(dve, lambda e: e.memset(u0[j][:, 0:1], 0.0), W=[bu0[j]])
                            elif t0 % SEG == 0:
                                P.op(dve, lambda e: e.tensor_scalar(out=u0[j][:, 0:1], in0=u0[j][:, 0:1], scalar1=segflag[:, 0:1],
                                                                    scalar2=None, op0=ALU.mult), R=[bu0[j], bconst], W=[bu0[j]])
                            if t0 + NT == T:
                                P.op(dve, lambda e: e.memset(u0[j][:, NT + 1:NT + 2], 0.0), W=[bu0[j]])
                            elif (t0 + NT) % SEG == 0:
                                P.op(dve, lambda e: e.tensor_scalar(out=u0[j][:, NT + 1:NT + 2], in0=u0[j][:, NT + 1:NT + 2],
                                                                    scalar1=segflag[:, 0:1], scalar2=None, op0=ALU.mult),
                                     R=[bu0[j], bconst], W=[bu0[j]])
                            wq = lambda k_, cc=cc: spt[:, O_HCW + cc * 3 + k_:O_HCW + cc * 3 + k_ + 1]
                            P.op(dve, lambda e, wq=wq: e.tensor_scalar(out=c0[:], in0=u0[j][:, 0:NT], scalar1=wq(0),
                                                                      scalar2=spt[:, O_HCB + cc:O_HCB + cc + 1], op0=ALU.mult, op1=ALU.add),
                                 R=[bu0[j], bsp], W=[bc0])
                            for k_ in (1, 2):
                                P.op(dve, lambda e, k_=k_, wq=wq: e.scalar_tensor_tensor(out=c0[:], in0=u0[j][:, k_:k_ + NT], scalar=wq(k_),
                                                                                        in1=c0[:], op0=ALU.mult, op1=ALU.add),
                                     R=[bu0[j], bsp, bc0], W=[bc0])
                            pb = 4 + i % 2

                            def tr(e, pb=pb, t0=t0):
                                for bq in range(4):
                                    i_ = e.transpose(out=psf[pb][:, bq * 128:(bq + 1) * 128], in_=yT[:, t0 // 128 + bq, :], identity=ident_f[:])
                                return i_
                            P.op(pe, tr, R=[byT, bconst], W=[bpsf[pb]])
                            P.op(dve, lambda e, pb=pb: e.tensor_tensor(out=yfo[j][:], in0=psf[pb][:], in1=c0[:], op=ALU.mult),
                                 R=[bpsf[pb], bc0], W=[byfo[j]])
                            P.dma(HF[cc * 128:(cc + 1) * 128, t0:t0 + NT], yfo[j][:], R=[byfo[j]], W=[bHF], via=byfo[j])
                with contextlib.ExitStack() as p3:
                    P.phase_end()
                    NT = 512
                    yf = [sb(f"hy_yf{i}", [128, 2, NT], F32, p3) for i in range(2)]
                    byf = [Buf(f"hy_yf{i}") for i in range(2)]
                    sq = sb("hy_sq", [128, 2, NT], BF16, p3)
                    bsq = Buf("hy_sq")
                    rstd = sb("hy_rstd", [128, NT], F32, p3)
                    brstd = Buf("hy_rstd")
                    yo = [sb(f"hy_yo{i}", [128, 2, NT], BF16, p3) for i in range(2)]
                    byo = [Buf(f"hy_yo{i}") for i in range(2)]
                    HFv = HF.rearrange("(q p) t -> p q t", p=128)
                    YM0 = YM[0:256, :].rearrange("(q p) t -> p q t", p=128)
                    for i in range(T // NT if _DBG_SUB >= 4 else 0):
                        j = i % 2
                        t0 = i * NT
                        P.dma(yf[j][:], HFv[:, :, t0:t0 + NT], R=[bHF], W=[byf[j]], via=byf[j])
                        rms_rstd(yf[j][:], 2, NT, sq, rstd, byf[j], bsq, brstd, psf[2], bpsf[2], 1.0 / 256)
                        for q in range(2):
                            P.op(dve, lambda e, q=q: e.scalar_tensor_tensor(out=yo[j][:, q, :], in0=yf[j][:, q, :],
                                                                          scalar=spt[:, O_ON + q:O_ON + q + 1], in1=rstd[:],
                                                                          op0=ALU.mult, op1=ALU.mult),
                                 R=[byf[j], brstd, bsp], W=[byo[j]])
                        P.dma(YM0[:, :, t0:t0 + NT], yo[j][:], R=[byo[j]], W=[bYM], via=byo[j])

            dbg(f"omlt_c{l}", omlt[:].rearrange("p a b -> p (a b)"), blay, [128, 8])
            if lvl < 5:
                break
            with contextlib.ExitStack() as ph:
                P.phase_end()
                NT = 512
                NCH = NT // 64
                KW = [16, 32, 48, 64]
                KO = [0, NCH * 16, NCH * 48, NCH * 96]
                Ug = U[768:768 + 2560, :].rearrange("(g h p) t -> p g h t", g=5, h=4)
                OFv = OF.rearrange("(h p) t -> p h t", p=128)
                YM1 = YM[256:768, :].rearrange("(h p) t -> p h t", p=128)
                mask4 = sb("hg_mask4", [64, NCH, 4, 16], F32, ph)
                bmask4 = Buf("hg_mask4")
                P.op(dve, lambda e: e.memset(mask4[:], 1.0), W=[bmask4])
                P.op(pool, lambda e: e.affine_select(out=mask4[:], in_=mask4[:], pattern=[[0, NCH], [16, 4], [1, 16]],
                                                     compare_op=ALU.is_ge, fill=0.0, base=0, channel_multiplier=-1),
                     R=[bmask4], W=[bmask4])
                sqh = sb("hg_sq", [128, 1, NT], BF16, ph)
                bsqh = Buf("hg_sq")
                rstdh = sb("hg_rstd", [128, NT], F32, ph)
                brstdh = Buf("hg_rstd")
                S = [sb(f"hg_S{h}", [128, 128], F32, ph) for h in range(4)]
                bS = [Buf(f"hg_S{h}") for h in range(4)]

                class HeadBufs:
                    pass
                HB = []
                for h in range(4):
                    B = HeadBufs()
                    for nm, shape, dt_ in (("qv", [128, 2, NT], F32), ("zt", [128, NT], F32), ("gt", [128, NT], F32), ("oft", [128, NT], F32),
                                           ("qr", [128, 2, NT], F32), ("fA", [128, NT], F32), ("fB", [128, NT], F32), ("fC", [128, NT], F32),
                                           ("fD", [128, NT], F32), ("fK", [128, NCH * 160], F32), ("rr", [128, NCH, 4], F32), ("ebl", [128, NCH], F32),
                                           ("Qh", [128, NT], BF16), ("Kh", [128, NT], BF16), ("Qs", [128, NT], BF16), ("KI", [128, NCH * 160], BF16),
                                           ("vbf", [128, NT], BF16), ("Ktok", [64, NCH, 128], BF16), ("Vtok", [64, NCH, 128], BF16),
                                           ("Am", [64, NCH, 4, 16], BF16), ("Spb0", [128, 128], BF16), ("Spb1", [128, 128], BF16), ("yo", [128, NT], BF16)):
                        setattr(B, nm, sb(f"hg{h}_{nm}", shape, dt_, ph))
                        setattr(B, "b_" + nm, Buf(f"hg{h}_{nm}"))
                    P.op(dve, lambda e, B=B: e.memset(B.rr[:], 0.0), W=[B.b_rr])
                    HB.append(B)

                def unit(dr, i, h):
                    B = HB[h]
                    t0 = i * NT
                    ob = 1 + h
                    sbank = 5 + (h % 2)
                    P.dma(B.qv[:], Ug[:, 0:2, h, t0:t0 + NT], R=[bU], W=[B.b_qv], via=B.b_qv)
                    P.dma(B.zt[:], Ug[:, 2 + dr, h, t0:t0 + NT], R=[bU], W=[B.b_zt], via=B.b_zt)
                    if dr == 1:
                        P.dma(B.gt[:], Ug[:, 4, h, t0:t0 + NT], R=[bU], W=[B.b_gt], via=B.b_gt)
                        P.dma(B.oft[:], OFv[:, h, t0:t0 + NT], R=[bOF], W=[B.b_oft], via=B.b_oft)
                    at_seg = (t0 % SEG == 0) if dr == 0 else ((t0 + NT) % SEG == 0)
                    if at_seg:
                        P.op(dve, lambda e: e.tensor_scalar(out=S[h][:], in0=S[h][:], scalar1=segflag[:, 0:1], scalar2=None,
                                                            op0=ALU.mult), R=[bS[h], bconst], W=[bS[h]])
                    yield
                    if dr == 0:
                        P.op(act, lambda e: e.activation(out=B.fA[:], in_=B.zt[:], func=AF.Sigmoid), R=[B.b_zt], W=[B.b_fA])
                        P.op(pool, lambda e: e.tensor_copy(out=B.qr[:], in_=B.qv[:]), R=[B.b_qv], W=[B.b_qr])
                    else:
                        P.op(dve, lambda e: e.tensor_copy(out=B.fD[:], in_=B.zt[:, ::-1]), R=[B.b_zt], W=[B.b_fD])
                        yield
                        P.op(act, lambda e: e.activation(out=B.fA[:], in_=B.fD[:], func=AF.Sigmoid), R=[B.b_fD], W=[B.b_fA])
                        for g_ in range(2):
                            P.op(dve, lambda e, g_=g_: e.tensor_copy(out=B.qr[:, g_, :], in_=B.qv[:, g_, ::-1]), R=[B.b_qv], W=[B.b_qr])
                    yield
                    P.op(dve, lambda e: e.tensor_scalar(out=B.fA[:], in0=B.fA[:], scalar1=omlt[:, dr, h:h + 1], scalar2=lbt[:, dr, h:h + 1],
                                                        op0=ALU.mult, op1=ALU.add), R=[B.b_fA, blay], W=[B.b_fA])
                    yield
                    P.op(act, lambda e: e.activation(out=B.fB[:], in_=B.fA[:], func=AF.Ln), R=[B.b_fA], W=[B.b_fB])
                    P.op(act, lambda e: e.activation(out=B.vbf[:], in_=B.qr[:, 1, :], func=AF.Copy), R=[B.b_qr], W=[B.b_vbf])
                    yield
                    P.op(pool, lambda e: e.tensor_scalar(out=B.fA[:], in0=B.fA[:], scalar1=-1.0, scalar2=1.0, op0=ALU.mult, op1=ALU.add),
                         R=[B.b_fA], W=[B.b_fA])
                    P.op(dve, lambda e: e.tensor_tensor_scan(out=B.fC[:], data0=cm_f[:], data1=B.fB[:], initial=0.0, op0=ALU.mult, op1=ALU.add),
                         R=[B.b_fB, bconst], W=[B.b_fC])

                    def trV(e):
                        for c in range(NCH):
                            i_ = e.transpose(out=psb[0][0:64, c * 128:(c + 1) * 128], in_=B.vbf[:, c * 64:(c + 1) * 64], identity=ident_bf[:])
                        return i_
                    P.op(pe, trV, R=[B.b_vbf, bconst], W=[bpsb[0]])
                    P.op(act, lambda e: e.activation(out=B.Vtok[:], in_=psb[0][0:64, :].rearrange("p (c k) -> p c k", k=128), func=AF.Copy),
                         R=[bpsb[0]], W=[B.b_Vtok])
                    yield
                    b3 = B.fC[:].rearrange("p (c t) -> p c t", t=64)
                    b4 = B.fC[:].rearrange("p (c i t) -> p c i t", i=4, t=16)
                    kk3 = B.fA[:].rearrange("p (c t) -> p c t", t=64)
                    P.op(act, lambda e: e.activation(out=B.ebl[:], in_=b3[:, :, 63], func=AF.Exp), R=[B.b_fC], W=[B.b_ebl])
                    P.op(act, lambda e: e.activation(out=B.fB[:], in_=B.fC[:], func=AF.Exp), R=[B.b_fC], W=[B.b_fB])
                    P.op(dve, lambda e: e.tensor_tensor(out=B.fD[:].rearrange("p (c t) -> p c t", t=64),
                                                        in0=b3[:, :, 63:64].broadcast_to([128, NCH, 64]), in1=b3, op=ALU.subtract),
                         R=[B.b_fC], W=[B.b_fD])
                    P.op(dve, lambda e: e.tensor_copy(out=B.rr[:, :, 1:4], in_=b3[:, :, 15:48:16]), R=[B.b_fC], W=[B.b_rr])
                    yield
                    P.op(dve, lambda e: e.tensor_tensor(out=B.Qh[:], in0=B.qr[:, 0, :], in1=B.fB[:], op=ALU.mult), R=[B.b_qr, B.b_fB], W=[B.b_Qh])
                    P.op(act, lambda e: e.activation(out=B.fD[:], in_=B.fD[:], func=AF.Exp), R=[B.b_fD], W=[B.b_fD])
                    for I in range(4):
                        P.op(dve, lambda e, I=I: e.tensor_tensor(
                            out=B.fK[:, KO[I]:KO[I] + NCH * KW[I]].rearrange("p (c t) -> p c t", t=KW[I]),
                            in0=B.rr[:, :, I:I + 1].broadcast_to([128, NCH, KW[I]]), in1=b3[:, :, 0:KW[I]], op=ALU.subtract),
                            R=[B.b_fC, B.b_rr], W=[B.b_fK])
                    yield
                    P.op(act, lambda e: e.activation(out=B.fK[:], in_=B.fK[:], func=AF.Exp), R=[B.b_fK], W=[B.b_fK])
                    P.op(dve, lambda e: e.tensor_tensor(out=B.Kh[:], in0=B.fA[:], in1=B.fD[:], op=ALU.mult), R=[B.b_fA, B.b_fD], W=[B.b_Kh])
                    P.op(dve, lambda e: e.tensor_tensor(out=B.fB[:].rearrange("p (c i t) -> p c i t", i=4, t=16), in0=b4,
                                                        in1=B.rr[:].unsqueeze(3).broadcast_to([128, NCH, 4, 16]), op=ALU.subtract),
                         R=[B.b_fC, B.b_rr, B.b_Qh], W=[B.b_fB])
                    yield
                    P.op(act, lambda e: e.activation(out=B.fB[:], in_=B.fB[:], func=AF.Exp), R=[B.b_fB], W=[B.b_fB])

                    def trK(e):
                        for c in range(NCH):
                            i_ = e.transpose(out=psb[0][0:64, c * 128:(c + 1) * 128], in_=B.Kh[:, c * 64:(c + 1) * 64], identity=ident_bf[:])
                        return i_
                    P.op(pe, trK, R=[B.b_Kh, bconst], W=[bpsb[0]])
                    P.op(act, lambda e: e.activation(out=B.Ktok[:], in_=psb[0][0:64, :].rearrange("p (c k) -> p c k", k=128), func=AF.Copy),
                         R=[bpsb[0]], W=[B.b_Ktok])
                    for I in range(4):
                        P.op(dve, lambda e, I=I: e.tensor_tensor(
                            out=B.KI[:, KO[I]:KO[I] + NCH * KW[I]].rearrange("p (c t) -> p c t", t=KW[I]),
                            in0=B.fK[:, KO[I]:KO[I] + NCH * KW[I]].rearrange("p (c t) -> p c t", t=KW[I]), in1=kk3[:, :, 0:KW[I]], op=ALU.mult),
                            R=[B.b_fK, B.b_fA], W=[B.b_KI])
                    yield
                    P.op(dve, lambda e: e.tensor_tensor(out=B.Qs[:], in0=B.qr[:, 0, :], in1=B.fB[:], op=ALU.mult), R=[B.b_qr, B.b_fB], W=[B.b_Qs])
                    yield

                    def mmA(e):
                        for c in range(NCH):
                            for I in range(4):
                                col = (c * 4 + I) * 16
                                i_ = e.matmul(psf[0][0:KW[I], col:col + 16], lhsT=B.KI[:, KO[I] + c * KW[I]:KO[I] + (c + 1) * KW[I]],
                                              rhs=B.Qs[:, c * 64 + I * 16:c * 64 + (I + 1) * 16], start=True, stop=True, skip_group_check=True)
                        return i_
                    P.op(pe, mmA, R=[B.b_KI, B.b_Qs], W=[bpsf[0]])
                    P.op(dve, lambda e: e.tensor_tensor(out=B.Am[:], in0=psf[0][0:64, :].rearrange("p (c i t) -> p c i t", i=4, t=16), in1=mask4[:],
                                                        op=ALU.mult), R=[bpsf[0], bmask4], W=[B.b_Am])
                    yield
                    for c in range(NCH):
                        Spb_, bSpb_ = (B.Spb0, B.b_Spb0) if c % 2 == 0 else (B.Spb1, B.b_Spb1)
                        P.op(act, lambda e, Spb_=Spb_: e.activation(out=Spb_[:], in_=S[h][:], func=AF.Copy), R=[bS[h]], W=[bSpb_])

                        def mmo(e, c=c, Spb_=Spb_):
                            e.matmul(psf[ob][:, c * 64:(c + 1) * 64], lhsT=Spb_[:], rhs=B.Qh[:, c * 64:(c + 1) * 64], start=True, stop=False,
                                     skip_group_check=True)
                            for I in range(4):
                                i_ = e.matmul(psf[ob][:, c * 64 + I * 16:c * 64 + (I + 1) * 16], lhsT=B.Vtok[0:KW[I], c, :], rhs=B.Am[0:KW[I], c, I, :],
                                              start=False, stop=(I == 3), skip_group_check=True)
                            return i_
                        P.op(pe, mmo, R=[B.b_Vtok, B.b_Am, bSpb_, B.b_Qh], W=[bpsf[ob]])
                        P.op(pe, lambda e, c=c: e.matmul(psf[sbank][:, 0:128], lhsT=B.Ktok[:, c, :], rhs=B.Vtok[:, c, :], start=True, stop=True,
                                                         skip_group_check=True), R=[B.b_Ktok, B.b_Vtok], W=[bpsf[sbank]])
                        P.op(dve, lambda e, c=c: e.scalar_tensor_tensor(out=S[h][:], in0=S[h][:], scalar=B.ebl[:, c:c + 1],
                                                                       in1=psf[sbank][:, 0:128], op0=ALU.mult, op1=ALU.add),
                             R=[bpsf[sbank], B.b_ebl, bS[h]], W=[bS[h]])
                        yield
                    if dr == 0:
                        P.op(act, lambda e: e.activation(out=B.oft[:], in_=psf[ob][:], func=AF.Copy), R=[bpsf[ob]], W=[B.b_oft])
                        P.dma(OFv[:, h, t0:t0 + NT], B.oft[:], R=[B.b_oft], W=[bOF], via=B.b_oft)
                    else:
                        P.op(act, lambda e: e.activation(out=B.fD[:], in_=psf[ob][:], func=AF.Copy), R=[bpsf[ob], B.b_Kh], W=[B.b_fD])
                        yield
                        P.op(dve, lambda e: e.tensor_tensor(out=B.oft[:], in0=B.fD[:, ::-1], in1=B.oft[:], op=ALU.add),
                             R=[B.b_fD, B.b_oft], W=[B.b_oft])
                        rms_rstd(B.oft[:].unsqueeze(1), 1, NT, sqh, rstdh, B.b_oft, bsqh, brstdh, psf[0], bpsf[0], 1.0 / 128)
                        P.op(dve, lambda e: e.scalar_tensor_tensor(out=B.oft[:], in0=B.oft[:], scalar=spt[:, O_ON + 2 + h:O_ON + 3 + h],
                                                                  in1=rstdh[:], op0=ALU.mult, op1=ALU.mult),
                             R=[B.b_oft, brstdh, bsp], W=[B.b_oft])
                        P.op(act, lambda e: e.activation(out=B.gt[:], in_=B.gt[:], func=AF.Silu), R=[B.b_gt], W=[B.b_gt])
                        yield
                        P.op(dve, lambda e: e.tensor_tensor(out=B.yo[:], in0=B.oft[:], in1=B.gt[:], op=ALU.mult),
                             R=[B.b_oft, B.b_gt], W=[B.b_yo])
                        P.dma(YM1[:, h, t0:t0 + NT], B.yo[:], R=[B.b_yo], W=[bYM], via=B.b_yo)

                for dr in range(2):
                    tiles = list(range(T // NT)) if dr == 0 else list(range(T // NT - 1, -1, -1))
                    for h in range(4):
                        P.op(dve, lambda e, h=h: e.memset(S[h][:], 0.0), W=[bS[h]])
                    for i in tiles:
                        alive = [unit(dr, i, h) for h in range(4)]
                        while alive:
                            for g_ in list(alive):
                                try:
                                    next(g_)
                                except StopIteration:
                                    alive.remove(g_)

            if lvl < 6:
                break
            with contextlib.ExitStack() as ph:
                P.phase_end()
                NT = 512
                Ur = U[3328:3840, :].rearrange("(g c p) t -> p g c t", g=2, c=2)
                HFv = HF.rearrange("(c p) t -> p c t", p=128)
                YM2 = YM[768:1024, :].rearrange("(c p) t -> p c t", p=128)
                wa = sb("rg_wa", [128, 2, 2, 128], F32, ph)
                wx = sb("rg_wx", [128, 2, 2, 128], F32, ph)
                wab = sb("rg_wab", [128, 2, 2, 128], BF16, ph)
                wxb = sb("rg_wxb", [128, 2, 2, 128], BF16, ph)
                bwr = Buf("rg_w")
                P.op(dve, lambda e: e.memset(wa[:], 0.0), W=[bwr])
                P.op(dve, lambda e: e.memset(wx[:], 0.0), W=[bwr])
                for d_ in range(2):
                    for hd in range(4):
                        cc, hh_ = hd // 2, hd % 2
                        P.dma(wa[hh_ * 64:(hh_ + 1) * 64, d_, cc, hh_ * 64:(hh_ + 1) * 64], rwa_d[l, d_, hd], R=[bIN], W=[bwr], via=bwr, nowaw=(d_ + hd > 0))
                        P.dma(wx[hh_ * 64:(hh_ + 1) * 64, d_, cc, hh_ * 64:(hh_ + 1) * 64], rwx_d[l, d_, hd], R=[bIN], W=[bwr], via=bwr, nowaw=True)
                P.op(dve, lambda e: e.tensor_copy(out=wab[:], in_=wa[:]), R=[bwr], W=[bwr])
                P.op(dve, lambda e: e.tensor_copy(out=wxb[:], in_=wx[:]), R=[bwr], W=[bwr])
                sq = sb("rg_sq", [128, 2, NT], BF16, ph)
                bsq = Buf("rg_sq")
                rstd = sb("rg_rstd", [128, NT], F32, ph)
                brstd = Buf("rg_rstd")
                hst = sb("rg_hst", [128, 2], F32, ph)
                bhst = Buf("rg_hst")

                class RgBufs:
                    pass
                RB = []
                for k_ in range(2):
                    B = RgBufs()
                    for nm, shape, dt_ in (("ur", [128, 2, NT + 3], F32), ("gr", [128, 2, NT], F32), ("hf", [128, 2, NT], F32), ("xr", [128, 2, NT], F32),
                                           ("xrb", [128, 2, NT], BF16), ("rr", [128, 2, NT], F32), ("ig", [128, 2, NT], F32), ("aa", [128, 2, NT], F32),
                                           ("bb", [128, 2, NT], F32), ("ga", [128, 2, NT], F32), ("yo", [128, 2, NT], BF16)):
                        setattr(B, nm, sb(f"rg{k_}_{nm}", shape, dt_, ph))
                        setattr(B, "b_" + nm, Buf(f"rg{k_}_{nm}"))
                    B.pb = (2 * k_, 2 * k_ + 1)
                    RB.append(B)

                def rg_unit(dr, i, B):
                    t0 = i * NT
                    lo = max(t0 - 2, 0)
                    hi = min(t0 + NT + 1, T)
                    P.dma(B.ur[:, :, lo - (t0 - 2):hi - (t0 - 2)], Ur[:, 0, :, lo:hi], R=[bU], W=[B.b_ur], via=B.b_ur)
                    if dr == 1:
                        P.dma(B.gr[:], Ur[:, 1, :, t0:t0 + NT], R=[bU], W=[B.b_gr], via=B.b_gr)
                        P.dma(B.hf[:], HFv[:, :, t0:t0 + NT], R=[bHF], W=[B.b_hf], via=B.b_hf)
                    if t0 == 0:
                        P.op(dve, lambda e: e.memset(B.ur[:, :, 0:2], 0.0), W=[B.b_ur])
                    elif t0 % SEG == 0:
                        P.op(dve, lambda e: e.tensor_scalar(out=B.ur[:, :, 0:2], in0=B.ur[:, :, 0:2], scalar1=segflag[:, 0:1], scalar2=None,
                                                            op0=ALU.mult), R=[B.b_ur, bconst], W=[B.b_ur])
                    if t0 + NT == T:
                        P.op(dve, lambda e: e.memset(B.ur[:, :, NT + 2:NT + 3], 0.0), W=[B.b_ur])
                    elif (t0 + NT) % SEG == 0:
                        P.op(dve, lambda e: e.tensor_scalar(out=B.ur[:, :, NT + 2:NT + 3], in0=B.ur[:, :, NT + 2:NT + 3], scalar1=segflag[:, 0:1],
                                                            scalar2=None, op0=ALU.mult), R=[B.b_ur, bconst], W=[B.b_ur])
                    yield
                    for cc in range(2):
                        wq = lambda k, cc=cc: spt[:, O_RCW + cc * 4 + k:O_RCW + cc * 4 + k + 1]
                        P.op(dve, lambda e, cc=cc, wq=wq: e.tensor_scalar(out=B.xr[:, cc, :], in0=B.ur[:, cc, 0:NT], scalar1=wq(0),
                                                                         scalar2=spt[:, O_RCB + cc:O_RCB + cc + 1], op0=ALU.mult, op1=ALU.add),
                             R=[B.b_ur, bsp], W=[B.b_xr])
                        for k in (1, 2, 3):
                            P.op(dve, lambda e, cc=cc, k=k, wq=wq: e.scalar_tensor_tensor(out=B.xr[:, cc, :], in0=B.ur[:, cc, k:k + NT], scalar=wq(k),
                                                                                         in1=B.xr[:, cc, :], op0=ALU.mult, op1=ALU.add),
                                 R=[B.b_ur, bsp, B.b_xr], W=[B.b_xr])
                    yield
                    P.op(act, lambda e: e.activation(out=B.xrb[:], in_=B.xr[:], func=AF.Copy), R=[B.b_xr], W=[B.b_xrb])
                    yield
                    for cc in range(2):
                        pr, pi_ = B.pb
                        P.op(pe, lambda e, cc=cc, pr=pr: e.matmul(psf[pr][:], lhsT=wab[:, dr, cc, :], rhs=B.xrb[:, cc, :], start=True, stop=True),
                             R=[bwr, B.b_xrb], W=[bpsf[pr]])
                        P.op(act, lambda e, cc=cc, pr=pr: e.activation(out=B.rr[:, cc, :], in_=psf[pr][:], func=AF.Sigmoid,
                                                                       bias=spt[:, O_RBA + dr * 2 + cc:O_RBA + dr * 2 + cc + 1]),
                             R=[bpsf[pr], bsp], W=[B.b_rr])
                        P.op(pe, lambda e, cc=cc, pi_=pi_: e.matmul(psf[pi_][:], lhsT=wxb[:, dr, cc, :], rhs=B.xrb[:, cc, :], start=True, stop=True),
                             R=[bwr, B.b_xrb], W=[bpsf[pi_]])
                        P.op(act, lambda e, cc=cc, pi_=pi_: e.activation(out=B.ig[:, cc, :], in_=psf[pi_][:], func=AF.Sigmoid,
                                                                         bias=spt[:, O_RBX + dr * 2 + cc:O_RBX + dr * 2 + cc + 1]),
                             R=[bpsf[pi_], bsp], W=[B.b_ig])
                        yield
                    for cc in range(2):
                        P.op(act, lambda e, cc=cc: e.activation(out=B.aa[:, cc, :], in_=B.rr[:, cc, :], func=AF.Exp,
                                                                scale=clam[:, dr * 2 + cc:dr * 2 + cc + 1]), R=[B.b_rr, blay], W=[B.b_aa])
                        P.op(act, lambda e, cc=cc: e.activation(out=B.bb[:, cc, :], in_=B.rr[:, cc, :], func=AF.Exp,
                                                                scale=clam2[:, dr * 2 + cc:dr * 2 + cc + 1]), R=[B.b_rr, blay], W=[B.b_bb])
                    P.op(pool, lambda e: e.tensor_tensor(out=B.ig[:], in0=B.ig[:], in1=B.xr[:], op=ALU.mult), R=[B.b_ig, B.b_xr], W=[B.b_ig])
                    yield
                    P.op(dve, lambda e: e.tensor_scalar(out=B.bb[:], in0=B.bb[:], scalar1=-1.0, scalar2=1.0, op0=ALU.mult, op1=ALU.add), R=[B.b_bb], W=[B.b_bb])
                    P.op(dve, lambda e: e.tensor_scalar(out=B.bb[:], in0=B.bb[:], scalar1=0.0, scalar2=None, op0=ALU.max), R=[B.b_bb], W=[B.b_bb])
                    yield
                    P.op(act, lambda e: e.activation(out=B.bb[:], in_=B.bb[:], func=AF.Sqrt), R=[B.b_bb], W=[B.b_bb])
                    yield
                    P.op(dve, lambda e: e.tensor_tensor(out=B.bb[:], in0=B.bb[:], in1=B.ig[:], op=ALU.mult), R=[B.b_bb, B.b_ig], W=[B.b_bb])
                    if dr == 1:
                        P.op(pool, lambda e: e.tensor_tensor(out=B.ga[:], in0=B.gr[:], in1=B.gr[:], op=ALU.mult), R=[B.b_gr], W=[B.b_ga])
                        P.op(pool, lambda e: e.tensor_scalar(out=B.ga[:], in0=B.ga[:], scalar1=0.044715, scalar2=1.0, op0=ALU.mult, op1=ALU.add),
                             R=[B.b_ga], W=[B.b_ga])
                        P.op(pool, lambda e: e.tensor_tensor(out=B.ga[:], in0=B.ga[:], in1=B.gr[:], op=ALU.mult), R=[B.b_ga, B.b_gr], W=[B.b_ga])
                    yield
                    at_seg = (t0 % SEG == 0) if dr == 0 else ((t0 + NT) % SEG == 0)
                    if at_seg:
                        P.op(dve, lambda e: e.tensor_scalar(out=hst[:], in0=hst[:], scalar1=segflag[:, 0:1], scalar2=None, op0=ALU.mult),
                             R=[bhst, bconst], W=[bhst])
                    for cc in range(2):
                        if dr == 0:
                            P.op(dve, lambda e, cc=cc: e.tensor_tensor_scan(out=B.hf[:, cc, :], data0=B.aa[:, cc, :], data1=B.bb[:, cc, :],
                                                                            initial=hst[:, cc:cc + 1], op0=ALU.mult, op1=ALU.add),
                                 R=[B.b_aa, B.b_bb, bhst], W=[B.b_hf])
                            P.op(dve, lambda e, cc=cc: e.tensor_copy(out=hst[:, cc:cc + 1], in_=B.hf[:, cc, NT - 1:NT]), R=[B.b_hf], W=[bhst])
                        else:
                            P.op(dve, lambda e, cc=cc: e.tensor_tensor_scan(out=B.rr[:, cc, ::-1], data0=B.aa[:, cc, ::-1], data1=B.bb[:, cc, ::-1],
                                                                            initial=hst[:, cc:cc + 1], op0=ALU.mult, op1=ALU.add),
                                 R=[B.b_aa, B.b_bb, bhst, B.b_rr], W=[B.b_rr])
                            P.op(dve, lambda e, cc=cc: e.tensor_copy(out=hst[:, cc:cc + 1], in_=B.rr[:, cc, 0:1]), R=[B.b_rr], W=[bhst])
                    if dr == 0:
                        P.dma(HFv[:, :, t0:t0 + NT], B.hf[:], R=[B.b_hf], W=[bHF], via=B.b_hf)
                        return
                    yield
                    P.op(act, lambda e: e.activation(out=B.ga[:], in_=B.ga[:], func=AF.Sigmoid, scale=1.5957691216), R=[B.b_ga], W=[B.b_ga])
                    P.op(pool, lambda e: e.tensor_tensor(out=B.hf[:], in0=B.hf[:], in1=B.rr[:], op=ALU.add), R=[B.b_hf, B.b_rr], W=[B.b_hf])
                    yield
                    P.op(pool, lambda e: e.tensor_tensor(out=B.ga[:], in0=B.ga[:], in1=B.gr[:], op=ALU.mult), R=[B.b_ga, B.b_gr], W=[B.b_ga])
                    yield
                    P.op(dve, lambda e: e.tensor_tensor(out=B.hf[:], in0=B.hf[:], in1=B.ga[:], op=ALU.mult), R=[B.b_hf, B.b_ga], W=[B.b_hf])
                    rms_rstd(B.hf[:], 2, NT, sq, rstd, B.b_hf, bsq, brstd, psf[4], bpsf[4], 1.0 / 256)
                    for cc in range(2):
                        P.op(dve, lambda e, cc=cc: e.scalar_tensor_tensor(out=B.yo[:, cc, :], in0=B.hf[:, cc, :],
                                                                        scalar=spt[:, O_ON + 6 + cc:O_ON + 7 + cc], in1=rstd[:],
                                                                        op0=ALU.mult, op1=ALU.mult), R=[B.b_hf, brstd, bsp], W=[B.b_yo])
                    P.dma(YM2[:, :, t0:t0 + NT], B.yo[:], R=[B.b_yo], W=[bYM], via=B.b_yo)

                for dr in range(2):
                    tiles = list(range(T // NT)) if dr == 0 else list(range(T // NT - 1, -1, -1))
                    P.op(dve, lambda e: e.memset(hst[:], 0.0), W=[bhst])
                    for p_ in range(0, len(tiles), 2):
                        alive = [rg_unit(dr, tiles[p_ + k_], RB[k_]) for k_ in range(2) if p_ + k_ < len(tiles)]
                        while alive:
                            for g_ in list(alive):
                                try:
                                    next(g_)
                                except StopIteration:
                                    alive.remove(g_)

            if lvl < 7:
                break
            with contextlib.ExitStack() as ph:
                P.phase_end()
                NT = 256
                NFC = DFF // 128
                wo = sb("p3wo", [128, 8, D], BF16, ph)
                wg = sb("p3wg", [128, 8, DFF], BF16, ph)
                wu = sb("p3wu", [128, 8, DFF], BF16, ph)
                wd = sb("p3wd", [128, NFC, D], BF16, ph)
                bwo, bwg, bwu, bwd = (Buf(n, multi=True) for n in ("p3wo", "p3wg", "p3wu", "p3wd"))
                with contextlib.ExitStack() as wl:
                    P.phase_end()
                    load_cast(wl, wo, w_out_d[l].rearrange("(kc p) n -> p kc n", p=128), 8, D, 256, bwo, "wo")
                    load_cast(wl, wg, wg_d[l].rearrange("(kc p) n -> p kc n", p=128), 8, DFF, 256, bwg, "wg")
                    load_cast(wl, wu, wu_d[l].rearrange("(kc p) n -> p kc n", p=128), 8, DFF, 256, bwu, "wu")
                with contextlib.ExitStack() as wl:
                    P.phase_end()
                    load_cast(wl, wd, wd_d[l].rearrange("(fc p) n -> p fc n", p=128), NFC, D, 64, bwd, "wd")
                P.phase_end()
                xt = [sb(f"p3x{i}", [128, 8, NT], F32, ph) for i in range(2)]
                bxt = [Buf(f"p3x{i}") for i in range(2)]
                ym = [sb(f"p3ym{i}", [128, 8, NT], BF16, ph) for i in range(2)]
                bym = [Buf(f"p3ym{i}") for i in range(2)]
                hb = sb("p3h", [128, 8, NT], BF16, ph)
                bhb = Buf("p3h")
                tmpn = [sb(f"p3tmp{i}", [128, NT], F32, ph) for i in range(2)]
                btmpn = [Buf(f"p3tmp{i}") for i in range(2)]
                sq = sb("p3sq", [128, 8, NT], BF16, ph)
                bsq = Buf("p3sq")
                rstd = sb("p3rstd", [128, NT], F32, ph)
                brstd = Buf("p3rstd")
                av = sb("p3a", [128, NFC, NT], BF16, ph)
                bav = Buf("p3a", multi=True)
                sg = [sb(f"p3sg{i}", [128, NT], F32, ph) for i in range(2)]
                bsg = [Buf(f"p3sg{i}") for i in range(2)]
                XTv = XT.rearrange("(kc p) t -> p kc t", p=128)
                YMv = YM.rearrange("(kc p) t -> p kc t", p=128)
                for i in range(T // NT):
                    j = i % 2
                    t0 = i * NT
                    seg = t0 // SEG
                    P.dma(xt[j][:], XTv[:, :, t0:t0 + NT], R=[bXT], W=[bxt[j]], via=bxt[j])
                    P.dma(ym[j][:], YMv[:, :, t0:t0 + NT], R=[bYM], W=[bym[j]], via=bym[j])
                    for mc in range(8):
                        pb = mc % 2

                        def mm(e, mc=mc, pb=pb):
                            for kc in range(8):
                                i_ = e.matmul(psf[pb][:, 0:NT], lhsT=wo[:, kc, mc * 128:(mc + 1) * 128], rhs=ym[j][:, kc, :], start=(kc == 0), stop=(kc == 7))
                            return i_
                        P.op(pe, mm, R=[bwo, bym[j]], W=[bpsf[pb]])
                        P.op(dve, lambda e, mc=mc, pb=pb: e.scalar_tensor_tensor(out=xt[j][:, mc, :], in0=psf[pb][:, 0:NT], scalar=modG[0][:, mc, seg:seg + 1],
                                                                                in1=xt[j][:, mc, :], op0=ALU.mult, op1=ALU.add),
                             R=[bpsf[pb], bmod, bxt[j]], W=[bxt[j]])
                    rms_rstd(xt[j][:], 8, NT, sq, rstd, bxt[j], bsq, brstd, psf[2], bpsf[2], 1.0 / D)
                    for kc in range(8):
                        tb = kc % 2
                        P.op(pool, lambda e, kc=kc, tb=tb: e.tensor_tensor(out=tmpn[tb][:], in0=xt[j][:, kc, :], in1=rstd[:], op=ALU.mult),
                             R=[bxt[j], brstd], W=[btmpn[tb]])
                        P.op(act, lambda e, kc=kc, tb=tb: e.activation(out=hb[:, kc, :], in_=tmpn[tb][:], func=AF.Identity,
                                                                       scale=modA[1][:, kc, seg:seg + 1], bias=modB[1][:, kc, seg:seg + 1]),
                             R=[btmpn[tb], bmod], W=[bhb])
                    for fc in range(NFC):
                        pb = 3 + fc % 3

                        def mm(e, fc=fc, pb=pb):
                            for kc in range(8):
                                e.matmul(psf[pb][:, 0:NT], lhsT=wg[:, kc, fc * 128:(fc + 1) * 128], rhs=hb[:, kc, :], start=(kc == 0), stop=(kc == 7),
                                         skip_group_check=True)
                            for kc in range(8):
                                i_ = e.matmul(psf[pb][:, NT:2 * NT], lhsT=wu[:, kc, fc * 128:(fc + 1) * 128], rhs=hb[:, kc, :], start=(kc == 0), stop=(kc == 7),
                                              skip_group_check=True)
                            return i_
                        P.op(pe, mm, R=[bwg, bwu, bhb], W=[bpsf[pb]])
                        s = fc % 2
                        P.op(act, lambda e, pb=pb, s=s: e.activation(out=sg[s][:], in_=psf[pb][:, 0:NT], func=AF.Silu), R=[bpsf[pb]], W=[bsg[s]])
                        P.op(dve, lambda e, pb=pb, s=s, fc=fc: e.tensor_tensor(out=av[:, fc, :], in0=sg[s][:], in1=psf[pb][:, NT:2 * NT], op=ALU.mult),
                             R=[bsg[s], bpsf[pb]], W=[bav])
                    for mc in range(8):
                        pb = mc % 2

                        def mm(e, mc=mc, pb=pb):
                            for fc in range(NFC):
                                i_ = e.matmul(psf[pb][:, 0:NT], lhsT=wd[:, fc, mc * 128:(mc + 1) * 128], rhs=av[:, fc, :], start=(fc == 0), stop=(fc == NFC - 1))
                            return i_
                        P.op(pe, mm, R=[bwd, bav], W=[bpsf[pb]])
                        P.op(dve, lambda e, mc=mc, pb=pb: e.scalar_tensor_tensor(out=xt[j][:, mc, :], in0=psf[pb][:, 0:NT], scalar=modG[1][:, mc, seg:seg + 1],
                                                                                in1=xt[j][:, mc, :], op0=ALU.mult, op1=ALU.add),
                             R=[bpsf[pb], bmod, bxt[j]], W=[bxt[j]])
                    P.dma(XTv[:, :, t0:t0 + NT], xt[j][:], R=[bxt[j]], W=[bXT], via=bxt[j])

        with contextlib.ExitStack() as ph:
            P.phase_end()
            NT = 512
            xt = [sb(f"p4x{i}", [128, 8, NT], F32, ph) for i in range(2)]
            bxt = [Buf(f"p4x{i}") for i in range(2)]
            sq = sb("p4sq", [128, 8, NT], BF16, ph)
            bsq = Buf("p4sq")
            rstd = sb("p4rstd", [128, NT], F32, ph)
            brstd = Buf("p4rstd")
            yo = [sb(f"p4y{i}", [128, D], F32, ph) for i in range(2)]
            byo = [Buf(f"p4y{i}") for i in range(2)]
            XTv = XT.rearrange("(kc p) t -> p kc t", p=128)
            oc = 0
            for i in range(T // NT if lvl >= 2 else 0):
                j = i % 2
                t0 = i * NT
                P.dma(xt[j][:], XTv[:, :, t0:t0 + NT], R=[bXT], W=[bxt[j]], via=bxt[j])
                rms_rstd(xt[j][:], 8, NT, sq, rstd, bxt[j], bsq, brstd, psf[0], bpsf[0], 1.0 / D)
                P.op(dve, lambda e: e.tensor_tensor(out=xt[j][:], in0=xt[j][:], in1=rstd[:].unsqueeze(1).broadcast_to([128, 8, NT]), op=ALU.mult),
                     R=[bxt[j], brstd], W=[bxt[j]])
                P.op(dve, lambda e: e.tensor_tensor(out=xt[j][:], in0=xt[j][:],
                                                    in1=spt[:, O_FN:O_FN + 8].unsqueeze(2).broadcast_to([128, 8, NT]), op=ALU.mult),
                     R=[bxt[j], bsp], W=[bxt[j]])
                for bq in range(NT // 128):
                    o = oc % 2
                    oc += 1
                    for half in range(2):
                        pb = 1 + (2 * oc + half) % 4

                        def tr(e, half=half, pb=pb, bq=bq):
                            for q in range(4):
                                kc = half * 4 + q
                                i_ = e.transpose(out=psf[pb][:, q * 128:(q + 1) * 128], in_=xt[j][:, kc, bq * 128:(bq + 1) * 128], identity=ident_f[:])
                            return i_
                        P.op(pe, tr, R=[bxt[j], bconst], W=[bpsf[pb]])
                        if half == 0:
                            P.op(act, lambda e, pb=pb, o=o: e.activation(out=yo[o][:, 0:512], in_=psf[pb][:], func=AF.Copy), R=[bpsf[pb]], W=[byo[o]])
                        else:
                            P.op(dve, lambda e, pb=pb, o=o: e.tensor_copy(out=yo[o][:, 512:1024], in_=psf[pb][:]), R=[bpsf[pb]], W=[byo[o]])
                    P.dma(y_out[t0 + bq * 128:t0 + (bq + 1) * 128, :], yo[o][:], R=[byo[o]], W=[bY], via=byo[o])
        P.finish()
    return nc


def _col(v):
    v = np.asarray(v, np.float32)
    return np.ascontiguousarray(v.reshape(-1, 128).T)


def pack_small(inp):
    out = np.zeros((DEPTH, 128, NSP), np.float32)
    deltas = np.abs(np.linspace(math.log(1e-2) / 1.5, math.log(1e-2) / 0.3, 256, dtype=np.float32)).astype(np.float32)
    for l in range(DEPTH):
        o = out[l]
        o[:, O_N1:O_N1 + 8] = _col(inp["norm1_w"][l])
        o[:, O_N2:O_N2 + 8] = _col(inp["norm2_w"][l])
        o[:, O_ON:O_ON + 8] = _col(inp["out_norm_w"][l])
        o[:, O_ADAB:O_ADAB + 48] = _col(inp["ada_b"][l])
        cw = np.asarray(inp["hy_conv_w"][l], np.float32)
        o[:, O_HCW:O_HCW + 18] = cw.T.reshape(6, 128, 3).transpose(1, 0, 2).reshape(128, 18)
        o[:, O_HCB:O_HCB + 6] = _col(inp["hy_conv_b"][l])
        o[:, O_HBIAS:O_HBIAS + 2] = _col(inp["hy_bias"][l])
        rw = np.asarray(inp["rg_conv_w"][l], np.float32)
        o[:, O_RCW:O_RCW + 8] = rw.T.reshape(2, 128, 4).transpose(1, 0, 2).reshape(128, 8)
        o[:, O_RCB:O_RCB + 2] = _col(inp["rg_conv_b"][l])
        for nm, off in (("rg_ba", O_RBA), ("rg_bx", O_RBX), ("rg_lambda", O_RLAM)):
            a = np.asarray(inp[nm][l], np.float32)
            o[:, off:off + 4] = a.reshape(2, 2, 128).transpose(2, 0, 1).reshape(128, 4)
        lb = np.asarray(inp["hg_lb_logits"], np.float32)
        o[:, O_LB:O_LB + 16] = lb.reshape(2, DEPTH, 4, 128).transpose(3, 0, 1, 2).reshape(128, 16)
        o[0:64, O_HYB + 0] = inp["hy_b1"][l]
        o[0:64, O_HYB + 1] = inp["hy_b2"][l]
        o[0:64, O_HYB + 2] = inp["hy_b3"][l]
        o[0:64, O_HYB + 3] = inp["hy_freq"][l]
        o[:, O_FN:O_FN + 8] = _col(inp["final_norm_w"])
        o[:, O_ND:O_ND + 2] = -_col(deltas)
    return out


def pos_tables(L, T):
    f32 = np.float32
    t = np.linspace(0.0, 1.0, L, dtype=f32)
    n = np.arange(L, dtype=f32)
    fb = np.linspace(1e-4, 15.0, 16, dtype=f32)
    ang = (fb[None, :] * n[:, None]) * f32(2.0 * math.pi / L)
    z = np.concatenate([t[:, None], np.cos(ang), -np.sin(ang)], axis=-1).astype(f32)
    zfeat = np.zeros((HY_EMB, T), f32)
    zfeat[:, :L] = z.T
    trow = np.full((1, T), 1e4, f32)
    trow[0, :L] = t
    return zfeat, trow


_CACHE = {}


def _get_prog(T):
    if T not in _CACHE:
        _CACHE[T] = build_program(T)
    return _CACHE[T]


def run_cores(core_specs, inp, T):
    nc = _get_prog(T)
    sp = pack_small(inp)
    shared = {k: np.ascontiguousarray(np.asarray(inp[k], np.float32)) for k in
              ("w_in", "w_out", "ada_w", "ffn_wg", "ffn_wu", "ffn_wd", "hy_w1", "hy_w2", "hy_w3", "hy_w4", "rg_wa", "rg_wx")}
    in_maps = []
    tabs = {}
    for (x, c_rows, L) in core_specs:
        if L not in tabs:
            tabs[L] = pos_tables(L, T)
        zfeat, trow = tabs[L]
        m = dict(shared)
        m["x"] = np.ascontiguousarray(x, dtype=np.float32)
        m["cT"] = np.ascontiguousarray(np.asarray(c_rows, np.float32).T)
        m["segflag"] = np.full((128, 1), 1.0 if L == T else 0.0, np.float32)
        m["zfeat"] = zfeat
        m["trow"] = trow
        m["sp"] = sp
        in_maps.append(m)
    res = run_bass_kernel_spmd(nc, in_maps, core_ids=list(range(len(in_maps))))
    return [r["y"] for r in res.results]


def kernel(**inp):
    T = 16384
    xp = np.asarray(inp["x_prompt"], np.float32)
    xs = np.asarray(inp["x_sample"], np.float32)
    cp = np.asarray(inp["c_prompt"], np.float32)
    cs = np.asarray(inp["c_sample"], np.float32)
    specs = []
    for s in range(2):
        specs.append((xs[s], np.repeat(cs[s:s + 1], NSEG, axis=0), T))
    for g in range(2):
        specs.append((xp[4 * g:4 * g + 4].reshape(T, D), cp[4 * g:4 * g + 4], T // NSEG))
    for _ in range(4):
        specs.append((np.zeros((T, D), np.float32), np.zeros((NSEG, D), np.float32), T // NSEG))
    ys = run_cores(specs, inp, T)
    y_sample = np.stack([ys[0], ys[1]], axis=0)
    y_prompt = np.concatenate([ys[2].reshape(4, T // NSEG, D), ys[3].reshape(4, T // NSEG, D)], axis=0)
    return (y_prompt, y_sample)
```
